# Optimizing a Trainium2 kernel written in Bass

```python
import jax, jax.numpy as jnp
from jax import lax
import numpy as np

D_MODEL = 1024
BATCH = 2
SEQ = 8192
DEPTH = 2
DEC_BATCH = 8
DEC_SEQ = 2048
PAST_LEN = 128

N_META = 16
MIX_WIDTH = D_MODEL
ATT_HEADS = 8
QK_NOPE = 64
QK_ROPE = 32
V_HEAD = 64
Q_LORA = 256
KV_LORA = 128
ATT_WIDTH = ATT_HEADS * V_HEAD
RWKV_HEAD = 64
RWKV_WIDTH = MIX_WIDTH - ATT_WIDTH
RWKV_HEADS = RWKV_WIDTH // RWKV_HEAD
DECAY_LORA = 64
ICLR_LORA = 64
GATE_LORA = 160
N_DIR = 2
D_FF = 2816
Q_BLOCK = 128
ROPE_THETA = 10000.0
RMS_EPS = 1e-6
LNX_EPS = 64e-5
MLA_COLS = Q_LORA + KV_LORA + QK_ROPE
RWKV_COLS = 3 * RWKV_WIDTH + N_DIR * DECAY_LORA + N_DIR * ICLR_LORA + GATE_LORA
IN_COLS = MLA_COLS + RWKV_COLS

kernel_name = 'hymba_mla_rwkv7_macaron_encoder'


def rmsnorm(x, g):
    xf = x.astype(jnp.float32)
    y = xf * lax.rsqrt(jnp.mean(xf * xf, axis=-1, keepdims=True) + RMS_EPS)
    return (y * g.astype(jnp.float32)).astype(x.dtype)


def swiglu(x, w_gate, w_up, w_down):
    return (jax.nn.silu(x @ w_gate) * (x @ w_up)) @ w_down


def rope_tables(length, dim):
    inv = 1.0 / (ROPE_THETA ** (jnp.arange(0, dim, 2, dtype=jnp.float32) / dim))
    ang = jnp.arange(length, dtype=jnp.float32)[:, None] * inv[None, :]
    return jnp.cos(ang), jnp.sin(ang)


def apply_rope(x, cos, sin):
    half = x.shape[-1] // 2
    x1 = x[..., :half].astype(jnp.float32)
    x2 = x[..., half:].astype(jnp.float32)
    return jnp.concatenate([x1 * cos - x2 * sin, x2 * cos + x1 * sin], axis=-1).astype(x.dtype)


def mla_attention(p_mla, q_norm, w_uq, kv_norm, w_ukv):
    b, l, _ = p_mla.shape
    c_q = rmsnorm(p_mla[..., :Q_LORA], q_norm)
    c_kv = rmsnorm(p_mla[..., Q_LORA:Q_LORA + KV_LORA], kv_norm)
    q = (c_q @ w_uq).reshape(b, l, ATT_HEADS, QK_NOPE + QK_ROPE)
    kv = (c_kv @ w_ukv).reshape(b, l, ATT_HEADS, QK_NOPE + V_HEAD)
    cos, sin = rope_tables(l, QK_ROPE)
    q_nope = q[..., :QK_NOPE]
    q_rope = apply_rope(q[..., QK_NOPE:], cos[:, None, :], sin[:, None, :])
    k_rope = apply_rope(p_mla[..., Q_LORA + KV_LORA:], cos, sin)
    k_nope = kv[..., :QK_NOPE]
    v = kv[..., QK_NOPE:]
    scale = (QK_NOPE + QK_ROPE) ** -0.5
    n_blk = -(-l // Q_BLOCK)
    pad = n_blk * Q_BLOCK - l

    def blocks(t):
        t = jnp.pad(t, ((0, 0), (0, pad), (0, 0), (0, 0)))
        return jnp.moveaxis(t.reshape(b, n_blk, Q_BLOCK, ATT_HEADS, t.shape[-1]), 1, 0)

    def attend(qb):
        qn, qr = qb
        s = jnp.einsum('bqhd,bkhd->bhqk', qn, k_nope) + jnp.einsum('bqhr,bkr->bhqk', qr, k_rope)
        pr = jax.nn.softmax(s.astype(jnp.float32) * scale, axis=-1)
        return jnp.einsum('bhqk,bkhd->bqhd', pr.astype(v.dtype), v)

    o = lax.map(attend, (blocks(q_nope), blocks(q_rope)))
    return jnp.moveaxis(o, 0, 1).reshape(b, n_blk * Q_BLOCK, ATT_WIDTH)[:, :l]


def token_shift_centred(p, mu):
    prev = jnp.pad(p, ((0, 0), (1, 0), (0, 0)))[:, :-1]
    nxt = jnp.pad(p, ((0, 0), (0, 1), (0, 0)))[:, 1:]
    return p + mu[0] * (prev - p) + mu[1] * (nxt - p)


def to_scan(t):
    t = jnp.moveaxis(t.astype(jnp.float32), 1, 0)
    t = jnp.stack([t[:, :, 0], t[::-1, :, 1]], axis=1)
    return t.reshape(t.shape[0], N_DIR, t.shape[2], RWKV_HEADS, RWKV_HEAD)


def rwkv7_scan(r, w, k, v, aa, bb):
    b = r.shape[2]
    s0 = jnp.zeros((N_DIR, b, RWKV_HEADS, RWKV_HEAD, RWKV_HEAD), jnp.float32)

    def step(S, inp):
        rt, wt, kt, vt, at, bt = inp
        sa = jnp.einsum('dbhvk,dbhk->dbhv', S, at)
        S = S * wt[..., None, :] + sa[..., :, None] * bt[..., None, :] + vt[..., :, None] * kt[..., None, :]
        return S, jnp.einsum('dbhvk,dbhk->dbhv', S, rt)

    _, ys = lax.scan(step, s0, (r, w, k, v, aa, bb))
    return ys


def rwkv7_bidir(p_rwkv, shift_mu, decay_w0, decay_w2, iclr_a0, iclr_a2, gate_g2,
                key_k_k, key_k_a, bonus_r_k, lnx_w, lnx_b):
    b, l, _ = p_rwkv.shape
    C = RWKV_WIDTH
    p = token_shift_centred(p_rwkv, shift_mu)
    o1 = 3 * C
    o2 = o1 + N_DIR * DECAY_LORA
    o3 = o2 + N_DIR * ICLR_LORA
    r = p[..., :C]
    k = p[..., C:2 * C]
    v = p[..., 2 * C:o1]
    dw = p[..., o1:o2].reshape(b, l, N_DIR, DECAY_LORA)
    da = p[..., o2:o3].reshape(b, l, N_DIR, ICLR_LORA)
    dg = p[..., o3:]
    wl = (decay_w0 + jnp.einsum('bldr,drc->bldc', jnp.tanh(dw), decay_w2)).astype(jnp.float32)
    wl = -jax.nn.softplus(-wl) - 0.5
    decay = jnp.exp(-jnp.exp(wl))
    a = jax.nn.sigmoid((iclr_a0 + jnp.einsum('bldr,drc->bldc', da, iclr_a2)).astype(jnp.float32))
    g = jax.nn.sigmoid(dg) @ gate_g2
    kk = (k * key_k_k).astype(jnp.float32).reshape(b, l, RWKV_HEADS, RWKV_HEAD)
    kk = (kk / jnp.maximum(jnp.linalg.norm(kk, axis=-1, keepdims=True), 1e-12)).reshape(b, l, C)
    k_dir = k.astype(jnp.float32)[:, :, None, :] * (1.0 + (a - 1.0) * key_k_a.astype(jnp.float32))
    both = lambda t: jnp.broadcast_to(t[:, :, None, :], (b, l, N_DIR, C))
    ys = rwkv7_scan(to_scan(both(r)), to_scan(decay), to_scan(k_dir), to_scan(both(v)),
                    to_scan(both(-kk)), to_scan(kk[:, :, None, :] * a))
    y = jnp.moveaxis(ys[:, 0] + ys[::-1, 1], 0, 1)
    mu = jnp.mean(y, axis=-1, keepdims=True)
    var = jnp.mean(jnp.square(y - mu), axis=-1, keepdims=True)
    yn = ((y - mu) * lax.rsqrt(var + LNX_EPS)).reshape(b, l, C) * lnx_w + lnx_b
    rh = r.astype(jnp.float32).reshape(b, l, RWKV_HEADS, RWKV_HEAD)
    kh = jnp.sum(k_dir, axis=2).reshape(b, l, RWKV_HEADS, RWKV_HEAD)
    vh = v.astype(jnp.float32).reshape(b, l, RWKV_HEADS, RWKV_HEAD)
    bonus = (jnp.sum(rh * kh * bonus_r_k, axis=-1, keepdims=True) * vh).reshape(b, l, C)
    return ((yn + bonus) * g.astype(jnp.float32)).astype(p_rwkv.dtype)


def encode(x, meta_tokens, ffn1_norm, ffn1_w_gate, ffn1_w_up, ffn1_w_down, mix_norm, w_in, shift_mu,
           q_norm, w_uq, kv_norm, w_ukv, decay_w0, decay_w2, iclr_a0, iclr_a2, gate_g2, key_k_k,
           key_k_a, bonus_r_k, lnx_w, lnx_b, w_out, ffn2_norm, ffn2_w_gate, ffn2_w_up, ffn2_w_down,
           final_norm):
    b = x.shape[0]
    meta = jnp.broadcast_to(meta_tokens.astype(x.dtype)[None], (b, N_META, D_MODEL))
    h = jnp.concatenate([meta, x], axis=1)
    for i in range(DEPTH):
        h = h + 0.5 * swiglu(rmsnorm(h, ffn1_norm[i]), ffn1_w_gate[i], ffn1_w_up[i], ffn1_w_down[i])
        proj = rmsnorm(h, mix_norm[i]) @ w_in[i]
        att = mla_attention(proj[..., :MLA_COLS], q_norm[i], w_uq[i], kv_norm[i], w_ukv[i])
        rw = rwkv7_bidir(proj[..., MLA_COLS:], shift_mu[i], decay_w0[i], decay_w2[i], iclr_a0[i],
                         iclr_a2[i], gate_g2[i], key_k_k[i], key_k_a[i], bonus_r_k[i], lnx_w[i], lnx_b[i])
        h = h + jnp.concatenate([att, rw], axis=-1) @ w_out[i]
        h = h + 0.5 * swiglu(rmsnorm(h, ffn2_norm[i]), ffn2_w_gate[i], ffn2_w_up[i], ffn2_w_down[i])
    return rmsnorm(h, final_norm)[:, N_META:]


def setup_inputs(seed: int = 0) -> dict:
    key = jax.random.key(seed)
    ks = jax.random.split(key, 40)
    f32 = jnp.float32
    nrm = lambda k, shape, fan_in: jax.random.normal(k, shape, f32) * fan_in ** -0.5
    gain = lambda k, shape: 1.0 + 0.02 * jax.random.normal(k, shape, f32)
    L = DEPTH
    C = RWKV_WIDTH
    return {
        'x_prompt': jax.random.normal(ks[0], (BATCH, SEQ, D_MODEL), f32),
        'x_sample': jax.random.normal(ks[1], (DEC_BATCH, DEC_SEQ, D_MODEL), f32),
        'meta_tokens': jax.random.normal(ks[2], (N_META, D_MODEL), f32),
        'ffn1_norm': gain(ks[3], (L, D_MODEL)),
        'ffn1_w_gate': nrm(ks[4], (L, D_MODEL, D_FF), D_MODEL),
        'ffn1_w_up': nrm(ks[5], (L, D_MODEL, D_FF), D_MODEL),
        'ffn1_w_down': nrm(ks[6], (L, D_FF, D_MODEL), D_FF),
        'mix_norm': gain(ks[7], (L, D_MODEL)),
        'w_in': nrm(ks[8], (L, D_MODEL, IN_COLS), D_MODEL),
        'shift_mu': jax.random.uniform(ks[9], (L, 2, RWKV_COLS), f32, 0.0, 0.5),
        'q_norm': gain(ks[10], (L, Q_LORA)),
        'w_uq': nrm(ks[11], (L, Q_LORA, ATT_HEADS * (QK_NOPE + QK_ROPE)), Q_LORA),
        'kv_norm': gain(ks[12], (L, KV_LORA)),
        'w_ukv': nrm(ks[13], (L, KV_LORA, ATT_HEADS * (QK_NOPE + V_HEAD)), KV_LORA),
        'decay_w0': jax.random.uniform(ks[14], (L, N_DIR, C), f32, -6.0, 1.0),
        'decay_w2': nrm(ks[15], (L, N_DIR, DECAY_LORA, C), DECAY_LORA),
        'iclr_a0': 0.1 * jax.random.normal(ks[16], (L, N_DIR, C), f32),
        'iclr_a2': nrm(ks[17], (L, N_DIR, ICLR_LORA, C), ICLR_LORA),
        'gate_g2': nrm(ks[18], (L, GATE_LORA, C), GATE_LORA),
        'key_k_k': 0.85 + 0.02 * jax.random.normal(ks[19], (L, C), f32),
        'key_k_a': gain(ks[20], (L, C)),
        'bonus_r_k': 0.1 * jax.random.normal(ks[21], (L, RWKV_HEADS, RWKV_HEAD), f32),
        'lnx_w': gain(ks[22], (L, C)),
        'lnx_b': 0.02 * jax.random.normal(ks[23], (L, C), f32),
        'w_out': nrm(ks[24], (L, MIX_WIDTH, D_MODEL), MIX_WIDTH),
        'ffn2_norm': gain(ks[25], (L, D_MODEL)),
        'ffn2_w_gate': nrm(ks[26], (L, D_MODEL, D_FF), D_MODEL),
        'ffn2_w_up': nrm(ks[27], (L, D_MODEL, D_FF), D_MODEL),
        'ffn2_w_down': nrm(ks[28], (L, D_FF, D_MODEL), D_FF),
        'final_norm': gain(ks[29], (D_MODEL,)),
    }


def reference(x_prompt, x_sample, meta_tokens, ffn1_norm, ffn1_w_gate, ffn1_w_up, ffn1_w_down, mix_norm,
              w_in, shift_mu, q_norm, w_uq, kv_norm, w_ukv, decay_w0, decay_w2, iclr_a0, iclr_a2, gate_g2,
              key_k_k, key_k_a, bonus_r_k, lnx_w, lnx_b, w_out, ffn2_norm, ffn2_w_gate, ffn2_w_up,
              ffn2_w_down, final_norm):
    weights = (meta_tokens, ffn1_norm, ffn1_w_gate, ffn1_w_up, ffn1_w_down, mix_norm, w_in, shift_mu,
               q_norm, w_uq, kv_norm, w_ukv, decay_w0, decay_w2, iclr_a0, iclr_a2, gate_g2, key_k_k,
               key_k_a, bonus_r_k, lnx_w, lnx_b, w_out, ffn2_norm, ffn2_w_gate, ffn2_w_up, ffn2_w_down,
               final_norm)
    y_prompt = encode(x_prompt, *weights)
    y_sample = encode(x_sample, *weights)
    return (y_prompt, y_sample)
```

```python
import os
import numpy as np
import ml_dtypes
from contextlib import ExitStack
import concourse.bass as bass
import concourse.mybir as mybir
from concourse.bass_utils import run_bass_kernel_spmd

F32 = mybir.dt.float32
BF16 = mybir.dt.bfloat16
AF = mybir.ActivationFunctionType
ALU = mybir.AluOpType
AX = mybir.AxisListType

D = 1024
DFF = 2816
NH = 8
NMETA = 16
RMS_EPS = 1e-6
LNX_EPS = 64e-5
SCALE = 96 ** -0.5
CDEC = float(np.exp(-0.5))
NCOLP = 2592

GROUPS = {}
_o = 0
for _n, _w in ([("cq0", 128), ("cq1", 128), ("ckv", 128), ("kr1", 128), ("kr2", 128)]
               + [(f"r{i}", 128) for i in range(4)] + [(f"k{i}", 128) for i in range(4)]
               + [(f"v{i}", 128) for i in range(4)] + [("dw", 128), ("da", 128), ("dg0", 128), ("dg1", 32)]):
    GROUPS[_n] = (_o, _w)
    _o += _w
assert _o == NCOLP
RW_GROUPS = [f"r{i}" for i in range(4)] + [f"k{i}" for i in range(4)] + [f"v{i}" for i in range(4)] + ["dw", "da", "dg0", "dg1"]


class Sched:
    ENG = ("pe", "act", "dve", "pool", "sp")

    def __init__(self, nc, es, n_dsem=12):
        self.nc = nc
        self.e = dict(pe=nc.tensor, act=nc.scalar, dve=nc.vector, pool=nc.gpsimd, sp=nc.sync)
        self.semobj = {}
        self.cnt = {}
        for k in self.ENG:
            self.semobj[("e", k)] = es.enter_context(nc.semaphore("s_" + k))
            self.cnt[k] = 0
        self.dq = {}
        self.dqi = {}
        for q in ("sp", "act", "pool"):
            self.dq[q] = []
            for i in range(n_dsem):
                self.semobj[("d", q, i)] = es.enter_context(nc.semaphore(f"d_{q}{i}"))
                self.dq[q].append(0)
            self.dqi[q] = 0
        self.seen = {k: {} for k in self.ENG}
        self.lastw = {}
        self.lastr = {}
        self.n_ins = 0

    def _wait(self, eng, sk, val):
        if val <= 0 or self.seen[eng].get(sk, 0) >= val:
            return
        if sk == ("e", "pe") and eng == "pe":
            return
        self.e[eng].wait_ge(self.semobj[sk], val)
        self.seen[eng][sk] = val

    def _deps(self, eng, r, w):
        for res in r:
            lw = self.lastw.get(res)
            if lw:
                self._wait(eng, *lw)
        for res in w:
            lw = self.lastw.get(res)
            if lw:
                self._wait(eng, *lw)
            for sk, v in self.lastr.get(res, {}).items():
                self._wait(eng, sk, v)

    def _mark(self, sk, v, r, w):
        for res in r:
            self.lastr.setdefault(res, {})[sk] = v
        for res in w:
            self.lastw[res] = (sk, v)
            self.lastr[res] = {}

    PSUM_NAMES = ("pT", "pG", "pU", "pD", "pA", "pN", "pS", "pO", "pB", "pX", "pY", "pZ")

    def op(self, eng, fn, r=(), w=()):
        extra = [k for k in r if (k[0] if isinstance(k, tuple) else k) in self.PSUM_NAMES]
        if extra:
            w = list(w) + extra
        self._deps(eng, r, w)
        ins = fn(self.e[eng])
        self.cnt[eng] += 1
        ins.then_inc(self.semobj[("e", eng)], 1)
        self._mark(("e", eng), self.cnt[eng], r, w)
        self.n_ins += 1
        return ins

    def dma(self, q, out, in_, r=(), w=()):
        self._deps(q, r, w)
        i = self.dqi[q]
        self.dqi[q] = (i + 1) % len(self.dq[q])
        sk = ("d", q, i)
        self._wait(q, sk, self.dq[q][i])
        self.dq[q][i] += 16
        self.e[q].dma_start(out=out, in_=in_).then_inc(self.semobj[sk], 16)
        self._mark(sk, self.dq[q][i], r, w)
        self.n_ins += 1

    def barrier(self):
        for eng in self.ENG:
            for k in self.ENG:
                self._wait(eng, ("e", k), self.cnt[k])
            for q in self.dq:
                for i, v in enumerate(self.dq[q]):
                    self._wait(eng, ("d", q, i), v)
        self.lastw = {}
        self.lastr = {}

    def finish(self):
        for k in self.ENG:
            self._wait("sp", ("e", k), self.cnt[k])
        for q in self.dq:
            for i, v in enumerate(self.dq[q]):
                self._wait("sp", ("d", q, i), v)


class Cfg:
    def __init__(self, LS=(2064, 8208), NL=2, debug=False, stages="all"):
        self.LS = list(LS)
        self.NL = NL
        self.debug = debug
        self.stages = stages
        self.NB = [-(-L // 128) for L in self.LS]
        self.LP = [nb * 128 for nb in self.NB]
        self.BASE = [0]
        for lp in self.LP[:-1]:
            self.BASE.append(self.BASE[-1] + lp)
        self.TP = sum(self.LP)
        self.NBT = self.TP // 128
        self.tiles = []
        b = 0
        while b < self.NBT:
            n = min(4, self.NBT - b)
            self.tiles.append((b, n))
            b += n


def build(cfg):
    nc = bass.Bass("TRN2", target_bir_lowering=False)
    NL, TP = cfg.NL, cfg.TP

    def din(name, shape, dt=F32):
        return nc.dram_tensor(name, list(shape), dt, kind="ExternalInput").ap()

    def dscr(name, shape, dt=F32):
        if cfg.debug:
            return nc.dram_tensor(name, list(shape), dt, kind="ExternalOutput").ap()
        return nc.dram_tensor(name, list(shape), dt).ap()

    xin = [din(f"x{s}", [cfg.LS[s] - NMETA, D]) for s in range(2)]
    yout = [nc.dram_tensor(f"y{s}", [cfg.LS[s] - NMETA, D], F32, kind="ExternalOutput").ap() for s in range(2)]
    meta = din("meta", [NMETA, D])
    wg = [din(f"ffn{k}_wg", [NL, D, DFF]) for k in (1, 2)]
    wu = [din(f"ffn{k}_wu", [NL, D, DFF]) for k in (1, 2)]
    wd = [din(f"ffn{k}_wd", [NL, DFF, D]) for k in (1, 2)]
    gains = din("gains", [128, NL, 3, 8])
    fnorm = din("fnorm", [128, D])
    ident_bf = din("ident_bf", [128, 128], BF16)
    zeros = din("zeros", [128, D])

    w_in = din("w_in", [NL, D, NCOLP])
    w_uq = din("w_uq", [NL, 256, 768])
    w_ukv = din("w_ukv", [NL, 128, 1024])
    w_out = din("w_out", [NL, D, D])
    qkg = din("qkg", [128, NL, 3])
    cosT = din("cosT", [128, TP])
    sinT = din("sinT", [128, TP])
    seln_in = din("seln", [128, 4, 32], BF16)
    selr_in = din("selr", [128, 32], BF16)
    ones_in = din("ones_bf", [128, 512], BF16)
    onesf_in = din("ones_f", [128, 128])
    onesrow_in = din("ones_row", [NH, 512], BF16)
    identf_in = din("ident_f", [128, 128])
    masks_in = din("masks", [128, 4, 128])
    mu_in = din("mu", [128, NL, 16, 2])
    rwp = din("rwp", [128, NL, NH, 9, 64])
    vmask_in = din("vmask", [128, 2])
    lw_in = din("lw", [NL, NH, 128, 2, 64])
    g2_in = din("g2", [NL, NH, 160, 64])

    H = dscr("H", [TP, D])
    XNT = dscr("XNT", [D, TP], BF16)
    QTd = dscr("QTd", [97, NH, TP], BF16)
    KTd = dscr("KTd", [97, NH, TP], BF16)
    VA = dscr("VA", [TP, NH, 65], BF16)
    QNd = dscr("QNd", [NH, TP])
    KNd = dscr("KNd", [NH, TP])
    RAW = dscr("RAW", [1952, TP])
    RWS = dscr("RWS", [1536, TP])
    LOR = dscr("LOR", [416, TP], BF16)
    MIXT = dscr("MIXT", [D, TP], BF16)
    EPI = dscr("EPI", [TP, 2, 64])

    es = ExitStack()
    S = Sched(nc, es)

    uid = [0]

    def sb(st, name, shape, dt):
        uid[0] += 1
        return st.enter_context(nc.sbuf_tensor(f"{name}_{uid[0]}", list(shape), dt))

    def ps(st, name, shape, dt=F32):
        uid[0] += 1
        return st.enter_context(nc.psum_tensor(f"{name}_{uid[0]}", list(shape), dt))

    cst = ExitStack()
    ident = sb(cst, "ident", [128, 128], BF16)
    gn = sb(cst, "gn", [128, NL, 3, 8], F32)
    fn_sb = sb(cst, "fn_sb", [128, D], F32)
    S.dma("sp", ident[:], ident_bf, w=["ident"])
    S.dma("sp", gn[:], gains, w=["gn"])
    S.dma("sp", fn_sb[:], fnorm, w=["fn"])

    for s in range(2):
        b0 = cfg.BASE[s]
        L = cfg.LS[s]
        S.dma("sp", H[b0:b0 + NMETA, :], meta, w=[("H", "init", s)])
        nrow = L - NMETA
        r0 = 0
        while r0 < nrow:
            n = min(2048, nrow - r0)
            S.dma("act" if (r0 // 2048) % 2 else "sp", H[b0 + NMETA + r0:b0 + NMETA + r0 + n, :], xin[s][r0:r0 + n, :], w=[("H", "init", s, r0)])
            r0 += n
        if cfg.LP[s] > L:
            S.dma("pool", H[b0 + L:b0 + cfg.LP[s], :], zeros[0:cfg.LP[s] - L, :], w=[("H", "initz", s)])
    S.barrier()

    def ffn_pass(l, k, half, with_norm, final=False):
        st = ExitStack()
        NF = 11
        Wg = sb(st, "Wg", [128, 8, NF * 128], BF16)
        Wu = sb(st, "Wu", [128, 8, NF * 128], BF16)
        Wd = sb(st, "Wd", [128, NF, D], BF16)
        stg = [sb(st, f"stg{i}", [128, NF * 128], F32) for i in range(2)]
        hb = [sb(st, f"hb{i}", [128, D], F32) for i in range(2)]
        hc = [sb(st, f"hc{i}", [128, D], F32) for i in range(2)]
        xnb = sb(st, "xnb", [128, 4, D], BF16)
        xnT = [sb(st, f"xnT{i}", [128, 8, 512], BF16) for i in range(2)]
        hT = [sb(st, f"hT{i}", [128, NF, 512], BF16) for i in range(2)]
        sg = [sb(st, f"sg{i}", [128, 512], F32) for i in range(2)]
        ssq = sb(st, "ssq", [128, 8], F32)
        rs = sb(st, "rs", [128, 8], F32)
        junk = sb(st, "junk", [128, D], BF16)
        yb = [sb(st, f"yb{i}", [128, D], F32) for i in range(2)]
        pT = [ps(st, f"pT{i}", [128, 1024], BF16) for i in range(2)]
        pG = [ps(st, f"pG{i}", [128, 512]) for i in range(2)]
        pU = [ps(st, f"pU{i}", [128, 512]) for i in range(2)]
        pD = [ps(st, f"pD{i}", [128, 512]) for i in range(2)]
        f0 = half * NF * 128
        do_mix = (k == 1 and half == 0 and cfg.stages != "ffn")
        if do_mix:
            Wout = sb(st, "Wout", [128, 8, D], BF16)
            mx = [sb(st, f"mx{i}", [128, 8, 512], BF16) for i in range(2)]
            for fc in range(8):
                load_cast(Wout[:, fc, :], w_out[l, fc * 128:(fc + 1) * 128, :], stg[fc % 2][:, 0:D], ("stg", fc % 2), "sp" if fc % 2 == 0 else "act", ["dve", "pool"][fc % 2])
        ci = 0
        cast_eng = ["dve", "pool", "act"]
        for (Wsb, wsrc) in ((Wg, wg[k]), (Wu, wu[k])):
            for dc in range(8):
                sgi = ci % 2
                S.dma("sp" if ci % 2 == 0 else "act", stg[sgi][:], wsrc[l, dc * 128:(dc + 1) * 128, f0:f0 + NF * 128], w=[("stg", sgi)])
                eng = cast_eng[ci % 3]
                if eng == "act":
                    S.op("act", lambda e, o=Wsb[:, dc, :], i=stg[sgi][:]: e.copy(out=o, in_=i), r=[("stg", sgi)], w=[("W",)])
                else:
                    S.op(eng, lambda e, o=Wsb[:, dc, :], i=stg[sgi][:]: e.tensor_copy(out=o, in_=i), r=[("stg", sgi)], w=[("W",)])
                ci += 1
        for fc in range(NF):
            sgi = ci % 2
            S.dma("sp" if ci % 2 == 0 else "act", stg[sgi][:, 0:D], wd[k][l, f0 + fc * 128:f0 + (fc + 1) * 128, :], w=[("stg", sgi)])
            eng = cast_eng[ci % 3]
            if eng == "act":
                S.op("act", lambda e, o=Wd[:, fc, :], i=stg[sgi][:, 0:D]: e.copy(out=o, in_=i), r=[("stg", sgi)], w=[("W",)])
            else:
                S.op(eng, lambda e, o=Wd[:, fc, :], i=stg[sgi][:, 0:D]: e.tensor_copy(out=o, in_=i), r=[("stg", sgi)], w=[("W",)])
            ci += 1
        gidx = 0 if k == 0 else 2

        def stageA(ti):
            b0, nblk = cfg.tiles[ti]
            nt = nblk * 128
            xt = xnT[ti % 2]
            XK = ("xnT", ti % 2)
            if not with_norm:
                S.dma("sp", xt[:, :, 0:nt], XNT.rearrange("(c p) n -> p c n", p=128)[:, :, b0 * 128:b0 * 128 + nt], r=[("XNT", ti)], w=[XK])
                return
            if do_mix:
                mt = mx[ti % 2]
                MXK = ("mx", ti % 2)
                S.dma("pool", mt[:, :, 0:nt], MIXT.rearrange("(c p) n -> p c n", p=128)[:, :, b0 * 128:b0 * 128 + nt], r=[("MIXT",)], w=[MXK])
            for b in range(nblk):
                h = hb[b % 2]
                HK = ("hb", b % 2)
                S.dma("sp" if b % 2 == 0 else "act", h[:], H[(b0 + b) * 128:(b0 + b + 1) * 128, :], r=[("H", b0 + b)], w=[HK])
                if do_mix:
                    for hf in range(2):
                        for fc in range(8):
                            S.op("pe", lambda e, fc=fc, hf=hf, b=b: e.matmul(pD[hf][:, :], lhsT=mt[:, fc, b * 128:(b + 1) * 128], rhs=Wout[:, fc, hf * 512:(hf + 1) * 512], start=(fc == 0), stop=(fc == 7)),
                                 r=[MXK, "W"], w=[("pD", hf)])
                        S.op("dve", lambda e, hf=hf, h=h: e.tensor_tensor(out=h[:, hf * 512:(hf + 1) * 512], in0=pD[hf][:, :], in1=h[:, hf * 512:(hf + 1) * 512], op=ALU.add), r=[("pD", hf), HK], w=[HK])
                    S.dma("pool", H[(b0 + b) * 128:(b0 + b + 1) * 128, :], h[:], r=[HK], w=[("H", b0 + b)])
                S.op("act", lambda e, h=h, b=b: e.activation(out=junk[:], in_=h[:], func=AF.Square, accum_out=ssq[:, b:b + 1]), r=[HK], w=["junk", ("ssq", b)])
                S.op("dve", lambda e, b=b: e.tensor_scalar(out=rs[:, b:b + 1], in0=ssq[:, b:b + 1], scalar1=1.0 / D, scalar2=RMS_EPS, op0=ALU.mult, op1=ALU.add), r=[("ssq", b)], w=[("rs", b)])
                S.op("act", lambda e, b=b: e.activation(out=rs[:, b:b + 1], in_=rs[:, b:b + 1], func=AF.Sqrt), r=[("rs", b)], w=[("rs", b)])
                S.op("dve", lambda e, b=b: e.reciprocal(out=rs[:, b:b + 1], in_=rs[:, b:b + 1]), r=[("rs", b)], w=[("rs", b)])
                S.op("pool", lambda e, h=h, b=b: e.tensor_scalar(out=xnb[:, b, :], in0=h[:], scalar1=rs[:, b:b + 1], scalar2=None, op0=ALU.mult), r=[HK, ("rs", b)], w=[("xnb", b)])
            for rnd in range(2):
                for b in range(nblk):
                    for j in range(4):
                        dc = rnd * 4 + j
                        S.op("pe", lambda e, b=b, dc=dc, j=j: e.transpose(out=pT[j // 2][:, (j % 2) * 512 + b * 128:(j % 2) * 512 + (b + 1) * 128], in_=xnb[:, b, dc * 128:(dc + 1) * 128], identity=ident[:]),
                             r=[("xnb", b), "ident"], w=[("pT", j // 2)])
                for j in range(4):
                    dc = rnd * 4 + j
                    eng = "act" if j % 2 == 0 else "dve"
                    if eng == "act":
                        S.op("act", lambda e, dc=dc, j=j: e.activation(out=xt[:, dc, 0:nt], in_=pT[j // 2][:, (j % 2) * 512:(j % 2) * 512 + nt], func=AF.Copy, scale=gn[:, l, gidx, dc:dc + 1]),
                             r=[("pT", j // 2), "gn"], w=[XK])
                    else:
                        S.op("dve", lambda e, dc=dc, j=j: e.tensor_scalar(out=xt[:, dc, 0:nt], in0=pT[j // 2][:, (j % 2) * 512:(j % 2) * 512 + nt], scalar1=gn[:, l, gidx, dc:dc + 1], scalar2=None, op0=ALU.mult),
                             r=[("pT", j // 2), "gn"], w=[XK])
            if half == 0:
                S.dma("pool", XNT.rearrange("(c p) n -> p c n", p=128)[:, :, b0 * 128:b0 * 128 + nt], xt[:, :, 0:nt], r=[XK], w=[("XNT", ti)])

        def stageB(ti):
            b0, nblk = cfg.tiles[ti]
            nt = nblk * 128
            xt = xnT[ti % 2]
            XK = ("xnT", ti % 2)
            ht = hT[ti % 2]
            HTK = ("hT", ti % 2)
            for fc in range(NF):
                for dc in range(8):
                    S.op("pe", lambda e, fc=fc, dc=dc: e.matmul(pG[fc % 2][:, 0:nt], lhsT=Wg[:, dc, fc * 128:(fc + 1) * 128], rhs=xt[:, dc, 0:nt], start=(dc == 0), stop=(dc == 7)),
                         r=[XK, ("W",), "W"], w=[("pG", fc % 2)])
                for dc in range(8):
                    S.op("pe", lambda e, fc=fc, dc=dc: e.matmul(pU[fc % 2][:, 0:nt], lhsT=Wu[:, dc, fc * 128:(fc + 1) * 128], rhs=xt[:, dc, 0:nt], start=(dc == 0), stop=(dc == 7)),
                         r=[XK, ("W",), "W"], w=[("pU", fc % 2)])
                S.op("act", lambda e, fc=fc: e.activation(out=sg[fc % 2][:, 0:nt], in_=pG[fc % 2][:, 0:nt], func=AF.Silu), r=[("pG", fc % 2)], w=[("sg", fc % 2)])
                S.op("dve", lambda e, fc=fc: e.tensor_tensor(out=ht[:, fc, 0:nt], in0=sg[fc % 2][:, 0:nt], in1=pU[fc % 2][:, 0:nt], op=ALU.mult), r=[("sg", fc % 2), ("pU", fc % 2)], w=[HTK])

        def stageC(ti):
            b0, nblk = cfg.tiles[ti]
            ht = hT[ti % 2]
            HTK = ("hT", ti % 2)
            for b in range(nblk):
                h = hc[b % 2]
                HK = ("hc", b % 2)
                S.dma("act" if b % 2 == 0 else "sp", h[:], H[(b0 + b) * 128:(b0 + b + 1) * 128, :], r=[("H", b0 + b)], w=[HK])
                for hf in range(2):
                    for fc in range(NF):
                        S.op("pe", lambda e, fc=fc, hf=hf, b=b: e.matmul(pD[hf][:, :], lhsT=ht[:, fc, b * 128:(b + 1) * 128], rhs=Wd[:, fc, hf * 512:(hf + 1) * 512], start=(fc == 0), stop=(fc == NF - 1)),
                             r=[HTK, ("W",), "W"], w=[("pD", hf)])
                    S.op("dve", lambda e, hf=hf, h=h: e.scalar_tensor_tensor(out=h[:, hf * 512:(hf + 1) * 512], in0=pD[hf][:, :], scalar=0.5, in1=h[:, hf * 512:(hf + 1) * 512], op0=ALU.mult, op1=ALU.add),
                         r=[("pD", hf), HK], w=[HK])
                if not final:
                    S.dma("pool", H[(b0 + b) * 128:(b0 + b + 1) * 128, :], h[:], r=[HK], w=[("H", b0 + b)])
                else:
                    y = yb[b % 2]
                    YK = ("yb", b % 2)
                    c = 4 + (b % 2)
                    S.op("act", lambda e, h=h, c=c: e.activation(out=junk[:], in_=h[:], func=AF.Square, accum_out=ssq[:, c:c + 1]), r=[HK], w=["junk", ("ssq", c)])
                    S.op("dve", lambda e, c=c: e.tensor_scalar(out=rs[:, c:c + 1], in0=ssq[:, c:c + 1], scalar1=1.0 / D, scalar2=RMS_EPS, op0=ALU.mult, op1=ALU.add), r=[("ssq", c)], w=[("rs", c)])
                    S.op("act", lambda e, c=c: e.activation(out=rs[:, c:c + 1], in_=rs[:, c:c + 1], func=AF.Sqrt), r=[("rs", c)], w=[("rs", c)])
                    S.op("dve", lambda e, c=c: e.reciprocal(out=rs[:, c:c + 1], in_=rs[:, c:c + 1]), r=[("rs", c)], w=[("rs", c)])
                    S.op("dve", lambda e, h=h, y=y, c=c: e.scalar_tensor_tensor(out=y[:], in0=h[:], scalar=rs[:, c:c + 1], in1=fn_sb[:], op0=ALU.mult, op1=ALU.mult), r=[HK, ("rs", c), "fn"], w=[YK])
                    g0 = (b0 + b) * 128
                    for s in range(2):
                        lo = max(g0, cfg.BASE[s] + NMETA)
                        hi = min(g0 + 128, cfg.BASE[s] + cfg.LS[s])
                        if hi > lo:
                            S.dma("pool", yout[s][lo - cfg.BASE[s] - NMETA:hi - cfg.BASE[s] - NMETA, :], y[lo - g0:hi - g0, :], r=[YK], w=[("y", s, lo)])

        nT = len(cfg.tiles)
        stageA(0)
        for ti in range(nT):
            stageB(ti)
            if ti + 1 < nT:
                stageA(ti + 1)
            stageC(ti)
        S.barrier()
        st.close()


    seln = sb(cst, "seln", [128, 4, 32], BF16)
    selr = sb(cst, "selr", [128, 32], BF16)
    ones_sb = sb(cst, "ones_sb", [128, 512], BF16)
    onesf = sb(cst, "onesf", [128, 128], F32)
    identf = sb(cst, "identf", [128, 128], F32)
    qk_sb = sb(cst, "qk_sb", [128, NL, 3], F32)
    S.dma("sp", seln[:], seln_in, w=["seln"])
    S.dma("sp", selr[:], selr_in, w=["selr"])
    S.dma("sp", ones_sb[:], ones_in, w=["ones"])
    S.dma("sp", onesf[:], onesf_in, w=["onesf"])
    S.dma("sp", identf[:], identf_in, w=["identf"])
    S.dma("sp", qk_sb[:], qkg, w=["qk"])
    S.barrier()

    def load_cast(dst, src, stg_ap, stg_key, q, eng):
        S.dma(q, stg_ap, src, w=[stg_key])
        if eng == "act":
            S.op("act", lambda e: e.copy(out=dst, in_=stg_ap), r=[stg_key], w=["W"])
        else:
            S.op(eng, lambda e: e.tensor_copy(out=dst, in_=stg_ap), r=[stg_key], w=["W"])

    def proj_pass(l):
        st = ExitStack()
        Win = sb(st, "Win", [128, 8, NCOLP], BF16)
        Wuq = sb(st, "Wuq", [128, 2, 768], BF16)
        Wukv = sb(st, "Wukv", [128, 1024], BF16)
        stg = [sb(st, f"stg{i}", [128, NCOLP], F32) for i in range(2)]
        hb = [sb(st, f"hb{i}", [128, D], F32) for i in range(2)]
        xnb = sb(st, "xnb", [128, 4, D], BF16)
        xnT = [sb(st, f"xnT{i}", [128, 8, 512], BF16) for i in range(2)]
        ssq = sb(st, "ssq", [128, 8], F32)
        rs = sb(st, "rs", [128, 8], F32)
        junk = sb(st, "junk", [128, D], BF16)
        cq_sb = sb(st, "cq_sb", [128, 3, 512], F32)
        sqb = [sb(st, f"sqb{i}", [128, 512], BF16) for i in range(2)]
        sq6 = sb(st, "sq6", [128, 6, 512], BF16)
        if os.environ.get("DUMMY_KB"):
            dummy = sb(st, "dummy", [128, int(os.environ["DUMMY_KB"]) * 256], F32)
        rstd = [sb(st, f"rstd{i}", [128, 512], F32) for i in range(2)]
        cqn = sb(st, "cqn", [128, 2, 512], BF16)
        ckvn = sb(st, "ckvn", [128, 512], BF16)
        qn_sb = sb(st, "qn_sb", [128, 4, 512], BF16)
        kn_sb = sb(st, "kn_sb", [128, 4, 512], BF16)
        xr = [sb(st, f"xr{i}", [128, 512], F32) for i in range(2)]
        tt = [sb(st, f"tt{i}", [128, 512], F32) for i in range(4)]
        rr = [sb(st, f"rr{i}", [128, 512], BF16) for i in range(4)]
        cs = sb(st, "cs", [128, 512], F32)
        sn = sb(st, "sn", [128, 512], F32)
        VAt = [sb(st, f"VAt{i}", [128, 4, NH, 65], BF16) for i in range(2)]
        rwb = [sb(st, f"rwb{i}", [128, 512], F32) for i in range(4)]
        nrm = sb(st, "nrm", [8, 2, 512], F32)
        pT = [ps(st, f"pT{i}", [128, 1024], BF16) for i in range(2)]
        pA = [ps(st, f"pA{i}", [128, 512]) for i in range(5)]
        pN = ps(st, "pN", [128, 512])
        for i in range(2):
            S.op("pool", lambda e, i=i: e.memset(VAt[i][:], 1.0), w=[("VAt", i)])
        for dc in range(8):
            load_cast(Win[:, dc, :], w_in[l, dc * 128:(dc + 1) * 128, :], stg[dc % 2][:], ("stg", dc % 2), "sp" if dc % 2 == 0 else "act", ["dve", "pool"][dc % 2])
        load_cast(Wuq[:], w_uq[l].rearrange("(k p) n -> p k n", p=128), stg[0][:, 0:1536].rearrange("p (k n) -> p k n", k=2), ("stg", 0), "sp", "dve")
        load_cast(Wukv[:], w_ukv[l], stg[1][:, 0:1024], ("stg", 1), "act", "pool")

        def stageA(ti):
            b0, nblk = cfg.tiles[ti]
            nt = nblk * 128
            xt = xnT[ti % 2]
            XK = ("xnT", ti % 2)
            for b in range(nblk):
                h = hb[b % 2]
                HK = ("hb", b % 2)
                S.dma("sp" if b % 2 == 0 else "act", h[:], H[(b0 + b) * 128:(b0 + b + 1) * 128, :], r=[("H", b0 + b)], w=[HK])
                S.op("act", lambda e, h=h, b=b: e.activation(out=junk[:], in_=h[:], func=AF.Square, accum_out=ssq[:, b:b + 1]), r=[HK], w=["junk", ("ssq", b)])
                S.op("dve", lambda e, b=b: e.tensor_scalar(out=rs[:, b:b + 1], in0=ssq[:, b:b + 1], scalar1=1.0 / D, scalar2=RMS_EPS, op0=ALU.mult, op1=ALU.add), r=[("ssq", b)], w=[("rs", b)])
                S.op("act", lambda e, b=b: e.activation(out=rs[:, b:b + 1], in_=rs[:, b:b + 1], func=AF.Sqrt), r=[("rs", b)], w=[("rs", b)])
                S.op("dve", lambda e, b=b: e.reciprocal(out=rs[:, b:b + 1], in_=rs[:, b:b + 1]), r=[("rs", b)], w=[("rs", b)])
                S.op("pool", lambda e, h=h, b=b: e.tensor_scalar(out=xnb[:, b, :], in0=h[:], scalar1=rs[:, b:b + 1], scalar2=None, op0=ALU.mult), r=[HK, ("rs", b)], w=[("xnb", b)])
            for rnd in range(2):
                for b in range(nblk):
                    for j in range(4):
                        dc = rnd * 4 + j
                        S.op("pe", lambda e, b=b, dc=dc, j=j: e.transpose(out=pT[j // 2][:, (j % 2) * 512 + b * 128:(j % 2) * 512 + (b + 1) * 128], in_=xnb[:, b, dc * 128:(dc + 1) * 128], identity=ident[:]),
                             r=[("xnb", b), "ident"], w=[("pT", j // 2)])
                for j in range(4):
                    dc = rnd * 4 + j
                    if j % 2 == 0:
                        S.op("act", lambda e, dc=dc, j=j: e.activation(out=xt[:, dc, 0:nt], in_=pT[j // 2][:, (j % 2) * 512:(j % 2) * 512 + nt], func=AF.Copy, scale=gn[:, l, 1, dc:dc + 1]),
                             r=[("pT", j // 2), "gn"], w=[XK])
                    else:
                        S.op("dve", lambda e, dc=dc, j=j: e.tensor_scalar(out=xt[:, dc, 0:nt], in0=pT[j // 2][:, (j % 2) * 512:(j % 2) * 512 + nt], scalar1=gn[:, l, 1, dc:dc + 1], scalar2=None, op0=ALU.mult),
                             r=[("pT", j // 2), "gn"], w=[XK])

        def stageB(ti):
            b0, nblk = cfg.tiles[ti]
            nt = nblk * 128
            c0 = b0 * 128
            xt = xnT[ti % 2]
            XK = ("xnT", ti % 2)
            S.dma("sp", cs[:, 0:nt], cosT[:, c0:c0 + nt], w=["cs"])
            S.dma("act", sn[:, 0:nt], sinT[:, c0:c0 + nt], w=["sn"])
            bank = [0]
            evi = [0]

            def nextbank():
                i = bank[0] % 5
                bank[0] += 1
                return pA[i], ("pA", i)

            def evac(out, in_, r, w):
                evi[0] += 1
                if evi[0] % 2 == 0:
                    S.op("act", lambda e: e.copy(out=out, in_=in_), r=r, w=w)
                else:
                    S.op("dve", lambda e: e.tensor_copy(out=out, in_=in_), r=r, w=w)

            def win_group(name):
                off, wdt = GROUPS[name]
                p, pk = nextbank()
                for dc in range(8):
                    S.op("pe", lambda e, dc=dc: e.matmul(p[0:wdt, 0:nt], lhsT=Win[:, dc, off:off + wdt], rhs=xt[:, dc, 0:nt], start=(dc == 0), stop=(dc == 7)), r=[XK, "W"], w=[pk])
                return p, pk

            def rms_feat(chunks, n_feat, gcol0, outs):
                for j, (p, pk, sbuf, sk) in enumerate(chunks):
                    S.op("act", lambda e, p=p, sbuf=sbuf: e.copy(out=sbuf, in_=p[:, 0:nt]), r=[pk], w=[sk])
                    S.op("act", lambda e, p=p, j=j: e.activation(out=sqb[j][:, 0:nt], in_=p[:, 0:nt], func=AF.Square), r=[pk], w=[("sqb", j)])
                p2, pk2 = nextbank()
                for j in range(len(chunks)):
                    S.op("pe", lambda e, j=j: e.matmul(p2[:, 0:nt], lhsT=ones_sb[:, 0:128], rhs=sqb[j][:, 0:nt], start=(j == 0), stop=(j == len(chunks) - 1)), r=[("sqb", j), "ones"], w=[pk2])
                rsd = rstd[0]
                S.op("dve", lambda e: e.tensor_scalar(out=rsd[:, 0:nt], in0=p2[:, 0:nt], scalar1=1.0 / n_feat, scalar2=RMS_EPS, op0=ALU.mult, op1=ALU.add), r=[pk2], w=["rstd"])
                S.op("act", lambda e: e.activation(out=rsd[:, 0:nt], in_=rsd[:, 0:nt], func=AF.Sqrt), r=["rstd"], w=["rstd"])
                S.op("dve", lambda e: e.reciprocal(out=rsd[:, 0:nt], in_=rsd[:, 0:nt]), r=["rstd"], w=["rstd"])
                for j, (p, pk, sbuf, sk) in enumerate(chunks):
                    o, ok = outs[j]
                    S.op("dve", lambda e, sbuf=sbuf, o=o, j=j: e.scalar_tensor_tensor(out=o, in0=sbuf, scalar=qk_sb[:, l, gcol0 + j:gcol0 + j + 1], in1=rsd[:, 0:nt], op0=ALU.mult, op1=ALU.mult),
                         r=[sk, "rstd", "qk"], w=[ok])

            ch = []
            for j in range(2):
                p, pk = win_group(f"cq{j}")
                ch.append((p, pk, cq_sb[:, j, 0:nt], ("cq", j)))
            rms_feat(ch, 256, 0, [(cqn[:, 0, 0:nt], ("cqn", 0)), (cqn[:, 1, 0:nt], ("cqn", 1))])
            p, pk = win_group("ckv")
            rms_feat([(p, pk, cq_sb[:, 2, 0:nt], ("cq", 2))], 128, 2, [(ckvn[:, 0:nt], "ckvn")])

            def rope(x1, x2, o1, o2, k1, k2, ok1, ok2):
                S.op("pool", lambda e: e.tensor_tensor(out=tt[0][:, 0:nt], in0=x1[:, 0:nt], in1=cs[:, 0:nt], op=ALU.mult), r=[k1, "cs"], w=[("tt", 0)])
                S.op("pool", lambda e: e.tensor_tensor(out=tt[1][:, 0:nt], in0=x2[:, 0:nt], in1=sn[:, 0:nt], op=ALU.mult), r=[k2, "sn"], w=[("tt", 1)])
                S.op("dve", lambda e: e.tensor_tensor(out=o1[:, 0:nt], in0=tt[0][:, 0:nt], in1=tt[1][:, 0:nt], op=ALU.subtract), r=[("tt", 0), ("tt", 1)], w=[ok1])
                S.op("pool", lambda e: e.tensor_tensor(out=tt[2][:, 0:nt], in0=x2[:, 0:nt], in1=cs[:, 0:nt], op=ALU.mult), r=[k2, "cs"], w=[("tt", 2)])
                S.op("pool", lambda e: e.tensor_tensor(out=tt[3][:, 0:nt], in0=x1[:, 0:nt], in1=sn[:, 0:nt], op=ALU.mult), r=[k1, "sn"], w=[("tt", 3)])
                S.op("dve", lambda e: e.tensor_tensor(out=o2[:, 0:nt], in0=tt[2][:, 0:nt], in1=tt[3][:, 0:nt], op=ALU.add), r=[("tt", 2), ("tt", 3)], w=[ok2])

            SUB = os.environ.get("QK_SUB", "namdep")

            def qk_side(which, nope_sb, dst, nd, slot):
                KSKIP = os.environ.get("K_SKIP", "")
                for c in range(0 if (which == "k" and "nope" in KSKIP) else 4):
                    p, pk = nextbank()
                    if which == "q":
                        for kc in range(2):
                            S.op("pe", lambda e, kc=kc, c=c: e.matmul(p[:, 0:nt], lhsT=Wuq[:, kc, c * 128:(c + 1) * 128], rhs=cqn[:, kc, 0:nt], start=(kc == 0), stop=(kc == 1)), r=[("cqn", kc), "W"], w=[pk])
                    else:
                        if os.environ.get("KSPLIT"):
                            for hh in range(2):
                                S.op("pe", lambda e, c=c, hh=hh: e.matmul(p[:, 0:nt], lhsT=Wukv[64 * hh:64 * hh + 64, c * 128:(c + 1) * 128], rhs=ckvn[64 * hh:64 * hh + 64, 0:nt], start=(hh == 0), stop=(hh == 1)), r=["ckvn", "W"], w=[pk])
                        else:
                            S.op("pe", lambda e, c=c: e.matmul(p[:, 0:nt], lhsT=Wukv[:, c * 128:(c + 1) * 128], rhs=ckvn[:, 0:nt], start=True, stop=True), r=["ckvn", "W"], w=[pk])
                    S.op("act", lambda e, c=c: e.copy(out=nope_sb[:, c, 0:nt], in_=p[:, 0:nt]), r=[pk], w=[(which + "n", c)])
                    S.op("act", lambda e, c=c: e.activation(out=sq6[:, c, 0:nt], in_=p[:, 0:nt], func=AF.Square), r=[pk], w=[("sq6", c)])
                if os.environ.get("KBAR"):
                    S.barrier()
                for j in range(0 if (which == "k" and "rope" in KSKIP) else 2):
                    if which == "q":
                        p, pk = nextbank()
                        for kc in range(2):
                            S.op("pe", lambda e, kc=kc, j=j: e.matmul(p[:, 0:nt], lhsT=Wuq[:, kc, 512 + j * 128:512 + (j + 1) * 128], rhs=cqn[:, kc, 0:nt], start=(kc == 0), stop=(kc == 1)), r=[("cqn", kc), "W"], w=[pk])
                    else:
                        p, pk = win_group(f"kr{j + 1}")
                    S.op("dve", lambda e, j=j: e.tensor_copy(out=xr[j][:, 0:nt], in_=p[:, 0:nt]), r=[pk], w=[("xr", j)])
                    S.op("act", lambda e, j=j: e.activation(out=sq6[:, 4 + j, 0:nt], in_=p[:, 0:nt], func=AF.Square), r=[pk], w=[("sq6", 4 + j)])
                if "n" in SUB:
                    for c in range(6):
                        S.op("pe", lambda e, c=c: e.matmul(pN[0:32, 0:nt], lhsT=(seln[:, c, :] if c < 4 else selr[:, :]), rhs=sq6[:, c, 0:nt], start=(c == 0), stop=(c == 5)), r=[("sq6", c), "seln", "selr"], w=["pN"])
                o1, o2 = rr[2 * slot], rr[2 * slot + 1]
                if "p" in SUB:
                    rope(xr[0], xr[1], o1, o2, ("xr", 0), ("xr", 1), ("rr", 2 * slot), ("rr", 2 * slot + 1))
                if "a" not in SUB:
                    pass
                elif which == "q":
                    S.op("act", lambda e: e.activation(out=nrm[:, slot, 0:nt], in_=pN[0:8, 0:nt], func=AF.Sqrt), r=["pN"], w=[("nrm", slot)])
                else:
                    S.op("act", lambda e: e.copy(out=nrm[:, slot, 0:nt], in_=pN[0:8, 0:nt]), r=["pN"], w=[("nrm", slot)])
                if "m" in SUB:
                    S.dma("pool", nd[:, c0:c0 + nt], nrm[:, slot, 0:nt], r=[("nrm", slot)], w=[(which + "nd", ti)])
                for two in range(2 if "d" in SUB else 0):
                    S.dma("sp" if two == 0 else "act", dst[0:64, :, c0:c0 + nt].rearrange("r (c two) n -> r c two n", two=2)[:, :, two, :], nope_sb[64 * two:64 * two + 64, :, 0:nt], r=[(which + "n", c) for c in range(4)], w=[(which + "T", ti, two)])
                if "e" in SUB:
                    S.dma("pool", dst[64:80, :, c0:c0 + nt].rearrange("i h n -> (i h) n"), o1[:, 0:nt], r=[("rr", 2 * slot)], w=[(which + "T", ti, 2)])
                    S.dma("pool", dst[80:96, :, c0:c0 + nt].rearrange("i h n -> (i h) n"), o2[:, 0:nt], r=[("rr", 2 * slot + 1)], w=[(which + "T", ti, 3)])

            PARTS = os.environ.get("PROJ_PARTS", "qkovr")
            if "q" in PARTS:
                qk_side("q", qn_sb, QTd, QNd, 0)
            if "k" in PARTS:
                qk_side("k", kn_sb, KTd, KNd, 1)
            if "o" in PARTS:
                S.dma("sp", KTd[96, :, c0:c0 + nt], onesrow_in[:, 0:nt], w=[("kT", ti, 4)])
            vt = VAt[ti % 2]
            VK = ("VAt", ti % 2)
            for b in range(nblk if "v" in PARTS else 0):
                p, pk = nextbank()
                S.op("pe", lambda e, b=b: e.matmul(p[:, 0:512], lhsT=ckvn[:, b * 128:(b + 1) * 128], rhs=Wukv[:, 512:1024], start=True, stop=True), r=["ckvn", "W"], w=[pk])
                evac(vt[:, b, :, 0:64], p[:, 0:512].rearrange("p (h d) -> p h d", d=64), [pk], [VK])
            if "v" in PARTS:
                S.dma("sp", VA[c0:c0 + nt, :, :].rearrange("(b p) h c -> p b h c", p=128), vt[:, 0:nblk, :, :], r=[VK], w=[("VA", ti)])
            roff = GROUPS["r0"][0]
            for gi, name in enumerate(RW_GROUPS if "r" in PARTS else []):
                off, wdt = GROUPS[name]
                p, pk = win_group(name)
                evac(rwb[gi % 4][0:wdt, 0:nt], p[0:wdt, 0:nt], [pk], [("rwb", gi % 4)])
                S.dma(["sp", "act", "pool"][gi % 3], RAW[off - roff:off - roff + wdt, c0:c0 + nt], rwb[gi % 4][0:wdt, 0:nt], r=[("rwb", gi % 4)], w=[("RAW", ti, gi)])

        nT = len(cfg.tiles)
        stageA(0)
        for ti in range(nT):
            if ti + 1 < nT:
                stageA(ti + 1)
            stageB(ti)
        S.barrier()
        st.close()


    LPM = max(cfg.LP)
    NBM = max(cfg.NB)

    def shift_phase(l):
        st = ExitStack()
        mu = sb(st, "mu", [128, 16, 2], F32)
        c0t = sb(st, "c0t", [128, 16], F32)
        X = [sb(st, f"X{i}", [128, 514], F32) for i in range(4)]
        T = [sb(st, f"T{i}", [128, 512], F32) for i in range(4)]
        Tb = [sb(st, f"Tb{i}", [128, 512], BF16) for i in range(2)]
        S.dma("sp", mu[:], mu_in[:, l, :, :], w=["mu"])
        S.op("dve", lambda e: e.tensor_tensor(out=c0t[:], in0=mu[:, :, 0], in1=mu[:, :, 1], op=ALU.add), r=["mu"], w=["c0"])
        S.op("dve", lambda e: e.tensor_scalar(out=c0t[:], in0=c0t[:], scalar1=-1.0, scalar2=1.0, op0=ALU.mult, op1=ALU.add), r=["c0"], w=["c0"])
        roff0 = GROUPS["r0"][0]
        it = 0
        for s_ in range(2):
            base, L, Lp = cfg.BASE[s_], cfg.LS[s_], cfg.LP[s_]
            for t0 in range(0, Lp, 512):
                nt = min(512, Lp - t0)
                valid = max(0, min(nt, L - t0))
                for gi, name in enumerate(RW_GROUPS):
                    off, wdt = GROUPS[name]
                    ro = off - roff0
                    x = X[it % 4]
                    XK = ("X", it % 4)
                    t = T[it % 4]
                    TK = ("T", it % 4)
                    eng = "dve"
                    lo = max(t0 - 1, 0)
                    hi = min(t0 + nt + 1, L)
                    jlo, jhi = lo - (t0 - 1), hi - (t0 - 1)
                    if jlo > 0:
                        S.op("pool", lambda e: e.memset(x[0:wdt, 0:jlo], 0.0), w=[XK])
                    if jhi < nt + 2:
                        S.op("pool", lambda e: e.memset(x[0:wdt, max(jhi, 0):nt + 2], 0.0), w=[XK])
                    if jhi > jlo:
                        S.dma("sp" if it % 2 == 0 else "act", x[0:wdt, jlo:jhi], RAW[ro:ro + wdt, base + lo:base + hi], r=[("RAW",)], w=[XK])
                    S.op(eng, lambda e: e.tensor_scalar(out=t[0:wdt, 0:nt], in0=x[0:wdt, 1:nt + 1], scalar1=c0t[0:wdt, gi:gi + 1], scalar2=None, op0=ALU.mult), r=[XK, "c0"], w=[TK])
                    S.op(eng, lambda e: e.scalar_tensor_tensor(out=t[0:wdt, 0:nt], in0=x[0:wdt, 0:nt], scalar=mu[0:wdt, gi, 0:1], in1=t[0:wdt, 0:nt], op0=ALU.mult, op1=ALU.add), r=[XK, TK, "mu"], w=[TK])
                    is_da = (name == "da")
                    if is_da:
                        tb = Tb[it % 2]
                        TBK = ("Tb", it % 2)
                        S.op(eng, lambda e: e.scalar_tensor_tensor(out=tb[0:wdt, 0:nt], in0=x[0:wdt, 2:nt + 2], scalar=mu[0:wdt, gi, 1:2], in1=t[0:wdt, 0:nt], op0=ALU.mult, op1=ALU.add), r=[XK, TK, "mu"], w=[TBK])
                        if valid < nt:
                            S.op(eng, lambda e: e.memset(tb[0:wdt, valid:nt], 0.0), w=[TBK])
                        S.dma("pool", LOR[128:256, base + t0:base + t0 + nt], tb[0:wdt, 0:nt], r=[TBK], w=[("LOR", it)])
                    else:
                        S.op(eng, lambda e: e.scalar_tensor_tensor(out=t[0:wdt, 0:nt], in0=x[0:wdt, 2:nt + 2], scalar=mu[0:wdt, gi, 1:2], in1=t[0:wdt, 0:nt], op0=ALU.mult, op1=ALU.add), r=[XK, TK, "mu"], w=[TK])
                        if valid < nt:
                            S.op(eng, lambda e: e.memset(t[0:wdt, valid:nt], 0.0), w=[TK])
                        if gi < 12:
                            S.dma("pool" if it % 2 == 0 else "act", RWS[ro:ro + wdt, base + t0:base + t0 + nt], t[0:wdt, 0:nt], r=[TK], w=[("RWS", it)])
                        else:
                            tb = Tb[it % 2]
                            TBK = ("Tb", it % 2)
                            fn_ = AF.Tanh if name == "dw" else AF.Sigmoid
                            S.op("act", lambda e: e.activation(out=tb[0:wdt, 0:nt], in_=t[0:wdt, 0:nt], func=fn_), r=[TK], w=[TBK])
                            lo_r = {"dw": 0, "dg0": 256, "dg1": 384}[name]
                            S.dma("pool", LOR[lo_r:lo_r + wdt, base + t0:base + t0 + nt], tb[0:wdt, 0:nt], r=[TBK], w=[("LOR", it)])
                    it += 1
        S.barrier()
        st.close()

    def attn_phase(l):
        st = ExitStack()
        Ksb = [sb(st, f"Ksb{i}", [97, LPM], BF16) for i in range(2)]
        Qsb = [sb(st, f"Qsb{i}", [97, LPM], BF16) for i in range(2)]
        Vsb = [sb(st, f"Vsb{i}", [128, NBM, 65], BF16) for i in range(2)]
        knq = sb(st, "knq", [8, LPM], F32)
        augb = sb(st, "augb", [8, LPM], BF16)
        kmax = sb(st, "kmax", [8, 2], F32)
        Pt = [sb(st, f"Pt{i}", [128, 512], BF16) for i in range(4)]
        Osb = [sb(st, f"Osb{i}", [65, 512], F32) for i in range(2)]
        rec = [sb(st, f"rec{i}", [65, 512], F32) for i in range(2)]
        Ob = [sb(st, f"Ob{i}", [64, 512], BF16) for i in range(2)]
        pS = [ps(st, f"pS{i}", [128, 512]) for i in range(4)]
        pO = [ps(st, f"pO{i}", [128, 512]) for i in range(2)]
        pB = [ps(st, f"pB{i}", [128, 512]) for i in range(2)]
        hs = 0
        qt = 0
        si = 0
        pending = []
        for s_ in range(2):
            base, L, Lp, nb = cfg.BASE[s_], cfg.LS[s_], cfg.LP[s_], cfg.NB[s_]
            S.dma("sp", knq[:, 0:L], KNd[:, base:base + L], r=[("KNd",)], w=["knq"])
            S.op("dve", lambda e: e.tensor_reduce(out=kmax[:, 0:1], in_=knq[:, 0:L], axis=AX.X, op=ALU.max), r=["knq"], w=["kmax"])
            S.op("act", lambda e: e.activation(out=kmax[:, 0:1], in_=kmax[:, 0:1], func=AF.Sqrt), r=["kmax"], w=["kmax"])
            S.dma("sp", knq[:, 0:L], QNd[:, base:base + L], r=[("QNd",)], w=["knq"])
            S.op("dve", lambda e: e.tensor_scalar(out=augb[:, 0:L], in0=knq[:, 0:L], scalar1=kmax[:, 0:1], scalar2=-1.0, op0=ALU.mult, op1=ALU.mult), r=["knq", "kmax"], w=["augb"])
            S.dma("sp", QTd[96, :, base:base + L], augb[:, 0:L], r=["augb"], w=[("QTd", s_)])
            for h in range(NH):
                bf = hs % 2
                hs += 1
                K_, Q_, V_ = Ksb[bf], Qsb[bf], Vsb[bf]
                KK, QK, VK = ("K", bf), ("Q", bf), ("V", bf)
                S.dma("sp", K_[:, 0:L], KTd[:, h, base:base + L], r=[("KTd",)], w=[KK])
                S.dma("act", Q_[:, 0:L], QTd[:, h, base:base + L], r=[("QTd", s_)], w=[QK])
                S.dma("pool", V_[:, 0:nb, :], VA[base:base + Lp, h, :].rearrange("(b p) c -> p b c", p=128), r=[("VA",)], w=[VK])
                nkb = -(-L // 128)
                for q0 in range(0, L, 512):
                    qw = min(512, L - q0)
                    j = qt % 2
                    qt += 1
                    OK_ = ("pO", j)

                    def emitS(kb):
                        kw = min(128, L - kb * 128)
                        i = (si + kb) % 4
                        S.op("pe", lambda e: e.matmul(pS[i][0:kw, 0:qw], lhsT=K_[:, kb * 128:kb * 128 + kw], rhs=Q_[:, q0:q0 + qw], start=True, stop=True), r=[KK, QK], w=[("pS", i)])
                        S.op("act", lambda e: e.activation(out=Pt[i][0:kw, 0:qw], in_=pS[i][0:kw, 0:qw], func=AF.Exp, scale=SCALE), r=[("pS", i)], w=[("Pt", i)])

                    def emitPV(kb):
                        kw = min(128, L - kb * 128)
                        i = (si + kb) % 4
                        S.op("pe", lambda e: e.matmul(pO[j][0:65, 0:qw], lhsT=V_[0:kw, kb, :], rhs=Pt[i][0:kw, 0:qw], start=(kb == 0), stop=(kb == nkb - 1)), r=[VK, ("Pt", i)], w=[OK_])

                    SK = 2
                    for kb in range(nkb + SK):
                        if kb < nkb:
                            emitS(kb)
                        if kb == 1 and pending:
                            pending.pop()()
                        if kb >= SK:
                            emitPV(kb - SK)
                    si = (si + nkb) % 4

                    def fin(j=j, qw=qw, q0=q0, h=h, base=base):
                        S.op("pe", lambda e: e.matmul(pB[j][0:64, 0:qw], lhsT=onesf[64:65, 0:64], rhs=rec[j][64:65, 0:qw], start=True, stop=True), r=[("rec", j), "onesf"], w=[("pB", j)])
                        S.op("dve", lambda e: e.tensor_tensor(out=Ob[j][:, 0:qw], in0=Osb[j][0:64, 0:qw], in1=pB[j][0:64, 0:qw], op=ALU.mult), r=[("Osb", j), ("pB", j)], w=[("Ob", j)])
                        S.dma("pool", MIXT[64 * h:64 * h + 64, base + q0:base + q0 + qw], Ob[j][:, 0:qw], r=[("Ob", j)], w=[("MIXT", h, base + q0)])

                    S.op("dve", lambda e: e.tensor_copy(out=Osb[j][:, 0:qw], in_=pO[j][0:65, 0:qw]), r=[OK_], w=[("Osb", j)])
                    S.op("dve", lambda e: e.reciprocal(out=rec[j][64:65, 0:qw], in_=Osb[j][64:65, 0:qw]), r=[("Osb", j)], w=[("rec", j)])
                    if pending:
                        pending.pop()()
                    pending.append(fin)
        while pending:
            pending.pop()()
        S.barrier()
        st.close()


    def rwkv_phase(l):
        st = ExitStack()
        c_ = CDEC
        MK = sb(st, "MK", [128, 4, 128], F32)
        MP1 = sb(st, "MP1", [128, 2, 2, 128], F32)
        MP3 = sb(st, "MP3", [128, 2, 128], F32)
        vm = sb(st, "vm", [128, 2], F32)
        prm = sb(st, "prm", [128, 9, 64], F32)
        omk = sb(st, "omk", [128, 64], F32)
        LWf = sb(st, "LWf", [64, 2, 2, 64], F32)
        LW = sb(st, "LW", [64, 2, 2, 64], BF16)
        G2f = sb(st, "G2f", [128, 2, 64], F32)
        G2 = sb(st, "G2", [128, 2, 64], BF16)
        S.dma("sp", MK[:], masks_in, w=["MK"])
        S.dma("sp", vm[:], vmask_in, w=["vm"])
        for d, (ms, mi, m3) in enumerate(((2, 0, 3), (3, 1, 2))):
            S.op("dve", lambda e: e.tensor_copy(out=MP1[:, d, 0, :], in_=MK[:, ms, :]), r=["MK"], w=["MP"])
            S.op("dve", lambda e: e.tensor_copy(out=MP1[:, d, 1, :], in_=MK[:, mi, :]), r=["MK"], w=["MP"])
            S.op("dve", lambda e: e.tensor_copy(out=MP3[:, d, :], in_=MK[:, m3, :]), r=["MK"], w=["MP"])
        GT = sb(st, "GT", [64, NBM, 2, 128], BF16)
        PHIT = sb(st, "PHIT", [64, NBM, 2, 64], BF16)
        PSI = sb(st, "PSI", [64, NBM, 2, 64], F32)
        Yacc = sb(st, "Yacc", [128, NBM, 64], F32)
        Sb = [sb(st, f"Sb{i}", [64, 2, 64], BF16) for i in range(2)]
        RKs = sb(st, "RKs", [128, 512], F32)
        Vs = sb(st, "Vs", [64, 512], F32)
        TW = sb(st, "TW", [64, 2, 512], BF16)
        DAs = sb(st, "DAs", [64, 2, 512], BF16)
        SG0 = sb(st, "SG0", [128, 512], BF16)
        SG1 = sb(st, "SG1", [32, 512], BF16)
        RKtm = sb(st, "RKtm", [128, 4, 128], F32)
        Vtm = sb(st, "Vtm", [128, 4, 64], BF16)
        Vt32 = sb(st, "Vt32", [128, 4, 64], F32)
        SGM = sb(st, "SGM", [128, 4, 2, 64], F32)
        Aa = sb(st, "Aa", [128, 4, 2, 64], F32)
        EG = sb(st, "EG", [128, 4, 2, 64], F32)
        kkr = sb(st, "kkr", [128, 4, 64], F32)
        kk = sb(st, "kk", [128, 4, 64], F32)
        sq = sb(st, "sq", [128, 4, 64], F32)
        n2 = sb(st, "n2", [128, 8], F32)
        kd = sb(st, "kd", [128, 4, 2, 64], F32)
        be = sb(st, "be", [128, 4, 2, 64], F32)
        t1 = sb(st, "t1", [128, 4, 2, 64], F32)
        EX = [sb(st, f"EX{i}", [128, 2, 4, 64], F32) for i in range(5)]
        sc = [sb(st, f"sc{i}", [128, 4, 2, 64], BF16) for i in range(5)]
        TA = sb(st, "TA", [128, 8, 128], BF16)
        FT = sb(st, "FT", [64, 8, 4, 128], BF16)
        QM = sb(st, "QM", [128, 8, 2, 128], BF16)
        KM = sb(st, "KM", [128, 8, 2, 128], BF16)
        PP = [sb(st, f"PP{i}", [128, 8, 128], BF16) for i in range(2)]
        QQ = [sb(st, f"QQ{i}", [128, 8, 128], BF16) for i in range(2)]
        TT = [sb(st, f"TT{i}", [128, 8, 128], BF16) for i in range(2)]
        WU = sb(st, "WU", [128, 8, 128], BF16)
        tmpd = sb(st, "tmpd", [64, 8, 64], F32)
        ey = sb(st, "ey", [128, 4, 64], F32)
        es1 = sb(st, "es1", [128, 8], F32)
        eo = sb(st, "eo", [128, 4, 64], F32)
        eob = sb(st, "eob", [64, 512], BF16)
        egl = sb(st, "egl", [128, 4, 2, 64], F32)
        banks = [ps(st, f"pX{i}", [128, 512]) for i in range(8)]
        bki = [0]

        def nb_():
            i = bki[0] % 8
            bki[0] += 1
            return banks[i], ("pX", i)

        alt = [0]

        def ew(fn, r, w):
            alt[0] += 1
            S.op("dve" if alt[0] % 2 else "pool", fn, r=r, w=w)

        def bc(ap, shape):
            return ap.broadcast_to(shape)

        RW_STOP = os.environ.get("RW_STOP", "Z")
        for s_ in range(2):
            base, L, Lp, nb = cfg.BASE[s_], cfg.LS[s_], cfg.LP[s_], cfg.NB[s_]
            ngrp = -(-nb // 4)
            for h in range(NH):
                S.dma("sp", prm[:], rwp[:, l, h, :, :], w=["prm"])
                S.dma("act", LWf[:], lw_in[l, h].rearrange("(d k) a c -> k d a c", d=2), w=["LWf"])
                S.dma("sp", G2f[:, 0, :], g2_in[l, h, 0:128, :], w=["G2f"])
                S.dma("act", G2f[0:32, 1, :], g2_in[l, h, 128:160, :], w=["G2f"])
                S.op("dve", lambda e: e.tensor_copy(out=LW[:], in_=LWf[:]), r=["LWf"], w=["LW"])
                S.op("dve", lambda e: e.tensor_copy(out=G2[:, 0, :], in_=G2f[:, 0, :]), r=["G2f"], w=["G2"])
                S.op("dve", lambda e: e.tensor_copy(out=G2[0:32, 1, :], in_=G2f[0:32, 1, :]), r=["G2f"], w=["G2"])
                S.op("dve", lambda e: e.tensor_scalar(out=omk[:], in0=prm[:, 5, :], scalar1=-1.0, scalar2=1.0, op0=ALU.mult, op1=ALU.add), r=["prm"], w=["omk"])
                for g in range(ngrp):
                    ng = min(4, nb - 4 * g)
                    nq = 2 * ng
                    ncol = 128 * ng
                    cc0 = base + 512 * g
                    S.dma("sp", RKs[0:64, 0:ncol], RWS[64 * h:64 * h + 64, cc0:cc0 + ncol], r=[("RWS",)], w=["RKs"])
                    S.dma("act", RKs[64:128, 0:ncol], RWS[512 + 64 * h:512 + 64 * h + 64, cc0:cc0 + ncol], r=[("RWS",)], w=["RKs"])
                    S.dma("sp", Vs[:, 0:ncol], RWS[1024 + 64 * h:1024 + 64 * h + 64, cc0:cc0 + ncol], r=[("RWS",)], w=["Vs"])
                    S.dma("act", TW[:, :, 0:ncol], LOR[0:128, cc0:cc0 + ncol].rearrange("(d k) n -> k d n", d=2), r=[("LOR",)], w=["TW"])
                    S.dma("sp", DAs[:, :, 0:ncol], LOR[128:256, cc0:cc0 + ncol].rearrange("(d k) n -> k d n", d=2), r=[("LOR",)], w=["DAs"])
                    S.dma("act", SG0[:, 0:ncol], LOR[256:384, cc0:cc0 + ncol], r=[("LOR",)], w=["SG0"])
                    S.dma("sp", SG1[:, 0:ncol], LOR[384:416, cc0:cc0 + ncol], r=[("LOR",)], w=["SG1"])
                    pb, pk = nb_()
                    for j in range(ng):
                        S.op("pe", lambda e, j=j: e.transpose(out=pb[:, j * 128:(j + 1) * 128], in_=RKs[:, j * 128:(j + 1) * 128], identity=identf[:]), r=["RKs", "identf"], w=[pk])
                    S.op("act", lambda e: e.copy(out=RKtm[:, 0:ng, :], in_=pb[:, 0:ncol].rearrange("p (j c) -> p j c", c=128)), r=[pk], w=["RKtm"])
                    pb, pk = nb_()
                    for j in range(ng):
                        S.op("pe", lambda e, j=j: e.transpose(out=pb[:, j * 64:(j + 1) * 64], in_=Vs[:, j * 128:(j + 1) * 128], identity=identf[0:64, 0:64]), r=["Vs", "identf"], w=[pk])
                    S.op("act", lambda e: e.copy(out=Vtm[:, 0:ng, :], in_=pb[:, 0:64 * ng].rearrange("p (j c) -> p j c", c=64)), r=[pk], w=["Vtm"])
                    S.op("dve", lambda e: e.tensor_copy(out=Vt32[:, 0:ng, :], in_=pb[:, 0:64 * ng].rearrange("p (j c) -> p j c", c=64)), r=[pk], w=["Vt32"])
                    if RW_STOP <= "A":
                        continue
                    pw, pwk = nb_()
                    pa, pak = nb_()
                    pg, pgk = nb_()
                    for j in range(ng):
                        for d in range(1 if os.environ.get("RW_B2") == "d0only" else 2):
                            S.op("pe", lambda e, j=j, d=d: e.matmul(pw[:, (j * 2 + d) * 64:(j * 2 + d + 1) * 64], lhsT=TW[:, d, j * 128:(j + 1) * 128], rhs=LW[:, d, 0, :], start=True, stop=True), r=["TW", "LW"], w=[pwk])
                    for j in range(ng):
                        for d in range(1 if os.environ.get("RW_B2") == "d0only" else 2):
                            S.op("pe", lambda e, j=j, d=d: e.matmul(pa[:, (j * 2 + d) * 64:(j * 2 + d + 1) * 64], lhsT=DAs[:, d, j * 128:(j + 1) * 128], rhs=LW[:, d, 1, :], start=True, stop=True), r=["DAs", "LW"], w=[pak])
                    for j in range(0 if os.environ.get("RW_B2") == "nogate" else ng):
                        S.op("pe", lambda e, j=j: e.matmul(pg[:, j * 64:(j + 1) * 64], lhsT=SG0[:, j * 128:(j + 1) * 128], rhs=G2[:, 0, :], start=True, stop=False), r=["SG0", "G2"], w=[pgk])
                        S.op("pe", lambda e, j=j: e.matmul(pg[:, j * 64:(j + 1) * 64], lhsT=SG1[:, j * 128:(j + 1) * 128], rhs=G2[0:32, 1, :], start=False, stop=True), r=["SG1", "G2"], w=[pgk])
                    v4 = [128, ng, 2, 64]
                    if os.environ.get("RW_B") == "1":
                        continue
                    S.op("dve", lambda e: e.tensor_tensor(out=SGM[:, 0:ng], in0=pw[:, 0:128 * ng].rearrange("p (j d c) -> p j d c", d=2, c=64), in1=bc(prm[:, 0:2, :].unsqueeze(1), v4), op=ALU.add), r=[pwk, "prm"], w=["SGM"])
                    S.op("act", lambda e: e.activation(out=SGM[:, 0:ng], in_=SGM[:, 0:ng], func=AF.Sigmoid), r=["SGM"], w=["SGM"])
                    if 4 * g + ng == nb and L < Lp:
                        jl = ng - 1
                        S.op("dve", lambda e: e.tensor_scalar(out=SGM[:, jl], in0=SGM[:, jl], scalar1=vm[:, s_:s_ + 1], scalar2=None, op0=ALU.mult), r=["SGM", "vm"], w=["SGM"])
                    S.op("dve", lambda e: e.tensor_tensor(out=Aa[:, 0:ng], in0=pa[:, 0:128 * ng].rearrange("p (j d c) -> p j d c", d=2, c=64), in1=bc(prm[:, 2:4, :].unsqueeze(1), v4), op=ALU.add), r=[pak, "prm"], w=["Aa"])
                    S.op("act", lambda e: e.activation(out=Aa[:, 0:ng], in_=Aa[:, 0:ng], func=AF.Sigmoid), r=["Aa"], w=["Aa"])
                    S.op("act", lambda e: e.copy(out=EG[:, 0:ng, 0, :], in_=pg[:, 0:64 * ng].rearrange("p (j c) -> p j c", c=64)), r=[pgk], w=["EG"])
                    if RW_STOP <= "B":
                        continue
                    r_ = RKtm[:, 0:ng, 0:64]
                    k_ = RKtm[:, 0:ng, 64:128]
                    v3 = [128, ng, 64]
                    ew(lambda e: e.tensor_tensor(out=kkr[:, 0:ng], in0=k_, in1=bc(prm[:, 4:5, :], v3), op=ALU.mult), ["RKtm", "prm"], ["kkr"])
                    ew(lambda e: e.tensor_tensor(out=sq[:, 0:ng], in0=kkr[:, 0:ng], in1=kkr[:, 0:ng], op=ALU.mult), ["kkr"], ["sq"])
                    S.op("dve", lambda e: e.tensor_reduce(out=n2[:, 0:ng], in_=sq[:, 0:ng], axis=AX.X, op=ALU.add), r=["sq"], w=["n2"])
                    S.op("act", lambda e: e.activation(out=n2[:, 0:ng], in_=n2[:, 0:ng], func=AF.Sqrt), r=["n2"], w=["n2"])
                    S.op("dve", lambda e: e.tensor_scalar(out=n2[:, 0:ng], in0=n2[:, 0:ng], scalar1=1e-12, scalar2=None, op0=ALU.max), r=["n2"], w=["n2"])
                    S.op("dve", lambda e: e.reciprocal(out=n2[:, 0:ng], in_=n2[:, 0:ng]), r=["n2"], w=["n2"])
                    ew(lambda e: e.tensor_tensor(out=kk[:, 0:ng], in0=kkr[:, 0:ng], in1=bc(n2[:, 0:ng].unsqueeze(2), v3), op=ALU.mult), ["kkr", "n2"], ["kk"])
                    ew(lambda e: e.tensor_tensor(out=t1[:, 0:ng], in0=Aa[:, 0:ng], in1=bc(prm[:, 5:6, :].unsqueeze(1), v4), op=ALU.mult), ["Aa", "prm"], ["t1"])
                    ew(lambda e: e.tensor_tensor(out=t1[:, 0:ng], in0=t1[:, 0:ng], in1=bc(omk[:, :].unsqueeze(1).unsqueeze(1), v4), op=ALU.add), ["t1", "omk"], ["t1"])
                    ew(lambda e: e.tensor_tensor(out=kd[:, 0:ng], in0=t1[:, 0:ng], in1=bc(k_.unsqueeze(2), v4), op=ALU.mult), ["t1", "RKtm"], ["kd"])
                    ew(lambda e: e.tensor_tensor(out=be[:, 0:ng], in0=Aa[:, 0:ng], in1=bc(kk[:, 0:ng].unsqueeze(2), v4), op=ALU.mult), ["Aa", "kk"], ["be"])
                    ew(lambda e: e.tensor_tensor(out=sq[:, 0:ng], in0=kd[:, 0:ng, 0, :], in1=kd[:, 0:ng, 1, :], op=ALU.add), ["kd"], ["sq"])
                    ew(lambda e: e.tensor_tensor(out=sq[:, 0:ng], in0=sq[:, 0:ng], in1=r_, op=ALU.mult), ["sq", "RKtm"], ["sq"])
                    ew(lambda e: e.tensor_tensor(out=sq[:, 0:ng], in0=sq[:, 0:ng], in1=bc(prm[:, 8:9, :], v3), op=ALU.mult), ["sq", "prm"], ["sq"])
                    S.op("dve", lambda e: e.tensor_reduce(out=n2[:, 4:4 + ng], in_=sq[:, 0:ng], axis=AX.X, op=ALU.add), r=["sq"], w=["n2b"])
                    ew(lambda e: e.tensor_tensor(out=sq[:, 0:ng], in0=Vt32[:, 0:ng], in1=bc(n2[:, 4:4 + ng].unsqueeze(2), v3), op=ALU.mult), ["Vt32", "n2b"], ["sq"])
                    ew(lambda e: e.tensor_tensor(out=EG[:, 0:ng, 1, :], in0=sq[:, 0:ng], in1=EG[:, 0:ng, 0, :], op=ALU.mult), ["sq", "EG"], ["EG"])
                    S.dma("pool", EPI[cc0:cc0 + ncol, :, :].rearrange("(j p) a c -> p j a c", p=128), EG[:, 0:ng], r=["EG"], w=[("EPI", g)])
                    if RW_STOP <= "C":
                        continue
                    pcs = []
                    for kind, (mf, mb_) in enumerate(((0, 1), (2, 3), (3, 2), (None, None))):
                        pc, pck = nb_()
                        for d in range(2):
                            m = (mf, mb_)[d]
                            lhs = onesf[:, :] if m is None else MK[:, m, :]
                            S.op("pe", lambda e, d=d, lhs=lhs: e.matmul(pc[:, d * 64 * ng:(d + 1) * 64 * ng], lhsT=lhs, rhs=SGM[:, 0:ng, d, :], start=True, stop=True), r=["SGM", "MK", "onesf"], w=[pck])
                        pcs.append((pc, pck))
                    for i, (kind, sgn) in enumerate(((0, -1.0), (0, 1.0), (1, -1.0), (2, -1.0), (3, -1.0))):
                        pc, pck = pcs[kind]
                        S.op("act", lambda e, i=i, pc=pc, sgn=sgn: e.activation(out=EX[i][:, :, 0:ng, :], in_=pc[:, 0:128 * ng].rearrange("p (d j c) -> p d j c", d=2, c=64), func=AF.Exp, scale=sgn * c_), r=[pck], w=[("EX", i)])
                    Ep, Em, Ex_, Eh, PCb = [EX[i][:, :, 0:ng, :].rearrange("p d j c -> p j d c") for i in range(5)]
                    ew(lambda e: e.tensor_tensor(out=sc[0][:, 0:ng], in0=Ep, in1=bc(r_.unsqueeze(2), v4), op=ALU.mult), [("EX", 0), "RKtm"], [("sc", 0)])
                    ew(lambda e: e.tensor_tensor(out=sc[1][:, 0:ng], in0=kd[:, 0:ng], in1=Em, op=ALU.mult), [("EX", 1), "kd"], [("sc", 1)])
                    ew(lambda e: e.tensor_tensor(out=sc[2][:, 0:ng], in0=be[:, 0:ng], in1=Em, op=ALU.mult), [("EX", 1), "be"], [("sc", 2)])
                    ew(lambda e: e.tensor_scalar(out=kkr[:, 0:ng], in0=kk[:, 0:ng], scalar1=-1.0, scalar2=None, op0=ALU.mult), ["kk"], ["kkr"])
                    ew(lambda e: e.tensor_tensor(out=TA[:, 0:nq, 0:64].rearrange("p (j d) c -> p j d c", d=2), in0=Ex_, in1=bc(kkr[:, 0:ng].unsqueeze(2), v4), op=ALU.mult), [("EX", 2), "kkr"], ["TAa"])
                    ew(lambda e: e.tensor_tensor(out=sc[3][:, 0:ng], in0=be[:, 0:ng], in1=Eh, op=ALU.mult), [("EX", 3), "be"], [("sc", 3)])
                    ew(lambda e: e.tensor_tensor(out=sc[4][:, 0:ng], in0=kd[:, 0:ng], in1=Eh, op=ALU.mult), [("EX", 3), "kd"], [("sc", 4)])
                    if RW_STOP <= "D":
                        continue
                    srcs = [(lambda q: TA[:, q, 0:64], "TAa"), (lambda q: sc[0][:, q // 2, q % 2, :], ("sc", 0)), (lambda q: sc[2][:, q // 2, q % 2, :], ("sc", 2)), (lambda q: sc[1][:, q // 2, q % 2, :], ("sc", 1))]
                    for a, (fsrc, skey) in enumerate(srcs):
                        pb, pk = nb_()
                        pbv = pb[:].bitcast(BF16)
                        for q in range(nq):
                            S.op("pe", lambda e, q=q: e.transpose(out=pbv[0:64, q * 128:(q + 1) * 128], in_=fsrc(q), identity=ident[:]), r=[skey, "ident"], w=[pk])
                        if a % 2 == 0:
                            S.op("act", lambda e: e.copy(out=FT[:, 0:nq, a, :], in_=pbv[0:64, 0:128 * nq].rearrange("p (q c) -> p q c", c=128)), r=[pk], w=[("FT", a)])
                        else:
                            S.op("dve", lambda e: e.tensor_copy(out=FT[:, 0:nq, a, :], in_=pbv[0:64, 0:128 * nq].rearrange("p (q c) -> p q c", c=128)), r=[pk], w=[("FT", a)])
                    FK = [("FT", a) for a in range(4)]
                    if RW_STOP <= "E":
                        continue
                    for j in range(ng):
                        p1, p1k = nb_()
                        p2, p2k = nb_()
                        for d in range(2):
                            q = 2 * j + d
                            S.op("pe", lambda e, q=q, d=d: e.matmul(p1[:, d * 256:(d + 1) * 256], lhsT=FT[:, q, 2, :], rhs=FT[:, q, 0:2, :], start=True, stop=True), r=FK, w=[p1k])
                            S.op("pe", lambda e, q=q, d=d: e.matmul(p2[:, d * 256:(d + 1) * 256], lhsT=FT[:, q, 3, :], rhs=FT[:, q, 0:2, :], start=True, stop=True), r=FK, w=[p2k])
                        S.op("dve", lambda e, j=j: e.tensor_tensor(out=QM[:, 2 * j:2 * j + 2], in0=p1[:, :].rearrange("p (d a c) -> p d a c", d=2, a=2), in1=MP1[:], op=ALU.mult), r=[p1k, "MP"], w=["QM"])
                        S.op("dve", lambda e, j=j: e.tensor_tensor(out=KM[:, 2 * j:2 * j + 2], in0=p2[:, :].rearrange("p (d a c) -> p d a c", d=2, a=2), in1=MP1[:], op=ALU.mult), r=[p2k, "MP"], w=["KM"])
                    for jj in range(0, ng, 2):
                        p3, p3k = nb_()
                        njj = min(2, ng - jj)
                        for j in range(jj, jj + njj):
                            for d in range(2):
                                q = 2 * j + d
                                S.op("pe", lambda e, q=q, j=j, d=d: e.matmul(p3[:, ((j - jj) * 2 + d) * 128:((j - jj) * 2 + d + 1) * 128], lhsT=FT[:, q, 0, :], rhs=FT[:, q, 2, :], start=True, stop=True), r=FK, w=[p3k])
                        S.op("dve", lambda e: e.tensor_tensor(out=PP[0][:, 2 * jj:2 * jj + 2 * njj].rearrange("p (j d) c -> p j d c", d=2), in0=p3[:, 0:256 * njj].rearrange("p (j d c) -> p j d c", d=2, c=128), in1=bc(MP3[:].unsqueeze(1), [128, njj, 2, 128]), op=ALU.mult), r=[p3k, "MP"], w=[("PP", 0)])
                    if RW_STOP <= "F":
                        continue
                    S.op("pool", lambda e: e.tensor_copy(out=QQ[0][:, 0:nq], in_=QM[:, 0:nq, 0, :]), r=["QM"], w=[("QQ", 0)])
                    S.op("dve", lambda e: e.tensor_tensor(out=TT[0][:, 0:nq], in0=QM[:, 0:nq, 0, :], in1=bc(ident[:, :].unsqueeze(1), [128, nq, 128]), op=ALU.add), r=["QM", "ident"], w=[("TT", 0)])
                    cur = 0
                    for lev in range(6):
                        nx = 1 - cur
                        for q0 in range(0, nq, 4):
                            nqq = min(4, nq - q0)
                            pp_, ppk = nb_()
                            pq_, pqk = nb_()
                            for q in range(q0, q0 + nqq):
                                S.op("pe", lambda e, q=q: e.matmul(pp_[:, (q - q0) * 128:(q - q0 + 1) * 128], lhsT=QQ[cur][:, q, :], rhs=PP[cur][:, q, :], start=True, stop=True), r=[("QQ", cur), ("PP", cur)], w=[ppk])
                            for q in range(q0, q0 + nqq):
                                S.op("pe", lambda e, q=q: e.matmul(pq_[:, (q - q0) * 128:(q - q0 + 1) * 128], lhsT=PP[cur][:, q, :], rhs=QQ[cur][:, q, :], start=True, stop=True), r=[("QQ", cur), ("PP", cur)], w=[pqk])
                            S.op("act", lambda e: e.copy(out=PP[nx][:, q0:q0 + nqq], in_=pp_[:, 0:128 * nqq].rearrange("p (q c) -> p q c", c=128)), r=[ppk], w=[("PP", nx)])
                            S.op("act", lambda e: e.copy(out=QQ[nx][:, q0:q0 + nqq], in_=pq_[:, 0:128 * nqq].rearrange("p (q c) -> p q c", c=128)), r=[pqk], w=[("QQ", nx)])
                        for q0 in range(0, nq, 4):
                            nqq = min(4, nq - q0)
                            pt_, ptk = nb_()
                            for q in range(q0, q0 + nqq):
                                S.op("pe", lambda e, q=q: e.matmul(pt_[:, (q - q0) * 128:(q - q0 + 1) * 128], lhsT=PP[nx][:, q, :], rhs=TT[cur][:, q, :], start=True, stop=True), r=[("PP", nx), ("TT", cur)], w=[ptk])
                            S.op("dve", lambda e: e.tensor_tensor(out=TT[nx][:, q0:q0 + nqq], in0=pt_[:, 0:128 * nqq].rearrange("p (q c) -> p q c", c=128), in1=TT[cur][:, q0:q0 + nqq], op=ALU.add), r=[ptk, ("TT", cur)], w=[("TT", nx)])
                        cur = nx
                    TTf = TT[cur]
                    TTK = ("TT", cur)
                    if RW_STOP <= "G":
                        continue
                    pb, pk = nb_()
                    for q in range(nq):
                        S.op("pe", lambda e, q=q: e.matmul(pb[:, q * 64:(q + 1) * 64], lhsT=KM[:, q, 0, :], rhs=Vtm[:, q // 2, :], start=True, stop=True), r=["KM", "Vtm"], w=[pk])
                    S.op("act", lambda e: e.copy(out=TA[:, 0:nq, 64:128], in_=pb[:, 0:64 * nq].rearrange("p (q c) -> p q c", c=64)), r=[pk], w=["TAx"])
                    for q0 in range(0, nq, 4):
                        nqq = min(4, nq - q0)
                        pb, pk = nb_()
                        for q in range(q0, q0 + nqq):
                            S.op("pe", lambda e, q=q: e.matmul(pb[:, (q - q0) * 128:(q - q0 + 1) * 128], lhsT=TTf[:, q, :], rhs=TA[:, q, :], start=True, stop=True), r=[TTK, "TAa", "TAx"], w=[pk])
                        S.op("act", lambda e: e.copy(out=WU[:, q0:q0 + nqq], in_=pb[:, 0:128 * nqq].rearrange("p (q c) -> p q c", c=128)), r=[pk], w=["WU"])
                    for q0 in range(0, nq, 4):
                        nqq = min(4, nq - q0)
                        pb, pk = nb_()
                        for q in range(q0, q0 + nqq):
                            S.op("pe", lambda e, q=q: e.matmul(pb[0:64, (q - q0) * 128:(q - q0 + 1) * 128], lhsT=WU[:, q, 0:64], rhs=QM[:, q, 1, :], start=True, stop=True), r=["WU", "QM"], w=[pk])
                        S.op("dve", lambda e: e.tensor_tensor(out=GT[:, 4 * g + q0 // 2:4 * g + (q0 + nqq) // 2].rearrange("p j d c -> p (j d) c"), in0=pb[0:64, 0:128 * nqq].rearrange("p (q c) -> p q c", c=128), in1=FT[:, q0:q0 + nqq, 1, :], op=ALU.add), r=[pk, ("FT", 1)], w=["GT"])
                    pb, pk = nb_()
                    for j in range(ng):
                        for d in range(2):
                            q = 2 * j + d
                            S.op("pe", lambda e, q=q, j=j, d=d: e.matmul(pb[:, j * 64:(j + 1) * 64], lhsT=QM[:, q, 1, :], rhs=WU[:, q, 64:128], start=(d == 0), stop=False), r=["QM", "WU"], w=[pk])
                            S.op("pe", lambda e, q=q, j=j, d=d: e.matmul(pb[:, j * 64:(j + 1) * 64], lhsT=KM[:, q, 1, :], rhs=Vtm[:, j, :], start=False, stop=(d == 1)), r=["KM", "Vtm"], w=[pk])
                    S.op("act", lambda e: e.copy(out=Yacc[:, 4 * g:4 * g + ng, :], in_=pb[:, 0:64 * ng].rearrange("p (j c) -> p j c", c=64)), r=[pk], w=["Yacc"])
                    pb, pk = nb_()
                    for q in range(nq):
                        S.op("pe", lambda e, q=q: e.matmul(pb[0:64, q * 64:(q + 1) * 64], lhsT=WU[:, q, 0:64], rhs=sc[3][:, q // 2, q % 2, :], start=True, stop=True), r=["WU", ("sc", 3)], w=[pk])
                    S.op("pool", lambda e: e.tensor_tensor(out=tmpd[:, 0:nq].rearrange("p (j d) c -> p j d c", d=2), in0=PCb[0:64], in1=bc(identf[0:64, 0:64].unsqueeze(1).unsqueeze(1), [64, ng, 2, 64]), op=ALU.mult), r=[("EX", 4), "identf"], w=["tmpd"])
                    S.op("dve", lambda e: e.tensor_tensor(out=PHIT[:, 4 * g:4 * g + ng].rearrange("p j d c -> p (j d) c"), in0=pb[0:64, 0:64 * nq].rearrange("p (q c) -> p q c", c=64), in1=tmpd[:, 0:nq], op=ALU.add), r=[pk, "tmpd"], w=["PHIT"])
                    pb, pk = nb_()
                    for q in range(nq):
                        S.op("pe", lambda e, q=q: e.matmul(pb[0:64, q * 64:(q + 1) * 64], lhsT=sc[3][:, q // 2, q % 2, :], rhs=WU[:, q, 64:128], start=True, stop=False), r=["WU", ("sc", 3)], w=[pk])
                        S.op("pe", lambda e, q=q: e.matmul(pb[0:64, q * 64:(q + 1) * 64], lhsT=sc[4][:, q // 2, q % 2, :], rhs=Vtm[:, q // 2, :], start=False, stop=True), r=["Vtm", ("sc", 4)], w=[pk])
                    S.op("act", lambda e: e.copy(out=PSI[:, 4 * g:4 * g + ng].rearrange("p j d c -> p (j d) c"), in_=pb[0:64, 0:64 * nq].rearrange("p (q c) -> p q c", c=64)), r=[pk], w=["PSI"])
                if RW_STOP <= "H":
                    continue
                S.op("pool", lambda e: e.memset(Sb[0][:], 0.0), w=[("Sb", 0)])
                cur = 0
                for i in range(nb):
                    nx = 1 - cur
                    for d, c in ((0, i), (1, nb - 1 - i)):
                        py, pyk = nb_()
                        S.op("pe", lambda e, d=d, c=c: e.matmul(py[:, 0:64], lhsT=GT[:, c, d, :], rhs=Sb[cur][:, d, :], start=True, stop=True), r=["GT", ("Sb", cur)], w=[pyk])
                        S.op("pe", lambda e, d=d, c=c: e.matmul(py[0:64, 64:128], lhsT=PHIT[:, c, d, :], rhs=Sb[cur][:, d, :], start=True, stop=True), r=["PHIT", ("Sb", cur)], w=[pyk])
                        S.op("dve", lambda e, d=d, c=c: e.tensor_tensor(out=Sb[nx][:, d, :], in0=py[0:64, 64:128], in1=PSI[:, c, d, :], op=ALU.add), r=[pyk, "PSI"], w=[("Sb", nx)])
                        S.op("dve", lambda e, d=d, c=c: e.tensor_tensor(out=Yacc[:, c, :], in0=py[:, 0:64], in1=Yacc[:, c, :], op=ALU.add), r=[pyk, "Yacc"], w=["Yacc"])
                    cur = nx
                if RW_STOP <= "I":
                    continue
                for g in range(ngrp):
                    ng = min(4, nb - 4 * g)
                    ncol = 128 * ng
                    cc0 = base + 512 * g
                    v3 = [128, ng, 64]
                    S.dma("sp", egl[:, 0:ng], EPI[cc0:cc0 + ncol, :, :].rearrange("(j p) a c -> p j a c", p=128), r=[("EPI", g)], w=["egl"])
                    y_ = Yacc[:, 4 * g:4 * g + ng, :]
                    S.op("dve", lambda e: e.tensor_reduce(out=es1[:, 0:ng], in_=y_, axis=AX.X, op=ALU.add), r=["Yacc"], w=["es1"])
                    S.op("dve", lambda e: e.tensor_scalar(out=es1[:, 0:ng], in0=es1[:, 0:ng], scalar1=-1.0 / 64, scalar2=None, op0=ALU.mult), r=["es1"], w=["es1"])
                    ew(lambda e: e.tensor_tensor(out=ey[:, 0:ng], in0=y_, in1=bc(es1[:, 0:ng].unsqueeze(2), v3), op=ALU.add), ["Yacc", "es1"], ["ey"])
                    ew(lambda e: e.tensor_tensor(out=eo[:, 0:ng], in0=ey[:, 0:ng], in1=ey[:, 0:ng], op=ALU.mult), ["ey"], ["eo"])
                    S.op("dve", lambda e: e.tensor_reduce(out=es1[:, 4:4 + ng], in_=eo[:, 0:ng], axis=AX.X, op=ALU.add), r=["eo"], w=["es2"])
                    S.op("dve", lambda e: e.tensor_scalar(out=es1[:, 4:4 + ng], in0=es1[:, 4:4 + ng], scalar1=1.0 / 64, scalar2=LNX_EPS, op0=ALU.mult, op1=ALU.add), r=["es2"], w=["es2"])
                    S.op("act", lambda e: e.activation(out=es1[:, 4:4 + ng], in_=es1[:, 4:4 + ng], func=AF.Sqrt), r=["es2"], w=["es2"])
                    S.op("dve", lambda e: e.reciprocal(out=es1[:, 4:4 + ng], in_=es1[:, 4:4 + ng]), r=["es2"], w=["es2"])
                    ew(lambda e: e.tensor_tensor(out=ey[:, 0:ng], in0=ey[:, 0:ng], in1=bc(es1[:, 4:4 + ng].unsqueeze(2), v3), op=ALU.mult), ["ey", "es2"], ["ey"])
                    ew(lambda e: e.tensor_tensor(out=ey[:, 0:ng], in0=ey[:, 0:ng], in1=bc(prm[:, 6:7, :], v3), op=ALU.mult), ["ey", "prm"], ["ey"])
                    ew(lambda e: e.tensor_tensor(out=ey[:, 0:ng], in0=ey[:, 0:ng], in1=bc(prm[:, 7:8, :], v3), op=ALU.add), ["ey", "prm"], ["ey"])
                    ew(lambda e: e.tensor_tensor(out=ey[:, 0:ng], in0=ey[:, 0:ng], in1=egl[:, 0:ng, 0, :], op=ALU.mult), ["ey", "egl"], ["ey"])
                    ew(lambda e: e.tensor_tensor(out=eo[:, 0:ng], in0=ey[:, 0:ng], in1=egl[:, 0:ng, 1, :], op=ALU.add), ["ey", "egl"], ["eo"])
                    pb, pk = nb_()
                    for j in range(ng):
                        S.op("pe", lambda e, j=j: e.transpose(out=pb[0:64, j * 128:(j + 1) * 128], in_=eo[:, j, :], identity=identf[:]), r=["eo", "identf"], w=[pk])
                    S.op("act", lambda e: e.copy(out=eob[:, 0:ncol], in_=pb[0:64, 0:ncol]), r=[pk], w=["eob"])
                    S.dma("pool", MIXT[512 + 64 * h:512 + 64 * h + 64, cc0:cc0 + ncol], eob[:, 0:ncol], r=["eob"], w=[("MIXT", h, g)])
        S.barrier()
        st.close()

    for l in range(NL):
        ffn_pass(l, 0, 0, True)
        ffn_pass(l, 0, 1, False)
        if cfg.stages != "ffn":
            proj_pass(l)
            if cfg.stages != "proj":
                shift_phase(l)
                if cfg.stages != "shift":
                    if cfg.stages != "rwkv":
                        attn_phase(l)
                    if cfg.stages != "attn":
                        rwkv_phase(l)
        ffn_pass(l, 1, 0, True)
        ffn_pass(l, 1, 1, False, final=(l == NL - 1))

    S.finish()
    cst.close()
    es.close()
    return nc, S


def host_maps(cfg, inp, xs_list):
    NL = cfg.NL
    f32 = np.float32
    common = {}
    common["meta"] = np.ascontiguousarray(inp["meta_tokens"], f32)
    for k, nm in ((1, "ffn1"), (2, "ffn2")):
        common[f"ffn{k}_wg"] = np.ascontiguousarray(inp[f"{nm}_w_gate"][:NL], f32)
        common[f"ffn{k}_wu"] = np.ascontiguousarray(inp[f"{nm}_w_up"][:NL], f32)
        common[f"ffn{k}_wd"] = np.ascontiguousarray(inp[f"{nm}_w_down"][:NL], f32)
    g = np.stack([np.asarray(inp["ffn1_norm"][:NL], f32), np.asarray(inp["mix_norm"][:NL], f32), np.asarray(inp["ffn2_norm"][:NL], f32)], axis=1)
    common["gains"] = np.ascontiguousarray(g.reshape(NL, 3, 8, 128).transpose(3, 0, 1, 2))
    common["fnorm"] = np.ascontiguousarray(np.broadcast_to(np.asarray(inp["final_norm"], f32)[None, :], (128, D)))
    common["ident_bf"] = np.eye(128, dtype=f32).astype(ml_dtypes.bfloat16)
    common["zeros"] = np.zeros((128, D), f32)
    bf = ml_dtypes.bfloat16
    ih = [(i, h) for i in range(16) for h in range(NH)]
    perm = list(range(0, 384)) + [384 + i for (i, h) in ih] + [400 + i for (i, h) in ih] + list(range(416, 2368))
    assert len(perm) == NCOLP
    common["w_in"] = np.ascontiguousarray(np.asarray(inp["w_in"][:NL], f32)[:, :, perm])
    pq = [96 * h + j for h in range(NH) for j in range(64)] + [96 * h + 64 + i for (i, h) in ih] + [96 * h + 80 + i for (i, h) in ih]
    common["w_uq"] = np.ascontiguousarray(np.asarray(inp["w_uq"][:NL], f32)[:, :, pq])
    pkv = [128 * h + j for h in range(NH) for j in range(64)] + [128 * h + 64 + j for h in range(NH) for j in range(64)]
    common["w_ukv"] = np.ascontiguousarray(np.asarray(inp["w_ukv"][:NL], f32)[:, :, pkv])
    common["w_out"] = np.ascontiguousarray(inp["w_out"][:NL], f32)
    qn = np.asarray(inp["q_norm"][:NL], f32)
    kvn = np.asarray(inp["kv_norm"][:NL], f32)
    common["qkg"] = np.ascontiguousarray(np.stack([qn[:, 0:128], qn[:, 128:256], kvn], axis=-1).transpose(1, 0, 2))
    seln = np.zeros((128, 4, 32), f32)
    for row in range(128):
        for c in range(4):
            seln[row, c, 2 * c + row // 64] = 1
    selr = np.zeros((128, 32), f32)
    for row in range(128):
        selr[row, row % 8] = 1
    common["seln"] = seln.astype(bf)
    common["selr"] = selr.astype(bf)
    common["ones_bf"] = np.ones((128, 512), f32).astype(bf)
    common["ones_f"] = np.ones((128, 128), f32)
    common["ones_row"] = np.ones((NH, 512), f32).astype(bf)
    common["ident_f"] = np.eye(128, dtype=f32)
    idx = np.arange(128)
    rowi, coli = idx[:, None], idx[None, :]
    common["masks"] = np.ascontiguousarray(np.stack([rowi <= coli, rowi >= coli, rowi < coli, rowi > coli], axis=1).astype(f32))
    inv = (1.0 / (np.float32(10000.0) ** (np.arange(0, 32, 2, dtype=f32) / np.float32(32)))).astype(f32)
    pos = np.concatenate([np.arange(lp, dtype=f32) for lp in cfg.LP])
    ang = (pos[:, None] * inv[None, :]).astype(f32)
    common["cosT"] = np.ascontiguousarray(np.repeat(np.cos(ang).T.astype(f32), NH, axis=0))
    common["sinT"] = np.ascontiguousarray(np.repeat(np.sin(ang).T.astype(f32), NH, axis=0))
    smu = np.asarray(inp["shift_mu"][:NL], f32)
    mu = np.zeros((128, NL, 16, 2), f32)
    roff0 = GROUPS["r0"][0]
    for gi, name in enumerate(RW_GROUPS):
        off, wdt = GROUPS[name]
        mu[0:wdt, :, gi, :] = smu[:, :, off - roff0:off - roff0 + wdt].transpose(2, 0, 1)
    common["mu"] = mu
    common.update(rwkv_host(cfg, inp))
    maps = []
    for c in range(len(xs_list)):
        m = dict(common)
        m["x0"] = np.ascontiguousarray(xs_list[c][0], f32)
        m["x1"] = np.ascontiguousarray(xs_list[c][1], f32)
        maps.append(m)
    return maps


def rwkv_host(cfg, inp):
    NL = cfg.NL
    f32 = np.float32
    out = {}
    rwp = np.zeros((128, NL, NH, 9, 64), f32)
    lw = np.zeros((NL, NH, 128, 2, 64), f32)
    g2 = np.zeros((NL, NH, 160, 64), f32)
    for l in range(NL):
        for h in range(NH):
            hs = slice(64 * h, 64 * h + 64)
            rows = [inp["decay_w0"][l, 0, hs], inp["decay_w0"][l, 1, hs], inp["iclr_a0"][l, 0, hs], inp["iclr_a0"][l, 1, hs],
                    inp["key_k_k"][l, hs], inp["key_k_a"][l, hs], inp["lnx_w"][l, hs], inp["lnx_b"][l, hs], inp["bonus_r_k"][l, h, :]]
            rwp[:, l, h, :, :] = np.stack([np.asarray(r_, f32) for r_ in rows], 0)[None]
            for d in range(2):
                lw[l, h, 64 * d:64 * d + 64, 0, :] = inp["decay_w2"][l, d, :, hs]
                lw[l, h, 64 * d:64 * d + 64, 1, :] = inp["iclr_a2"][l, d, :, hs]
            g2[l, h] = inp["gate_g2"][l, :, hs]
    out["rwp"] = rwp
    out["lw"] = lw
    out["g2"] = g2
    vm = np.zeros((128, 2), f32)
    for s_ in range(2):
        nvalid = cfg.LS[s_] - 128 * (cfg.NB[s_] - 1)
        vm[:nvalid, s_] = 1
    out["vmask"] = vm
    return out


_CACHE = {}


def kernel(**inp):
    cfg = Cfg()
    if "nc" not in _CACHE:
        _CACHE["nc"] = build(cfg)[0]
    nc = _CACHE["nc"]
    xp = np.asarray(inp["x_prompt"])
    xsm = np.asarray(inp["x_sample"])
    xs_list = [(xsm[c], xp[c % 2]) for c in range(8)]
    maps = host_maps(cfg, inp, xs_list)
    res = run_bass_kernel_spmd(nc, maps, core_ids=list(range(8)))
    y_s = np.stack([np.asarray(res.results[c]["y0"], np.float32) for c in range(8)], axis=0)
    y_p = np.stack([np.asarray(res.results[c]["y1"], np.float32) for c in range(2)], axis=0)
    return (y_p, y_s)
```

```python
import os
import numpy as np
import ml_dtypes
from contextlib import ExitStack
import concourse.bass as bass
import concourse.mybir as mybir
from concourse.bass_utils import run_bass_kernel_spmd

F32 = mybir.dt.float32
BF16 = mybir.dt.bfloat16
AF = mybir.ActivationFunctionType
ALU = mybir.AluOpType
AX = mybir.AxisListType

D = 1024
DFF = 2816
NH = 8
NMETA = 16
RMS_EPS = 1e-6
LNX_EPS = 64e-5
SCALE = 96 ** -0.5
CDEC = float(np.exp(-0.5))
NCOLP = 2592

GROUPS = {}
_o = 0
for _n, _w in ([("cq0", 128), ("cq1", 128), ("ckv", 128), ("kr1", 128), ("kr2", 128)]
               + [(f"r{i}", 128) for i in range(4)] + [(f"k{i}", 128) for i in range(4)]
               + [(f"v{i}", 128) for i in range(4)] + [("dw", 128), ("da", 128), ("dg0", 128), ("dg1", 32)]):
    GROUPS[_n] = (_o, _w)
    _o += _w
assert _o == NCOLP
RW_GROUPS = [f"r{i}" for i in range(4)] + [f"k{i}" for i in range(4)] + [f"v{i}" for i in range(4)] + ["dw", "da", "dg0", "dg1"]


class Sched:
    ENG = ("pe", "act", "dve", "pool", "sp")

    def __init__(self, nc, es, n_dsem=12):
        self.nc = nc
        self.e = dict(pe=nc.tensor, act=nc.scalar, dve=nc.vector, pool=nc.gpsimd, sp=nc.sync)
        self.semobj = {}
        self.cnt = {}
        for k in self.ENG:
            self.semobj[("e", k)] = es.enter_context(nc.semaphore("s_" + k))
            self.cnt[k] = 0
        self.dq = {}
        self.dqi = {}
        for q in ("sp", "act", "pool"):
            self.dq[q] = []
            for i in range(n_dsem):
                self.semobj[("d", q, i)] = es.enter_context(nc.semaphore(f"d_{q}{i}"))
                self.dq[q].append(0)
            self.dqi[q] = 0
        self.seen = {k: {} for k in self.ENG}
        self.lastw = {}
        self.lastr = {}
        self.n_ins = 0

    def _wait(self, eng, sk, val):
        if val <= 0 or self.seen[eng].get(sk, 0) >= val:
            return
        if sk == ("e", "pe") and eng == "pe":
            return
        self.e[eng].wait_ge(self.semobj[sk], val)
        self.seen[eng][sk] = val

    def _deps(self, eng, r, w):
        for res in r:
            lw = self.lastw.get(res)
            if lw:
                self._wait(eng, *lw)
        for res in w:
            lw = self.lastw.get(res)
            if lw:
                self._wait(eng, *lw)
            for sk, v in self.lastr.get(res, {}).items():
                self._wait(eng, sk, v)

    def _mark(self, sk, v, r, w):
        for res in r:
            self.lastr.setdefault(res, {})[sk] = v
        for res in w:
            self.lastw[res] = (sk, v)
            self.lastr[res] = {}

    PSUM_NAMES = ("pT", "pG", "pU", "pD", "pA", "pN", "pS", "pO", "pB", "pX", "pY", "pZ")

    def op(self, eng, fn, r=(), w=()):
        extra = [k for k in r if (k[0] if isinstance(k, tuple) else k) in self.PSUM_NAMES]
        if extra:
            w = list(w) + extra
        self._deps(eng, r, w)
        ins = fn(self.e[eng])
        self.cnt[eng] += 1
        ins.then_inc(self.semobj[("e", eng)], 1)
        self._mark(("e", eng), self.cnt[eng], r, w)
        self.n_ins += 1
        return ins

    def dma(self, q, out, in_, r=(), w=()):
        self._deps(q, r, w)
        i = self.dqi[q]
        self.dqi[q] = (i + 1) % len(self.dq[q])
        sk = ("d", q, i)
        self._wait(q, sk, self.dq[q][i])
        self.dq[q][i] += 16
        self.e[q].dma_start(out=out, in_=in_).then_inc(self.semobj[sk], 16)
        self._mark(sk, self.dq[q][i], r, w)
        self.n_ins += 1

    def mark(self, label):
        if not hasattr(self, "marks"):
            self.marks = []
        self.marks.append((label, dict(self.cnt)))

    def barrier(self):
        for eng in self.ENG:
            for k in self.ENG:
                self._wait(eng, ("e", k), self.cnt[k])
            for q in self.dq:
                for i, v in enumerate(self.dq[q]):
                    self._wait(eng, ("d", q, i), v)
        self.lastw = {}
        self.lastr = {}

    def finish(self):
        for k in self.ENG:
            self._wait("sp", ("e", k), self.cnt[k])
        for q in self.dq:
            for i, v in enumerate(self.dq[q]):
                self._wait("sp", ("d", q, i), v)


class Cfg:
    def __init__(self, LS=(2064, 8208), NL=2, debug=False, stages="all"):
        self.LS = list(LS)
        self.NL = NL
        self.debug = debug
        self.stages = stages
        self.NB = [-(-L // 128) for L in self.LS]
        self.LP = [nb * 128 for nb in self.NB]
        self.BASE = [0]
        for lp in self.LP[:-1]:
            self.BASE.append(self.BASE[-1] + lp)
        self.TP = sum(self.LP)
        self.NBT = self.TP // 128
        self.tiles = []
        b = 0
        while b < self.NBT:
            n = min(4, self.NBT - b)
            self.tiles.append((b, n))
            b += n


def build(cfg):
    nc = bass.Bass("TRN2", target_bir_lowering=False)
    NL, TP = cfg.NL, cfg.TP

    def din(name, shape, dt=F32):
        return nc.dram_tensor(name, list(shape), dt, kind="ExternalInput").ap()

    def dscr(name, shape, dt=F32):
        if cfg.debug:
            return nc.dram_tensor(name, list(shape), dt, kind="ExternalOutput").ap()
        return nc.dram_tensor(name, list(shape), dt).ap()

    xin = [din(f"x{s}", [cfg.LS[s] - NMETA, D]) for s in range(2)]
    yout = [nc.dram_tensor(f"y{s}", [cfg.LS[s] - NMETA, D], F32, kind="ExternalOutput").ap() for s in range(2)]
    meta = din("meta", [NMETA, D])
    wg = [din(f"ffn{k}_wg", [NL, D, DFF]) for k in (1, 2)]
    wu = [din(f"ffn{k}_wu", [NL, D, DFF]) for k in (1, 2)]
    wd = [din(f"ffn{k}_wd", [NL, DFF, D]) for k in (1, 2)]
    gains = din("gains", [128, NL, 3, 8])
    fnorm = din("fnorm", [128, D])
    ident_bf = din("ident_bf", [128, 128], BF16)
    zeros = din("zeros", [128, D])

    w_in = din("w_in", [NL, D, NCOLP])
    w_uq = din("w_uq", [NL, 256, 768])
    w_ukv = din("w_ukv", [NL, 128, 1024])
    w_out = din("w_out", [NL, D, D])
    qkg = din("qkg", [128, NL, 3])
    cosT = din("cosT", [128, TP])
    sinT = din("sinT", [128, TP])
    seln_in = din("seln", [128, 4, 32], BF16)
    selr_in = din("selr", [128, 32], BF16)
    ones_in = din("ones_bf", [128, 512], BF16)
    onesf_in = din("ones_f", [128, 128])
    onesrow_in = din("ones_row", [NH, 512], BF16)
    identf_in = din("ident_f", [128, 128])
    masks_in = din("masks", [128, 4, 128])
    mu_in = din("mu", [128, NL, 16, 2])
    rwp = din("rwp", [128, NL, NH, 9, 64])
    vmask_in = din("vmask", [128, 2])
    lw_in = din("lw", [NL, NH, 128, 2, 64])
    g2_in = din("g2", [NL, NH, 160, 64])

    H = dscr("H", [TP, D])
    XNT = dscr("XNT", [D, TP], BF16)
    QTd = dscr("QTd", [97, NH, TP], BF16)
    KTd = dscr("KTd", [97, NH, TP], BF16)
    VA = dscr("VA", [TP, NH, 65], BF16)
    QNd = dscr("QNd", [NH, TP])
    KNd = dscr("KNd", [NH, TP])
    RAW = dscr("RAW", [1952, TP])
    RWS = dscr("RWS", [1536, TP])
    LOR = dscr("LOR", [416, TP], BF16)
    MIXT = dscr("MIXT", [D, TP], BF16)
    EPI = dscr("EPI", [TP, 2, 64])

    es = ExitStack()
    S = Sched(nc, es)

    uid = [0]

    def sb(st, name, shape, dt):
        uid[0] += 1
        return st.enter_context(nc.sbuf_tensor(f"{name}_{uid[0]}", list(shape), dt))

    def ps(st, name, shape, dt=F32):
        uid[0] += 1
        return st.enter_context(nc.psum_tensor(f"{name}_{uid[0]}", list(shape), dt))

    cst = ExitStack()
    ident = sb(cst, "ident", [128, 128], BF16)
    gn = sb(cst, "gn", [128, NL, 3, 8], F32)
    fn_sb = sb(cst, "fn_sb", [128, D], F32)
    S.dma("sp", ident[:], ident_bf, w=["ident"])
    S.dma("sp", gn[:], gains, w=["gn"])
    S.dma("sp", fn_sb[:], fnorm, w=["fn"])

    for s in range(2):
        b0 = cfg.BASE[s]
        L = cfg.LS[s]
        S.dma("sp", H[b0:b0 + NMETA, :], meta, w=[("H", "init", s)])
        nrow = L - NMETA
        r0 = 0
        while r0 < nrow:
            n = min(2048, nrow - r0)
            S.dma("act" if (r0 // 2048) % 2 else "sp", H[b0 + NMETA + r0:b0 + NMETA + r0 + n, :], xin[s][r0:r0 + n, :], w=[("H", "init", s, r0)])
            r0 += n
        if cfg.LP[s] > L:
            S.dma("pool", H[b0 + L:b0 + cfg.LP[s], :], zeros[0:cfg.LP[s] - L, :], w=[("H", "initz", s)])
    S.barrier()

    def ffn_pass(l, k, half, with_norm, final=False):
        S.mark(f"ffn{k}h{half}_L{l}")
        st = ExitStack()
        NF = 11
        Wg = sb(st, "Wg", [128, 8, NF * 128], BF16)
        Wu = sb(st, "Wu", [128, 8, NF * 128], BF16)
        Wd = sb(st, "Wd", [128, NF, D], BF16)
        stg = [sb(st, f"stg{i}", [128, NF * 128], F32) for i in range(2)]
        hb = [sb(st, f"hb{i}", [128, D], F32) for i in range(2)]
        hc = [sb(st, f"hc{i}", [128, D], F32) for i in range(2)]
        xnbs = [sb(st, f"xnb{i}", [128, 4, D], BF16) for i in range(2)]
        xnT = [sb(st, f"xnT{i}", [128, 8, 512], BF16) for i in range(2)]
        hT = [sb(st, f"hT{i}", [128, NF, 512], BF16) for i in range(2)]
        sg = [sb(st, f"sg{i}", [128, 512], F32) for i in range(2)]
        ssq = sb(st, "ssq", [128, 12], F32)
        rs = sb(st, "rs", [128, 12], F32)
        junk = sb(st, "junk", [128, D], BF16)
        yb = [sb(st, f"yb{i}", [128, D], F32) for i in range(2)]
        pT = [ps(st, f"pT{i}", [128, 1024], BF16) for i in range(2)]
        pG = [ps(st, f"pG{i}", [128, 512]) for i in range(2)]
        pU = [ps(st, f"pU{i}", [128, 512]) for i in range(2)]
        pD = [ps(st, f"pD{i}", [128, 512]) for i in range(2)]
        f0 = half * NF * 128
        do_mix = (k == 1 and half == 0 and cfg.stages != "ffn")
        if do_mix:
            Wout = sb(st, "Wout", [128, 8, D], BF16)
            mx = [sb(st, f"mx{i}", [128, 8, 512], BF16) for i in range(2)]
            for fc in range(8):
                load_cast(Wout[:, fc, :], w_out[l, fc * 128:(fc + 1) * 128, :], stg[fc % 2][:, 0:D], ("stg", fc % 2), "sp" if fc % 2 == 0 else "act", ["dve", "pool"][fc % 2])
        ci = 0
        cast_eng = ["dve", "pool", "act"]
        for (Wsb, wsrc) in ((Wg, wg[k]), (Wu, wu[k])):
            for dc in range(8):
                sgi = ci % 2
                S.dma("sp" if ci % 2 == 0 else "act", stg[sgi][:], wsrc[l, dc * 128:(dc + 1) * 128, f0:f0 + NF * 128], w=[("stg", sgi)])
                eng = cast_eng[ci % 3]
                if eng == "act":
                    S.op("act", lambda e, o=Wsb[:, dc, :], i=stg[sgi][:]: e.copy(out=o, in_=i), r=[("stg", sgi)], w=[("W",)])
                else:
                    S.op(eng, lambda e, o=Wsb[:, dc, :], i=stg[sgi][:]: e.tensor_copy(out=o, in_=i), r=[("stg", sgi)], w=[("W",)])
                ci += 1
        for fc in range(NF):
            sgi = ci % 2
            S.dma("sp" if ci % 2 == 0 else "act", stg[sgi][:, 0:D], wd[k][l, f0 + fc * 128:f0 + (fc + 1) * 128, :], w=[("stg", sgi)])
            eng = cast_eng[ci % 3]
            if eng == "act":
                S.op("act", lambda e, o=Wd[:, fc, :], i=stg[sgi][:, 0:D]: e.copy(out=o, in_=i), r=[("stg", sgi)], w=[("W",)])
            else:
                S.op(eng, lambda e, o=Wd[:, fc, :], i=stg[sgi][:, 0:D]: e.tensor_copy(out=o, in_=i), r=[("stg", sgi)], w=[("W",)])
            ci += 1
        gidx = 0 if k == 0 else 2

        def stageA(ti):
            b0, nblk = cfg.tiles[ti]
            nt = nblk * 128
            xt = xnT[ti % 2]
            XK = ("xnT", ti % 2)
            if not with_norm:
                S.dma("sp", xt[:, :, 0:nt], XNT.rearrange("(c p) n -> p c n", p=128)[:, :, b0 * 128:b0 * 128 + nt], r=[("XNT", ti)], w=[XK])
                return
            if do_mix:
                mt = mx[ti % 2]
                MXK = ("mx", ti % 2)
                S.dma("pool", mt[:, :, 0:nt], MIXT.rearrange("(c p) n -> p c n", p=128)[:, :, b0 * 128:b0 * 128 + nt], r=[("MIXT",)], w=[MXK])
            for b in range(nblk):
                h = hb[b % 2]
                HK = ("hb", b % 2)
                S.dma("sp" if b % 2 == 0 else "act", h[:], H[(b0 + b) * 128:(b0 + b + 1) * 128, :], r=[("H", b0 + b)], w=[HK])
                if do_mix:
                    for hf in range(2):
                        for fc in range(8):
                            S.op("pe", lambda e, fc=fc, hf=hf, b=b: e.matmul(pD[hf][:, :], lhsT=mt[:, fc, b * 128:(b + 1) * 128], rhs=Wout[:, fc, hf * 512:(hf + 1) * 512], start=(fc == 0), stop=(fc == 7)),
                                 r=[MXK, "W"], w=[("pD", hf)])
                        S.op("dve", lambda e, hf=hf, h=h: e.tensor_tensor(out=h[:, hf * 512:(hf + 1) * 512], in0=pD[hf][:, :], in1=h[:, hf * 512:(hf + 1) * 512], op=ALU.add), r=[("pD", hf), HK], w=[HK])
                    S.dma("pool", H[(b0 + b) * 128:(b0 + b + 1) * 128, :], h[:], r=[HK], w=[("H", b0 + b)])
                S.op("act", lambda e, h=h, b=b: e.activation(out=junk[:], in_=h[:], func=AF.Square, accum_out=ssq[:, 4 * (ti % 2) + b:4 * (ti % 2) + b + 1]), r=[HK], w=["junk", ("ssq", 4 * (ti % 2) + b)])
                S.op("dve", lambda e, b=b: e.tensor_scalar(out=rs[:, 4 * (ti % 2) + b:4 * (ti % 2) + b + 1], in0=ssq[:, 4 * (ti % 2) + b:4 * (ti % 2) + b + 1], scalar1=1.0 / D, scalar2=RMS_EPS, op0=ALU.mult, op1=ALU.add), r=[("ssq", 4 * (ti % 2) + b)], w=[("rs", 4 * (ti % 2) + b)])
                S.op("act", lambda e, b=b: e.activation(out=rs[:, 4 * (ti % 2) + b:4 * (ti % 2) + b + 1], in_=rs[:, 4 * (ti % 2) + b:4 * (ti % 2) + b + 1], func=AF.Sqrt), r=[("rs", 4 * (ti % 2) + b)], w=[("rs", 4 * (ti % 2) + b)])
                S.op("dve", lambda e, b=b: e.reciprocal(out=rs[:, 4 * (ti % 2) + b:4 * (ti % 2) + b + 1], in_=rs[:, 4 * (ti % 2) + b:4 * (ti % 2) + b + 1]), r=[("rs", 4 * (ti % 2) + b)], w=[("rs", 4 * (ti % 2) + b)])
                S.op("act", lambda e, h=h, b=b: e.activation(out=xnbs[ti % 2][:, b, :], in_=h[:], func=AF.Copy, scale=rs[:, 4 * (ti % 2) + b:4 * (ti % 2) + b + 1]), r=[HK, ("rs", 4 * (ti % 2) + b)], w=[("xnb", ti % 2, b)])

        def stageA2(ti):
            b0, nblk = cfg.tiles[ti]
            nt = nblk * 128
            xt = xnT[ti % 2]
            XK = ("xnT", ti % 2)
            xnb = xnbs[ti % 2]
            if not with_norm:
                return
            for rnd in range(2):
                for b in range(nblk):
                    for j in range(4):
                        dc = rnd * 4 + j
                        S.op("pe", lambda e, b=b, dc=dc, j=j: e.transpose(out=pT[j // 2][:, (j % 2) * 512 + b * 128:(j % 2) * 512 + (b + 1) * 128], in_=xnb[:, b, dc * 128:(dc + 1) * 128], identity=ident[:]),
                             r=[("xnb", ti % 2, b), "ident"], w=[("pT", j // 2)])
                for j in range(4):
                    dc = rnd * 4 + j
                    eng = "act" if j % 2 == 0 else "dve"
                    if eng == "act":
                        S.op("act", lambda e, dc=dc, j=j: e.activation(out=xt[:, dc, 0:nt], in_=pT[j // 2][:, (j % 2) * 512:(j % 2) * 512 + nt], func=AF.Copy, scale=gn[:, l, gidx, dc:dc + 1]),
                             r=[("pT", j // 2), "gn"], w=[XK])
                    else:
                        S.op("dve", lambda e, dc=dc, j=j: e.tensor_scalar(out=xt[:, dc, 0:nt], in0=pT[j // 2][:, (j % 2) * 512:(j % 2) * 512 + nt], scalar1=gn[:, l, gidx, dc:dc + 1], scalar2=None, op0=ALU.mult),
                             r=[("pT", j // 2), "gn"], w=[XK])
            if half == 0:
                S.dma("pool", XNT.rearrange("(c p) n -> p c n", p=128)[:, :, b0 * 128:b0 * 128 + nt], xt[:, :, 0:nt], r=[XK], w=[("XNT", ti)])

        def stageB(ti):
            b0, nblk = cfg.tiles[ti]
            nt = nblk * 128
            xt = xnT[ti % 2]
            XK = ("xnT", ti % 2)
            ht = hT[ti % 2]
            HTK = ("hT", ti % 2)
            for fc in range(NF):
                for dc in range(8):
                    S.op("pe", lambda e, fc=fc, dc=dc: e.matmul(pG[fc % 2][:, 0:nt], lhsT=Wg[:, dc, fc * 128:(fc + 1) * 128], rhs=xt[:, dc, 0:nt], start=(dc == 0), stop=(dc == 7)),
                         r=[XK, ("W",), "W"], w=[("pG", fc % 2)])
                for dc in range(8):
                    S.op("pe", lambda e, fc=fc, dc=dc: e.matmul(pU[fc % 2][:, 0:nt], lhsT=Wu[:, dc, fc * 128:(fc + 1) * 128], rhs=xt[:, dc, 0:nt], start=(dc == 0), stop=(dc == 7)),
                         r=[XK, ("W",), "W"], w=[("pU", fc % 2)])
                S.op("act", lambda e, fc=fc: e.activation(out=sg[fc % 2][:, 0:nt], in_=pG[fc % 2][:, 0:nt], func=AF.Silu), r=[("pG", fc % 2)], w=[("sg", fc % 2)])
                S.op("dve", lambda e, fc=fc: e.tensor_tensor(out=ht[:, fc, 0:nt], in0=sg[fc % 2][:, 0:nt], in1=pU[fc % 2][:, 0:nt], op=ALU.mult), r=[("sg", fc % 2), ("pU", fc % 2)], w=[HTK])

        def stageC(ti):
            b0, nblk = cfg.tiles[ti]
            ht = hT[ti % 2]
            HTK = ("hT", ti % 2)
            for b in range(nblk):
                h = hc[b % 2]
                HK = ("hc", b % 2)
                S.dma("act" if b % 2 == 0 else "sp", h[:], H[(b0 + b) * 128:(b0 + b + 1) * 128, :], r=[("H", b0 + b)], w=[HK])
                for hf in range(2):
                    for fc in range(NF):
                        S.op("pe", lambda e, fc=fc, hf=hf, b=b: e.matmul(pD[hf][:, :], lhsT=ht[:, fc, b * 128:(b + 1) * 128], rhs=Wd[:, fc, hf * 512:(hf + 1) * 512], start=(fc == 0), stop=(fc == NF - 1)),
                             r=[HTK, ("W",), "W"], w=[("pD", hf)])
                    S.op("dve", lambda e, hf=hf, h=h: e.scalar_tensor_tensor(out=h[:, hf * 512:(hf + 1) * 512], in0=pD[hf][:, :], scalar=0.5, in1=h[:, hf * 512:(hf + 1) * 512], op0=ALU.mult, op1=ALU.add),
                         r=[("pD", hf), HK], w=[HK])
                if not final:
                    S.dma("pool", H[(b0 + b) * 128:(b0 + b + 1) * 128, :], h[:], r=[HK], w=[("H", b0 + b)])
                else:
                    y = yb[b % 2]
                    YK = ("yb", b % 2)
                    c = 8 + (b % 2)
                    S.op("act", lambda e, h=h, c=c: e.activation(out=junk[:], in_=h[:], func=AF.Square, accum_out=ssq[:, c:c + 1]), r=[HK], w=["junk", ("ssq", c)])
                    S.op("dve", lambda e, c=c: e.tensor_scalar(out=rs[:, c:c + 1], in0=ssq[:, c:c + 1], scalar1=1.0 / D, scalar2=RMS_EPS, op0=ALU.mult, op1=ALU.add), r=[("ssq", c)], w=[("rs", c)])
                    S.op("act", lambda e, c=c: e.activation(out=rs[:, c:c + 1], in_=rs[:, c:c + 1], func=AF.Sqrt), r=[("rs", c)], w=[("rs", c)])
                    S.op("dve", lambda e, c=c: e.reciprocal(out=rs[:, c:c + 1], in_=rs[:, c:c + 1]), r=[("rs", c)], w=[("rs", c)])
                    S.op("dve", lambda e, h=h, y=y, c=c: e.scalar_tensor_tensor(out=y[:], in0=h[:], scalar=rs[:, c:c + 1], in1=fn_sb[:], op0=ALU.mult, op1=ALU.mult), r=[HK, ("rs", c), "fn"], w=[YK])
                    g0 = (b0 + b) * 128
                    for s in range(2):
                        lo = max(g0, cfg.BASE[s] + NMETA)
                        hi = min(g0 + 128, cfg.BASE[s] + cfg.LS[s])
                        if hi > lo:
                            S.dma("pool", yout[s][lo - cfg.BASE[s] - NMETA:hi - cfg.BASE[s] - NMETA, :], y[lo - g0:hi - g0, :], r=[YK], w=[("y", s, lo)])

        nT = len(cfg.tiles)
        stageA(0)
        stageA2(0)
        if nT > 1:
            stageA(1)
        for ti in range(nT):
            stageB(ti)
            if ti + 1 < nT:
                stageA2(ti + 1)
            stageC(ti)
            if ti + 2 < nT:
                stageA(ti + 2)
        S.barrier()
        st.close()


    seln = sb(cst, "seln", [128, 4, 32], BF16)
    selr = sb(cst, "selr", [128, 32], BF16)
    ones_sb = sb(cst, "ones_sb", [128, 512], BF16)
    onesf = sb(cst, "onesf", [128, 128], F32)
    identf = sb(cst, "identf", [128, 128], F32)
    qk_sb = sb(cst, "qk_sb", [128, NL, 3], F32)
    S.dma("sp", seln[:], seln_in, w=["seln"])
    S.dma("sp", selr[:], selr_in, w=["selr"])
    S.dma("sp", ones_sb[:], ones_in, w=["ones"])
    S.dma("sp", onesf[:], onesf_in, w=["onesf"])
    S.dma("sp", identf[:], identf_in, w=["identf"])
    S.dma("sp", qk_sb[:], qkg, w=["qk"])
    S.barrier()

    def load_cast(dst, src, stg_ap, stg_key, q, eng):
        S.dma(q, stg_ap, src, w=[stg_key])
        if eng == "act":
            S.op("act", lambda e: e.copy(out=dst, in_=stg_ap), r=[stg_key], w=["W"])
        else:
            S.op(eng, lambda e: e.tensor_copy(out=dst, in_=stg_ap), r=[stg_key], w=["W"])

    def proj_pass(l):
        S.mark(f"proj_L{l}")
        st = ExitStack()
        Win = sb(st, "Win", [128, 8, NCOLP], BF16)
        Wuq = sb(st, "Wuq", [128, 2, 768], BF16)
        Wukv = sb(st, "Wukv", [128, 1024], BF16)
        stg = [sb(st, f"stg{i}", [128, NCOLP], F32) for i in range(2)]
        hb = [sb(st, f"hb{i}", [128, D], F32) for i in range(2)]
        xnbs = [sb(st, f"xnb{i}", [128, 4, D], BF16) for i in range(2)]
        xnT = [sb(st, f"xnT{i}", [128, 8, 512], BF16) for i in range(2)]
        ssq = sb(st, "ssq", [128, 12], F32)
        rs = sb(st, "rs", [128, 12], F32)
        junk = sb(st, "junk", [128, D], BF16)
        cq_sb = sb(st, "cq_sb", [128, 3, 512], F32)
        sqb = [sb(st, f"sqb{i}", [128, 512], BF16) for i in range(3)]
        sq6 = sb(st, "sq6", [128, 6, 512], BF16)
        if os.environ.get("DUMMY_KB"):
            dummy = sb(st, "dummy", [128, int(os.environ["DUMMY_KB"]) * 256], F32)
        rstd = [sb(st, f"rstd{i}", [128, 512], F32) for i in range(2)]
        cqn = sb(st, "cqn", [128, 2, 512], BF16)
        ckvn = sb(st, "ckvn", [128, 512], BF16)
        qn_sb = sb(st, "qn_sb", [128, 4, 512], BF16)
        kn_sb = sb(st, "kn_sb", [128, 4, 512], BF16)
        xr = [sb(st, f"xr{i}", [128, 512], F32) for i in range(2)]
        tt = [sb(st, f"tt{i}", [128, 512], F32) for i in range(4)]
        rr = [sb(st, f"rr{i}", [128, 512], BF16) for i in range(4)]
        cs = sb(st, "cs", [128, 512], F32)
        sn = sb(st, "sn", [128, 512], F32)
        VAt = [sb(st, f"VAt{i}", [128, 4, NH, 65], BF16) for i in range(2)]
        rwb = [sb(st, f"rwb{i}", [128, 512], F32) for i in range(4)]
        nrm = sb(st, "nrm", [8, 2, 512], F32)
        pT = [ps(st, f"pT{i}", [128, 1024], BF16) for i in range(2)]
        pA = [ps(st, f"pA{i}", [128, 512]) for i in range(5)]
        pN = ps(st, "pN", [128, 512])
        for i in range(2):
            S.op("pool", lambda e, i=i: e.memset(VAt[i][:], 1.0), w=[("VAt", i)])
        for dc in range(8):
            load_cast(Win[:, dc, :], w_in[l, dc * 128:(dc + 1) * 128, :], stg[dc % 2][:], ("stg", dc % 2), "sp" if dc % 2 == 0 else "act", ["dve", "pool"][dc % 2])
        load_cast(Wuq[:], w_uq[l].rearrange("(k p) n -> p k n", p=128), stg[0][:, 0:1536].rearrange("p (k n) -> p k n", k=2), ("stg", 0), "sp", "dve")
        load_cast(Wukv[:], w_ukv[l], stg[1][:, 0:1024], ("stg", 1), "act", "pool")

        def stageA(ti):
            b0, nblk = cfg.tiles[ti]
            nt = nblk * 128
            xt = xnT[ti % 2]
            XK = ("xnT", ti % 2)
            for b in range(nblk):
                h = hb[b % 2]
                HK = ("hb", b % 2)
                S.dma("sp" if b % 2 == 0 else "act", h[:], H[(b0 + b) * 128:(b0 + b + 1) * 128, :], r=[("H", b0 + b)], w=[HK])
                S.op("act", lambda e, h=h, b=b: e.activation(out=junk[:], in_=h[:], func=AF.Square, accum_out=ssq[:, 4 * (ti % 2) + b:4 * (ti % 2) + b + 1]), r=[HK], w=["junk", ("ssq", 4 * (ti % 2) + b)])
                S.op("dve", lambda e, b=b: e.tensor_scalar(out=rs[:, 4 * (ti % 2) + b:4 * (ti % 2) + b + 1], in0=ssq[:, 4 * (ti % 2) + b:4 * (ti % 2) + b + 1], scalar1=1.0 / D, scalar2=RMS_EPS, op0=ALU.mult, op1=ALU.add), r=[("ssq", 4 * (ti % 2) + b)], w=[("rs", 4 * (ti % 2) + b)])
                S.op("act", lambda e, b=b: e.activation(out=rs[:, 4 * (ti % 2) + b:4 * (ti % 2) + b + 1], in_=rs[:, 4 * (ti % 2) + b:4 * (ti % 2) + b + 1], func=AF.Sqrt), r=[("rs", 4 * (ti % 2) + b)], w=[("rs", 4 * (ti % 2) + b)])
                S.op("dve", lambda e, b=b: e.reciprocal(out=rs[:, 4 * (ti % 2) + b:4 * (ti % 2) + b + 1], in_=rs[:, 4 * (ti % 2) + b:4 * (ti % 2) + b + 1]), r=[("rs", 4 * (ti % 2) + b)], w=[("rs", 4 * (ti % 2) + b)])
                S.op("act", lambda e, h=h, b=b: e.activation(out=xnbs[ti % 2][:, b, :], in_=h[:], func=AF.Copy, scale=rs[:, 4 * (ti % 2) + b:4 * (ti % 2) + b + 1]), r=[HK, ("rs", 4 * (ti % 2) + b)], w=[("xnb", ti % 2, b)])

        def stageA2(ti):
            b0, nblk = cfg.tiles[ti]
            nt = nblk * 128
            xt = xnT[ti % 2]
            XK = ("xnT", ti % 2)
            xnb = xnbs[ti % 2]
            for rnd in range(2):
                for b in range(nblk):
                    for j in range(4):
                        dc = rnd * 4 + j
                        S.op("pe", lambda e, b=b, dc=dc, j=j: e.transpose(out=pT[j // 2][:, (j % 2) * 512 + b * 128:(j % 2) * 512 + (b + 1) * 128], in_=xnb[:, b, dc * 128:(dc + 1) * 128], identity=ident[:]),
                             r=[("xnb", ti % 2, b), "ident"], w=[("pT", j // 2)])
                for j in range(4):
                    dc = rnd * 4 + j
                    if j % 2 == 0:
                        S.op("act", lambda e, dc=dc, j=j: e.activation(out=xt[:, dc, 0:nt], in_=pT[j // 2][:, (j % 2) * 512:(j % 2) * 512 + nt], func=AF.Copy, scale=gn[:, l, 1, dc:dc + 1]),
                             r=[("pT", j // 2), "gn"], w=[XK])
                    else:
                        S.op("dve", lambda e, dc=dc, j=j: e.tensor_scalar(out=xt[:, dc, 0:nt], in0=pT[j // 2][:, (j % 2) * 512:(j % 2) * 512 + nt], scalar1=gn[:, l, 1, dc:dc + 1], scalar2=None, op0=ALU.mult),
                             r=[("pT", j // 2), "gn"], w=[XK])

        def stageB(ti):
            b0, nblk = cfg.tiles[ti]
            nt = nblk * 128
            c0 = b0 * 128
            xt = xnT[ti % 2]
            XK = ("xnT", ti % 2)
            S.dma("sp", cs[:, 0:nt], cosT[:, c0:c0 + nt], w=["cs"])
            S.dma("act", sn[:, 0:nt], sinT[:, c0:c0 + nt], w=["sn"])
            bank = [0]
            evi = [0]

            def nextbank():
                i = bank[0] % 5
                bank[0] += 1
                return pA[i], ("pA", i)

            def evac(out, in_, r, w):
                evi[0] += 1
                if evi[0] % 2 == 0:
                    S.op("act", lambda e: e.copy(out=out, in_=in_), r=r, w=w)
                else:
                    S.op("dve", lambda e: e.tensor_copy(out=out, in_=in_), r=r, w=w)

            def win_group(name):
                off, wdt = GROUPS[name]
                p, pk = nextbank()
                for dc in range(8):
                    S.op("pe", lambda e, dc=dc: e.matmul(p[0:wdt, 0:nt], lhsT=Win[:, dc, off:off + wdt], rhs=xt[:, dc, 0:nt], start=(dc == 0), stop=(dc == 7)), r=[XK, "W"], w=[pk])
                return p, pk

            def rms_part1(chunks, slot0):
                for j, (p, pk, sbuf, sk) in enumerate(chunks):
                    S.op("act", lambda e, p=p, sbuf=sbuf: e.copy(out=sbuf, in_=p[:, 0:nt]), r=[pk], w=[sk])
                    S.op("act", lambda e, p=p, j=j: e.activation(out=sqb[slot0 + j][:, 0:nt], in_=p[:, 0:nt], func=AF.Square), r=[pk], w=[("sqb", slot0 + j)])

            def rms_part2(chunks, slot0, n_feat, gcol0, outs, rsd, rkey):
                p2, pk2 = nextbank()
                for j in range(len(chunks)):
                    S.op("pe", lambda e, j=j: e.matmul(p2[:, 0:nt], lhsT=ones_sb[:, 0:128], rhs=sqb[slot0 + j][:, 0:nt], start=(j == 0), stop=(j == len(chunks) - 1)), r=[("sqb", slot0 + j), "ones"], w=[pk2])
                S.op("dve", lambda e: e.tensor_scalar(out=rsd[:, 0:nt], in0=p2[:, 0:nt], scalar1=1.0 / n_feat, scalar2=RMS_EPS, op0=ALU.mult, op1=ALU.add), r=[pk2], w=[rkey])
                S.op("act", lambda e: e.activation(out=rsd[:, 0:nt], in_=rsd[:, 0:nt], func=AF.Sqrt), r=[rkey], w=[rkey])
                S.op("dve", lambda e: e.reciprocal(out=rsd[:, 0:nt], in_=rsd[:, 0:nt]), r=[rkey], w=[rkey])
                for j, (p, pk, sbuf, sk) in enumerate(chunks):
                    o, ok = outs[j]
                    S.op("dve", lambda e, sbuf=sbuf, o=o, j=j: e.scalar_tensor_tensor(out=o, in0=sbuf, scalar=qk_sb[:, l, gcol0 + j:gcol0 + j + 1], in1=rsd[:, 0:nt], op0=ALU.mult, op1=ALU.mult),
                         r=[sk, rkey, "qk"], w=[ok])

            def raw_groups(lo, hi):
                roff = GROUPS["r0"][0]
                for gi in range(lo, hi):
                    name = RW_GROUPS[gi]
                    off, wdt = GROUPS[name]
                    p, pk = win_group(name)
                    evac(rwb[gi % 4][0:wdt, 0:nt], p[0:wdt, 0:nt], [pk], [("rwb", gi % 4)])
                    S.dma(["sp", "act", "pool"][gi % 3], RAW[off - roff:off - roff + wdt, c0:c0 + nt], rwb[gi % 4][0:wdt, 0:nt], r=[("rwb", gi % 4)], w=[("RAW", ti, gi)])

            ch = []
            for j in range(2):
                p, pk = win_group(f"cq{j}")
                ch.append((p, pk, cq_sb[:, j, 0:nt], ("cq", j)))
            p, pk = win_group("ckv")
            chkv = [(p, pk, cq_sb[:, 2, 0:nt], ("cq", 2))]
            rms_part1(ch, 0)
            rms_part1(chkv, 2)
            raw_groups(0, 8)
            rms_part2(ch, 0, 256, 0, [(cqn[:, 0, 0:nt], ("cqn", 0)), (cqn[:, 1, 0:nt], ("cqn", 1))], rstd[0], "rstd0")
            rms_part2(chkv, 2, 128, 2, [(ckvn[:, 0:nt], "ckvn")], rstd[1], "rstd1")

            def rope(x1, x2, o1, o2, k1, k2, ok1, ok2):
                S.op("pool", lambda e: e.tensor_tensor(out=tt[0][:, 0:nt], in0=x1[:, 0:nt], in1=cs[:, 0:nt], op=ALU.mult), r=[k1, "cs"], w=[("tt", 0)])
                S.op("pool", lambda e: e.tensor_tensor(out=tt[1][:, 0:nt], in0=x2[:, 0:nt], in1=sn[:, 0:nt], op=ALU.mult), r=[k2, "sn"], w=[("tt", 1)])
                S.op("dve", lambda e: e.tensor_tensor(out=o1[:, 0:nt], in0=tt[0][:, 0:nt], in1=tt[1][:, 0:nt], op=ALU.subtract), r=[("tt", 0), ("tt", 1)], w=[ok1])
                S.op("pool", lambda e: e.tensor_tensor(out=tt[2][:, 0:nt], in0=x2[:, 0:nt], in1=cs[:, 0:nt], op=ALU.mult), r=[k2, "cs"], w=[("tt", 2)])
                S.op("pool", lambda e: e.tensor_tensor(out=tt[3][:, 0:nt], in0=x1[:, 0:nt], in1=sn[:, 0:nt], op=ALU.mult), r=[k1, "sn"], w=[("tt", 3)])
                S.op("dve", lambda e: e.tensor_tensor(out=o2[:, 0:nt], in0=tt[2][:, 0:nt], in1=tt[3][:, 0:nt], op=ALU.add), r=[("tt", 2), ("tt", 3)], w=[ok2])

            SUB = os.environ.get("QK_SUB", "namdep")

            def qk_side(which, nope_sb, dst, nd, slot):
                KSKIP = os.environ.get("K_SKIP", "")
                for c in range(0 if (which == "k" and "nope" in KSKIP) else 4):
                    p, pk = nextbank()
                    if which == "q":
                        for kc in range(2):
                            S.op("pe", lambda e, kc=kc, c=c: e.matmul(p[:, 0:nt], lhsT=Wuq[:, kc, c * 128:(c + 1) * 128], rhs=cqn[:, kc, 0:nt], start=(kc == 0), stop=(kc == 1)), r=[("cqn", kc), "W"], w=[pk])
                    else:
                        if os.environ.get("KSPLIT"):
                            for hh in range(2):
                                S.op("pe", lambda e, c=c, hh=hh: e.matmul(p[:, 0:nt], lhsT=Wukv[64 * hh:64 * hh + 64, c * 128:(c + 1) * 128], rhs=ckvn[64 * hh:64 * hh + 64, 0:nt], start=(hh == 0), stop=(hh == 1)), r=["ckvn", "W"], w=[pk])
                        else:
                            S.op("pe", lambda e, c=c: e.matmul(p[:, 0:nt], lhsT=Wukv[:, c * 128:(c + 1) * 128], rhs=ckvn[:, 0:nt], start=True, stop=True), r=["ckvn", "W"], w=[pk])
                    S.op("act", lambda e, c=c: e.copy(out=nope_sb[:, c, 0:nt], in_=p[:, 0:nt]), r=[pk], w=[(which + "n", c)])
                    S.op("act", lambda e, c=c: e.activation(out=sq6[:, c, 0:nt], in_=p[:, 0:nt], func=AF.Square), r=[pk], w=[("sq6", c)])
                if os.environ.get("KBAR"):
                    S.barrier()
                for j in range(0 if (which == "k" and "rope" in KSKIP) else 2):
                    if which == "q":
                        p, pk = nextbank()
                        for kc in range(2):
                            S.op("pe", lambda e, kc=kc, j=j: e.matmul(p[:, 0:nt], lhsT=Wuq[:, kc, 512 + j * 128:512 + (j + 1) * 128], rhs=cqn[:, kc, 0:nt], start=(kc == 0), stop=(kc == 1)), r=[("cqn", kc), "W"], w=[pk])
                    else:
                        p, pk = win_group(f"kr{j + 1}")
                    S.op("dve", lambda e, j=j: e.tensor_copy(out=xr[j][:, 0:nt], in_=p[:, 0:nt]), r=[pk], w=[("xr", j)])
                    S.op("act", lambda e, j=j: e.activation(out=sq6[:, 4 + j, 0:nt], in_=p[:, 0:nt], func=AF.Square), r=[pk], w=[("sq6", 4 + j)])
                if "n" in SUB:
                    for c in range(6):
                        S.op("pe", lambda e, c=c: e.matmul(pN[0:32, 0:nt], lhsT=(seln[:, c, :] if c < 4 else selr[:, :]), rhs=sq6[:, c, 0:nt], start=(c == 0), stop=(c == 5)), r=[("sq6", c), "seln", "selr"], w=["pN"])
                o1, o2 = rr[2 * slot], rr[2 * slot + 1]
                if "p" in SUB:
                    rope(xr[0], xr[1], o1, o2, ("xr", 0), ("xr", 1), ("rr", 2 * slot), ("rr", 2 * slot + 1))
                if "a" not in SUB:
                    pass
                elif which == "q":
                    S.op("act", lambda e: e.activation(out=nrm[:, slot, 0:nt], in_=pN[0:8, 0:nt], func=AF.Sqrt), r=["pN"], w=[("nrm", slot)])
                else:
                    S.op("act", lambda e: e.copy(out=nrm[:, slot, 0:nt], in_=pN[0:8, 0:nt]), r=["pN"], w=[("nrm", slot)])
                if "m" in SUB:
                    S.dma("pool", nd[:, c0:c0 + nt], nrm[:, slot, 0:nt], r=[("nrm", slot)], w=[(which + "nd", ti)])
                for two in range(2 if "d" in SUB else 0):
                    S.dma("sp" if two == 0 else "act", dst[0:64, :, c0:c0 + nt].rearrange("r (c two) n -> r c two n", two=2)[:, :, two, :], nope_sb[64 * two:64 * two + 64, :, 0:nt], r=[(which + "n", c) for c in range(4)], w=[(which + "T", ti, two)])
                if "e" in SUB:
                    S.dma("pool", dst[64:80, :, c0:c0 + nt].rearrange("i h n -> (i h) n"), o1[:, 0:nt], r=[("rr", 2 * slot)], w=[(which + "T", ti, 2)])
                    S.dma("pool", dst[80:96, :, c0:c0 + nt].rearrange("i h n -> (i h) n"), o2[:, 0:nt], r=[("rr", 2 * slot + 1)], w=[(which + "T", ti, 3)])

            PARTS = os.environ.get("PROJ_PARTS", "qkovr")
            if "q" in PARTS:
                qk_side("q", qn_sb, QTd, QNd, 0)
            if "k" in PARTS:
                qk_side("k", kn_sb, KTd, KNd, 1)
            if "o" in PARTS:
                S.dma("sp", KTd[96, :, c0:c0 + nt], onesrow_in[:, 0:nt], w=[("kT", ti, 4)])
            vt = VAt[ti % 2]
            VK = ("VAt", ti % 2)
            for b in range(nblk if "v" in PARTS else 0):
                p, pk = nextbank()
                S.op("pe", lambda e, b=b: e.matmul(p[:, 0:512], lhsT=ckvn[:, b * 128:(b + 1) * 128], rhs=Wukv[:, 512:1024], start=True, stop=True), r=["ckvn", "W"], w=[pk])
                evac(vt[:, b, :, 0:64], p[:, 0:512].rearrange("p (h d) -> p h d", d=64), [pk], [VK])
            if "v" in PARTS:
                S.dma("sp", VA[c0:c0 + nt, :, :].rearrange("(b p) h c -> p b h c", p=128), vt[:, 0:nblk, :, :], r=[VK], w=[("VA", ti)])
            raw_groups(8, 16)

        nT = len(cfg.tiles)
        stageA(0)
        stageA2(0)
        if nT > 1:
            stageA(1)
        for ti in range(nT):
            if ti + 1 < nT:
                stageA2(ti + 1)
            stageB(ti)
            if ti + 2 < nT:
                stageA(ti + 2)
        S.barrier()
        st.close()


    LPM = max(cfg.LP)
    NBM = max(cfg.NB)

    def shift_phase(l):
        S.mark(f"shift_L{l}")
        st = ExitStack()
        mu = sb(st, "mu", [128, 16, 2], F32)
        c0t = sb(st, "c0t", [128, 16], F32)
        X = [sb(st, f"X{i}", [128, 514], F32) for i in range(4)]
        T = [sb(st, f"T{i}", [128, 512], F32) for i in range(4)]
        Tb = [sb(st, f"Tb{i}", [128, 512], BF16) for i in range(2)]
        S.dma("sp", mu[:], mu_in[:, l, :, :], w=["mu"])
        S.op("dve", lambda e: e.tensor_tensor(out=c0t[:], in0=mu[:, :, 0], in1=mu[:, :, 1], op=ALU.add), r=["mu"], w=["c0"])
        S.op("dve", lambda e: e.tensor_scalar(out=c0t[:], in0=c0t[:], scalar1=-1.0, scalar2=1.0, op0=ALU.mult, op1=ALU.add), r=["c0"], w=["c0"])
        roff0 = GROUPS["r0"][0]
        it = 0
        for s_ in range(2):
            base, L, Lp = cfg.BASE[s_], cfg.LS[s_], cfg.LP[s_]
            for t0 in range(0, Lp, 512):
                nt = min(512, Lp - t0)
                valid = max(0, min(nt, L - t0))
                for gi, name in enumerate(RW_GROUPS):
                    off, wdt = GROUPS[name]
                    ro = off - roff0
                    x = X[it % 4]
                    XK = ("X", it % 4)
                    t = T[it % 4]
                    TK = ("T", it % 4)
                    eng = "dve"
                    lo = max(t0 - 1, 0)
                    hi = min(t0 + nt + 1, L)
                    jlo, jhi = lo - (t0 - 1), hi - (t0 - 1)
                    if jlo > 0:
                        S.op("pool", lambda e: e.memset(x[0:wdt, 0:jlo], 0.0), w=[XK])
                    if jhi < nt + 2:
                        S.op("pool", lambda e: e.memset(x[0:wdt, max(jhi, 0):nt + 2], 0.0), w=[XK])
                    if jhi > jlo:
                        S.dma("sp" if it % 2 == 0 else "act", x[0:wdt, jlo:jhi], RAW[ro:ro + wdt, base + lo:base + hi], r=[("RAW",)], w=[XK])
                    S.op(eng, lambda e: e.tensor_scalar(out=t[0:wdt, 0:nt], in0=x[0:wdt, 1:nt + 1], scalar1=c0t[0:wdt, gi:gi + 1], scalar2=None, op0=ALU.mult), r=[XK, "c0"], w=[TK])
                    S.op(eng, lambda e: e.scalar_tensor_tensor(out=t[0:wdt, 0:nt], in0=x[0:wdt, 0:nt], scalar=mu[0:wdt, gi, 0:1], in1=t[0:wdt, 0:nt], op0=ALU.mult, op1=ALU.add), r=[XK, TK, "mu"], w=[TK])
                    is_da = (name == "da")
                    if is_da:
                        tb = Tb[it % 2]
                        TBK = ("Tb", it % 2)
                        S.op(eng, lambda e: e.scalar_tensor_tensor(out=tb[0:wdt, 0:nt], in0=x[0:wdt, 2:nt + 2], scalar=mu[0:wdt, gi, 1:2], in1=t[0:wdt, 0:nt], op0=ALU.mult, op1=ALU.add), r=[XK, TK, "mu"], w=[TBK])
                        if valid < nt:
                            S.op(eng, lambda e: e.memset(tb[0:wdt, valid:nt], 0.0), w=[TBK])
                        S.dma("pool", LOR[128:256, base + t0:base + t0 + nt], tb[0:wdt, 0:nt], r=[TBK], w=[("LOR", it)])
                    else:
                        S.op(eng, lambda e: e.scalar_tensor_tensor(out=t[0:wdt, 0:nt], in0=x[0:wdt, 2:nt + 2], scalar=mu[0:wdt, gi, 1:2], in1=t[0:wdt, 0:nt], op0=ALU.mult, op1=ALU.add), r=[XK, TK, "mu"], w=[TK])
                        if valid < nt:
                            S.op(eng, lambda e: e.memset(t[0:wdt, valid:nt], 0.0), w=[TK])
                        if gi < 12:
                            S.dma("pool" if it % 2 == 0 else "act", RWS[ro:ro + wdt, base + t0:base + t0 + nt], t[0:wdt, 0:nt], r=[TK], w=[("RWS", it)])
                        else:
                            tb = Tb[it % 2]
                            TBK = ("Tb", it % 2)
                            fn_ = AF.Tanh if name == "dw" else AF.Sigmoid
                            S.op("act", lambda e: e.activation(out=tb[0:wdt, 0:nt], in_=t[0:wdt, 0:nt], func=fn_), r=[TK], w=[TBK])
                            lo_r = {"dw": 0, "dg0": 256, "dg1": 384}[name]
                            S.dma("pool", LOR[lo_r:lo_r + wdt, base + t0:base + t0 + nt], tb[0:wdt, 0:nt], r=[TBK], w=[("LOR", it)])
                    it += 1
        S.barrier()
        st.close()

    def attn_phase(l):
        S.mark(f"attn_L{l}")
        st = ExitStack()
        Ksb = [sb(st, f"Ksb{i}", [97, LPM], BF16) for i in range(2)]
        Qsb = [sb(st, f"Qsb{i}", [97, LPM], BF16) for i in range(2)]
        Vsb = [sb(st, f"Vsb{i}", [128, NBM, 65], BF16) for i in range(2)]
        knq = sb(st, "knq", [8, LPM], F32)
        augb = sb(st, "augb", [8, LPM], BF16)
        kmax = sb(st, "kmax", [8, 2], F32)
        Pt = [sb(st, f"Pt{i}", [128, 512], BF16) for i in range(4)]
        Osb = [sb(st, f"Osb{i}", [65, 512], F32) for i in range(2)]
        rec = [sb(st, f"rec{i}", [65, 512], F32) for i in range(2)]
        Ob = [sb(st, f"Ob{i}", [64, 512], BF16) for i in range(2)]
        pS = [ps(st, f"pS{i}", [128, 512]) for i in range(4)]
        pO = [ps(st, f"pO{i}", [128, 512]) for i in range(2)]
        pB = [ps(st, f"pB{i}", [128, 512]) for i in range(2)]
        hs = 0
        qt = 0
        si = 0
        pending = []
        for s_ in range(2):
            base, L, Lp, nb = cfg.BASE[s_], cfg.LS[s_], cfg.LP[s_], cfg.NB[s_]
            S.dma("sp", knq[:, 0:L], KNd[:, base:base + L], r=[("KNd",)], w=["knq"])
            S.op("dve", lambda e: e.tensor_reduce(out=kmax[:, 0:1], in_=knq[:, 0:L], axis=AX.X, op=ALU.max), r=["knq"], w=["kmax"])
            S.op("act", lambda e: e.activation(out=kmax[:, 0:1], in_=kmax[:, 0:1], func=AF.Sqrt), r=["kmax"], w=["kmax"])
            S.dma("sp", knq[:, 0:L], QNd[:, base:base + L], r=[("QNd",)], w=["knq"])
            S.op("dve", lambda e: e.tensor_scalar(out=augb[:, 0:L], in0=knq[:, 0:L], scalar1=kmax[:, 0:1], scalar2=-1.0, op0=ALU.mult, op1=ALU.mult), r=["knq", "kmax"], w=["augb"])
            S.dma("sp", QTd[96, :, base:base + L], augb[:, 0:L], r=["augb"], w=[("QTd", s_)])
            for h in range(NH):
                bf = hs % 2
                hs += 1
                K_, Q_, V_ = Ksb[bf], Qsb[bf], Vsb[bf]
                KK, QK, VK = ("K", bf), ("Q", bf), ("V", bf)
                S.dma("sp", K_[:, 0:L], KTd[:, h, base:base + L], r=[("KTd",)], w=[KK])
                S.dma("act", Q_[:, 0:L], QTd[:, h, base:base + L], r=[("QTd", s_)], w=[QK])
                S.dma("pool", V_[:, 0:nb, :], VA[base:base + Lp, h, :].rearrange("(b p) c -> p b c", p=128), r=[("VA",)], w=[VK])
                nkb = -(-L // 128)
                for q0 in range(0, L, 512):
                    qw = min(512, L - q0)
                    j = qt % 2
                    qt += 1
                    OK_ = ("pO", j)

                    def emitS(kb):
                        kw = min(128, L - kb * 128)
                        i = (si + kb) % 4
                        S.op("pe", lambda e: e.matmul(pS[i][0:kw, 0:qw], lhsT=K_[:, kb * 128:kb * 128 + kw], rhs=Q_[:, q0:q0 + qw], start=True, stop=True), r=[KK, QK], w=[("pS", i)])
                        S.op("act", lambda e: e.activation(out=Pt[i][0:kw, 0:qw], in_=pS[i][0:kw, 0:qw], func=AF.Exp, scale=SCALE), r=[("pS", i)], w=[("Pt", i)])

                    def emitPV(kb):
                        kw = min(128, L - kb * 128)
                        i = (si + kb) % 4
                        S.op("pe", lambda e: e.matmul(pO[j][0:65, 0:qw], lhsT=V_[0:kw, kb, :], rhs=Pt[i][0:kw, 0:qw], start=(kb == 0), stop=(kb == nkb - 1)), r=[VK, ("Pt", i)], w=[OK_])

                    SK = 2
                    for kb in range(nkb + SK):
                        if kb < nkb:
                            emitS(kb)
                        if kb == 1 and pending:
                            pending.pop()()
                        if kb >= SK:
                            emitPV(kb - SK)
                    si = (si + nkb) % 4

                    def fin(j=j, qw=qw, q0=q0, h=h, base=base):
                        S.op("pe", lambda e: e.matmul(pB[j][0:64, 0:qw], lhsT=onesf[64:65, 0:64], rhs=rec[j][64:65, 0:qw], start=True, stop=True), r=[("rec", j), "onesf"], w=[("pB", j)])
                        S.op("dve", lambda e: e.tensor_tensor(out=Ob[j][:, 0:qw], in0=Osb[j][0:64, 0:qw], in1=pB[j][0:64, 0:qw], op=ALU.mult), r=[("Osb", j), ("pB", j)], w=[("Ob", j)])
                        S.dma("pool", MIXT[64 * h:64 * h + 64, base + q0:base + q0 + qw], Ob[j][:, 0:qw], r=[("Ob", j)], w=[("MIXT", h, base + q0)])

                    S.op("dve", lambda e: e.tensor_copy(out=Osb[j][:, 0:qw], in_=pO[j][0:65, 0:qw]), r=[OK_], w=[("Osb", j)])
                    S.op("dve", lambda e: e.reciprocal(out=rec[j][64:65, 0:qw], in_=Osb[j][64:65, 0:qw]), r=[("Osb", j)], w=[("rec", j)])
                    if pending:
                        pending.pop()()
                    pending.append(fin)
        while pending:
            pending.pop()()
        S.barrier()
        st.close()


    def rwkv_phase(l):
        S.mark(f"rwkv_L{l}")
        st = ExitStack()
        c_ = CDEC
        MK = sb(st, "MK", [128, 4, 128], F32)
        MP1 = sb(st, "MP1", [128, 2, 2, 128], F32)
        MP3 = sb(st, "MP3", [128, 2, 128], F32)
        vm = sb(st, "vm", [128, 2], F32)
        prm = sb(st, "prm", [128, 9, 64], F32)
        omk = sb(st, "omk", [128, 64], F32)
        LWf = sb(st, "LWf", [64, 2, 2, 64], F32)
        LW = sb(st, "LW", [64, 2, 2, 64], BF16)
        G2f = sb(st, "G2f", [128, 2, 64], F32)
        G2 = sb(st, "G2", [128, 2, 64], BF16)
        S.dma("sp", MK[:], masks_in, w=["MK"])
        S.dma("sp", vm[:], vmask_in, w=["vm"])
        for d, (ms, mi, m3) in enumerate(((2, 0, 3), (3, 1, 2))):
            S.op("dve", lambda e: e.tensor_copy(out=MP1[:, d, 0, :], in_=MK[:, ms, :]), r=["MK"], w=["MP"])
            S.op("dve", lambda e: e.tensor_copy(out=MP1[:, d, 1, :], in_=MK[:, mi, :]), r=["MK"], w=["MP"])
            S.op("dve", lambda e: e.tensor_copy(out=MP3[:, d, :], in_=MK[:, m3, :]), r=["MK"], w=["MP"])
        GT = sb(st, "GT", [64, NBM, 2, 128], BF16)
        PHIT = sb(st, "PHIT", [64, NBM, 2, 64], BF16)
        PSI = sb(st, "PSI", [64, NBM, 2, 64], F32)
        Yacc = sb(st, "Yacc", [128, NBM, 64], F32)
        Sb = [sb(st, f"Sb{i}", [64, 2, 64], BF16) for i in range(2)]
        ey = sb(st, "ey", [128, 4, 64], F32)
        es1 = sb(st, "es1", [128, 8], F32)
        eo = sb(st, "eo", [128, 4, 64], F32)
        eob = sb(st, "eob", [64, 512], BF16)
        egl = sb(st, "egl", [128, 4, 2, 64], F32)
        banks = [ps(st, f"pX{i}", [128, 512]) for i in range(8)]
        bki = [0]

        def nb_():
            i = bki[0] % 8
            bki[0] += 1
            return banks[i], ("pX", i)

        alt = [0]

        def ew(fn, r, w):
            alt[0] += 1
            S.op("dve" if alt[0] % 2 else "pool", fn, r=r, w=w)

        def bc(ap, shape):
            return ap.broadcast_to(shape)

        RW_STOP = os.environ.get("RW_STOP", "Z")

        GLOBAL_KEYS = ("GT", "PHIT", "PSI", "Yacc", "prm", "LW", "G2", "omk", "MK", "MP", "vm", "ident", "identf", "onesf", "RWS", "LOR", "EPI", "pX", "LWf", "G2f")

        def make_thread(tid):
            def mk(k):
                name = k[0] if isinstance(k, tuple) else k
                return k if name in GLOBAL_KEYS else (k, "t", tid)

            class _TS:
                @staticmethod
                def op(eng, fn, r=(), w=()):
                    return S.op(eng, fn, r=[mk(k) for k in r], w=[mk(k) for k in w])

                @staticmethod
                def dma(q, out, in_, r=(), w=()):
                    return S.dma(q, out, in_, r=[mk(k) for k in r], w=[mk(k) for k in w])
            TS = _TS
            alt = [tid]

            def ew(fn, r, w):
                alt[0] += 1
                TS.op("dve" if alt[0] % 3 == 0 else "pool", fn, r=r, w=w)
            RKs = sb(st, "RKs", [128, 256], F32)
            Vs = sb(st, "Vs", [64, 256], F32)
            TW = sb(st, "TW", [64, 2, 256], BF16)
            DAs = sb(st, "DAs", [64, 2, 256], BF16)
            SG0 = sb(st, "SG0", [128, 256], BF16)
            SG1 = sb(st, "SG1", [32, 256], BF16)
            RKtm = sb(st, "RKtm", [128, 2, 128], F32)
            Vtm = sb(st, "Vtm", [128, 2, 64], BF16)
            Vt32 = sb(st, "Vt32", [128, 2, 64], F32)
            SGM = sb(st, "SGM", [128, 2, 2, 64], F32)
            Aa = sb(st, "Aa", [128, 2, 2, 64], F32)
            EG = sb(st, "EG", [128, 2, 2, 64], F32)
            kkr = sb(st, "kkr", [128, 2, 64], F32)
            kk = sb(st, "kk", [128, 2, 64], F32)
            sq = sb(st, "sq", [128, 2, 64], F32)
            n2 = sb(st, "n2", [128, 8], F32)
            kd = sb(st, "kd", [128, 2, 2, 64], F32)
            be = sb(st, "be", [128, 2, 2, 64], F32)
            t1 = sb(st, "t1", [128, 2, 2, 64], F32)
            EX = [sb(st, f"EX{i}", [128, 2, 2, 64], F32) for i in range(5)]
            sc = [sb(st, f"sc{i}", [128, 2, 2, 64], BF16) for i in range(5)]
            TA = sb(st, "TA", [128, 4, 128], BF16)
            FT = sb(st, "FT", [64, 4, 4, 128], BF16)
            QM = sb(st, "QM", [128, 4, 2, 128], BF16)
            KM = sb(st, "KM", [128, 4, 2, 128], BF16)
            PP = [sb(st, f"PP{i}", [128, 4, 128], BF16) for i in range(2)]
            QQ = [sb(st, f"QQ{i}", [128, 4, 128], BF16) for i in range(2)]
            TT = [sb(st, f"TT{i}", [128, 4, 128], BF16) for i in range(2)]
            WU = sb(st, "WU", [128, 4, 128], BF16)
            tmpd = sb(st, "tmpd", [64, 4, 64], F32)

            def group(s_, h, g, base, L, Lp, nb):
                ng = min(2, nb - 2 * g)
                nq = 2 * ng
                ncol = 128 * ng
                cc0 = base + 256 * g
                TS.dma("sp", RKs[0:64, 0:ncol], RWS[64 * h:64 * h + 64, cc0:cc0 + ncol], r=[("RWS",)], w=["RKs"])
                TS.dma("act", RKs[64:128, 0:ncol], RWS[512 + 64 * h:512 + 64 * h + 64, cc0:cc0 + ncol], r=[("RWS",)], w=["RKs"])
                TS.dma("sp", Vs[:, 0:ncol], RWS[1024 + 64 * h:1024 + 64 * h + 64, cc0:cc0 + ncol], r=[("RWS",)], w=["Vs"])
                TS.dma("act", TW[:, :, 0:ncol], LOR[0:128, cc0:cc0 + ncol].rearrange("(d k) n -> k d n", d=2), r=[("LOR",)], w=["TW"])
                TS.dma("sp", DAs[:, :, 0:ncol], LOR[128:256, cc0:cc0 + ncol].rearrange("(d k) n -> k d n", d=2), r=[("LOR",)], w=["DAs"])
                TS.dma("act", SG0[:, 0:ncol], LOR[256:384, cc0:cc0 + ncol], r=[("LOR",)], w=["SG0"])
                TS.dma("sp", SG1[:, 0:ncol], LOR[384:416, cc0:cc0 + ncol], r=[("LOR",)], w=["SG1"])
                yield
                pb, pk = nb_()
                for j in range(ng):
                    TS.op("pe", lambda e, j=j: e.transpose(out=pb[:, j * 128:(j + 1) * 128], in_=RKs[:, j * 128:(j + 1) * 128], identity=identf[:]), r=["RKs", "identf"], w=[pk])
                TS.op("act", lambda e: e.copy(out=RKtm[:, 0:ng, :], in_=pb[:, 0:ncol].rearrange("p (j c) -> p j c", c=128)), r=[pk], w=["RKtm"])
                pb, pk = nb_()
                for j in range(ng):
                    TS.op("pe", lambda e, j=j: e.transpose(out=pb[:, j * 64:(j + 1) * 64], in_=Vs[:, j * 128:(j + 1) * 128], identity=identf[0:64, 0:64]), r=["Vs", "identf"], w=[pk])
                TS.op("act", lambda e: e.copy(out=Vtm[:, 0:ng, :], in_=pb[:, 0:64 * ng].rearrange("p (j c) -> p j c", c=64)), r=[pk], w=["Vtm"])
                TS.op("dve", lambda e: e.tensor_copy(out=Vt32[:, 0:ng, :], in_=pb[:, 0:64 * ng].rearrange("p (j c) -> p j c", c=64)), r=[pk], w=["Vt32"])
                if RW_STOP <= "A":
                    return
                yield
                pw, pwk = nb_()
                pa, pak = nb_()
                pg, pgk = nb_()
                for j in range(ng):
                    for d in range(1 if os.environ.get("RW_B2") == "d0only" else 2):
                        TS.op("pe", lambda e, j=j, d=d: e.matmul(pw[:, (j * 2 + d) * 64:(j * 2 + d + 1) * 64], lhsT=TW[:, d, j * 128:(j + 1) * 128], rhs=LW[:, d, 0, :], start=True, stop=True), r=["TW", "LW"], w=[pwk])
                for j in range(ng):
                    for d in range(1 if os.environ.get("RW_B2") == "d0only" else 2):
                        TS.op("pe", lambda e, j=j, d=d: e.matmul(pa[:, (j * 2 + d) * 64:(j * 2 + d + 1) * 64], lhsT=DAs[:, d, j * 128:(j + 1) * 128], rhs=LW[:, d, 1, :], start=True, stop=True), r=["DAs", "LW"], w=[pak])
                for j in range(0 if os.environ.get("RW_B2") == "nogate" else ng):
                    TS.op("pe", lambda e, j=j: e.matmul(pg[:, j * 64:(j + 1) * 64], lhsT=SG0[:, j * 128:(j + 1) * 128], rhs=G2[:, 0, :], start=True, stop=False), r=["SG0", "G2"], w=[pgk])
                    TS.op("pe", lambda e, j=j: e.matmul(pg[:, j * 64:(j + 1) * 64], lhsT=SG1[:, j * 128:(j + 1) * 128], rhs=G2[0:32, 1, :], start=False, stop=True), r=["SG1", "G2"], w=[pgk])
                v4 = [128, ng, 2, 64]
                if os.environ.get("RW_B") == "1":
                    return
                TS.op("dve", lambda e: e.tensor_tensor(out=SGM[:, 0:ng], in0=pw[:, 0:128 * ng].rearrange("p (j d c) -> p j d c", d=2, c=64), in1=bc(prm[:, 0:2, :].unsqueeze(1), v4), op=ALU.add), r=[pwk, "prm"], w=["SGM"])
                TS.op("act", lambda e: e.activation(out=SGM[:, 0:ng], in_=SGM[:, 0:ng], func=AF.Sigmoid), r=["SGM"], w=["SGM"])
                if 2 * g + ng == nb and L < Lp:
                    jl = ng - 1
                    TS.op("dve", lambda e: e.tensor_scalar(out=SGM[:, jl], in0=SGM[:, jl], scalar1=vm[:, s_:s_ + 1], scalar2=None, op0=ALU.mult), r=["SGM", "vm"], w=["SGM"])
                TS.op("dve", lambda e: e.tensor_tensor(out=Aa[:, 0:ng], in0=pa[:, 0:128 * ng].rearrange("p (j d c) -> p j d c", d=2, c=64), in1=bc(prm[:, 2:4, :].unsqueeze(1), v4), op=ALU.add), r=[pak, "prm"], w=["Aa"])
                TS.op("act", lambda e: e.activation(out=Aa[:, 0:ng], in_=Aa[:, 0:ng], func=AF.Sigmoid), r=["Aa"], w=["Aa"])
                TS.op("act", lambda e: e.copy(out=EG[:, 0:ng, 0, :], in_=pg[:, 0:64 * ng].rearrange("p (j c) -> p j c", c=64)), r=[pgk], w=["EG"])
                if RW_STOP <= "B":
                    return
                yield
                r_ = RKtm[:, 0:ng, 0:64]
                k_ = RKtm[:, 0:ng, 64:128]
                v3 = [128, ng, 64]
                ew(lambda e: e.tensor_tensor(out=kkr[:, 0:ng], in0=k_, in1=bc(prm[:, 4:5, :], v3), op=ALU.mult), ["RKtm", "prm"], ["kkr"])
                ew(lambda e: e.tensor_tensor(out=sq[:, 0:ng], in0=kkr[:, 0:ng], in1=kkr[:, 0:ng], op=ALU.mult), ["kkr"], ["sq"])
                TS.op("dve", lambda e: e.tensor_reduce(out=n2[:, 0:ng], in_=sq[:, 0:ng], axis=AX.X, op=ALU.add), r=["sq"], w=["n2"])
                TS.op("act", lambda e: e.activation(out=n2[:, 0:ng], in_=n2[:, 0:ng], func=AF.Sqrt), r=["n2"], w=["n2"])
                TS.op("dve", lambda e: e.tensor_scalar(out=n2[:, 0:ng], in0=n2[:, 0:ng], scalar1=1e-12, scalar2=None, op0=ALU.max), r=["n2"], w=["n2"])
                TS.op("dve", lambda e: e.reciprocal(out=n2[:, 0:ng], in_=n2[:, 0:ng]), r=["n2"], w=["n2"])
                ew(lambda e: e.tensor_tensor(out=kk[:, 0:ng], in0=kkr[:, 0:ng], in1=bc(n2[:, 0:ng].unsqueeze(2), v3), op=ALU.mult), ["kkr", "n2"], ["kk"])
                ew(lambda e: e.tensor_tensor(out=t1[:, 0:ng], in0=Aa[:, 0:ng], in1=bc(prm[:, 5:6, :].unsqueeze(1), v4), op=ALU.mult), ["Aa", "prm"], ["t1"])
                ew(lambda e: e.tensor_tensor(out=t1[:, 0:ng], in0=t1[:, 0:ng], in1=bc(omk[:, :].unsqueeze(1).unsqueeze(1), v4), op=ALU.add), ["t1", "omk"], ["t1"])
                ew(lambda e: e.tensor_tensor(out=kd[:, 0:ng], in0=t1[:, 0:ng], in1=bc(k_.unsqueeze(2), v4), op=ALU.mult), ["t1", "RKtm"], ["kd"])
                ew(lambda e: e.tensor_tensor(out=be[:, 0:ng], in0=Aa[:, 0:ng], in1=bc(kk[:, 0:ng].unsqueeze(2), v4), op=ALU.mult), ["Aa", "kk"], ["be"])
                ew(lambda e: e.tensor_tensor(out=sq[:, 0:ng], in0=kd[:, 0:ng, 0, :], in1=kd[:, 0:ng, 1, :], op=ALU.add), ["kd"], ["sq"])
                ew(lambda e: e.tensor_tensor(out=sq[:, 0:ng], in0=sq[:, 0:ng], in1=r_, op=ALU.mult), ["sq", "RKtm"], ["sq"])
                ew(lambda e: e.tensor_tensor(out=sq[:, 0:ng], in0=sq[:, 0:ng], in1=bc(prm[:, 8:9, :], v3), op=ALU.mult), ["sq", "prm"], ["sq"])
                TS.op("dve", lambda e: e.tensor_reduce(out=n2[:, 4:4 + ng], in_=sq[:, 0:ng], axis=AX.X, op=ALU.add), r=["sq"], w=["n2b"])
                ew(lambda e: e.tensor_tensor(out=sq[:, 0:ng], in0=Vt32[:, 0:ng], in1=bc(n2[:, 4:4 + ng].unsqueeze(2), v3), op=ALU.mult), ["Vt32", "n2b"], ["sq"])
                ew(lambda e: e.tensor_tensor(out=EG[:, 0:ng, 1, :], in0=sq[:, 0:ng], in1=EG[:, 0:ng, 0, :], op=ALU.mult), ["sq", "EG"], ["EG"])
                TS.dma("pool", EPI[cc0:cc0 + ncol, :, :].rearrange("(j p) a c -> p j a c", p=128), EG[:, 0:ng], r=["EG"], w=[("EPI", g)])
                if RW_STOP <= "C":
                    return
                yield
                pcs = []
                for kind, (mf, mb_) in enumerate(((0, 1), (2, 3), (3, 2), (None, None))):
                    pc, pck = nb_()
                    for d in range(2):
                        m = (mf, mb_)[d]
                        lhs = onesf[:, :] if m is None else MK[:, m, :]
                        TS.op("pe", lambda e, d=d, lhs=lhs: e.matmul(pc[:, d * 64 * ng:(d + 1) * 64 * ng], lhsT=lhs, rhs=SGM[:, 0:ng, d, :], start=True, stop=True), r=["SGM", "MK", "onesf"], w=[pck])
                    pcs.append((pc, pck))
                for i, (kind, sgn) in enumerate(((0, -1.0), (0, 1.0), (1, -1.0), (2, -1.0), (3, -1.0))):
                    pc, pck = pcs[kind]
                    TS.op("act", lambda e, i=i, pc=pc, sgn=sgn: e.activation(out=EX[i][:, :, 0:ng, :], in_=pc[:, 0:128 * ng].rearrange("p (d j c) -> p d j c", d=2, c=64), func=AF.Exp, scale=sgn * c_), r=[pck], w=[("EX", i)])
                Ep, Em, Ex_, Eh, PCb = [EX[i][:, :, 0:ng, :].rearrange("p d j c -> p j d c") for i in range(5)]
                ew(lambda e: e.tensor_tensor(out=sc[0][:, 0:ng], in0=Ep, in1=bc(r_.unsqueeze(2), v4), op=ALU.mult), [("EX", 0), "RKtm"], [("sc", 0)])
                ew(lambda e: e.tensor_tensor(out=sc[1][:, 0:ng], in0=kd[:, 0:ng], in1=Em, op=ALU.mult), [("EX", 1), "kd"], [("sc", 1)])
                ew(lambda e: e.tensor_tensor(out=sc[2][:, 0:ng], in0=be[:, 0:ng], in1=Em, op=ALU.mult), [("EX", 1), "be"], [("sc", 2)])
                ew(lambda e: e.tensor_scalar(out=kkr[:, 0:ng], in0=kk[:, 0:ng], scalar1=-1.0, scalar2=None, op0=ALU.mult), ["kk"], ["kkr"])
                ew(lambda e: e.tensor_tensor(out=TA[:, 0:nq, 0:64].rearrange("p (j d) c -> p j d c", d=2), in0=Ex_, in1=bc(kkr[:, 0:ng].unsqueeze(2), v4), op=ALU.mult), [("EX", 2), "kkr"], ["TAa"])
                ew(lambda e: e.tensor_tensor(out=sc[3][:, 0:ng], in0=be[:, 0:ng], in1=Eh, op=ALU.mult), [("EX", 3), "be"], [("sc", 3)])
                ew(lambda e: e.tensor_tensor(out=sc[4][:, 0:ng], in0=kd[:, 0:ng], in1=Eh, op=ALU.mult), [("EX", 3), "kd"], [("sc", 4)])
                if RW_STOP <= "D":
                    return
                yield
                srcs = [(lambda q: TA[:, q, 0:64], "TAa"), (lambda q: sc[0][:, q // 2, q % 2, :], ("sc", 0)), (lambda q: sc[2][:, q // 2, q % 2, :], ("sc", 2)), (lambda q: sc[1][:, q // 2, q % 2, :], ("sc", 1))]
                for a, (fsrc, skey) in enumerate(srcs):
                    pb, pk = nb_()
                    pbv = pb[:].bitcast(BF16)
                    for q in range(nq):
                        TS.op("pe", lambda e, q=q: e.transpose(out=pbv[0:64, q * 128:(q + 1) * 128], in_=fsrc(q), identity=ident[:]), r=[skey, "ident"], w=[pk])
                    if a % 2 == 0:
                        TS.op("act", lambda e: e.copy(out=FT[:, 0:nq, a, :], in_=pbv[0:64, 0:128 * nq].rearrange("p (q c) -> p q c", c=128)), r=[pk], w=[("FT", a)])
                    else:
                        TS.op("dve", lambda e: e.tensor_copy(out=FT[:, 0:nq, a, :], in_=pbv[0:64, 0:128 * nq].rearrange("p (q c) -> p q c", c=128)), r=[pk], w=[("FT", a)])
                FK = [("FT", a) for a in range(4)]
                if RW_STOP <= "E":
                    return
                yield
                for j in range(ng):
                    p1, p1k = nb_()
                    p2, p2k = nb_()
                    for d in range(2):
                        q = 2 * j + d
                        TS.op("pe", lambda e, q=q, d=d: e.matmul(p1[:, d * 256:(d + 1) * 256], lhsT=FT[:, q, 2, :], rhs=FT[:, q, 0:2, :], start=True, stop=True), r=FK, w=[p1k])
                        TS.op("pe", lambda e, q=q, d=d: e.matmul(p2[:, d * 256:(d + 1) * 256], lhsT=FT[:, q, 3, :], rhs=FT[:, q, 0:2, :], start=True, stop=True), r=FK, w=[p2k])
                    TS.op("dve", lambda e, j=j: e.tensor_tensor(out=QM[:, 2 * j:2 * j + 2], in0=p1[:, :].rearrange("p (d a c) -> p d a c", d=2, a=2), in1=MP1[:], op=ALU.mult), r=[p1k, "MP"], w=["QM"])
                    TS.op("dve", lambda e, j=j: e.tensor_tensor(out=KM[:, 2 * j:2 * j + 2], in0=p2[:, :].rearrange("p (d a c) -> p d a c", d=2, a=2), in1=MP1[:], op=ALU.mult), r=[p2k, "MP"], w=["KM"])
                for jj in range(0, ng, 2):
                    p3, p3k = nb_()
                    njj = min(2, ng - jj)
                    for j in range(jj, jj + njj):
                        for d in range(2):
                            q = 2 * j + d
                            TS.op("pe", lambda e, q=q, j=j, d=d: e.matmul(p3[:, ((j - jj) * 2 + d) * 128:((j - jj) * 2 + d + 1) * 128], lhsT=FT[:, q, 0, :], rhs=FT[:, q, 2, :], start=True, stop=True), r=FK, w=[p3k])
                    TS.op("dve", lambda e: e.tensor_tensor(out=PP[0][:, 2 * jj:2 * jj + 2 * njj].rearrange("p (j d) c -> p j d c", d=2), in0=p3[:, 0:256 * njj].rearrange("p (j d c) -> p j d c", d=2, c=128), in1=bc(MP3[:].unsqueeze(1), [128, njj, 2, 128]), op=ALU.mult), r=[p3k, "MP"], w=[("PP", 0)])
                if RW_STOP <= "F":
                    return
                yield
                TS.op("pool", lambda e: e.tensor_copy(out=QQ[0][:, 0:nq], in_=QM[:, 0:nq, 0, :]), r=["QM"], w=[("QQ", 0)])
                TS.op("dve", lambda e: e.tensor_tensor(out=TT[0][:, 0:nq], in0=QM[:, 0:nq, 0, :], in1=bc(ident[:, :].unsqueeze(1), [128, nq, 128]), op=ALU.add), r=["QM", "ident"], w=[("TT", 0)])
                cur = 0
                for lev in range(6):
                    nx = 1 - cur
                    for q0 in range(0, nq, 4):
                        nqq = min(4, nq - q0)
                        pp_, ppk = nb_()
                        pq_, pqk = nb_()
                        for q in range(q0, q0 + nqq):
                            TS.op("pe", lambda e, q=q: e.matmul(pp_[:, (q - q0) * 128:(q - q0 + 1) * 128], lhsT=QQ[cur][:, q, :], rhs=PP[cur][:, q, :], start=True, stop=True), r=[("QQ", cur), ("PP", cur)], w=[ppk])
                        for q in range(q0, q0 + nqq):
                            TS.op("pe", lambda e, q=q: e.matmul(pq_[:, (q - q0) * 128:(q - q0 + 1) * 128], lhsT=PP[cur][:, q, :], rhs=QQ[cur][:, q, :], start=True, stop=True), r=[("QQ", cur), ("PP", cur)], w=[pqk])
                        TS.op("act", lambda e: e.copy(out=PP[nx][:, q0:q0 + nqq], in_=pp_[:, 0:128 * nqq].rearrange("p (q c) -> p q c", c=128)), r=[ppk], w=[("PP", nx)])
                        TS.op("dve", lambda e: e.tensor_copy(out=QQ[nx][:, q0:q0 + nqq], in_=pq_[:, 0:128 * nqq].rearrange("p (q c) -> p q c", c=128)), r=[pqk], w=[("QQ", nx)])
                    yield
                    for q0 in range(0, nq, 4):
                        nqq = min(4, nq - q0)
                        pt_, ptk = nb_()
                        for q in range(q0, q0 + nqq):
                            TS.op("pe", lambda e, q=q: e.matmul(pt_[:, (q - q0) * 128:(q - q0 + 1) * 128], lhsT=PP[nx][:, q, :], rhs=TT[cur][:, q, :], start=True, stop=True), r=[("PP", nx), ("TT", cur)], w=[ptk])
                        TS.op("dve", lambda e: e.tensor_tensor(out=TT[nx][:, q0:q0 + nqq], in0=pt_[:, 0:128 * nqq].rearrange("p (q c) -> p q c", c=128), in1=TT[cur][:, q0:q0 + nqq], op=ALU.add), r=[ptk, ("TT", cur)], w=[("TT", nx)])
                    cur = nx
                    yield
                TTf = TT[cur]
                TTK = ("TT", cur)
                if RW_STOP <= "G":
                    return
                yield
                yield
                pb, pk = nb_()
                for q in range(nq):
                    TS.op("pe", lambda e, q=q: e.matmul(pb[:, q * 64:(q + 1) * 64], lhsT=KM[:, q, 0, :], rhs=Vtm[:, q // 2, :], start=True, stop=True), r=["KM", "Vtm"], w=[pk])
                TS.op("act", lambda e: e.copy(out=TA[:, 0:nq, 64:128], in_=pb[:, 0:64 * nq].rearrange("p (q c) -> p q c", c=64)), r=[pk], w=["TAx"])
                for q0 in range(0, nq, 4):
                    nqq = min(4, nq - q0)
                    pb, pk = nb_()
                    for q in range(q0, q0 + nqq):
                        TS.op("pe", lambda e, q=q: e.matmul(pb[:, (q - q0) * 128:(q - q0 + 1) * 128], lhsT=TTf[:, q, :], rhs=TA[:, q, :], start=True, stop=True), r=[TTK, "TAa", "TAx"], w=[pk])
                    TS.op("act", lambda e: e.copy(out=WU[:, q0:q0 + nqq], in_=pb[:, 0:128 * nqq].rearrange("p (q c) -> p q c", c=128)), r=[pk], w=["WU"])
                for q0 in range(0, nq, 4):
                    nqq = min(4, nq - q0)
                    pb, pk = nb_()
                    for q in range(q0, q0 + nqq):
                        TS.op("pe", lambda e, q=q: e.matmul(pb[0:64, (q - q0) * 128:(q - q0 + 1) * 128], lhsT=WU[:, q, 0:64], rhs=QM[:, q, 1, :], start=True, stop=True), r=["WU", "QM"], w=[pk])
                    TS.op("dve", lambda e: e.tensor_tensor(out=GT[:, 2 * g + q0 // 2:2 * g + (q0 + nqq) // 2].rearrange("p j d c -> p (j d) c"), in0=pb[0:64, 0:128 * nqq].rearrange("p (q c) -> p q c", c=128), in1=FT[:, q0:q0 + nqq, 1, :], op=ALU.add), r=[pk, ("FT", 1)], w=["GT"])
                yield
                pb, pk = nb_()
                for j in range(ng):
                    for d in range(2):
                        q = 2 * j + d
                        TS.op("pe", lambda e, q=q, j=j, d=d: e.matmul(pb[:, j * 64:(j + 1) * 64], lhsT=QM[:, q, 1, :], rhs=WU[:, q, 64:128], start=(d == 0), stop=False), r=["QM", "WU"], w=[pk])
                        TS.op("pe", lambda e, q=q, j=j, d=d: e.matmul(pb[:, j * 64:(j + 1) * 64], lhsT=KM[:, q, 1, :], rhs=Vtm[:, j, :], start=False, stop=(d == 1)), r=["KM", "Vtm"], w=[pk])
                TS.op("act", lambda e: e.copy(out=Yacc[:, 2 * g:2 * g + ng, :], in_=pb[:, 0:64 * ng].rearrange("p (j c) -> p j c", c=64)), r=[pk], w=["Yacc"])
                yield
                pb, pk = nb_()
                for q in range(nq):
                    TS.op("pe", lambda e, q=q: e.matmul(pb[0:64, q * 64:(q + 1) * 64], lhsT=WU[:, q, 0:64], rhs=sc[3][:, q // 2, q % 2, :], start=True, stop=True), r=["WU", ("sc", 3)], w=[pk])
                TS.op("pool", lambda e: e.tensor_tensor(out=tmpd[:, 0:nq].rearrange("p (j d) c -> p j d c", d=2), in0=PCb[0:64], in1=bc(identf[0:64, 0:64].unsqueeze(1).unsqueeze(1), [64, ng, 2, 64]), op=ALU.mult), r=[("EX", 4), "identf"], w=["tmpd"])
                TS.op("dve", lambda e: e.tensor_tensor(out=PHIT[:, 2 * g:2 * g + ng].rearrange("p j d c -> p (j d) c"), in0=pb[0:64, 0:64 * nq].rearrange("p (q c) -> p q c", c=64), in1=tmpd[:, 0:nq], op=ALU.add), r=[pk, "tmpd"], w=["PHIT"])
                yield
                pb, pk = nb_()
                for q in range(nq):
                    TS.op("pe", lambda e, q=q: e.matmul(pb[0:64, q * 64:(q + 1) * 64], lhsT=sc[3][:, q // 2, q % 2, :], rhs=WU[:, q, 64:128], start=True, stop=False), r=["WU", ("sc", 3)], w=[pk])
                    TS.op("pe", lambda e, q=q: e.matmul(pb[0:64, q * 64:(q + 1) * 64], lhsT=sc[4][:, q // 2, q % 2, :], rhs=Vtm[:, q // 2, :], start=False, stop=True), r=["Vtm", ("sc", 4)], w=[pk])
                TS.op("act", lambda e: e.copy(out=PSI[:, 2 * g:2 * g + ng].rearrange("p j d c -> p (j d) c"), in_=pb[0:64, 0:64 * nq].rearrange("p (q c) -> p q c", c=64)), r=[pk], w=["PSI"])

            return group

        threads = [make_thread(0), make_thread(1)]
        for s_ in range(2):
            base, L, Lp, nb = cfg.BASE[s_], cfg.LS[s_], cfg.LP[s_], cfg.NB[s_]
            ngrp = -(-nb // 4)
            for h in range(NH):
                S.dma("sp", prm[:], rwp[:, l, h, :, :], w=["prm"])
                S.dma("act", LWf[:], lw_in[l, h].rearrange("(d k) a c -> k d a c", d=2), w=["LWf"])
                S.dma("sp", G2f[:, 0, :], g2_in[l, h, 0:128, :], w=["G2f"])
                S.dma("act", G2f[0:32, 1, :], g2_in[l, h, 128:160, :], w=["G2f"])
                S.op("dve", lambda e: e.tensor_copy(out=LW[:], in_=LWf[:]), r=["LWf"], w=["LW"])
                S.op("dve", lambda e: e.tensor_copy(out=G2[:, 0, :], in_=G2f[:, 0, :]), r=["G2f"], w=["G2"])
                S.op("dve", lambda e: e.tensor_copy(out=G2[0:32, 1, :], in_=G2f[0:32, 1, :]), r=["G2f"], w=["G2"])
                S.op("dve", lambda e: e.tensor_scalar(out=omk[:], in0=prm[:, 5, :], scalar1=-1.0, scalar2=1.0, op0=ALU.mult, op1=ALU.add), r=["prm"], w=["omk"])
                gens = [threads[g % 2](s_, h, g, base, L, Lp, nb) for g in range(-(-nb // 2))]
                for gi in range(0, len(gens), 2):
                    pair = gens[gi:gi + 2]
                    live = list(pair)
                    while live:
                        for gn in list(live):
                            try:
                                next(gn)
                            except StopIteration:
                                live.remove(gn)
                if RW_STOP <= "H":
                    continue
                S.op("pool", lambda e: e.memset(Sb[0][:], 0.0), w=[("Sb", 0)])
                cur = 0
                for i in range(nb):
                    nx = 1 - cur
                    for d, c in ((0, i), (1, nb - 1 - i)):
                        py, pyk = nb_()
                        S.op("pe", lambda e, d=d, c=c: e.matmul(py[:, 0:64], lhsT=GT[:, c, d, :], rhs=Sb[cur][:, d, :], start=True, stop=True), r=["GT", ("Sb", cur)], w=[pyk])
                        S.op("pe", lambda e, d=d, c=c: e.matmul(py[0:64, 64:128], lhsT=PHIT[:, c, d, :], rhs=Sb[cur][:, d, :], start=True, stop=True), r=["PHIT", ("Sb", cur)], w=[pyk])
                        S.op("dve", lambda e, d=d, c=c: e.tensor_tensor(out=Sb[nx][:, d, :], in0=py[0:64, 64:128], in1=PSI[:, c, d, :], op=ALU.add), r=[pyk, "PSI"], w=[("Sb", nx)])
                        S.op("dve", lambda e, d=d, c=c: e.tensor_tensor(out=Yacc[:, c, :], in0=py[:, 0:64], in1=Yacc[:, c, :], op=ALU.add), r=[pyk, "Yacc"], w=["Yacc"])
                    cur = nx
                if RW_STOP <= "I":
                    continue
                for g in range(ngrp):
                    ng = min(4, nb - 4 * g)
                    ncol = 128 * ng
                    cc0 = base + 512 * g
                    v3 = [128, ng, 64]
                    S.dma("sp", egl[:, 0:ng], EPI[cc0:cc0 + ncol, :, :].rearrange("(j p) a c -> p j a c", p=128), r=[("EPI", 2 * g), ("EPI", 2 * g + 1)], w=["egl"])
                    y_ = Yacc[:, 4 * g:4 * g + ng, :]
                    S.op("dve", lambda e: e.tensor_reduce(out=es1[:, 0:ng], in_=y_, axis=AX.X, op=ALU.add), r=["Yacc"], w=["es1"])
                    S.op("dve", lambda e: e.tensor_scalar(out=es1[:, 0:ng], in0=es1[:, 0:ng], scalar1=-1.0 / 64, scalar2=None, op0=ALU.mult), r=["es1"], w=["es1"])
                    ew(lambda e: e.tensor_tensor(out=ey[:, 0:ng], in0=y_, in1=bc(es1[:, 0:ng].unsqueeze(2), v3), op=ALU.add), ["Yacc", "es1"], ["ey"])
                    ew(lambda e: e.tensor_tensor(out=eo[:, 0:ng], in0=ey[:, 0:ng], in1=ey[:, 0:ng], op=ALU.mult), ["ey"], ["eo"])
                    S.op("dve", lambda e: e.tensor_reduce(out=es1[:, 4:4 + ng], in_=eo[:, 0:ng], axis=AX.X, op=ALU.add), r=["eo"], w=["es2"])
                    S.op("dve", lambda e: e.tensor_scalar(out=es1[:, 4:4 + ng], in0=es1[:, 4:4 + ng], scalar1=1.0 / 64, scalar2=LNX_EPS, op0=ALU.mult, op1=ALU.add), r=["es2"], w=["es2"])
                    S.op("act", lambda e: e.activation(out=es1[:, 4:4 + ng], in_=es1[:, 4:4 + ng], func=AF.Sqrt), r=["es2"], w=["es2"])
                    S.op("dve", lambda e: e.reciprocal(out=es1[:, 4:4 + ng], in_=es1[:, 4:4 + ng]), r=["es2"], w=["es2"])
                    ew(lambda e: e.tensor_tensor(out=ey[:, 0:ng], in0=ey[:, 0:ng], in1=bc(es1[:, 4:4 + ng].unsqueeze(2), v3), op=ALU.mult), ["ey", "es2"], ["ey"])
                    ew(lambda e: e.tensor_tensor(out=ey[:, 0:ng], in0=ey[:, 0:ng], in1=bc(prm[:, 6:7, :], v3), op=ALU.mult), ["ey", "prm"], ["ey"])
                    ew(lambda e: e.tensor_tensor(out=ey[:, 0:ng], in0=ey[:, 0:ng], in1=bc(prm[:, 7:8, :], v3), op=ALU.add), ["ey", "prm"], ["ey"])
                    ew(lambda e: e.tensor_tensor(out=ey[:, 0:ng], in0=ey[:, 0:ng], in1=egl[:, 0:ng, 0, :], op=ALU.mult), ["ey", "egl"], ["ey"])
                    ew(lambda e: e.tensor_tensor(out=eo[:, 0:ng], in0=ey[:, 0:ng], in1=egl[:, 0:ng, 1, :], op=ALU.add), ["ey", "egl"], ["eo"])
                    pb, pk = nb_()
                    for j in range(ng):
                        S.op("pe", lambda e, j=j: e.transpose(out=pb[0:64, j * 128:(j + 1) * 128], in_=eo[:, j, :], identity=identf[:]), r=["eo", "identf"], w=[pk])
                    S.op("act", lambda e: e.copy(out=eob[:, 0:ncol], in_=pb[0:64, 0:ncol]), r=[pk], w=["eob"])
                    S.dma("pool", MIXT[512 + 64 * h:512 + 64 * h + 64, cc0:cc0 + ncol], eob[:, 0:ncol], r=["eob"], w=[("MIXT", h, g)])
        S.barrier()
        st.close()

    for l in range(NL):
        ffn_pass(l, 0, 0, True)
        ffn_pass(l, 0, 1, False)
        if cfg.stages != "ffn":
            proj_pass(l)
            if cfg.stages != "proj":
                shift_phase(l)
                if cfg.stages != "shift":
                    if cfg.stages != "rwkv":
                        attn_phase(l)
                    if cfg.stages != "attn":
                        rwkv_phase(l)
        ffn_pass(l, 1, 0, True)
        ffn_pass(l, 1, 1, False, final=(l == NL - 1))

    S.mark("end")
    S.finish()
    cst.close()
    es.close()
    return nc, S


def host_maps(cfg, inp, xs_list):
    NL = cfg.NL
    f32 = np.float32
    common = {}
    common["meta"] = np.ascontiguousarray(inp["meta_tokens"], f32)
    for k, nm in ((1, "ffn1"), (2, "ffn2")):
        common[f"ffn{k}_wg"] = np.ascontiguousarray(inp[f"{nm}_w_gate"][:NL], f32)
        common[f"ffn{k}_wu"] = np.ascontiguousarray(inp[f"{nm}_w_up"][:NL], f32)
        common[f"ffn{k}_wd"] = np.ascontiguousarray(inp[f"{nm}_w_down"][:NL], f32)
    g = np.stack([np.asarray(inp["ffn1_norm"][:NL], f32), np.asarray(inp["mix_norm"][:NL], f32), np.asarray(inp["ffn2_norm"][:NL], f32)], axis=1)
    common["gains"] = np.ascontiguousarray(g.reshape(NL, 3, 8, 128).transpose(3, 0, 1, 2))
    common["fnorm"] = np.ascontiguousarray(np.broadcast_to(np.asarray(inp["final_norm"], f32)[None, :], (128, D)))
    common["ident_bf"] = np.eye(128, dtype=f32).astype(ml_dtypes.bfloat16)
    common["zeros"] = np.zeros((128, D), f32)
    bf = ml_dtypes.bfloat16
    ih = [(i, h) for i in range(16) for h in range(NH)]
    perm = list(range(0, 384)) + [384 + i for (i, h) in ih] + [400 + i for (i, h) in ih] + list(range(416, 2368))
    assert len(perm) == NCOLP
    common["w_in"] = np.ascontiguousarray(np.asarray(inp["w_in"][:NL], f32)[:, :, perm])
    pq = [96 * h + j for h in range(NH) for j in range(64)] + [96 * h + 64 + i for (i, h) in ih] + [96 * h + 80 + i for (i, h) in ih]
    common["w_uq"] = np.ascontiguousarray(np.asarray(inp["w_uq"][:NL], f32)[:, :, pq])
    pkv = [128 * h + j for h in range(NH) for j in range(64)] + [128 * h + 64 + j for h in range(NH) for j in range(64)]
    common["w_ukv"] = np.ascontiguousarray(np.asarray(inp["w_ukv"][:NL], f32)[:, :, pkv])
    common["w_out"] = np.ascontiguousarray(inp["w_out"][:NL], f32)
    qn = np.asarray(inp["q_norm"][:NL], f32)
    kvn = np.asarray(inp["kv_norm"][:NL], f32)
    common["qkg"] = np.ascontiguousarray(np.stack([qn[:, 0:128], qn[:, 128:256], kvn], axis=-1).transpose(1, 0, 2))
    seln = np.zeros((128, 4, 32), f32)
    for row in range(128):
        for c in range(4):
            seln[row, c, 2 * c + row // 64] = 1
    selr = np.zeros((128, 32), f32)
    for row in range(128):
        selr[row, row % 8] = 1
    common["seln"] = seln.astype(bf)
    common["selr"] = selr.astype(bf)
    common["ones_bf"] = np.ones((128, 512), f32).astype(bf)
    common["ones_f"] = np.ones((128, 128), f32)
    common["ones_row"] = np.ones((NH, 512), f32).astype(bf)
    common["ident_f"] = np.eye(128, dtype=f32)
    idx = np.arange(128)
    rowi, coli = idx[:, None], idx[None, :]
    common["masks"] = np.ascontiguousarray(np.stack([rowi <= coli, rowi >= coli, rowi < coli, rowi > coli], axis=1).astype(f32))
    inv = (1.0 / (np.float32(10000.0) ** (np.arange(0, 32, 2, dtype=f32) / np.float32(32)))).astype(f32)
    pos = np.concatenate([np.arange(lp, dtype=f32) for lp in cfg.LP])
    ang = (pos[:, None] * inv[None, :]).astype(f32)
    common["cosT"] = np.ascontiguousarray(np.repeat(np.cos(ang).T.astype(f32), NH, axis=0))
    common["sinT"] = np.ascontiguousarray(np.repeat(np.sin(ang).T.astype(f32), NH, axis=0))
    smu = np.asarray(inp["shift_mu"][:NL], f32)
    mu = np.zeros((128, NL, 16, 2), f32)
    roff0 = GROUPS["r0"][0]
    for gi, name in enumerate(RW_GROUPS):
        off, wdt = GROUPS[name]
        mu[0:wdt, :, gi, :] = smu[:, :, off - roff0:off - roff0 + wdt].transpose(2, 0, 1)
    common["mu"] = mu
    common.update(rwkv_host(cfg, inp))
    maps = []
    for c in range(len(xs_list)):
        m = dict(common)
        m["x0"] = np.ascontiguousarray(xs_list[c][0], f32)
        m["x1"] = np.ascontiguousarray(xs_list[c][1], f32)
        maps.append(m)
    return maps


def rwkv_host(cfg, inp):
    NL = cfg.NL
    f32 = np.float32
    out = {}
    rwp = np.zeros((128, NL, NH, 9, 64), f32)
    lw = np.zeros((NL, NH, 128, 2, 64), f32)
    g2 = np.zeros((NL, NH, 160, 64), f32)
    for l in range(NL):
        for h in range(NH):
            hs = slice(64 * h, 64 * h + 64)
            rows = [inp["decay_w0"][l, 0, hs], inp["decay_w0"][l, 1, hs], inp["iclr_a0"][l, 0, hs], inp["iclr_a0"][l, 1, hs],
                    inp["key_k_k"][l, hs], inp["key_k_a"][l, hs], inp["lnx_w"][l, hs], inp["lnx_b"][l, hs], inp["bonus_r_k"][l, h, :]]
            rwp[:, l, h, :, :] = np.stack([np.asarray(r_, f32) for r_ in rows], 0)[None]
            for d in range(2):
                lw[l, h, 64 * d:64 * d + 64, 0, :] = inp["decay_w2"][l, d, :, hs]
                lw[l, h, 64 * d:64 * d + 64, 1, :] = inp["iclr_a2"][l, d, :, hs]
            g2[l, h] = inp["gate_g2"][l, :, hs]
    out["rwp"] = rwp
    out["lw"] = lw
    out["g2"] = g2
    vm = np.zeros((128, 2), f32)
    for s_ in range(2):
        nvalid = cfg.LS[s_] - 128 * (cfg.NB[s_] - 1)
        vm[:nvalid, s_] = 1
    out["vmask"] = vm
    return out


_CACHE = {}


def kernel(**inp):
    cfg = Cfg()
    if "nc" not in _CACHE:
        _CACHE["nc"] = build(cfg)[0]
    nc = _CACHE["nc"]
    xp = np.asarray(inp["x_prompt"])
    xsm = np.asarray(inp["x_sample"])
    xs_list = [(xsm[c], xp[c % 2]) for c in range(8)]
    maps = host_maps(cfg, inp, xs_list)
    res = run_bass_kernel_spmd(nc, maps, core_ids=list(range(8)))
    y_s = np.stack([np.asarray(res.results[c]["y0"], np.float32) for c in range(8)], axis=0)
    y_p = np.stack([np.asarray(res.results[c]["y1"], np.float32) for c in range(2)], axis=0)
    return (y_p, y_s)
```

```python
import os
import numpy as np
import ml_dtypes
from contextlib import ExitStack
import concourse.bass as bass
import concourse.mybir as mybir
from concourse.bass_utils import run_bass_kernel_spmd

F32 = mybir.dt.float32
BF16 = mybir.dt.bfloat16
AF = mybir.ActivationFunctionType
ALU = mybir.AluOpType
AX = mybir.AxisListType

D = 1024
DFF = 2816
NH = 8
NMETA = 16
RMS_EPS = 1e-6
LNX_EPS = 64e-5
SCALE = 96 ** -0.5
CDEC = float(np.exp(-0.5))
NCOLP = 2592

GROUPS = {}
_o = 0
for _n, _w in ([("cq0", 128), ("cq1", 128), ("ckv", 128), ("kr1", 128), ("kr2", 128)]
               + [(f"r{i}", 128) for i in range(4)] + [(f"k{i}", 128) for i in range(4)]
               + [(f"v{i}", 128) for i in range(4)] + [("dw", 128), ("da", 128), ("dg0", 128), ("dg1", 32)]):
    GROUPS[_n] = (_o, _w)
    _o += _w
assert _o == NCOLP
RW_GROUPS = [f"r{i}" for i in range(4)] + [f"k{i}" for i in range(4)] + [f"v{i}" for i in range(4)] + ["dw", "da", "dg0", "dg1"]


class Sched:
    ENG = ("pe", "act", "dve", "pool", "sp")

    def __init__(self, nc, es, n_dsem=12):
        self.nc = nc
        self.e = dict(pe=nc.tensor, act=nc.scalar, dve=nc.vector, pool=nc.gpsimd, sp=nc.sync)
        self.semobj = {}
        self.cnt = {}
        for k in self.ENG:
            self.semobj[("e", k)] = es.enter_context(nc.semaphore("s_" + k))
            self.cnt[k] = 0
        self.dq = {}
        self.dqi = {}
        for q in ("sp", "act", "pool"):
            self.dq[q] = []
            for i in range(n_dsem):
                self.semobj[("d", q, i)] = es.enter_context(nc.semaphore(f"d_{q}{i}"))
                self.dq[q].append(0)
            self.dqi[q] = 0
        self.seen = {k: {} for k in self.ENG}
        self.lastw = {}
        self.lastr = {}
        self.n_ins = 0

    def _wait(self, eng, sk, val):
        if val <= 0 or self.seen[eng].get(sk, 0) >= val:
            return
        if sk == ("e", "pe") and eng == "pe":
            return
        self.e[eng].wait_ge(self.semobj[sk], val)
        self.seen[eng][sk] = val

    def _deps(self, eng, r, w):
        for res in r:
            lw = self.lastw.get(res)
            if lw:
                self._wait(eng, *lw)
        for res in w:
            lw = self.lastw.get(res)
            if lw:
                self._wait(eng, *lw)
            for sk, v in self.lastr.get(res, {}).items():
                self._wait(eng, sk, v)

    def _mark(self, sk, v, r, w):
        for res in r:
            self.lastr.setdefault(res, {})[sk] = v
        for res in w:
            self.lastw[res] = (sk, v)
            self.lastr[res] = {}

    PSUM_NAMES = ("pT", "pG", "pU", "pD", "pA", "pN", "pS", "pO", "pB", "pX", "pY", "pZ")

    def op(self, eng, fn, r=(), w=()):
        extra = [k for k in r if (k[0] if isinstance(k, tuple) else k) in self.PSUM_NAMES]
        if extra:
            w = list(w) + extra
        self._deps(eng, r, w)
        ins = fn(self.e[eng])
        self.cnt[eng] += 1
        ins.then_inc(self.semobj[("e", eng)], 1)
        self._mark(("e", eng), self.cnt[eng], r, w)
        self.n_ins += 1
        return ins

    def dma(self, q, out, in_, r=(), w=()):
        self._deps(q, r, w)
        i = self.dqi[q]
        self.dqi[q] = (i + 1) % len(self.dq[q])
        sk = ("d", q, i)
        self._wait(q, sk, self.dq[q][i])
        self.dq[q][i] += 16
        self.e[q].dma_start(out=out, in_=in_).then_inc(self.semobj[sk], 16)
        self._mark(sk, self.dq[q][i], r, w)
        self.n_ins += 1

    def mark(self, label):
        if not hasattr(self, "marks"):
            self.marks = []
        self.marks.append((label, dict(self.cnt)))

    def barrier(self):
        for eng in self.ENG:
            for k in self.ENG:
                self._wait(eng, ("e", k), self.cnt[k])
            for q in self.dq:
                for i, v in enumerate(self.dq[q]):
                    self._wait(eng, ("d", q, i), v)
        self.lastw = {}
        self.lastr = {}

    def finish(self):
        for k in self.ENG:
            self._wait("sp", ("e", k), self.cnt[k])
        for q in self.dq:
            for i, v in enumerate(self.dq[q]):
                self._wait("sp", ("d", q, i), v)


class Cfg:
    def __init__(self, LS=(2064, 8208), NL=2, debug=False, stages="all"):
        self.LS = list(LS)
        self.NL = NL
        self.debug = debug
        self.stages = stages
        self.NB = [-(-L // 128) for L in self.LS]
        self.LP = [nb * 128 for nb in self.NB]
        self.BASE = [0]
        for lp in self.LP[:-1]:
            self.BASE.append(self.BASE[-1] + lp)
        self.TP = sum(self.LP)
        self.NBT = self.TP // 128
        self.tiles = []
        b = 0
        while b < self.NBT:
            n = min(4, self.NBT - b)
            self.tiles.append((b, n))
            b += n


def build(cfg):
    nc = bass.Bass("TRN2", target_bir_lowering=False)
    NL, TP = cfg.NL, cfg.TP

    def din(name, shape, dt=F32):
        return nc.dram_tensor(name, list(shape), dt, kind="ExternalInput").ap()

    def dscr(name, shape, dt=F32):
        if cfg.debug:
            return nc.dram_tensor(name, list(shape), dt, kind="ExternalOutput").ap()
        return nc.dram_tensor(name, list(shape), dt).ap()

    xin = [din(f"x{s}", [cfg.LS[s] - NMETA, D]) for s in range(2)]
    yout = [nc.dram_tensor(f"y{s}", [cfg.LS[s] - NMETA, D], F32, kind="ExternalOutput").ap() for s in range(2)]
    meta = din("meta", [NMETA, D])
    wg = [din(f"ffn{k}_wg", [NL, D, DFF]) for k in (1, 2)]
    wu = [din(f"ffn{k}_wu", [NL, D, DFF]) for k in (1, 2)]
    wd = [din(f"ffn{k}_wd", [NL, DFF, D]) for k in (1, 2)]
    gains = din("gains", [128, NL, 3, 8])
    fnorm = din("fnorm", [128, D])
    ident_bf = din("ident_bf", [128, 128], BF16)
    zeros = din("zeros", [128, D])

    w_in = din("w_in", [NL, D, NCOLP])
    w_uq = din("w_uq", [NL, 256, 768])
    w_ukv = din("w_ukv", [NL, 128, 1024])
    w_out = din("w_out", [NL, D, D])
    qkg = din("qkg", [128, NL, 3])
    cosT = din("cosT", [128, TP])
    sinT = din("sinT", [128, TP])
    seln_in = din("seln", [128, 4, 32], BF16)
    selr_in = din("selr", [128, 32], BF16)
    ones_in = din("ones_bf", [128, 512], BF16)
    onesf_in = din("ones_f", [128, 128])
    onesrow_in = din("ones_row", [NH, 512], BF16)
    identf_in = din("ident_f", [128, 128])
    masks_in = din("masks", [128, 4, 128])
    mu_in = din("mu", [128, NL, 16, 2])
    rwp = din("rwp", [128, NL, NH, 9, 64])
    vmask_in = din("vmask", [128, 2])
    lw_in = din("lw", [NL, NH, 128, 2, 64])
    g2_in = din("g2", [NL, NH, 160, 64])

    H = dscr("H", [TP, D])
    XNT = dscr("XNT", [D, TP], BF16)
    QTd = dscr("QTd", [97, NH, TP], BF16)
    KTd = dscr("KTd", [97, NH, TP], BF16)
    VA = dscr("VA", [TP, NH, 65], BF16)
    QNd = dscr("QNd", [NH, TP])
    KNd = dscr("KNd", [NH, TP])
    RAW = dscr("RAW", [1952, TP])
    RWS = dscr("RWS", [1536, TP])
    LOR = dscr("LOR", [416, TP], BF16)
    MIXT = dscr("MIXT", [D, TP], BF16)
    EPI = dscr("EPI", [TP, 2, 64])

    es = ExitStack()
    S = Sched(nc, es)

    uid = [0]

    def sb(st, name, shape, dt):
        uid[0] += 1
        return st.enter_context(nc.sbuf_tensor(f"{name}_{uid[0]}", list(shape), dt))

    def ps(st, name, shape, dt=F32):
        uid[0] += 1
        return st.enter_context(nc.psum_tensor(f"{name}_{uid[0]}", list(shape), dt))

    cst = ExitStack()
    ident = sb(cst, "ident", [128, 128], BF16)
    gn = sb(cst, "gn", [128, NL, 3, 8], F32)
    fn_sb = sb(cst, "fn_sb", [128, D], F32)
    S.dma("sp", ident[:], ident_bf, w=["ident"])
    S.dma("sp", gn[:], gains, w=["gn"])
    S.dma("sp", fn_sb[:], fnorm, w=["fn"])

    for s in range(2):
        b0 = cfg.BASE[s]
        L = cfg.LS[s]
        S.dma("sp", H[b0:b0 + NMETA, :], meta, w=[("H", "init", s)])
        nrow = L - NMETA
        r0 = 0
        while r0 < nrow:
            n = min(2048, nrow - r0)
            S.dma("act" if (r0 // 2048) % 2 else "sp", H[b0 + NMETA + r0:b0 + NMETA + r0 + n, :], xin[s][r0:r0 + n, :], w=[("H", "init", s, r0)])
            r0 += n
        if cfg.LP[s] > L:
            S.dma("pool", H[b0 + L:b0 + cfg.LP[s], :], zeros[0:cfg.LP[s] - L, :], w=[("H", "initz", s)])
    S.barrier()

    def ffn_pass(l, k, half, with_norm, final=False):
        S.mark(f"ffn{k}h{half}_L{l}")
        st = ExitStack()
        NF = 11
        Wg = sb(st, "Wg", [128, 8, NF * 128], BF16)
        Wu = sb(st, "Wu", [128, 8, NF * 128], BF16)
        Wd = sb(st, "Wd", [128, NF, D], BF16)
        stg = [sb(st, f"stg{i}", [128, NF * 128], F32) for i in range(2)]
        hb = [sb(st, f"hb{i}", [128, D], F32) for i in range(2)]
        hc = [sb(st, f"hc{i}", [128, D], F32) for i in range(2)]
        xnbs = [sb(st, f"xnb{i}", [128, 4, D], BF16) for i in range(2)]
        xnT = [sb(st, f"xnT{i}", [128, 8, 512], BF16) for i in range(2)]
        hT = [sb(st, f"hT{i}", [128, NF, 512], BF16) for i in range(2)]
        sg = [sb(st, f"sg{i}", [128, 512], F32) for i in range(2)]
        ssq = sb(st, "ssq", [128, 12], F32)
        rs = sb(st, "rs", [128, 12], F32)
        junk = sb(st, "junk", [128, D], BF16)
        yb = [sb(st, f"yb{i}", [128, D], F32) for i in range(2)]
        pT = [ps(st, f"pT{i}", [128, 1024], BF16) for i in range(2)]
        pG = [ps(st, f"pG{i}", [128, 512]) for i in range(2)]
        pU = [ps(st, f"pU{i}", [128, 512]) for i in range(2)]
        pD = [ps(st, f"pD{i}", [128, 512]) for i in range(2)]
        f0 = half * NF * 128
        do_mix = (k == 1 and half == 0 and cfg.stages != "ffn")
        if do_mix:
            Wout = sb(st, "Wout", [128, 8, D], BF16)
            mx = [sb(st, f"mx{i}", [128, 8, 512], BF16) for i in range(2)]
            for fc in range(8):
                load_cast(Wout[:, fc, :], w_out[l, fc * 128:(fc + 1) * 128, :], stg[fc % 2][:, 0:D], ("stg", fc % 2), "sp" if fc % 2 == 0 else "act", ["dve", "pool"][fc % 2])
        ci = 0
        cast_eng = ["dve", "pool", "act"]
        for (Wsb, wsrc) in ((Wg, wg[k]), (Wu, wu[k])):
            for dc in range(8):
                sgi = ci % 2
                S.dma("sp" if ci % 2 == 0 else "act", stg[sgi][:], wsrc[l, dc * 128:(dc + 1) * 128, f0:f0 + NF * 128], w=[("stg", sgi)])
                eng = cast_eng[ci % 3]
                if eng == "act":
                    S.op("act", lambda e, o=Wsb[:, dc, :], i=stg[sgi][:]: e.copy(out=o, in_=i), r=[("stg", sgi)], w=[("W",)])
                else:
                    S.op(eng, lambda e, o=Wsb[:, dc, :], i=stg[sgi][:]: e.tensor_copy(out=o, in_=i), r=[("stg", sgi)], w=[("W",)])
                ci += 1
        for fc in range(NF):
            sgi = ci % 2
            S.dma("sp" if ci % 2 == 0 else "act", stg[sgi][:, 0:D], wd[k][l, f0 + fc * 128:f0 + (fc + 1) * 128, :], w=[("stg", sgi)])
            eng = cast_eng[ci % 3]
            if eng == "act":
                S.op("act", lambda e, o=Wd[:, fc, :], i=stg[sgi][:, 0:D]: e.copy(out=o, in_=i), r=[("stg", sgi)], w=[("W",)])
            else:
                S.op(eng, lambda e, o=Wd[:, fc, :], i=stg[sgi][:, 0:D]: e.tensor_copy(out=o, in_=i), r=[("stg", sgi)], w=[("W",)])
            ci += 1
        gidx = 0 if k == 0 else 2

        def stageA(ti):
            b0, nblk = cfg.tiles[ti]
            nt = nblk * 128
            xt = xnT[ti % 2]
            XK = ("xnT", ti % 2)
            if not with_norm:
                S.dma("sp", xt[:, :, 0:nt], XNT.rearrange("(c p) n -> p c n", p=128)[:, :, b0 * 128:b0 * 128 + nt], r=[("XNT", ti)], w=[XK])
                return
            if do_mix:
                mt = mx[ti % 2]
                MXK = ("mx", ti % 2)
                S.dma("pool", mt[:, :, 0:nt], MIXT.rearrange("(c p) n -> p c n", p=128)[:, :, b0 * 128:b0 * 128 + nt], r=[("MIXT",)], w=[MXK])
            for b in range(nblk):
                h = hb[b % 2]
                HK = ("hb", b % 2)
                S.dma("sp" if b % 2 == 0 else "act", h[:], H[(b0 + b) * 128:(b0 + b + 1) * 128, :], r=[("H", b0 + b)], w=[HK])
                if do_mix:
                    for hf in range(2):
                        for fc in range(8):
                            S.op("pe", lambda e, fc=fc, hf=hf, b=b: e.matmul(pD[hf][:, :], lhsT=mt[:, fc, b * 128:(b + 1) * 128], rhs=Wout[:, fc, hf * 512:(hf + 1) * 512], start=(fc == 0), stop=(fc == 7)),
                                 r=[MXK, "W"], w=[("pD", hf)])
                        S.op("dve", lambda e, hf=hf, h=h: e.tensor_tensor(out=h[:, hf * 512:(hf + 1) * 512], in0=pD[hf][:, :], in1=h[:, hf * 512:(hf + 1) * 512], op=ALU.add), r=[("pD", hf), HK], w=[HK])
                    S.dma("pool", H[(b0 + b) * 128:(b0 + b + 1) * 128, :], h[:], r=[HK], w=[("H", b0 + b)])
                S.op("act", lambda e, h=h, b=b: e.activation(out=junk[:], in_=h[:], func=AF.Square, accum_out=ssq[:, 4 * (ti % 2) + b:4 * (ti % 2) + b + 1]), r=[HK], w=["junk", ("ssq", 4 * (ti % 2) + b)])
                S.op("dve", lambda e, b=b: e.tensor_scalar(out=rs[:, 4 * (ti % 2) + b:4 * (ti % 2) + b + 1], in0=ssq[:, 4 * (ti % 2) + b:4 * (ti % 2) + b + 1], scalar1=1.0 / D, scalar2=RMS_EPS, op0=ALU.mult, op1=ALU.add), r=[("ssq", 4 * (ti % 2) + b)], w=[("rs", 4 * (ti % 2) + b)])
                S.op("act", lambda e, b=b: e.activation(out=rs[:, 4 * (ti % 2) + b:4 * (ti % 2) + b + 1], in_=rs[:, 4 * (ti % 2) + b:4 * (ti % 2) + b + 1], func=AF.Sqrt), r=[("rs", 4 * (ti % 2) + b)], w=[("rs", 4 * (ti % 2) + b)])
                S.op("dve", lambda e, b=b: e.reciprocal(out=rs[:, 4 * (ti % 2) + b:4 * (ti % 2) + b + 1], in_=rs[:, 4 * (ti % 2) + b:4 * (ti % 2) + b + 1]), r=[("rs", 4 * (ti % 2) + b)], w=[("rs", 4 * (ti % 2) + b)])
                S.op("act", lambda e, h=h, b=b: e.activation(out=xnbs[ti % 2][:, b, :], in_=h[:], func=AF.Copy, scale=rs[:, 4 * (ti % 2) + b:4 * (ti % 2) + b + 1]), r=[HK, ("rs", 4 * (ti % 2) + b)], w=[("xnb", ti % 2, b)])

        def stageA2(ti):
            b0, nblk = cfg.tiles[ti]
            nt = nblk * 128
            xt = xnT[ti % 2]
            XK = ("xnT", ti % 2)
            xnb = xnbs[ti % 2]
            if not with_norm:
                return
            for rnd in range(2):
                for b in range(nblk):
                    for j in range(4):
                        dc = rnd * 4 + j
                        S.op("pe", lambda e, b=b, dc=dc, j=j: e.transpose(out=pT[j // 2][:, (j % 2) * 512 + b * 128:(j % 2) * 512 + (b + 1) * 128], in_=xnb[:, b, dc * 128:(dc + 1) * 128], identity=ident[:]),
                             r=[("xnb", ti % 2, b), "ident"], w=[("pT", j // 2)])
                for j in range(4):
                    dc = rnd * 4 + j
                    eng = "act" if j % 2 == 0 else "dve"
                    if eng == "act":
                        S.op("act", lambda e, dc=dc, j=j: e.activation(out=xt[:, dc, 0:nt], in_=pT[j // 2][:, (j % 2) * 512:(j % 2) * 512 + nt], func=AF.Copy, scale=gn[:, l, gidx, dc:dc + 1]),
                             r=[("pT", j // 2), "gn"], w=[XK])
                    else:
                        S.op("dve", lambda e, dc=dc, j=j: e.tensor_scalar(out=xt[:, dc, 0:nt], in0=pT[j // 2][:, (j % 2) * 512:(j % 2) * 512 + nt], scalar1=gn[:, l, gidx, dc:dc + 1], scalar2=None, op0=ALU.mult),
                             r=[("pT", j // 2), "gn"], w=[XK])
            if half == 0:
                S.dma("pool", XNT.rearrange("(c p) n -> p c n", p=128)[:, :, b0 * 128:b0 * 128 + nt], xt[:, :, 0:nt], r=[XK], w=[("XNT", ti)])

        def stageB(ti):
            b0, nblk = cfg.tiles[ti]
            nt = nblk * 128
            xt = xnT[ti % 2]
            XK = ("xnT", ti % 2)
            ht = hT[ti % 2]
            HTK = ("hT", ti % 2)
            for fc in range(NF):
                for dc in range(8):
                    S.op("pe", lambda e, fc=fc, dc=dc: e.matmul(pG[fc % 2][:, 0:nt], lhsT=Wg[:, dc, fc * 128:(fc + 1) * 128], rhs=xt[:, dc, 0:nt], start=(dc == 0), stop=(dc == 7)),
                         r=[XK, ("W",), "W"], w=[("pG", fc % 2)])
                for dc in range(8):
                    S.op("pe", lambda e, fc=fc, dc=dc: e.matmul(pU[fc % 2][:, 0:nt], lhsT=Wu[:, dc, fc * 128:(fc + 1) * 128], rhs=xt[:, dc, 0:nt], start=(dc == 0), stop=(dc == 7)),
                         r=[XK, ("W",), "W"], w=[("pU", fc % 2)])
                S.op("act", lambda e, fc=fc: e.activation(out=sg[fc % 2][:, 0:nt], in_=pG[fc % 2][:, 0:nt], func=AF.Silu), r=[("pG", fc % 2)], w=[("sg", fc % 2)])
                S.op("dve", lambda e, fc=fc: e.tensor_tensor(out=ht[:, fc, 0:nt], in0=sg[fc % 2][:, 0:nt], in1=pU[fc % 2][:, 0:nt], op=ALU.mult), r=[("sg", fc % 2), ("pU", fc % 2)], w=[HTK])

        def stageC(ti):
            b0, nblk = cfg.tiles[ti]
            ht = hT[ti % 2]
            HTK = ("hT", ti % 2)
            for b in range(nblk):
                h = hc[b % 2]
                HK = ("hc", b % 2)
                S.dma("act" if b % 2 == 0 else "sp", h[:], H[(b0 + b) * 128:(b0 + b + 1) * 128, :], r=[("H", b0 + b)], w=[HK])
                for hf in range(2):
                    for fc in range(NF):
                        S.op("pe", lambda e, fc=fc, hf=hf, b=b: e.matmul(pD[hf][:, :], lhsT=ht[:, fc, b * 128:(b + 1) * 128], rhs=Wd[:, fc, hf * 512:(hf + 1) * 512], start=(fc == 0), stop=(fc == NF - 1)),
                             r=[HTK, ("W",), "W"], w=[("pD", hf)])
                    S.op("dve", lambda e, hf=hf, h=h: e.scalar_tensor_tensor(out=h[:, hf * 512:(hf + 1) * 512], in0=pD[hf][:, :], scalar=0.5, in1=h[:, hf * 512:(hf + 1) * 512], op0=ALU.mult, op1=ALU.add),
                         r=[("pD", hf), HK], w=[HK])
                if not final:
                    S.dma("pool", H[(b0 + b) * 128:(b0 + b + 1) * 128, :], h[:], r=[HK], w=[("H", b0 + b)])
                else:
                    y = yb[b % 2]
                    YK = ("yb", b % 2)
                    c = 8 + (b % 2)
                    S.op("act", lambda e, h=h, c=c: e.activation(out=junk[:], in_=h[:], func=AF.Square, accum_out=ssq[:, c:c + 1]), r=[HK], w=["junk", ("ssq", c)])
                    S.op("dve", lambda e, c=c: e.tensor_scalar(out=rs[:, c:c + 1], in0=ssq[:, c:c + 1], scalar1=1.0 / D, scalar2=RMS_EPS, op0=ALU.mult, op1=ALU.add), r=[("ssq", c)], w=[("rs", c)])
                    S.op("act", lambda e, c=c: e.activation(out=rs[:, c:c + 1], in_=rs[:, c:c + 1], func=AF.Sqrt), r=[("rs", c)], w=[("rs", c)])
                    S.op("dve", lambda e, c=c: e.reciprocal(out=rs[:, c:c + 1], in_=rs[:, c:c + 1]), r=[("rs", c)], w=[("rs", c)])
                    S.op("dve", lambda e, h=h, y=y, c=c: e.scalar_tensor_tensor(out=y[:], in0=h[:], scalar=rs[:, c:c + 1], in1=fn_sb[:], op0=ALU.mult, op1=ALU.mult), r=[HK, ("rs", c), "fn"], w=[YK])
                    g0 = (b0 + b) * 128
                    for s in range(2):
                        lo = max(g0, cfg.BASE[s] + NMETA)
                        hi = min(g0 + 128, cfg.BASE[s] + cfg.LS[s])
                        if hi > lo:
                            S.dma("pool", yout[s][lo - cfg.BASE[s] - NMETA:hi - cfg.BASE[s] - NMETA, :], y[lo - g0:hi - g0, :], r=[YK], w=[("y", s, lo)])

        nT = len(cfg.tiles)
        stageA(0)
        stageA2(0)
        if nT > 1:
            stageA(1)
        for ti in range(nT):
            stageB(ti)
            if ti + 1 < nT:
                stageA2(ti + 1)
            stageC(ti)
            if ti + 2 < nT:
                stageA(ti + 2)
        S.barrier()
        st.close()


    seln = sb(cst, "seln", [128, 4, 32], BF16)
    selr = sb(cst, "selr", [128, 32], BF16)
    ones_sb = sb(cst, "ones_sb", [128, 512], BF16)
    onesf = sb(cst, "onesf", [128, 128], F32)
    identf = sb(cst, "identf", [128, 128], F32)
    qk_sb = sb(cst, "qk_sb", [128, NL, 3], F32)
    S.dma("sp", seln[:], seln_in, w=["seln"])
    S.dma("sp", selr[:], selr_in, w=["selr"])
    S.dma("sp", ones_sb[:], ones_in, w=["ones"])
    S.dma("sp", onesf[:], onesf_in, w=["onesf"])
    S.dma("sp", identf[:], identf_in, w=["identf"])
    S.dma("sp", qk_sb[:], qkg, w=["qk"])
    S.barrier()

    def load_cast(dst, src, stg_ap, stg_key, q, eng):
        S.dma(q, stg_ap, src, w=[stg_key])
        if eng == "act":
            S.op("act", lambda e: e.copy(out=dst, in_=stg_ap), r=[stg_key], w=["W"])
        else:
            S.op(eng, lambda e: e.tensor_copy(out=dst, in_=stg_ap), r=[stg_key], w=["W"])

    def proj_pass(l):
        S.mark(f"proj_L{l}")
        st = ExitStack()
        Win = sb(st, "Win", [128, 8, NCOLP], BF16)
        Wuq = sb(st, "Wuq", [128, 2, 768], BF16)
        Wukv = sb(st, "Wukv", [128, 1024], BF16)
        stg = [sb(st, f"stg{i}", [128, NCOLP], F32) for i in range(2)]
        hb = [sb(st, f"hb{i}", [128, D], F32) for i in range(2)]
        xnbs = [sb(st, f"xnb{i}", [128, 4, D], BF16) for i in range(2)]
        xnT = [sb(st, f"xnT{i}", [128, 8, 512], BF16) for i in range(2)]
        ssq = sb(st, "ssq", [128, 12], F32)
        rs = sb(st, "rs", [128, 12], F32)
        junk = sb(st, "junk", [128, D], BF16)
        cq_sb = sb(st, "cq_sb", [128, 3, 512], F32)
        sqb = [sb(st, f"sqb{i}", [128, 512], BF16) for i in range(3)]
        sq6 = sb(st, "sq6", [128, 6, 512], BF16)
        if os.environ.get("DUMMY_KB"):
            dummy = sb(st, "dummy", [128, int(os.environ["DUMMY_KB"]) * 256], F32)
        rstd = [sb(st, f"rstd{i}", [128, 512], F32) for i in range(2)]
        cqn = sb(st, "cqn", [128, 2, 512], BF16)
        ckvn = sb(st, "ckvn", [128, 512], BF16)
        qn_sb = sb(st, "qn_sb", [128, 4, 512], BF16)
        kn_sb = sb(st, "kn_sb", [128, 4, 512], BF16)
        xr = [sb(st, f"xr{i}", [128, 512], F32) for i in range(2)]
        tt = [sb(st, f"tt{i}", [128, 512], F32) for i in range(4)]
        rr = [sb(st, f"rr{i}", [128, 512], BF16) for i in range(4)]
        cs = sb(st, "cs", [128, 512], F32)
        sn = sb(st, "sn", [128, 512], F32)
        VAt = [sb(st, f"VAt{i}", [128, 4, NH, 65], BF16) for i in range(2)]
        rwb = [sb(st, f"rwb{i}", [128, 512], F32) for i in range(4)]
        nrm = sb(st, "nrm", [8, 2, 512], F32)
        pT = [ps(st, f"pT{i}", [128, 1024], BF16) for i in range(2)]
        pA = [ps(st, f"pA{i}", [128, 512]) for i in range(5)]
        pN = ps(st, "pN", [128, 512])
        for i in range(2):
            S.op("pool", lambda e, i=i: e.memset(VAt[i][:], 1.0), w=[("VAt", i)])
        for dc in range(8):
            load_cast(Win[:, dc, :], w_in[l, dc * 128:(dc + 1) * 128, :], stg[dc % 2][:], ("stg", dc % 2), "sp" if dc % 2 == 0 else "act", ["dve", "pool"][dc % 2])
        load_cast(Wuq[:], w_uq[l].rearrange("(k p) n -> p k n", p=128), stg[0][:, 0:1536].rearrange("p (k n) -> p k n", k=2), ("stg", 0), "sp", "dve")
        load_cast(Wukv[:], w_ukv[l], stg[1][:, 0:1024], ("stg", 1), "act", "pool")

        def stageA(ti):
            b0, nblk = cfg.tiles[ti]
            nt = nblk * 128
            xt = xnT[ti % 2]
            XK = ("xnT", ti % 2)
            for b in range(nblk):
                h = hb[b % 2]
                HK = ("hb", b % 2)
                S.dma("sp" if b % 2 == 0 else "act", h[:], H[(b0 + b) * 128:(b0 + b + 1) * 128, :], r=[("H", b0 + b)], w=[HK])
                S.op("act", lambda e, h=h, b=b: e.activation(out=junk[:], in_=h[:], func=AF.Square, accum_out=ssq[:, 4 * (ti % 2) + b:4 * (ti % 2) + b + 1]), r=[HK], w=["junk", ("ssq", 4 * (ti % 2) + b)])
                S.op("dve", lambda e, b=b: e.tensor_scalar(out=rs[:, 4 * (ti % 2) + b:4 * (ti % 2) + b + 1], in0=ssq[:, 4 * (ti % 2) + b:4 * (ti % 2) + b + 1], scalar1=1.0 / D, scalar2=RMS_EPS, op0=ALU.mult, op1=ALU.add), r=[("ssq", 4 * (ti % 2) + b)], w=[("rs", 4 * (ti % 2) + b)])
                S.op("act", lambda e, b=b: e.activation(out=rs[:, 4 * (ti % 2) + b:4 * (ti % 2) + b + 1], in_=rs[:, 4 * (ti % 2) + b:4 * (ti % 2) + b + 1], func=AF.Sqrt), r=[("rs", 4 * (ti % 2) + b)], w=[("rs", 4 * (ti % 2) + b)])
                S.op("dve", lambda e, b=b: e.reciprocal(out=rs[:, 4 * (ti % 2) + b:4 * (ti % 2) + b + 1], in_=rs[:, 4 * (ti % 2) + b:4 * (ti % 2) + b + 1]), r=[("rs", 4 * (ti % 2) + b)], w=[("rs", 4 * (ti % 2) + b)])
                S.op("act", lambda e, h=h, b=b: e.activation(out=xnbs[ti % 2][:, b, :], in_=h[:], func=AF.Copy, scale=rs[:, 4 * (ti % 2) + b:4 * (ti % 2) + b + 1]), r=[HK, ("rs", 4 * (ti % 2) + b)], w=[("xnb", ti % 2, b)])

        def stageA2(ti):
            b0, nblk = cfg.tiles[ti]
            nt = nblk * 128
            xt = xnT[ti % 2]
            XK = ("xnT", ti % 2)
            xnb = xnbs[ti % 2]
            for rnd in range(2):
                for b in range(nblk):
                    for j in range(4):
                        dc = rnd * 4 + j
                        S.op("pe", lambda e, b=b, dc=dc, j=j: e.transpose(out=pT[j // 2][:, (j % 2) * 512 + b * 128:(j % 2) * 512 + (b + 1) * 128], in_=xnb[:, b, dc * 128:(dc + 1) * 128], identity=ident[:]),
                             r=[("xnb", ti % 2, b), "ident"], w=[("pT", j // 2)])
                for j in range(4):
                    dc = rnd * 4 + j
                    if j % 2 == 0:
                        S.op("act", lambda e, dc=dc, j=j: e.activation(out=xt[:, dc, 0:nt], in_=pT[j // 2][:, (j % 2) * 512:(j % 2) * 512 + nt], func=AF.Copy, scale=gn[:, l, 1, dc:dc + 1]),
                             r=[("pT", j // 2), "gn"], w=[XK])
                    else:
                        S.op("dve", lambda e, dc=dc, j=j: e.tensor_scalar(out=xt[:, dc, 0:nt], in0=pT[j // 2][:, (j % 2) * 512:(j % 2) * 512 + nt], scalar1=gn[:, l, 1, dc:dc + 1], scalar2=None, op0=ALU.mult),
                             r=[("pT", j // 2), "gn"], w=[XK])

        def stageB(ti):
            b0, nblk = cfg.tiles[ti]
            nt = nblk * 128
            c0 = b0 * 128
            xt = xnT[ti % 2]
            XK = ("xnT", ti % 2)
            S.dma("sp", cs[:, 0:nt], cosT[:, c0:c0 + nt], w=["cs"])
            S.dma("act", sn[:, 0:nt], sinT[:, c0:c0 + nt], w=["sn"])
            bank = [0]
            evi = [0]

            def nextbank():
                i = bank[0] % 5
                bank[0] += 1
                return pA[i], ("pA", i)

            def evac(out, in_, r, w):
                evi[0] += 1
                if evi[0] % 2 == 0:
                    S.op("act", lambda e: e.copy(out=out, in_=in_), r=r, w=w)
                else:
                    S.op("dve", lambda e: e.tensor_copy(out=out, in_=in_), r=r, w=w)

            def win_group(name):
                off, wdt = GROUPS[name]
                p, pk = nextbank()
                for dc in range(8):
                    S.op("pe", lambda e, dc=dc: e.matmul(p[0:wdt, 0:nt], lhsT=Win[:, dc, off:off + wdt], rhs=xt[:, dc, 0:nt], start=(dc == 0), stop=(dc == 7)), r=[XK, "W"], w=[pk])
                return p, pk

            def rms_part1(chunks, slot0):
                for j, (p, pk, sbuf, sk) in enumerate(chunks):
                    S.op("act", lambda e, p=p, sbuf=sbuf: e.copy(out=sbuf, in_=p[:, 0:nt]), r=[pk], w=[sk])
                    S.op("act", lambda e, p=p, j=j: e.activation(out=sqb[slot0 + j][:, 0:nt], in_=p[:, 0:nt], func=AF.Square), r=[pk], w=[("sqb", slot0 + j)])

            def rms_part2(chunks, slot0, n_feat, gcol0, outs, rsd, rkey):
                p2, pk2 = nextbank()
                for j in range(len(chunks)):
                    S.op("pe", lambda e, j=j: e.matmul(p2[:, 0:nt], lhsT=ones_sb[:, 0:128], rhs=sqb[slot0 + j][:, 0:nt], start=(j == 0), stop=(j == len(chunks) - 1)), r=[("sqb", slot0 + j), "ones"], w=[pk2])
                S.op("dve", lambda e: e.tensor_scalar(out=rsd[:, 0:nt], in0=p2[:, 0:nt], scalar1=1.0 / n_feat, scalar2=RMS_EPS, op0=ALU.mult, op1=ALU.add), r=[pk2], w=[rkey])
                S.op("act", lambda e: e.activation(out=rsd[:, 0:nt], in_=rsd[:, 0:nt], func=AF.Sqrt), r=[rkey], w=[rkey])
                S.op("dve", lambda e: e.reciprocal(out=rsd[:, 0:nt], in_=rsd[:, 0:nt]), r=[rkey], w=[rkey])
                for j, (p, pk, sbuf, sk) in enumerate(chunks):
                    o, ok = outs[j]
                    S.op("dve", lambda e, sbuf=sbuf, o=o, j=j: e.scalar_tensor_tensor(out=o, in0=sbuf, scalar=qk_sb[:, l, gcol0 + j:gcol0 + j + 1], in1=rsd[:, 0:nt], op0=ALU.mult, op1=ALU.mult),
                         r=[sk, rkey, "qk"], w=[ok])

            def raw_groups(lo, hi):
                roff = GROUPS["r0"][0]
                for gi in range(lo, hi):
                    name = RW_GROUPS[gi]
                    off, wdt = GROUPS[name]
                    p, pk = win_group(name)
                    evac(rwb[gi % 4][0:wdt, 0:nt], p[0:wdt, 0:nt], [pk], [("rwb", gi % 4)])
                    S.dma(["sp", "act", "pool"][gi % 3], RAW[off - roff:off - roff + wdt, c0:c0 + nt], rwb[gi % 4][0:wdt, 0:nt], r=[("rwb", gi % 4)], w=[("RAW", ti, gi)])

            ch = []
            for j in range(2):
                p, pk = win_group(f"cq{j}")
                ch.append((p, pk, cq_sb[:, j, 0:nt], ("cq", j)))
            p, pk = win_group("ckv")
            chkv = [(p, pk, cq_sb[:, 2, 0:nt], ("cq", 2))]
            rms_part1(ch, 0)
            rms_part1(chkv, 2)
            raw_groups(0, 8)
            rms_part2(ch, 0, 256, 0, [(cqn[:, 0, 0:nt], ("cqn", 0)), (cqn[:, 1, 0:nt], ("cqn", 1))], rstd[0], "rstd0")
            rms_part2(chkv, 2, 128, 2, [(ckvn[:, 0:nt], "ckvn")], rstd[1], "rstd1")

            def rope(x1, x2, o1, o2, k1, k2, ok1, ok2):
                S.op("pool", lambda e: e.tensor_tensor(out=tt[0][:, 0:nt], in0=x1[:, 0:nt], in1=cs[:, 0:nt], op=ALU.mult), r=[k1, "cs"], w=[("tt", 0)])
                S.op("pool", lambda e: e.tensor_tensor(out=tt[1][:, 0:nt], in0=x2[:, 0:nt], in1=sn[:, 0:nt], op=ALU.mult), r=[k2, "sn"], w=[("tt", 1)])
                S.op("dve", lambda e: e.tensor_tensor(out=o1[:, 0:nt], in0=tt[0][:, 0:nt], in1=tt[1][:, 0:nt], op=ALU.subtract), r=[("tt", 0), ("tt", 1)], w=[ok1])
                S.op("pool", lambda e: e.tensor_tensor(out=tt[2][:, 0:nt], in0=x2[:, 0:nt], in1=cs[:, 0:nt], op=ALU.mult), r=[k2, "cs"], w=[("tt", 2)])
                S.op("pool", lambda e: e.tensor_tensor(out=tt[3][:, 0:nt], in0=x1[:, 0:nt], in1=sn[:, 0:nt], op=ALU.mult), r=[k1, "sn"], w=[("tt", 3)])
                S.op("dve", lambda e: e.tensor_tensor(out=o2[:, 0:nt], in0=tt[2][:, 0:nt], in1=tt[3][:, 0:nt], op=ALU.add), r=[("tt", 2), ("tt", 3)], w=[ok2])

            SUB = os.environ.get("QK_SUB", "namdep")

            def qk_side(which, nope_sb, dst, nd, slot):
                KSKIP = os.environ.get("K_SKIP", "")
                for c in range(0 if (which == "k" and "nope" in KSKIP) else 4):
                    p, pk = nextbank()
                    if which == "q":
                        for kc in range(2):
                            S.op("pe", lambda e, kc=kc, c=c: e.matmul(p[:, 0:nt], lhsT=Wuq[:, kc, c * 128:(c + 1) * 128], rhs=cqn[:, kc, 0:nt], start=(kc == 0), stop=(kc == 1)), r=[("cqn", kc), "W"], w=[pk])
                    else:
                        if os.environ.get("KSPLIT"):
                            for hh in range(2):
                                S.op("pe", lambda e, c=c, hh=hh: e.matmul(p[:, 0:nt], lhsT=Wukv[64 * hh:64 * hh + 64, c * 128:(c + 1) * 128], rhs=ckvn[64 * hh:64 * hh + 64, 0:nt], start=(hh == 0), stop=(hh == 1)), r=["ckvn", "W"], w=[pk])
                        else:
                            S.op("pe", lambda e, c=c: e.matmul(p[:, 0:nt], lhsT=Wukv[:, c * 128:(c + 1) * 128], rhs=ckvn[:, 0:nt], start=True, stop=True), r=["ckvn", "W"], w=[pk])
                    S.op("act", lambda e, c=c: e.copy(out=nope_sb[:, c, 0:nt], in_=p[:, 0:nt]), r=[pk], w=[(which + "n", c)])
                    S.op("act", lambda e, c=c: e.activation(out=sq6[:, c, 0:nt], in_=p[:, 0:nt], func=AF.Square), r=[pk], w=[("sq6", c)])
                if os.environ.get("KBAR"):
                    S.barrier()
                for j in range(0 if (which == "k" and "rope" in KSKIP) else 2):
                    if which == "q":
                        p, pk = nextbank()
                        for kc in range(2):
                            S.op("pe", lambda e, kc=kc, j=j: e.matmul(p[:, 0:nt], lhsT=Wuq[:, kc, 512 + j * 128:512 + (j + 1) * 128], rhs=cqn[:, kc, 0:nt], start=(kc == 0), stop=(kc == 1)), r=[("cqn", kc), "W"], w=[pk])
                    else:
                        p, pk = win_group(f"kr{j + 1}")
                    S.op("dve", lambda e, j=j: e.tensor_copy(out=xr[j][:, 0:nt], in_=p[:, 0:nt]), r=[pk], w=[("xr", j)])
                    S.op("act", lambda e, j=j: e.activation(out=sq6[:, 4 + j, 0:nt], in_=p[:, 0:nt], func=AF.Square), r=[pk], w=[("sq6", 4 + j)])
                if "n" in SUB:
                    for c in range(6):
                        S.op("pe", lambda e, c=c: e.matmul(pN[0:32, 0:nt], lhsT=(seln[:, c, :] if c < 4 else selr[:, :]), rhs=sq6[:, c, 0:nt], start=(c == 0), stop=(c == 5)), r=[("sq6", c), "seln", "selr"], w=["pN"])
                o1, o2 = rr[2 * slot], rr[2 * slot + 1]
                if "p" in SUB:
                    rope(xr[0], xr[1], o1, o2, ("xr", 0), ("xr", 1), ("rr", 2 * slot), ("rr", 2 * slot + 1))
                if "a" not in SUB:
                    pass
                elif which == "q":
                    S.op("act", lambda e: e.activation(out=nrm[:, slot, 0:nt], in_=pN[0:8, 0:nt], func=AF.Sqrt), r=["pN"], w=[("nrm", slot)])
                else:
                    S.op("act", lambda e: e.copy(out=nrm[:, slot, 0:nt], in_=pN[0:8, 0:nt]), r=["pN"], w=[("nrm", slot)])
                if "m" in SUB:
                    S.dma("pool", nd[:, c0:c0 + nt], nrm[:, slot, 0:nt], r=[("nrm", slot)], w=[(which + "nd", ti)])
                for two in range(2 if "d" in SUB else 0):
                    S.dma("sp" if two == 0 else "act", dst[0:64, :, c0:c0 + nt].rearrange("r (c two) n -> r c two n", two=2)[:, :, two, :], nope_sb[64 * two:64 * two + 64, :, 0:nt], r=[(which + "n", c) for c in range(4)], w=[(which + "T", ti, two)])
                if "e" in SUB:
                    S.dma("pool", dst[64:80, :, c0:c0 + nt].rearrange("i h n -> (i h) n"), o1[:, 0:nt], r=[("rr", 2 * slot)], w=[(which + "T", ti, 2)])
                    S.dma("pool", dst[80:96, :, c0:c0 + nt].rearrange("i h n -> (i h) n"), o2[:, 0:nt], r=[("rr", 2 * slot + 1)], w=[(which + "T", ti, 3)])

            PARTS = os.environ.get("PROJ_PARTS", "qkovr")
            if "q" in PARTS:
                qk_side("q", qn_sb, QTd, QNd, 0)
            if "k" in PARTS:
                qk_side("k", kn_sb, KTd, KNd, 1)
            if "o" in PARTS:
                S.dma("sp", KTd[96, :, c0:c0 + nt], onesrow_in[:, 0:nt], w=[("kT", ti, 4)])
            vt = VAt[ti % 2]
            VK = ("VAt", ti % 2)
            for b in range(nblk if "v" in PARTS else 0):
                p, pk = nextbank()
                S.op("pe", lambda e, b=b: e.matmul(p[:, 0:512], lhsT=ckvn[:, b * 128:(b + 1) * 128], rhs=Wukv[:, 512:1024], start=True, stop=True), r=["ckvn", "W"], w=[pk])
                evac(vt[:, b, :, 0:64], p[:, 0:512].rearrange("p (h d) -> p h d", d=64), [pk], [VK])
            if "v" in PARTS:
                S.dma("sp", VA[c0:c0 + nt, :, :].rearrange("(b p) h c -> p b h c", p=128), vt[:, 0:nblk, :, :], r=[VK], w=[("VA", ti)])
            raw_groups(8, 16)

        nT = len(cfg.tiles)
        stageA(0)
        stageA2(0)
        if nT > 1:
            stageA(1)
        for ti in range(nT):
            if ti + 1 < nT:
                stageA2(ti + 1)
            stageB(ti)
            if ti + 2 < nT:
                stageA(ti + 2)
        S.barrier()
        st.close()


    LPM = max(cfg.LP)
    NBM = max(cfg.NB)

    def shift_phase(l):
        S.mark(f"shift_L{l}")
        st = ExitStack()
        mu = sb(st, "mu", [128, 16, 2], F32)
        c0t = sb(st, "c0t", [128, 16], F32)
        X = [sb(st, f"X{i}", [128, 514], F32) for i in range(4)]
        T = [sb(st, f"T{i}", [128, 512], F32) for i in range(4)]
        Tb = [sb(st, f"Tb{i}", [128, 512], BF16) for i in range(2)]
        S.dma("sp", mu[:], mu_in[:, l, :, :], w=["mu"])
        S.op("dve", lambda e: e.tensor_tensor(out=c0t[:], in0=mu[:, :, 0], in1=mu[:, :, 1], op=ALU.add), r=["mu"], w=["c0"])
        S.op("dve", lambda e: e.tensor_scalar(out=c0t[:], in0=c0t[:], scalar1=-1.0, scalar2=1.0, op0=ALU.mult, op1=ALU.add), r=["c0"], w=["c0"])
        roff0 = GROUPS["r0"][0]
        it = 0
        for s_ in range(2):
            base, L, Lp = cfg.BASE[s_], cfg.LS[s_], cfg.LP[s_]
            for t0 in range(0, Lp, 512):
                nt = min(512, Lp - t0)
                valid = max(0, min(nt, L - t0))
                for gi, name in enumerate(RW_GROUPS):
                    off, wdt = GROUPS[name]
                    ro = off - roff0
                    x = X[it % 4]
                    XK = ("X", it % 4)
                    t = T[it % 4]
                    TK = ("T", it % 4)
                    eng = "dve"
                    lo = max(t0 - 1, 0)
                    hi = min(t0 + nt + 1, L)
                    jlo, jhi = lo - (t0 - 1), hi - (t0 - 1)
                    if jlo > 0:
                        S.op("pool", lambda e: e.memset(x[0:wdt, 0:jlo], 0.0), w=[XK])
                    if jhi < nt + 2:
                        S.op("pool", lambda e: e.memset(x[0:wdt, max(jhi, 0):nt + 2], 0.0), w=[XK])
                    if jhi > jlo:
                        S.dma("sp" if it % 2 == 0 else "act", x[0:wdt, jlo:jhi], RAW[ro:ro + wdt, base + lo:base + hi], r=[("RAW",)], w=[XK])
                    S.op(eng, lambda e: e.tensor_scalar(out=t[0:wdt, 0:nt], in0=x[0:wdt, 1:nt + 1], scalar1=c0t[0:wdt, gi:gi + 1], scalar2=None, op0=ALU.mult), r=[XK, "c0"], w=[TK])
                    S.op(eng, lambda e: e.scalar_tensor_tensor(out=t[0:wdt, 0:nt], in0=x[0:wdt, 0:nt], scalar=mu[0:wdt, gi, 0:1], in1=t[0:wdt, 0:nt], op0=ALU.mult, op1=ALU.add), r=[XK, TK, "mu"], w=[TK])
                    is_da = (name == "da")
                    if is_da:
                        tb = Tb[it % 2]
                        TBK = ("Tb", it % 2)
                        S.op(eng, lambda e: e.scalar_tensor_tensor(out=tb[0:wdt, 0:nt], in0=x[0:wdt, 2:nt + 2], scalar=mu[0:wdt, gi, 1:2], in1=t[0:wdt, 0:nt], op0=ALU.mult, op1=ALU.add), r=[XK, TK, "mu"], w=[TBK])
                        if valid < nt:
                            S.op(eng, lambda e: e.memset(tb[0:wdt, valid:nt], 0.0), w=[TBK])
                        S.dma("pool", LOR[128:256, base + t0:base + t0 + nt], tb[0:wdt, 0:nt], r=[TBK], w=[("LOR", it)])
                    else:
                        S.op(eng, lambda e: e.scalar_tensor_tensor(out=t[0:wdt, 0:nt], in0=x[0:wdt, 2:nt + 2], scalar=mu[0:wdt, gi, 1:2], in1=t[0:wdt, 0:nt], op0=ALU.mult, op1=ALU.add), r=[XK, TK, "mu"], w=[TK])
                        if valid < nt:
                            S.op(eng, lambda e: e.memset(t[0:wdt, valid:nt], 0.0), w=[TK])
                        if gi < 12:
                            S.dma("pool" if it % 2 == 0 else "act", RWS[ro:ro + wdt, base + t0:base + t0 + nt], t[0:wdt, 0:nt], r=[TK], w=[("RWS", it)])
                        else:
                            tb = Tb[it % 2]
                            TBK = ("Tb", it % 2)
                            fn_ = AF.Tanh if name == "dw" else AF.Sigmoid
                            S.op("act", lambda e: e.activation(out=tb[0:wdt, 0:nt], in_=t[0:wdt, 0:nt], func=fn_), r=[TK], w=[TBK])
                            lo_r = {"dw": 0, "dg0": 256, "dg1": 384}[name]
                            S.dma("pool", LOR[lo_r:lo_r + wdt, base + t0:base + t0 + nt], tb[0:wdt, 0:nt], r=[TBK], w=[("LOR", it)])
                    it += 1
        S.barrier()
        st.close()

    def attn_phase(l):
        S.mark(f"attn_L{l}")
        st = ExitStack()
        Ksb = [sb(st, f"Ksb{i}", [97, LPM], BF16) for i in range(2)]
        Qsb = [sb(st, f"Qsb{i}", [97, LPM], BF16) for i in range(2)]
        Vsb = [sb(st, f"Vsb{i}", [128, NBM, 65], BF16) for i in range(2)]
        knq = sb(st, "knq", [8, LPM], F32)
        augb = sb(st, "augb", [8, LPM], BF16)
        kmax = sb(st, "kmax", [8, 2], F32)
        Pt = [sb(st, f"Pt{i}", [128, 512], BF16) for i in range(5)]
        Osb = [sb(st, f"Osb{i}", [65, 512], F32) for i in range(2)]
        rec = [sb(st, f"rec{i}", [65, 512], F32) for i in range(2)]
        Ob = [sb(st, f"Ob{i}", [64, 512], BF16) for i in range(2)]
        pS = [ps(st, f"pS{i}", [128, 512]) for i in range(5)]
        pO = [ps(st, f"pO{i}", [128, 512]) for i in range(2)]
        pB = [ps(st, f"pB{i}", [128, 512]) for i in range(1)] * 2
        hs = 0
        qt = 0
        si = 0
        pending = []
        for s_ in range(2):
            base, L, Lp, nb = cfg.BASE[s_], cfg.LS[s_], cfg.LP[s_], cfg.NB[s_]
            S.dma("sp", knq[:, 0:L], KNd[:, base:base + L], r=[("KNd",)], w=["knq"])
            S.op("dve", lambda e: e.tensor_reduce(out=kmax[:, 0:1], in_=knq[:, 0:L], axis=AX.X, op=ALU.max), r=["knq"], w=["kmax"])
            S.op("act", lambda e: e.activation(out=kmax[:, 0:1], in_=kmax[:, 0:1], func=AF.Sqrt), r=["kmax"], w=["kmax"])
            S.dma("sp", knq[:, 0:L], QNd[:, base:base + L], r=[("QNd",)], w=["knq"])
            S.op("dve", lambda e: e.tensor_scalar(out=augb[:, 0:L], in0=knq[:, 0:L], scalar1=kmax[:, 0:1], scalar2=-1.0, op0=ALU.mult, op1=ALU.mult), r=["knq", "kmax"], w=["augb"])
            S.dma("sp", QTd[96, :, base:base + L], augb[:, 0:L], r=["augb"], w=[("QTd", s_)])
            for h in range(NH):
                bf = hs % 2
                hs += 1
                K_, Q_, V_ = Ksb[bf], Qsb[bf], Vsb[bf]
                KK, QK, VK = ("K", bf), ("Q", bf), ("V", bf)
                S.dma("sp", K_[:, 0:L], KTd[:, h, base:base + L], r=[("KTd",)], w=[KK])
                S.dma("act", Q_[:, 0:L], QTd[:, h, base:base + L], r=[("QTd", s_)], w=[QK])
                S.dma("pool", V_[:, 0:nb, :], VA[base:base + Lp, h, :].rearrange("(b p) c -> p b c", p=128), r=[("VA",)], w=[VK])
                nkb = -(-L // 128)
                for q0 in range(0, L, 512):
                    qw = min(512, L - q0)
                    j = qt % 2
                    qt += 1
                    OK_ = ("pO", j)

                    def emitS(kb):
                        kw = min(128, L - kb * 128)
                        i = (si + kb) % 5
                        S.op("pe", lambda e: e.matmul(pS[i][0:kw, 0:qw], lhsT=K_[:, kb * 128:kb * 128 + kw], rhs=Q_[:, q0:q0 + qw], start=True, stop=True), r=[KK, QK], w=[("pS", i)])
                        S.op("act", lambda e: e.activation(out=Pt[i][0:kw, 0:qw], in_=pS[i][0:kw, 0:qw], func=AF.Exp, scale=SCALE), r=[("pS", i)], w=[("Pt", i)])

                    def emitPV(kb):
                        kw = min(128, L - kb * 128)
                        i = (si + kb) % 5
                        S.op("pe", lambda e: e.matmul(pO[j][0:65, 0:qw], lhsT=V_[0:kw, kb, :], rhs=Pt[i][0:kw, 0:qw], start=(kb == 0), stop=(kb == nkb - 1)), r=[VK, ("Pt", i)], w=[OK_])

                    SK = 3
                    for kb in range(nkb + SK):
                        if kb < nkb:
                            emitS(kb)
                        if kb == 1 and pending:
                            pending.pop()()
                        if kb >= SK:
                            emitPV(kb - SK)
                    si = (si + nkb) % 5

                    def fin(j=j, qw=qw, q0=q0, h=h, base=base):
                        S.op("pe", lambda e: e.matmul(pB[j][0:64, 0:qw], lhsT=onesf[64:65, 0:64], rhs=rec[j][64:65, 0:qw], start=True, stop=True), r=[("rec", j), "onesf"], w=[("pB", 0)])
                        S.op("dve", lambda e: e.tensor_tensor(out=Ob[j][:, 0:qw], in0=Osb[j][0:64, 0:qw], in1=pB[j][0:64, 0:qw], op=ALU.mult), r=[("Osb", j), ("pB", 0)], w=[("Ob", j)])
                        S.dma("pool", MIXT[64 * h:64 * h + 64, base + q0:base + q0 + qw], Ob[j][:, 0:qw], r=[("Ob", j)], w=[("MIXT", h, base + q0)])

                    S.op("dve", lambda e: e.tensor_copy(out=Osb[j][:, 0:qw], in_=pO[j][0:65, 0:qw]), r=[OK_], w=[("Osb", j)])
                    S.op("dve", lambda e: e.reciprocal(out=rec[j][64:65, 0:qw], in_=Osb[j][64:65, 0:qw]), r=[("Osb", j)], w=[("rec", j)])
                    if pending:
                        pending.pop()()
                    pending.append(fin)
        while pending:
            pending.pop()()
        S.barrier()
        st.close()


    def rwkv_phase(l):
        S.mark(f"rwkv_L{l}")
        st = ExitStack()
        c_ = CDEC
        MK = sb(st, "MK", [128, 4, 128], F32)
        MP1 = sb(st, "MP1", [128, 2, 2, 128], F32)
        MP3 = sb(st, "MP3", [128, 2, 128], F32)
        vm = sb(st, "vm", [128, 2], F32)
        prm = sb(st, "prm", [128, 9, 64], F32)
        omk = sb(st, "omk", [128, 64], F32)
        LWf = sb(st, "LWf", [64, 2, 2, 64], F32)
        LW = sb(st, "LW", [64, 2, 2, 64], BF16)
        G2f = sb(st, "G2f", [128, 2, 64], F32)
        G2 = sb(st, "G2", [128, 2, 64], BF16)
        S.dma("sp", MK[:], masks_in, w=["MK"])
        S.dma("sp", vm[:], vmask_in, w=["vm"])
        for d, (ms, mi, m3) in enumerate(((2, 0, 3), (3, 1, 2))):
            S.op("dve", lambda e: e.tensor_copy(out=MP1[:, d, 0, :], in_=MK[:, ms, :]), r=["MK"], w=["MP"])
            S.op("dve", lambda e: e.tensor_copy(out=MP1[:, d, 1, :], in_=MK[:, mi, :]), r=["MK"], w=["MP"])
            S.op("dve", lambda e: e.tensor_copy(out=MP3[:, d, :], in_=MK[:, m3, :]), r=["MK"], w=["MP"])
        GT = sb(st, "GT", [64, NBM, 2, 128], BF16)
        PHIT = sb(st, "PHIT", [64, NBM, 2, 64], BF16)
        PSI = sb(st, "PSI", [64, NBM, 2, 64], F32)
        Yacc = sb(st, "Yacc", [128, NBM, 64], F32)
        Sb = [sb(st, f"Sb{i}", [64, 2, 64], BF16) for i in range(2)]
        ey = sb(st, "ey", [128, 4, 64], F32)
        es1 = sb(st, "es1", [128, 8], F32)
        eo = sb(st, "eo", [128, 4, 64], F32)
        eob = sb(st, "eob", [64, 512], BF16)
        egl = sb(st, "egl", [128, 4, 2, 64], F32)
        banks = [ps(st, f"pX{i}", [128, 512]) for i in range(8)]
        bki = [0]

        def nb_():
            i = bki[0] % 8
            bki[0] += 1
            return banks[i], ("pX", i)

        alt = [0]

        def ew(fn, r, w):
            alt[0] += 1
            S.op("dve" if alt[0] % 2 else "pool", fn, r=r, w=w)

        def bc(ap, shape):
            return ap.broadcast_to(shape)

        RW_STOP = os.environ.get("RW_STOP", "Z")

        GLOBAL_KEYS = ("GT", "PHIT", "PSI", "Yacc", "prm", "LW", "G2", "omk", "MK", "MP", "vm", "ident", "identf", "onesf", "RWS", "LOR", "EPI", "pX", "LWf", "G2f")

        def make_thread(tid):
            def mk(k):
                name = k[0] if isinstance(k, tuple) else k
                return k if name in GLOBAL_KEYS else (k, "t", tid)

            class _TS:
                @staticmethod
                def op(eng, fn, r=(), w=()):
                    return S.op(eng, fn, r=[mk(k) for k in r], w=[mk(k) for k in w])

                @staticmethod
                def dma(q, out, in_, r=(), w=()):
                    return S.dma(q, out, in_, r=[mk(k) for k in r], w=[mk(k) for k in w])
            TS = _TS
            alt = [tid]

            def ew(fn, r, w):
                alt[0] += 1
                TS.op("dve" if alt[0] % 3 == 0 else "pool", fn, r=r, w=w)
            RKs = sb(st, "RKs", [128, 256], F32)
            Vs = sb(st, "Vs", [64, 256], F32)
            TW = sb(st, "TW", [64, 2, 256], BF16)
            DAs = sb(st, "DAs", [64, 2, 256], BF16)
            SG0 = sb(st, "SG0", [128, 256], BF16)
            SG1 = sb(st, "SG1", [32, 256], BF16)
            RKtm = sb(st, "RKtm", [128, 2, 128], F32)
            Vtm = sb(st, "Vtm", [128, 2, 64], BF16)
            Vt32 = sb(st, "Vt32", [128, 2, 64], F32)
            SGM = sb(st, "SGM", [128, 2, 2, 64], F32)
            Aa = sb(st, "Aa", [128, 2, 2, 64], F32)
            EG = sb(st, "EG", [128, 2, 2, 64], F32)
            kkr = sb(st, "kkr", [128, 2, 64], F32)
            kk = sb(st, "kk", [128, 2, 64], F32)
            sq = sb(st, "sq", [128, 2, 64], F32)
            n2 = sb(st, "n2", [128, 8], F32)
            kd = sb(st, "kd", [128, 2, 2, 64], F32)
            be = sb(st, "be", [128, 2, 2, 64], F32)
            t1 = sb(st, "t1", [128, 2, 2, 64], F32)
            EX = [sb(st, f"EX{i}", [128, 2, 2, 64], F32) for i in range(5)]
            sc = [sb(st, f"sc{i}", [128, 2, 2, 64], BF16) for i in range(5)]
            TA = sb(st, "TA", [128, 4, 128], BF16)
            FT = sb(st, "FT", [64, 4, 4, 128], BF16)
            QM = sb(st, "QM", [128, 4, 2, 128], BF16)
            KM = sb(st, "KM", [128, 4, 2, 128], BF16)
            PP = [sb(st, f"PP{i}", [128, 4, 128], BF16) for i in range(2)]
            QQ = [sb(st, f"QQ{i}", [128, 4, 128], BF16) for i in range(2)]
            TT = [sb(st, f"TT{i}", [128, 4, 128], BF16) for i in range(2)]
            WU = sb(st, "WU", [128, 4, 128], BF16)
            tmpd = sb(st, "tmpd", [64, 4, 64], F32)

            def group(s_, h, g, base, L, Lp, nb):
                ng = min(2, nb - 2 * g)
                nq = 2 * ng
                ncol = 128 * ng
                cc0 = base + 256 * g
                TS.dma("sp", RKs[0:64, 0:ncol], RWS[64 * h:64 * h + 64, cc0:cc0 + ncol], r=[("RWS",)], w=["RKs"])
                TS.dma("act", RKs[64:128, 0:ncol], RWS[512 + 64 * h:512 + 64 * h + 64, cc0:cc0 + ncol], r=[("RWS",)], w=["RKs"])
                TS.dma("sp", Vs[:, 0:ncol], RWS[1024 + 64 * h:1024 + 64 * h + 64, cc0:cc0 + ncol], r=[("RWS",)], w=["Vs"])
                TS.dma("act", TW[:, :, 0:ncol], LOR[0:128, cc0:cc0 + ncol].rearrange("(d k) n -> k d n", d=2), r=[("LOR",)], w=["TW"])
                TS.dma("sp", DAs[:, :, 0:ncol], LOR[128:256, cc0:cc0 + ncol].rearrange("(d k) n -> k d n", d=2), r=[("LOR",)], w=["DAs"])
                TS.dma("act", SG0[:, 0:ncol], LOR[256:384, cc0:cc0 + ncol], r=[("LOR",)], w=["SG0"])
                TS.dma("sp", SG1[:, 0:ncol], LOR[384:416, cc0:cc0 + ncol], r=[("LOR",)], w=["SG1"])
                yield
                pb, pk = nb_()
                for j in range(ng):
                    TS.op("pe", lambda e, j=j: e.transpose(out=pb[:, j * 128:(j + 1) * 128], in_=RKs[:, j * 128:(j + 1) * 128], identity=identf[:]), r=["RKs", "identf"], w=[pk])
                TS.op("act", lambda e: e.copy(out=RKtm[:, 0:ng, :], in_=pb[:, 0:ncol].rearrange("p (j c) -> p j c", c=128)), r=[pk], w=["RKtm"])
                pb, pk = nb_()
                for j in range(ng):
                    TS.op("pe", lambda e, j=j: e.transpose(out=pb[:, j * 64:(j + 1) * 64], in_=Vs[:, j * 128:(j + 1) * 128], identity=identf[0:64, 0:64]), r=["Vs", "identf"], w=[pk])
                TS.op("act", lambda e: e.copy(out=Vtm[:, 0:ng, :], in_=pb[:, 0:64 * ng].rearrange("p (j c) -> p j c", c=64)), r=[pk], w=["Vtm"])
                TS.op("dve", lambda e: e.tensor_copy(out=Vt32[:, 0:ng, :], in_=pb[:, 0:64 * ng].rearrange("p (j c) -> p j c", c=64)), r=[pk], w=["Vt32"])
                if RW_STOP <= "A":
                    return
                yield
                pw, pwk = nb_()
                pa, pak = nb_()
                pg, pgk = nb_()
                for j in range(ng):
                    for d in range(1 if os.environ.get("RW_B2") == "d0only" else 2):
                        TS.op("pe", lambda e, j=j, d=d: e.matmul(pw[:, (j * 2 + d) * 64:(j * 2 + d + 1) * 64], lhsT=TW[:, d, j * 128:(j + 1) * 128], rhs=LW[:, d, 0, :], start=True, stop=True), r=["TW", "LW"], w=[pwk])
                for j in range(ng):
                    for d in range(1 if os.environ.get("RW_B2") == "d0only" else 2):
                        TS.op("pe", lambda e, j=j, d=d: e.matmul(pa[:, (j * 2 + d) * 64:(j * 2 + d + 1) * 64], lhsT=DAs[:, d, j * 128:(j + 1) * 128], rhs=LW[:, d, 1, :], start=True, stop=True), r=["DAs", "LW"], w=[pak])
                for j in range(0 if os.environ.get("RW_B2") == "nogate" else ng):
                    TS.op("pe", lambda e, j=j: e.matmul(pg[:, j * 64:(j + 1) * 64], lhsT=SG0[:, j * 128:(j + 1) * 128], rhs=G2[:, 0, :], start=True, stop=False), r=["SG0", "G2"], w=[pgk])
                    TS.op("pe", lambda e, j=j: e.matmul(pg[:, j * 64:(j + 1) * 64], lhsT=SG1[:, j * 128:(j + 1) * 128], rhs=G2[0:32, 1, :], start=False, stop=True), r=["SG1", "G2"], w=[pgk])
                v4 = [128, ng, 2, 64]
                if os.environ.get("RW_B") == "1":
                    return
                TS.op("dve", lambda e: e.tensor_tensor(out=SGM[:, 0:ng], in0=pw[:, 0:128 * ng].rearrange("p (j d c) -> p j d c", d=2, c=64), in1=bc(prm[:, 0:2, :].unsqueeze(1), v4), op=ALU.add), r=[pwk, "prm"], w=["SGM"])
                TS.op("act", lambda e: e.activation(out=SGM[:, 0:ng], in_=SGM[:, 0:ng], func=AF.Sigmoid), r=["SGM"], w=["SGM"])
                if 2 * g + ng == nb and L < Lp:
                    jl = ng - 1
                    TS.op("dve", lambda e: e.tensor_scalar(out=SGM[:, jl], in0=SGM[:, jl], scalar1=vm[:, s_:s_ + 1], scalar2=None, op0=ALU.mult), r=["SGM", "vm"], w=["SGM"])
                TS.op("dve", lambda e: e.tensor_tensor(out=Aa[:, 0:ng], in0=pa[:, 0:128 * ng].rearrange("p (j d c) -> p j d c", d=2, c=64), in1=bc(prm[:, 2:4, :].unsqueeze(1), v4), op=ALU.add), r=[pak, "prm"], w=["Aa"])
                TS.op("act", lambda e: e.activation(out=Aa[:, 0:ng], in_=Aa[:, 0:ng], func=AF.Sigmoid), r=["Aa"], w=["Aa"])
                TS.op("act", lambda e: e.copy(out=EG[:, 0:ng, 0, :], in_=pg[:, 0:64 * ng].rearrange("p (j c) -> p j c", c=64)), r=[pgk], w=["EG"])
                if RW_STOP <= "B":
                    return
                yield
                r_ = RKtm[:, 0:ng, 0:64]
                k_ = RKtm[:, 0:ng, 64:128]
                v3 = [128, ng, 64]
                ew(lambda e: e.tensor_tensor(out=kkr[:, 0:ng], in0=k_, in1=bc(prm[:, 4:5, :], v3), op=ALU.mult), ["RKtm", "prm"], ["kkr"])
                ew(lambda e: e.tensor_tensor(out=sq[:, 0:ng], in0=kkr[:, 0:ng], in1=kkr[:, 0:ng], op=ALU.mult), ["kkr"], ["sq"])
                TS.op("dve", lambda e: e.tensor_reduce(out=n2[:, 0:ng], in_=sq[:, 0:ng], axis=AX.X, op=ALU.add), r=["sq"], w=["n2"])
                TS.op("act", lambda e: e.activation(out=n2[:, 0:ng], in_=n2[:, 0:ng], func=AF.Sqrt), r=["n2"], w=["n2"])
                TS.op("dve", lambda e: e.tensor_scalar(out=n2[:, 0:ng], in0=n2[:, 0:ng], scalar1=1e-12, scalar2=None, op0=ALU.max), r=["n2"], w=["n2"])
                TS.op("dve", lambda e: e.reciprocal(out=n2[:, 0:ng], in_=n2[:, 0:ng]), r=["n2"], w=["n2"])
                ew(lambda e: e.tensor_tensor(out=kk[:, 0:ng], in0=kkr[:, 0:ng], in1=bc(n2[:, 0:ng].unsqueeze(2), v3), op=ALU.mult), ["kkr", "n2"], ["kk"])
                ew(lambda e: e.tensor_tensor(out=t1[:, 0:ng], in0=Aa[:, 0:ng], in1=bc(prm[:, 5:6, :].unsqueeze(1), v4), op=ALU.mult), ["Aa", "prm"], ["t1"])
                ew(lambda e: e.tensor_tensor(out=t1[:, 0:ng], in0=t1[:, 0:ng], in1=bc(omk[:, :].unsqueeze(1).unsqueeze(1), v4), op=ALU.add), ["t1", "omk"], ["t1"])
                ew(lambda e: e.tensor_tensor(out=kd[:, 0:ng], in0=t1[:, 0:ng], in1=bc(k_.unsqueeze(2), v4), op=ALU.mult), ["t1", "RKtm"], ["kd"])
                ew(lambda e: e.tensor_tensor(out=be[:, 0:ng], in0=Aa[:, 0:ng], in1=bc(kk[:, 0:ng].unsqueeze(2), v4), op=ALU.mult), ["Aa", "kk"], ["be"])
                ew(lambda e: e.tensor_tensor(out=sq[:, 0:ng], in0=kd[:, 0:ng, 0, :], in1=kd[:, 0:ng, 1, :], op=ALU.add), ["kd"], ["sq"])
                ew(lambda e: e.tensor_tensor(out=sq[:, 0:ng], in0=sq[:, 0:ng], in1=r_, op=ALU.mult), ["sq", "RKtm"], ["sq"])
                ew(lambda e: e.tensor_tensor(out=sq[:, 0:ng], in0=sq[:, 0:ng], in1=bc(prm[:, 8:9, :], v3), op=ALU.mult), ["sq", "prm"], ["sq"])
                TS.op("dve", lambda e: e.tensor_reduce(out=n2[:, 4:4 + ng], in_=sq[:, 0:ng], axis=AX.X, op=ALU.add), r=["sq"], w=["n2b"])
                ew(lambda e: e.tensor_tensor(out=sq[:, 0:ng], in0=Vt32[:, 0:ng], in1=bc(n2[:, 4:4 + ng].unsqueeze(2), v3), op=ALU.mult), ["Vt32", "n2b"], ["sq"])
                ew(lambda e: e.tensor_tensor(out=EG[:, 0:ng, 1, :], in0=sq[:, 0:ng], in1=EG[:, 0:ng, 0, :], op=ALU.mult), ["sq", "EG"], ["EG"])
                TS.dma("pool", EPI[cc0:cc0 + ncol, :, :].rearrange("(j p) a c -> p j a c", p=128), EG[:, 0:ng], r=["EG"], w=[("EPI", g)])
                if RW_STOP <= "C":
                    return
                yield
                pcs = []
                for kind, (mf, mb_) in enumerate(((0, 1), (2, 3), (3, 2), (None, None))):
                    pc, pck = nb_()
                    for d in range(2):
                        m = (mf, mb_)[d]
                        lhs = onesf[:, :] if m is None else MK[:, m, :]
                        TS.op("pe", lambda e, d=d, lhs=lhs: e.matmul(pc[:, d * 64 * ng:(d + 1) * 64 * ng], lhsT=lhs, rhs=SGM[:, 0:ng, d, :], start=True, stop=True), r=["SGM", "MK", "onesf"], w=[pck])
                    pcs.append((pc, pck))
                for i, (kind, sgn) in enumerate(((0, -1.0), (0, 1.0), (1, -1.0), (2, -1.0), (3, -1.0))):
                    pc, pck = pcs[kind]
                    TS.op("act", lambda e, i=i, pc=pc, sgn=sgn: e.activation(out=EX[i][:, :, 0:ng, :], in_=pc[:, 0:128 * ng].rearrange("p (d j c) -> p d j c", d=2, c=64), func=AF.Exp, scale=sgn * c_), r=[pck], w=[("EX", i)])
                Ep, Em, Ex_, Eh, PCb = [EX[i][:, :, 0:ng, :].rearrange("p d j c -> p j d c") for i in range(5)]
                ew(lambda e: e.tensor_tensor(out=sc[0][:, 0:ng], in0=Ep, in1=bc(r_.unsqueeze(2), v4), op=ALU.mult), [("EX", 0), "RKtm"], [("sc", 0)])
                ew(lambda e: e.tensor_tensor(out=sc[1][:, 0:ng], in0=kd[:, 0:ng], in1=Em, op=ALU.mult), [("EX", 1), "kd"], [("sc", 1)])
                ew(lambda e: e.tensor_tensor(out=sc[2][:, 0:ng], in0=be[:, 0:ng], in1=Em, op=ALU.mult), [("EX", 1), "be"], [("sc", 2)])
                ew(lambda e: e.tensor_scalar(out=kkr[:, 0:ng], in0=kk[:, 0:ng], scalar1=-1.0, scalar2=None, op0=ALU.mult), ["kk"], ["kkr"])
                ew(lambda e: e.tensor_tensor(out=TA[:, 0:nq, 0:64].rearrange("p (j d) c -> p j d c", d=2), in0=Ex_, in1=bc(kkr[:, 0:ng].unsqueeze(2), v4), op=ALU.mult), [("EX", 2), "kkr"], ["TAa"])
                ew(lambda e: e.tensor_tensor(out=sc[3][:, 0:ng], in0=be[:, 0:ng], in1=Eh, op=ALU.mult), [("EX", 3), "be"], [("sc", 3)])
                ew(lambda e: e.tensor_tensor(out=sc[4][:, 0:ng], in0=kd[:, 0:ng], in1=Eh, op=ALU.mult), [("EX", 3), "kd"], [("sc", 4)])
                if RW_STOP <= "D":
                    return
                yield
                srcs = [(lambda q: TA[:, q, 0:64], "TAa"), (lambda q: sc[0][:, q // 2, q % 2, :], ("sc", 0)), (lambda q: sc[2][:, q // 2, q % 2, :], ("sc", 2)), (lambda q: sc[1][:, q // 2, q % 2, :], ("sc", 1))]
                for a, (fsrc, skey) in enumerate(srcs):
                    pb, pk = nb_()
                    pbv = pb[:].bitcast(BF16)
                    for q in range(nq):
                        TS.op("pe", lambda e, q=q: e.transpose(out=pbv[0:64, q * 128:(q + 1) * 128], in_=fsrc(q), identity=ident[:]), r=[skey, "ident"], w=[pk])
                    if a % 2 == 0:
                        TS.op("act", lambda e: e.copy(out=FT[:, 0:nq, a, :], in_=pbv[0:64, 0:128 * nq].rearrange("p (q c) -> p q c", c=128)), r=[pk], w=[("FT", a)])
                    else:
                        TS.op("dve", lambda e: e.tensor_copy(out=FT[:, 0:nq, a, :], in_=pbv[0:64, 0:128 * nq].rearrange("p (q c) -> p q c", c=128)), r=[pk], w=[("FT", a)])
                FK = [("FT", a) for a in range(4)]
                if RW_STOP <= "E":
                    return
                yield
                for j in range(ng):
                    p1, p1k = nb_()
                    p2, p2k = nb_()
                    for d in range(2):
                        q = 2 * j + d
                        TS.op("pe", lambda e, q=q, d=d: e.matmul(p1[:, d * 256:(d + 1) * 256], lhsT=FT[:, q, 2, :], rhs=FT[:, q, 0:2, :], start=True, stop=True), r=FK, w=[p1k])
                        TS.op("pe", lambda e, q=q, d=d: e.matmul(p2[:, d * 256:(d + 1) * 256], lhsT=FT[:, q, 3, :], rhs=FT[:, q, 0:2, :], start=True, stop=True), r=FK, w=[p2k])
                    TS.op("dve", lambda e, j=j: e.tensor_tensor(out=QM[:, 2 * j:2 * j + 2], in0=p1[:, :].rearrange("p (d a c) -> p d a c", d=2, a=2), in1=MP1[:], op=ALU.mult), r=[p1k, "MP"], w=["QM"])
                    TS.op("dve", lambda e, j=j: e.tensor_tensor(out=KM[:, 2 * j:2 * j + 2], in0=p2[:, :].rearrange("p (d a c) -> p d a c", d=2, a=2), in1=MP1[:], op=ALU.mult), r=[p2k, "MP"], w=["KM"])
                for jj in range(0, ng, 2):
                    p3, p3k = nb_()
                    njj = min(2, ng - jj)
                    for j in range(jj, jj + njj):
                        for d in range(2):
                            q = 2 * j + d
                            TS.op("pe", lambda e, q=q, j=j, d=d: e.matmul(p3[:, ((j - jj) * 2 + d) * 128:((j - jj) * 2 + d + 1) * 128], lhsT=FT[:, q, 0, :], rhs=FT[:, q, 2, :], start=True, stop=True), r=FK, w=[p3k])
                    TS.op("dve", lambda e: e.tensor_tensor(out=PP[0][:, 2 * jj:2 * jj + 2 * njj].rearrange("p (j d) c -> p j d c", d=2), in0=p3[:, 0:256 * njj].rearrange("p (j d c) -> p j d c", d=2, c=128), in1=bc(MP3[:].unsqueeze(1), [128, njj, 2, 128]), op=ALU.mult), r=[p3k, "MP"], w=[("PP", 0)])
                if RW_STOP <= "F":
                    return
                yield
                TS.op("pool", lambda e: e.tensor_copy(out=QQ[0][:, 0:nq], in_=QM[:, 0:nq, 0, :]), r=["QM"], w=[("QQ", 0)])
                TS.op("dve", lambda e: e.tensor_tensor(out=TT[0][:, 0:nq], in0=QM[:, 0:nq, 0, :], in1=bc(ident[:, :].unsqueeze(1), [128, nq, 128]), op=ALU.add), r=["QM", "ident"], w=[("TT", 0)])
                cur = 0
                for lev in range(6):
                    nx = 1 - cur
                    for q0 in range(0, nq, 4):
                        nqq = min(4, nq - q0)
                        pp_, ppk = nb_()
                        pq_, pqk = nb_()
                        for q in range(q0, q0 + nqq):
                            TS.op("pe", lambda e, q=q: e.matmul(pp_[:, (q - q0) * 128:(q - q0 + 1) * 128], lhsT=QQ[cur][:, q, :], rhs=PP[cur][:, q, :], start=True, stop=True), r=[("QQ", cur), ("PP", cur)], w=[ppk])
                        for q in range(q0, q0 + nqq):
                            TS.op("pe", lambda e, q=q: e.matmul(pq_[:, (q - q0) * 128:(q - q0 + 1) * 128], lhsT=PP[cur][:, q, :], rhs=QQ[cur][:, q, :], start=True, stop=True), r=[("QQ", cur), ("PP", cur)], w=[pqk])
                        TS.op("act", lambda e: e.copy(out=PP[nx][:, q0:q0 + nqq], in_=pp_[:, 0:128 * nqq].rearrange("p (q c) -> p q c", c=128)), r=[ppk], w=[("PP", nx)])
                        TS.op("dve", lambda e: e.tensor_copy(out=QQ[nx][:, q0:q0 + nqq], in_=pq_[:, 0:128 * nqq].rearrange("p (q c) -> p q c", c=128)), r=[pqk], w=[("QQ", nx)])
                    yield
                    for q0 in range(0, nq, 4):
                        nqq = min(4, nq - q0)
                        pt_, ptk = nb_()
                        for q in range(q0, q0 + nqq):
                            TS.op("pe", lambda e, q=q: e.matmul(pt_[:, (q - q0) * 128:(q - q0 + 1) * 128], lhsT=PP[nx][:, q, :], rhs=TT[cur][:, q, :], start=True, stop=True), r=[("PP", nx), ("TT", cur)], w=[ptk])
                        TS.op("dve", lambda e: e.tensor_tensor(out=TT[nx][:, q0:q0 + nqq], in0=pt_[:, 0:128 * nqq].rearrange("p (q c) -> p q c", c=128), in1=TT[cur][:, q0:q0 + nqq], op=ALU.add), r=[ptk, ("TT", cur)], w=[("TT", nx)])
                    cur = nx
                    yield
                TTf = TT[cur]
                TTK = ("TT", cur)
                if RW_STOP <= "G":
                    return
                yield
                yield
                pb, pk = nb_()
                for q in range(nq):
                    TS.op("pe", lambda e, q=q: e.matmul(pb[:, q * 64:(q + 1) * 64], lhsT=KM[:, q, 0, :], rhs=Vtm[:, q // 2, :], start=True, stop=True), r=["KM", "Vtm"], w=[pk])
                TS.op("act", lambda e: e.copy(out=TA[:, 0:nq, 64:128], in_=pb[:, 0:64 * nq].rearrange("p (q c) -> p q c", c=64)), r=[pk], w=["TAx"])
                for q0 in range(0, nq, 4):
                    nqq = min(4, nq - q0)
                    pb, pk = nb_()
                    for q in range(q0, q0 + nqq):
                        TS.op("pe", lambda e, q=q: e.matmul(pb[:, (q - q0) * 128:(q - q0 + 1) * 128], lhsT=TTf[:, q, :], rhs=TA[:, q, :], start=True, stop=True), r=[TTK, "TAa", "TAx"], w=[pk])
                    TS.op("act", lambda e: e.copy(out=WU[:, q0:q0 + nqq], in_=pb[:, 0:128 * nqq].rearrange("p (q c) -> p q c", c=128)), r=[pk], w=["WU"])
                for q0 in range(0, nq, 4):
                    nqq = min(4, nq - q0)
                    pb, pk = nb_()
                    for q in range(q0, q0 + nqq):
                        TS.op("pe", lambda e, q=q: e.matmul(pb[0:64, (q - q0) * 128:(q - q0 + 1) * 128], lhsT=WU[:, q, 0:64], rhs=QM[:, q, 1, :], start=True, stop=True), r=["WU", "QM"], w=[pk])
                    TS.op("dve", lambda e: e.tensor_tensor(out=GT[:, 2 * g + q0 // 2:2 * g + (q0 + nqq) // 2].rearrange("p j d c -> p (j d) c"), in0=pb[0:64, 0:128 * nqq].rearrange("p (q c) -> p q c", c=128), in1=FT[:, q0:q0 + nqq, 1, :], op=ALU.add), r=[pk, ("FT", 1)], w=["GT"])
                yield
                pb, pk = nb_()
                for j in range(ng):
                    for d in range(2):
                        q = 2 * j + d
                        TS.op("pe", lambda e, q=q, j=j, d=d: e.matmul(pb[:, j * 64:(j + 1) * 64], lhsT=QM[:, q, 1, :], rhs=WU[:, q, 64:128], start=(d == 0), stop=False), r=["QM", "WU"], w=[pk])
                        TS.op("pe", lambda e, q=q, j=j, d=d: e.matmul(pb[:, j * 64:(j + 1) * 64], lhsT=KM[:, q, 1, :], rhs=Vtm[:, j, :], start=False, stop=(d == 1)), r=["KM", "Vtm"], w=[pk])
                TS.op("act", lambda e: e.copy(out=Yacc[:, 2 * g:2 * g + ng, :], in_=pb[:, 0:64 * ng].rearrange("p (j c) -> p j c", c=64)), r=[pk], w=["Yacc"])
                yield
                pb, pk = nb_()
                for q in range(nq):
                    TS.op("pe", lambda e, q=q: e.matmul(pb[0:64, q * 64:(q + 1) * 64], lhsT=WU[:, q, 0:64], rhs=sc[3][:, q // 2, q % 2, :], start=True, stop=True), r=["WU", ("sc", 3)], w=[pk])
                TS.op("pool", lambda e: e.tensor_tensor(out=tmpd[:, 0:nq].rearrange("p (j d) c -> p j d c", d=2), in0=PCb[0:64], in1=bc(identf[0:64, 0:64].unsqueeze(1).unsqueeze(1), [64, ng, 2, 64]), op=ALU.mult), r=[("EX", 4), "identf"], w=["tmpd"])
                TS.op("dve", lambda e: e.tensor_tensor(out=PHIT[:, 2 * g:2 * g + ng].rearrange("p j d c -> p (j d) c"), in0=pb[0:64, 0:64 * nq].rearrange("p (q c) -> p q c", c=64), in1=tmpd[:, 0:nq], op=ALU.add), r=[pk, "tmpd"], w=["PHIT"])
                yield
                pb, pk = nb_()
                for q in range(nq):
                    TS.op("pe", lambda e, q=q: e.matmul(pb[0:64, q * 64:(q + 1) * 64], lhsT=sc[3][:, q // 2, q % 2, :], rhs=WU[:, q, 64:128], start=True, stop=False), r=["WU", ("sc", 3)], w=[pk])
                    TS.op("pe", lambda e, q=q: e.matmul(pb[0:64, q * 64:(q + 1) * 64], lhsT=sc[4][:, q // 2, q % 2, :], rhs=Vtm[:, q // 2, :], start=False, stop=True), r=["Vtm", ("sc", 4)], w=[pk])
                TS.op("act", lambda e: e.copy(out=PSI[:, 2 * g:2 * g + ng].rearrange("p j d c -> p (j d) c"), in_=pb[0:64, 0:64 * nq].rearrange("p (q c) -> p q c", c=64)), r=[pk], w=["PSI"])

            return group

        threads = [make_thread(0), make_thread(1)]
        for s_ in range(2):
            base, L, Lp, nb = cfg.BASE[s_], cfg.LS[s_], cfg.LP[s_], cfg.NB[s_]
            ngrp = -(-nb // 4)
            for h in range(NH):
                S.dma("sp", prm[:], rwp[:, l, h, :, :], w=["prm"])
                S.dma("act", LWf[:], lw_in[l, h].rearrange("(d k) a c -> k d a c", d=2), w=["LWf"])
                S.dma("sp", G2f[:, 0, :], g2_in[l, h, 0:128, :], w=["G2f"])
                S.dma("act", G2f[0:32, 1, :], g2_in[l, h, 128:160, :], w=["G2f"])
                S.op("dve", lambda e: e.tensor_copy(out=LW[:], in_=LWf[:]), r=["LWf"], w=["LW"])
                S.op("dve", lambda e: e.tensor_copy(out=G2[:, 0, :], in_=G2f[:, 0, :]), r=["G2f"], w=["G2"])
                S.op("dve", lambda e: e.tensor_copy(out=G2[0:32, 1, :], in_=G2f[0:32, 1, :]), r=["G2f"], w=["G2"])
                S.op("dve", lambda e: e.tensor_scalar(out=omk[:], in0=prm[:, 5, :], scalar1=-1.0, scalar2=1.0, op0=ALU.mult, op1=ALU.add), r=["prm"], w=["omk"])
                gens = [threads[g % 2](s_, h, g, base, L, Lp, nb) for g in range(-(-nb // 2))]
                for gi in range(0, len(gens), 2):
                    pair = gens[gi:gi + 2]
                    live = list(pair)
                    while live:
                        for gn in list(live):
                            try:
                                next(gn)
                            except StopIteration:
                                live.remove(gn)
                if RW_STOP <= "H":
                    continue
                S.op("pool", lambda e: e.memset(Sb[0][:], 0.0), w=[("Sb", 0)])
                cur = 0
                for i in range(nb):
                    nx = 1 - cur
                    for d, c in ((0, i), (1, nb - 1 - i)):
                        py, pyk = nb_()
                        S.op("pe", lambda e, d=d, c=c: e.matmul(py[:, 0:64], lhsT=GT[:, c, d, :], rhs=Sb[cur][:, d, :], start=True, stop=True), r=["GT", ("Sb", cur)], w=[pyk])
                        S.op("pe", lambda e, d=d, c=c: e.matmul(py[0:64, 64:128], lhsT=PHIT[:, c, d, :], rhs=Sb[cur][:, d, :], start=True, stop=True), r=["PHIT", ("Sb", cur)], w=[pyk])
                        S.op("dve", lambda e, d=d, c=c: e.tensor_tensor(out=Sb[nx][:, d, :], in0=py[0:64, 64:128], in1=PSI[:, c, d, :], op=ALU.add), r=[pyk, "PSI"], w=[("Sb", nx)])
                        S.op("dve", lambda e, d=d, c=c: e.tensor_tensor(out=Yacc[:, c, :], in0=py[:, 0:64], in1=Yacc[:, c, :], op=ALU.add), r=[pyk, "Yacc"], w=["Yacc"])
                    cur = nx
                if RW_STOP <= "I":
                    continue
                for g in range(ngrp):
                    ng = min(4, nb - 4 * g)
                    ncol = 128 * ng
                    cc0 = base + 512 * g
                    v3 = [128, ng, 64]
                    S.dma("sp", egl[:, 0:ng], EPI[cc0:cc0 + ncol, :, :].rearrange("(j p) a c -> p j a c", p=128), r=[("EPI", 2 * g), ("EPI", 2 * g + 1)], w=["egl"])
                    y_ = Yacc[:, 4 * g:4 * g + ng, :]
                    S.op("dve", lambda e: e.tensor_reduce(out=es1[:, 0:ng], in_=y_, axis=AX.X, op=ALU.add), r=["Yacc"], w=["es1"])
                    S.op("dve", lambda e: e.tensor_scalar(out=es1[:, 0:ng], in0=es1[:, 0:ng], scalar1=-1.0 / 64, scalar2=None, op0=ALU.mult), r=["es1"], w=["es1"])
                    ew(lambda e: e.tensor_tensor(out=ey[:, 0:ng], in0=y_, in1=bc(es1[:, 0:ng].unsqueeze(2), v3), op=ALU.add), ["Yacc", "es1"], ["ey"])
                    ew(lambda e: e.tensor_tensor(out=eo[:, 0:ng], in0=ey[:, 0:ng], in1=ey[:, 0:ng], op=ALU.mult), ["ey"], ["eo"])
                    S.op("dve", lambda e: e.tensor_reduce(out=es1[:, 4:4 + ng], in_=eo[:, 0:ng], axis=AX.X, op=ALU.add), r=["eo"], w=["es2"])
                    S.op("dve", lambda e: e.tensor_scalar(out=es1[:, 4:4 + ng], in0=es1[:, 4:4 + ng], scalar1=1.0 / 64, scalar2=LNX_EPS, op0=ALU.mult, op1=ALU.add), r=["es2"], w=["es2"])
                    S.op("act", lambda e: e.activation(out=es1[:, 4:4 + ng], in_=es1[:, 4:4 + ng], func=AF.Sqrt), r=["es2"], w=["es2"])
                    S.op("dve", lambda e: e.reciprocal(out=es1[:, 4:4 + ng], in_=es1[:, 4:4 + ng]), r=["es2"], w=["es2"])
                    ew(lambda e: e.tensor_tensor(out=ey[:, 0:ng], in0=ey[:, 0:ng], in1=bc(es1[:, 4:4 + ng].unsqueeze(2), v3), op=ALU.mult), ["ey", "es2"], ["ey"])
                    ew(lambda e: e.tensor_tensor(out=ey[:, 0:ng], in0=ey[:, 0:ng], in1=bc(prm[:, 6:7, :], v3), op=ALU.mult), ["ey", "prm"], ["ey"])
                    ew(lambda e: e.tensor_tensor(out=ey[:, 0:ng], in0=ey[:, 0:ng], in1=bc(prm[:, 7:8, :], v3), op=ALU.add), ["ey", "prm"], ["ey"])
                    ew(lambda e: e.tensor_tensor(out=ey[:, 0:ng], in0=ey[:, 0:ng], in1=egl[:, 0:ng, 0, :], op=ALU.mult), ["ey", "egl"], ["ey"])
                    ew(lambda e: e.tensor_tensor(out=eo[:, 0:ng], in0=ey[:, 0:ng], in1=egl[:, 0:ng, 1, :], op=ALU.add), ["ey", "egl"], ["eo"])
                    pb, pk = nb_()
                    for j in range(ng):
                        S.op("pe", lambda e, j=j: e.transpose(out=pb[0:64, j * 128:(j + 1) * 128], in_=eo[:, j, :], identity=identf[:]), r=["eo", "identf"], w=[pk])
                    S.op("act", lambda e: e.copy(out=eob[:, 0:ncol], in_=pb[0:64, 0:ncol]), r=[pk], w=["eob"])
                    S.dma("pool", MIXT[512 + 64 * h:512 + 64 * h + 64, cc0:cc0 + ncol], eob[:, 0:ncol], r=["eob"], w=[("MIXT", h, g)])
        S.barrier()
        st.close()

    for l in range(NL):
        ffn_pass(l, 0, 0, True)
        ffn_pass(l, 0, 1, False)
        if cfg.stages != "ffn":
            proj_pass(l)
            if cfg.stages != "proj":
                shift_phase(l)
                if cfg.stages != "shift":
                    if cfg.stages != "rwkv":
                        attn_phase(l)
                    if cfg.stages != "attn":
                        rwkv_phase(l)
        ffn_pass(l, 1, 0, True)
        ffn_pass(l, 1, 1, False, final=(l == NL - 1))

    S.mark("end")
    S.finish()
    cst.close()
    es.close()
    return nc, S


def host_maps(cfg, inp, xs_list):
    NL = cfg.NL
    f32 = np.float32
    common = {}
    common["meta"] = np.ascontiguousarray(inp["meta_tokens"], f32)
    for k, nm in ((1, "ffn1"), (2, "ffn2")):
        common[f"ffn{k}_wg"] = np.ascontiguousarray(inp[f"{nm}_w_gate"][:NL], f32)
        common[f"ffn{k}_wu"] = np.ascontiguousarray(inp[f"{nm}_w_up"][:NL], f32)
        common[f"ffn{k}_wd"] = np.ascontiguousarray(inp[f"{nm}_w_down"][:NL], f32)
    g = np.stack([np.asarray(inp["ffn1_norm"][:NL], f32), np.asarray(inp["mix_norm"][:NL], f32), np.asarray(inp["ffn2_norm"][:NL], f32)], axis=1)
    common["gains"] = np.ascontiguousarray(g.reshape(NL, 3, 8, 128).transpose(3, 0, 1, 2))
    common["fnorm"] = np.ascontiguousarray(np.broadcast_to(np.asarray(inp["final_norm"], f32)[None, :], (128, D)))
    common["ident_bf"] = np.eye(128, dtype=f32).astype(ml_dtypes.bfloat16)
    common["zeros"] = np.zeros((128, D), f32)
    bf = ml_dtypes.bfloat16
    ih = [(i, h) for i in range(16) for h in range(NH)]
    perm = list(range(0, 384)) + [384 + i for (i, h) in ih] + [400 + i for (i, h) in ih] + list(range(416, 2368))
    assert len(perm) == NCOLP
    common["w_in"] = np.ascontiguousarray(np.asarray(inp["w_in"][:NL], f32)[:, :, perm])
    pq = [96 * h + j for h in range(NH) for j in range(64)] + [96 * h + 64 + i for (i, h) in ih] + [96 * h + 80 + i for (i, h) in ih]
    common["w_uq"] = np.ascontiguousarray(np.asarray(inp["w_uq"][:NL], f32)[:, :, pq])
    pkv = [128 * h + j for h in range(NH) for j in range(64)] + [128 * h + 64 + j for h in range(NH) for j in range(64)]
    common["w_ukv"] = np.ascontiguousarray(np.asarray(inp["w_ukv"][:NL], f32)[:, :, pkv])
    common["w_out"] = np.ascontiguousarray(inp["w_out"][:NL], f32)
    qn = np.asarray(inp["q_norm"][:NL], f32)
    kvn = np.asarray(inp["kv_norm"][:NL], f32)
    common["qkg"] = np.ascontiguousarray(np.stack([qn[:, 0:128], qn[:, 128:256], kvn], axis=-1).transpose(1, 0, 2))
    seln = np.zeros((128, 4, 32), f32)
    for row in range(128):
        for c in range(4):
            seln[row, c, 2 * c + row // 64] = 1
    selr = np.zeros((128, 32), f32)
    for row in range(128):
        selr[row, row % 8] = 1
    common["seln"] = seln.astype(bf)
    common["selr"] = selr.astype(bf)
    common["ones_bf"] = np.ones((128, 512), f32).astype(bf)
    common["ones_f"] = np.ones((128, 128), f32)
    common["ones_row"] = np.ones((NH, 512), f32).astype(bf)
    common["ident_f"] = np.eye(128, dtype=f32)
    idx = np.arange(128)
    rowi, coli = idx[:, None], idx[None, :]
    common["masks"] = np.ascontiguousarray(np.stack([rowi <= coli, rowi >= coli, rowi < coli, rowi > coli], axis=1).astype(f32))
    inv = (1.0 / (np.float32(10000.0) ** (np.arange(0, 32, 2, dtype=f32) / np.float32(32)))).astype(f32)
    pos = np.concatenate([np.arange(lp, dtype=f32) for lp in cfg.LP])
    ang = (pos[:, None] * inv[None, :]).astype(f32)
    common["cosT"] = np.ascontiguousarray(np.repeat(np.cos(ang).T.astype(f32), NH, axis=0))
    common["sinT"] = np.ascontiguousarray(np.repeat(np.sin(ang).T.astype(f32), NH, axis=0))
    smu = np.asarray(inp["shift_mu"][:NL], f32)
    mu = np.zeros((128, NL, 16, 2), f32)
    roff0 = GROUPS["r0"][0]
    for gi, name in enumerate(RW_GROUPS):
        off, wdt = GROUPS[name]
        mu[0:wdt, :, gi, :] = smu[:, :, off - roff0:off - roff0 + wdt].transpose(2, 0, 1)
    common["mu"] = mu
    common.update(rwkv_host(cfg, inp))
    maps = []
    for c in range(len(xs_list)):
        m = dict(common)
        m["x0"] = np.ascontiguousarray(xs_list[c][0], f32)
        m["x1"] = np.ascontiguousarray(xs_list[c][1], f32)
        maps.append(m)
    return maps


def rwkv_host(cfg, inp):
    NL = cfg.NL
    f32 = np.float32
    out = {}
    rwp = np.zeros((128, NL, NH, 9, 64), f32)
    lw = np.zeros((NL, NH, 128, 2, 64), f32)
    g2 = np.zeros((NL, NH, 160, 64), f32)
    for l in range(NL):
        for h in range(NH):
            hs = slice(64 * h, 64 * h + 64)
            rows = [inp["decay_w0"][l, 0, hs], inp["decay_w0"][l, 1, hs], inp["iclr_a0"][l, 0, hs], inp["iclr_a0"][l, 1, hs],
                    inp["key_k_k"][l, hs], inp["key_k_a"][l, hs], inp["lnx_w"][l, hs], inp["lnx_b"][l, hs], inp["bonus_r_k"][l, h, :]]
            rwp[:, l, h, :, :] = np.stack([np.asarray(r_, f32) for r_ in rows], 0)[None]
            for d in range(2):
                lw[l, h, 64 * d:64 * d + 64, 0, :] = inp["decay_w2"][l, d, :, hs]
                lw[l, h, 64 * d:64 * d + 64, 1, :] = inp["iclr_a2"][l, d, :, hs]
            g2[l, h] = inp["gate_g2"][l, :, hs]
    out["rwp"] = rwp
    out["lw"] = lw
    out["g2"] = g2
    vm = np.zeros((128, 2), f32)
    for s_ in range(2):
        nvalid = cfg.LS[s_] - 128 * (cfg.NB[s_] - 1)
        vm[:nvalid, s_] = 1
    out["vmask"] = vm
    return out


_CACHE = {}


def kernel(**inp):
    cfg = Cfg()
    if "nc" not in _CACHE:
        _CACHE["nc"] = build(cfg)[0]
    nc = _CACHE["nc"]
    xp = np.asarray(inp["x_prompt"])
    xsm = np.asarray(inp["x_sample"])
    xs_list = [(xsm[c], xp[c % 2]) for c in range(8)]
    maps = host_maps(cfg, inp, xs_list)
    res = run_bass_kernel_spmd(nc, maps, core_ids=list(range(8)))
    y_s = np.stack([np.asarray(res.results[c]["y0"], np.float32) for c in range(8)], axis=0)
    y_p = np.stack([np.asarray(res.results[c]["y1"], np.float32) for c in range(2)], axis=0)
    return (y_p, y_s)
```

```python
import os
import numpy as np
import ml_dtypes
from contextlib import ExitStack
import concourse.bass as bass
import concourse.mybir as mybir
from concourse.bass_utils import run_bass_kernel_spmd

F32 = mybir.dt.float32
BF16 = mybir.dt.bfloat16
AF = mybir.ActivationFunctionType
ALU = mybir.AluOpType
AX = mybir.AxisListType

D = 1024
DFF = 2816
NH = 8
NMETA = 16
RMS_EPS = 1e-6
LNX_EPS = 64e-5
SCALE = 96 ** -0.5
CDEC = float(np.exp(-0.5))
NCOLP = 2592

GROUPS = {}
_o = 0
for _n, _w in ([("cq0", 128), ("cq1", 128), ("ckv", 128), ("kr1", 128), ("kr2", 128)]
               + [(f"r{i}", 128) for i in range(4)] + [(f"k{i}", 128) for i in range(4)]
               + [(f"v{i}", 128) for i in range(4)] + [("dw", 128), ("da", 128), ("dg0", 128), ("dg1", 32)]):
    GROUPS[_n] = (_o, _w)
    _o += _w
assert _o == NCOLP
RW_GROUPS = [f"r{i}" for i in range(4)] + [f"k{i}" for i in range(4)] + [f"v{i}" for i in range(4)] + ["dw", "da", "dg0", "dg1"]


class Sched:
    ENG = ("pe", "act", "dve", "pool", "sp")

    def __init__(self, nc, es, n_dsem=12):
        self.nc = nc
        self.e = dict(pe=nc.tensor, act=nc.scalar, dve=nc.vector, pool=nc.gpsimd, sp=nc.sync)
        self.semobj = {}
        self.cnt = {}
        for k in self.ENG:
            self.semobj[("e", k)] = es.enter_context(nc.semaphore("s_" + k))
            self.cnt[k] = 0
        self.dq = {}
        self.dqi = {}
        for q in ("sp", "act", "pool"):
            self.dq[q] = []
            for i in range(n_dsem):
                self.semobj[("d", q, i)] = es.enter_context(nc.semaphore(f"d_{q}{i}"))
                self.dq[q].append(0)
            self.dqi[q] = 0
        self.seen = {k: {} for k in self.ENG}
        self.lastw = {}
        self.lastr = {}
        self.n_ins = 0

    def _wait(self, eng, sk, val):
        if val <= 0 or self.seen[eng].get(sk, 0) >= val:
            return
        if sk == ("e", "pe") and eng == "pe":
            return
        self.e[eng].wait_ge(self.semobj[sk], val)
        self.seen[eng][sk] = val

    def _deps(self, eng, r, w):
        for res in r:
            lw = self.lastw.get(res)
            if lw:
                self._wait(eng, *lw)
        for res in w:
            lw = self.lastw.get(res)
            if lw:
                self._wait(eng, *lw)
            for sk, v in self.lastr.get(res, {}).items():
                self._wait(eng, sk, v)

    def _mark(self, sk, v, r, w):
        for res in r:
            self.lastr.setdefault(res, {})[sk] = v
        for res in w:
            self.lastw[res] = (sk, v)
            self.lastr[res] = {}

    PSUM_NAMES = ("pT", "pG", "pU", "pD", "pA", "pN", "pS", "pO", "pB", "pX", "pY", "pZ")

    def op(self, eng, fn, r=(), w=()):
        extra = [k for k in r if (k[0] if isinstance(k, tuple) else k) in self.PSUM_NAMES]
        if extra:
            w = list(w) + extra
        self._deps(eng, r, w)
        ins = fn(self.e[eng])
        self.cnt[eng] += 1
        ins.then_inc(self.semobj[("e", eng)], 1)
        self._mark(("e", eng), self.cnt[eng], r, w)
        self.n_ins += 1
        return ins

    def dma(self, q, out, in_, r=(), w=()):
        self._deps(q, r, w)
        i = self.dqi[q]
        self.dqi[q] = (i + 1) % len(self.dq[q])
        sk = ("d", q, i)
        self._wait(q, sk, self.dq[q][i])
        self.dq[q][i] += 16
        self.e[q].dma_start(out=out, in_=in_).then_inc(self.semobj[sk], 16)
        self._mark(sk, self.dq[q][i], r, w)
        self.n_ins += 1

    def mark(self, label):
        if not hasattr(self, "marks"):
            self.marks = []
        self.marks.append((label, dict(self.cnt)))

    def barrier(self):
        for eng in self.ENG:
            for k in self.ENG:
                self._wait(eng, ("e", k), self.cnt[k])
            for q in self.dq:
                for i, v in enumerate(self.dq[q]):
                    self._wait(eng, ("d", q, i), v)
        self.lastw = {}
        self.lastr = {}

    def finish(self):
        for k in self.ENG:
            self._wait("sp", ("e", k), self.cnt[k])
        for q in self.dq:
            for i, v in enumerate(self.dq[q]):
                self._wait("sp", ("d", q, i), v)


class Cfg:
    def __init__(self, LS=(2064, 8208), NL=2, debug=False, stages="all"):
        self.LS = list(LS)
        self.NL = NL
        self.debug = debug
        self.stages = stages
        self.NB = [-(-L // 128) for L in self.LS]
        self.LP = [nb * 128 for nb in self.NB]
        self.BASE = [0]
        for lp in self.LP[:-1]:
            self.BASE.append(self.BASE[-1] + lp)
        self.TP = sum(self.LP)
        self.NBT = self.TP // 128
        self.tiles = []
        b = 0
        while b < self.NBT:
            n = min(4, self.NBT - b)
            self.tiles.append((b, n))
            b += n


def build(cfg):
    nc = bass.Bass("TRN2", target_bir_lowering=False)
    NL, TP = cfg.NL, cfg.TP

    def din(name, shape, dt=F32):
        return nc.dram_tensor(name, list(shape), dt, kind="ExternalInput").ap()

    def dscr(name, shape, dt=F32):
        if cfg.debug:
            return nc.dram_tensor(name, list(shape), dt, kind="ExternalOutput").ap()
        return nc.dram_tensor(name, list(shape), dt).ap()

    xin = [din(f"x{s}", [cfg.LS[s] - NMETA, D]) for s in range(2)]
    yout = [nc.dram_tensor(f"y{s}", [cfg.LS[s] - NMETA, D], F32, kind="ExternalOutput").ap() for s in range(2)]
    meta = din("meta", [NMETA, D])
    wg = [din(f"ffn{k}_wg", [NL, D, DFF]) for k in (1, 2)]
    wu = [din(f"ffn{k}_wu", [NL, D, DFF]) for k in (1, 2)]
    wd = [din(f"ffn{k}_wd", [NL, DFF, D]) for k in (1, 2)]
    gains = din("gains", [128, NL, 3, 8])
    fnorm = din("fnorm", [128, D])
    ident_bf = din("ident_bf", [128, 128], BF16)
    zeros = din("zeros", [128, D])

    w_in = din("w_in", [NL, D, NCOLP])
    w_uq = din("w_uq", [NL, 256, 768])
    w_ukv = din("w_ukv", [NL, 128, 1024])
    w_out = din("w_out", [NL, D, D])
    qkg = din("qkg", [128, NL, 3])
    cosT = din("cosT", [128, TP])
    sinT = din("sinT", [128, TP])
    seln_in = din("seln", [128, 4, 32], BF16)
    selr_in = din("selr", [128, 32], BF16)
    ones_in = din("ones_bf", [128, 512], BF16)
    onesf_in = din("ones_f", [128, 128])
    onesrow_in = din("ones_row", [NH, 512], BF16)
    identf_in = din("ident_f", [128, 128])
    masks_in = din("masks", [128, 4, 128])
    mu_in = din("mu", [128, NL, 16, 2])
    rwp = din("rwp", [128, NL, NH, 9, 64])
    vmask_in = din("vmask", [128, 2])
    lw_in = din("lw", [NL, NH, 128, 2, 64])
    g2_in = din("g2", [NL, NH, 160, 64])

    H = dscr("H", [TP, D])
    XNT = dscr("XNT", [D, TP], BF16)
    QTd = dscr("QTd", [97, NH, TP], BF16)
    KTd = dscr("KTd", [97, NH, TP], BF16)
    VA = dscr("VA", [TP, NH, 65], BF16)
    QNd = dscr("QNd", [NH, TP])
    KNd = dscr("KNd", [NH, TP])
    RAW = dscr("RAW", [1952, TP])
    RWS = dscr("RWS", [1536, TP])
    LOR = dscr("LOR", [416, TP], BF16)
    MIXT = dscr("MIXT", [D, TP], BF16)
    EPI = dscr("EPI", [TP, 2, 64])

    es = ExitStack()
    S = Sched(nc, es)

    uid = [0]

    def sb(st, name, shape, dt):
        uid[0] += 1
        return st.enter_context(nc.sbuf_tensor(f"{name}_{uid[0]}", list(shape), dt))

    def ps(st, name, shape, dt=F32):
        uid[0] += 1
        return st.enter_context(nc.psum_tensor(f"{name}_{uid[0]}", list(shape), dt))

    cst = ExitStack()
    ident = sb(cst, "ident", [128, 128], BF16)
    gn = sb(cst, "gn", [128, NL, 3, 8], F32)
    fn_sb = sb(cst, "fn_sb", [128, D], F32)
    S.dma("sp", ident[:], ident_bf, w=["ident"])
    S.dma("sp", gn[:], gains, w=["gn"])
    S.dma("sp", fn_sb[:], fnorm, w=["fn"])

    for s in range(2):
        b0 = cfg.BASE[s]
        L = cfg.LS[s]
        S.dma("sp", H[b0:b0 + NMETA, :], meta, w=[("H", "init", s)])
        nrow = L - NMETA
        r0 = 0
        while r0 < nrow:
            n = min(2048, nrow - r0)
            S.dma("act" if (r0 // 2048) % 2 else "sp", H[b0 + NMETA + r0:b0 + NMETA + r0 + n, :], xin[s][r0:r0 + n, :], w=[("H", "init", s, r0)])
            r0 += n
        if cfg.LP[s] > L:
            S.dma("pool", H[b0 + L:b0 + cfg.LP[s], :], zeros[0:cfg.LP[s] - L, :], w=[("H", "initz", s)])
    S.barrier()

    def ffn_pass(l, k, half, with_norm, final=False):
        S.mark(f"ffn{k}h{half}_L{l}")
        st = ExitStack()
        NF = 11
        Wg = sb(st, "Wg", [128, 8, NF * 128], BF16)
        Wu = sb(st, "Wu", [128, 8, NF * 128], BF16)
        Wd = sb(st, "Wd", [128, NF, D], BF16)
        stg = [sb(st, f"stg{i}", [128, NF * 128], F32) for i in range(2)]
        hb = [sb(st, f"hb{i}", [128, D], F32) for i in range(2)]
        hc = [sb(st, f"hc{i}", [128, D], F32) for i in range(2)]
        xnbs = [sb(st, f"xnb{i}", [128, 4, D], BF16) for i in range(2)]
        xnT = [sb(st, f"xnT{i}", [128, 8, 512], BF16) for i in range(2)]
        hT = [sb(st, f"hT{i}", [128, NF, 512], BF16) for i in range(2)]
        sg = [sb(st, f"sg{i}", [128, 512], F32) for i in range(2)]
        ssq = sb(st, "ssq", [128, 12], F32)
        rs = sb(st, "rs", [128, 12], F32)
        junk = sb(st, "junk", [128, D], BF16)
        yb = [sb(st, f"yb{i}", [128, D], F32) for i in range(2)]
        pT = [ps(st, f"pT{i}", [128, 1024], BF16) for i in range(2)]
        pG = [ps(st, f"pG{i}", [128, 512]) for i in range(2)]
        pU = [ps(st, f"pU{i}", [128, 512]) for i in range(2)]
        pD = [ps(st, f"pD{i}", [128, 512]) for i in range(2)]
        f0 = half * NF * 128
        do_mix = (k == 1 and half == 0 and cfg.stages != "ffn")
        if do_mix:
            Wout = sb(st, "Wout", [128, 8, D], BF16)
            mx = [sb(st, f"mx{i}", [128, 8, 512], BF16) for i in range(2)]
            for fc in range(8):
                load_cast(Wout[:, fc, :], w_out[l, fc * 128:(fc + 1) * 128, :], stg[fc % 2][:, 0:D], ("stg", fc % 2), "sp" if fc % 2 == 0 else "act", ["dve", "pool"][fc % 2])
        ci = 0
        cast_eng = ["dve", "pool", "act"]
        for (Wsb, wsrc) in ((Wg, wg[k]), (Wu, wu[k])):
            for dc in range(8):
                sgi = ci % 2
                S.dma("sp" if ci % 2 == 0 else "act", stg[sgi][:], wsrc[l, dc * 128:(dc + 1) * 128, f0:f0 + NF * 128], w=[("stg", sgi)])
                eng = cast_eng[ci % 3]
                if eng == "act":
                    S.op("act", lambda e, o=Wsb[:, dc, :], i=stg[sgi][:]: e.copy(out=o, in_=i), r=[("stg", sgi)], w=[("W",)])
                else:
                    S.op(eng, lambda e, o=Wsb[:, dc, :], i=stg[sgi][:]: e.tensor_copy(out=o, in_=i), r=[("stg", sgi)], w=[("W",)])
                ci += 1
        for fc in range(NF):
            sgi = ci % 2
            S.dma("sp" if ci % 2 == 0 else "act", stg[sgi][:, 0:D], wd[k][l, f0 + fc * 128:f0 + (fc + 1) * 128, :], w=[("stg", sgi)])
            eng = cast_eng[ci % 3]
            if eng == "act":
                S.op("act", lambda e, o=Wd[:, fc, :], i=stg[sgi][:, 0:D]: e.copy(out=o, in_=i), r=[("stg", sgi)], w=[("W",)])
            else:
                S.op(eng, lambda e, o=Wd[:, fc, :], i=stg[sgi][:, 0:D]: e.tensor_copy(out=o, in_=i), r=[("stg", sgi)], w=[("W",)])
            ci += 1
        gidx = 0 if k == 0 else 2

        def stageA(ti):
            b0, nblk = cfg.tiles[ti]
            nt = nblk * 128
            xt = xnT[ti % 2]
            XK = ("xnT", ti % 2)
            if not with_norm:
                S.dma("sp", xt[:, :, 0:nt], XNT.rearrange("(c p) n -> p c n", p=128)[:, :, b0 * 128:b0 * 128 + nt], r=[("XNT", ti)], w=[XK])
                return
            if do_mix:
                mt = mx[ti % 2]
                MXK = ("mx", ti % 2)
                S.dma("pool", mt[:, :, 0:nt], MIXT.rearrange("(c p) n -> p c n", p=128)[:, :, b0 * 128:b0 * 128 + nt], r=[("MIXT",)], w=[MXK])
            for b in range(nblk):
                h = hb[b % 2]
                HK = ("hb", b % 2)
                S.dma("sp" if b % 2 == 0 else "act", h[:], H[(b0 + b) * 128:(b0 + b + 1) * 128, :], r=[("H", b0 + b)], w=[HK])
                if do_mix:
                    for hf in range(2):
                        for fc in range(8):
                            S.op("pe", lambda e, fc=fc, hf=hf, b=b: e.matmul(pD[hf][:, :], lhsT=mt[:, fc, b * 128:(b + 1) * 128], rhs=Wout[:, fc, hf * 512:(hf + 1) * 512], start=(fc == 0), stop=(fc == 7)),
                                 r=[MXK, "W"], w=[("pD", hf)])
                        S.op("dve", lambda e, hf=hf, h=h: e.tensor_tensor(out=h[:, hf * 512:(hf + 1) * 512], in0=pD[hf][:, :], in1=h[:, hf * 512:(hf + 1) * 512], op=ALU.add), r=[("pD", hf), HK], w=[HK])
                    S.dma("sp", H[(b0 + b) * 128:(b0 + b + 1) * 128, :], h[:], r=[HK], w=[("H", b0 + b)])
                S.op("act", lambda e, h=h, b=b: e.activation(out=junk[:], in_=h[:], func=AF.Square, accum_out=ssq[:, 4 * (ti % 2) + b:4 * (ti % 2) + b + 1]), r=[HK], w=["junk", ("ssq", 4 * (ti % 2) + b)])
                S.op("dve", lambda e, b=b: e.tensor_scalar(out=rs[:, 4 * (ti % 2) + b:4 * (ti % 2) + b + 1], in0=ssq[:, 4 * (ti % 2) + b:4 * (ti % 2) + b + 1], scalar1=1.0 / D, scalar2=RMS_EPS, op0=ALU.mult, op1=ALU.add), r=[("ssq", 4 * (ti % 2) + b)], w=[("rs", 4 * (ti % 2) + b)])
                S.op("act", lambda e, b=b: e.activation(out=rs[:, 4 * (ti % 2) + b:4 * (ti % 2) + b + 1], in_=rs[:, 4 * (ti % 2) + b:4 * (ti % 2) + b + 1], func=AF.Sqrt), r=[("rs", 4 * (ti % 2) + b)], w=[("rs", 4 * (ti % 2) + b)])
                S.op("dve", lambda e, b=b: e.reciprocal(out=rs[:, 4 * (ti % 2) + b:4 * (ti % 2) + b + 1], in_=rs[:, 4 * (ti % 2) + b:4 * (ti % 2) + b + 1]), r=[("rs", 4 * (ti % 2) + b)], w=[("rs", 4 * (ti % 2) + b)])
                S.op("act", lambda e, h=h, b=b: e.activation(out=xnbs[ti % 2][:, b, :], in_=h[:], func=AF.Copy, scale=rs[:, 4 * (ti % 2) + b:4 * (ti % 2) + b + 1]), r=[HK, ("rs", 4 * (ti % 2) + b)], w=[("xnb", ti % 2, b)])

        def stageA2(ti):
            b0, nblk = cfg.tiles[ti]
            nt = nblk * 128
            xt = xnT[ti % 2]
            XK = ("xnT", ti % 2)
            xnb = xnbs[ti % 2]
            if not with_norm:
                return
            for rnd in range(2):
                for b in range(nblk):
                    for j in range(4):
                        dc = rnd * 4 + j
                        S.op("pe", lambda e, b=b, dc=dc, j=j: e.transpose(out=pT[j // 2][:, (j % 2) * 512 + b * 128:(j % 2) * 512 + (b + 1) * 128], in_=xnb[:, b, dc * 128:(dc + 1) * 128], identity=ident[:]),
                             r=[("xnb", ti % 2, b), "ident"], w=[("pT", j // 2)])
                for j in range(4):
                    dc = rnd * 4 + j
                    eng = "act" if j % 2 == 0 else "dve"
                    if eng == "act":
                        S.op("act", lambda e, dc=dc, j=j: e.activation(out=xt[:, dc, 0:nt], in_=pT[j // 2][:, (j % 2) * 512:(j % 2) * 512 + nt], func=AF.Copy, scale=gn[:, l, gidx, dc:dc + 1]),
                             r=[("pT", j // 2), "gn"], w=[XK])
                    else:
                        S.op("dve", lambda e, dc=dc, j=j: e.tensor_scalar(out=xt[:, dc, 0:nt], in0=pT[j // 2][:, (j % 2) * 512:(j % 2) * 512 + nt], scalar1=gn[:, l, gidx, dc:dc + 1], scalar2=None, op0=ALU.mult),
                             r=[("pT", j // 2), "gn"], w=[XK])
            if half == 0:
                S.dma("sp", XNT.rearrange("(c p) n -> p c n", p=128)[:, :, b0 * 128:b0 * 128 + nt], xt[:, :, 0:nt], r=[XK], w=[("XNT", ti)])

        def stageB(ti):
            b0, nblk = cfg.tiles[ti]
            nt = nblk * 128
            xt = xnT[ti % 2]
            XK = ("xnT", ti % 2)
            ht = hT[ti % 2]
            HTK = ("hT", ti % 2)
            for fc in range(NF):
                for dc in range(8):
                    S.op("pe", lambda e, fc=fc, dc=dc: e.matmul(pG[fc % 2][:, 0:nt], lhsT=Wg[:, dc, fc * 128:(fc + 1) * 128], rhs=xt[:, dc, 0:nt], start=(dc == 0), stop=(dc == 7)),
                         r=[XK, ("W",), "W"], w=[("pG", fc % 2)])
                for dc in range(8):
                    S.op("pe", lambda e, fc=fc, dc=dc: e.matmul(pU[fc % 2][:, 0:nt], lhsT=Wu[:, dc, fc * 128:(fc + 1) * 128], rhs=xt[:, dc, 0:nt], start=(dc == 0), stop=(dc == 7)),
                         r=[XK, ("W",), "W"], w=[("pU", fc % 2)])
                S.op("act", lambda e, fc=fc: e.activation(out=sg[fc % 2][:, 0:nt], in_=pG[fc % 2][:, 0:nt], func=AF.Silu), r=[("pG", fc % 2)], w=[("sg", fc % 2)])
                S.op("dve", lambda e, fc=fc: e.tensor_tensor(out=ht[:, fc, 0:nt], in0=sg[fc % 2][:, 0:nt], in1=pU[fc % 2][:, 0:nt], op=ALU.mult), r=[("sg", fc % 2), ("pU", fc % 2)], w=[HTK])

        def stageC(ti):
            b0, nblk = cfg.tiles[ti]
            ht = hT[ti % 2]
            HTK = ("hT", ti % 2)
            for b in range(nblk):
                h = hc[b % 2]
                HK = ("hc", b % 2)
                S.dma("act", h[:], H[(b0 + b) * 128:(b0 + b + 1) * 128, :], r=[("H", b0 + b)], w=[HK])
                for hf in range(2):
                    for fc in range(NF):
                        S.op("pe", lambda e, fc=fc, hf=hf, b=b: e.matmul(pD[hf][:, :], lhsT=ht[:, fc, b * 128:(b + 1) * 128], rhs=Wd[:, fc, hf * 512:(hf + 1) * 512], start=(fc == 0), stop=(fc == NF - 1)),
                             r=[HTK, ("W",), "W"], w=[("pD", hf)])
                    S.op("dve", lambda e, hf=hf, h=h: e.scalar_tensor_tensor(out=h[:, hf * 512:(hf + 1) * 512], in0=pD[hf][:, :], scalar=0.5, in1=h[:, hf * 512:(hf + 1) * 512], op0=ALU.mult, op1=ALU.add),
                         r=[("pD", hf), HK], w=[HK])
                if not final:
                    S.dma("sp", H[(b0 + b) * 128:(b0 + b + 1) * 128, :], h[:], r=[HK], w=[("H", b0 + b)])
                else:
                    y = yb[b % 2]
                    YK = ("yb", b % 2)
                    c = 8 + (b % 2)
                    S.op("act", lambda e, h=h, c=c: e.activation(out=junk[:], in_=h[:], func=AF.Square, accum_out=ssq[:, c:c + 1]), r=[HK], w=["junk", ("ssq", c)])
                    S.op("dve", lambda e, c=c: e.tensor_scalar(out=rs[:, c:c + 1], in0=ssq[:, c:c + 1], scalar1=1.0 / D, scalar2=RMS_EPS, op0=ALU.mult, op1=ALU.add), r=[("ssq", c)], w=[("rs", c)])
                    S.op("act", lambda e, c=c: e.activation(out=rs[:, c:c + 1], in_=rs[:, c:c + 1], func=AF.Sqrt), r=[("rs", c)], w=[("rs", c)])
                    S.op("dve", lambda e, c=c: e.reciprocal(out=rs[:, c:c + 1], in_=rs[:, c:c + 1]), r=[("rs", c)], w=[("rs", c)])
                    S.op("dve", lambda e, h=h, y=y, c=c: e.scalar_tensor_tensor(out=y[:], in0=h[:], scalar=rs[:, c:c + 1], in1=fn_sb[:], op0=ALU.mult, op1=ALU.mult), r=[HK, ("rs", c), "fn"], w=[YK])
                    g0 = (b0 + b) * 128
                    for s in range(2):
                        lo = max(g0, cfg.BASE[s] + NMETA)
                        hi = min(g0 + 128, cfg.BASE[s] + cfg.LS[s])
                        if hi > lo:
                            S.dma("sp", yout[s][lo - cfg.BASE[s] - NMETA:hi - cfg.BASE[s] - NMETA, :], y[lo - g0:hi - g0, :], r=[YK], w=[("y", s, lo)])

        nT = len(cfg.tiles)
        stageA(0)
        stageA2(0)
        if nT > 1:
            stageA(1)
        for ti in range(nT):
            stageB(ti)
            if ti + 1 < nT:
                stageA2(ti + 1)
            stageC(ti)
            if ti + 2 < nT:
                stageA(ti + 2)
        S.barrier()
        st.close()


    seln = sb(cst, "seln", [128, 4, 32], BF16)
    selr = sb(cst, "selr", [128, 32], BF16)
    ones_sb = sb(cst, "ones_sb", [128, 512], BF16)
    onesf = sb(cst, "onesf", [128, 128], F32)
    identf = sb(cst, "identf", [128, 128], F32)
    qk_sb = sb(cst, "qk_sb", [128, NL, 3], F32)
    S.dma("sp", seln[:], seln_in, w=["seln"])
    S.dma("sp", selr[:], selr_in, w=["selr"])
    S.dma("sp", ones_sb[:], ones_in, w=["ones"])
    S.dma("sp", onesf[:], onesf_in, w=["onesf"])
    S.dma("sp", identf[:], identf_in, w=["identf"])
    S.dma("sp", qk_sb[:], qkg, w=["qk"])
    S.barrier()

    def load_cast(dst, src, stg_ap, stg_key, q, eng):
        S.dma(q, stg_ap, src, w=[stg_key])
        if eng == "act":
            S.op("act", lambda e: e.copy(out=dst, in_=stg_ap), r=[stg_key], w=["W"])
        else:
            S.op(eng, lambda e: e.tensor_copy(out=dst, in_=stg_ap), r=[stg_key], w=["W"])

    def proj_pass(l):
        S.mark(f"proj_L{l}")
        st = ExitStack()
        Win = sb(st, "Win", [128, 8, NCOLP], BF16)
        Wuq = sb(st, "Wuq", [128, 2, 768], BF16)
        Wukv = sb(st, "Wukv", [128, 1024], BF16)
        stg = [sb(st, f"stg{i}", [128, NCOLP], F32) for i in range(2)]
        hb = [sb(st, f"hb{i}", [128, D], F32) for i in range(2)]
        xnbs = [sb(st, f"xnb{i}", [128, 4, D], BF16) for i in range(2)]
        xnT = [sb(st, f"xnT{i}", [128, 8, 512], BF16) for i in range(2)]
        ssq = sb(st, "ssq", [128, 12], F32)
        rs = sb(st, "rs", [128, 12], F32)
        junk = sb(st, "junk", [128, D], BF16)
        cq_sb = sb(st, "cq_sb", [128, 3, 512], F32)
        sqb = [sb(st, f"sqb{i}", [128, 512], BF16) for i in range(3)]
        sq6 = sb(st, "sq6", [128, 6, 512], BF16)
        if os.environ.get("DUMMY_KB"):
            dummy = sb(st, "dummy", [128, int(os.environ["DUMMY_KB"]) * 256], F32)
        rstd = [sb(st, f"rstd{i}", [128, 512], F32) for i in range(2)]
        cqn = sb(st, "cqn", [128, 2, 512], BF16)
        ckvn = sb(st, "ckvn", [128, 512], BF16)
        qn_sb = sb(st, "qn_sb", [128, 4, 512], BF16)
        kn_sb = sb(st, "kn_sb", [128, 4, 512], BF16)
        xr = [sb(st, f"xr{i}", [128, 512], F32) for i in range(2)]
        tt = [sb(st, f"tt{i}", [128, 512], F32) for i in range(4)]
        rr = [sb(st, f"rr{i}", [128, 512], BF16) for i in range(4)]
        cs = sb(st, "cs", [128, 512], F32)
        sn = sb(st, "sn", [128, 512], F32)
        VAt = [sb(st, f"VAt{i}", [128, 4, NH, 65], BF16) for i in range(2)]
        rwb = [sb(st, f"rwb{i}", [128, 512], F32) for i in range(4)]
        nrm = sb(st, "nrm", [8, 2, 512], F32)
        pT = [ps(st, f"pT{i}", [128, 1024], BF16) for i in range(2)]
        pA = [ps(st, f"pA{i}", [128, 512]) for i in range(5)]
        pN = ps(st, "pN", [128, 512])
        for i in range(2):
            S.op("pool", lambda e, i=i: e.memset(VAt[i][:], 1.0), w=[("VAt", i)])
        for dc in range(8):
            load_cast(Win[:, dc, :], w_in[l, dc * 128:(dc + 1) * 128, :], stg[dc % 2][:], ("stg", dc % 2), "sp" if dc % 2 == 0 else "act", ["dve", "pool"][dc % 2])
        load_cast(Wuq[:], w_uq[l].rearrange("(k p) n -> p k n", p=128), stg[0][:, 0:1536].rearrange("p (k n) -> p k n", k=2), ("stg", 0), "sp", "dve")
        load_cast(Wukv[:], w_ukv[l], stg[1][:, 0:1024], ("stg", 1), "act", "pool")

        def stageA(ti):
            b0, nblk = cfg.tiles[ti]
            nt = nblk * 128
            xt = xnT[ti % 2]
            XK = ("xnT", ti % 2)
            for b in range(nblk):
                h = hb[b % 2]
                HK = ("hb", b % 2)
                S.dma("sp" if b % 2 == 0 else "act", h[:], H[(b0 + b) * 128:(b0 + b + 1) * 128, :], r=[("H", b0 + b)], w=[HK])
                S.op("act", lambda e, h=h, b=b: e.activation(out=junk[:], in_=h[:], func=AF.Square, accum_out=ssq[:, 4 * (ti % 2) + b:4 * (ti % 2) + b + 1]), r=[HK], w=["junk", ("ssq", 4 * (ti % 2) + b)])
                S.op("dve", lambda e, b=b: e.tensor_scalar(out=rs[:, 4 * (ti % 2) + b:4 * (ti % 2) + b + 1], in0=ssq[:, 4 * (ti % 2) + b:4 * (ti % 2) + b + 1], scalar1=1.0 / D, scalar2=RMS_EPS, op0=ALU.mult, op1=ALU.add), r=[("ssq", 4 * (ti % 2) + b)], w=[("rs", 4 * (ti % 2) + b)])
                S.op("act", lambda e, b=b: e.activation(out=rs[:, 4 * (ti % 2) + b:4 * (ti % 2) + b + 1], in_=rs[:, 4 * (ti % 2) + b:4 * (ti % 2) + b + 1], func=AF.Sqrt), r=[("rs", 4 * (ti % 2) + b)], w=[("rs", 4 * (ti % 2) + b)])
                S.op("dve", lambda e, b=b: e.reciprocal(out=rs[:, 4 * (ti % 2) + b:4 * (ti % 2) + b + 1], in_=rs[:, 4 * (ti % 2) + b:4 * (ti % 2) + b + 1]), r=[("rs", 4 * (ti % 2) + b)], w=[("rs", 4 * (ti % 2) + b)])
                S.op("act", lambda e, h=h, b=b: e.activation(out=xnbs[ti % 2][:, b, :], in_=h[:], func=AF.Copy, scale=rs[:, 4 * (ti % 2) + b:4 * (ti % 2) + b + 1]), r=[HK, ("rs", 4 * (ti % 2) + b)], w=[("xnb", ti % 2, b)])

        def stageA2(ti):
            b0, nblk = cfg.tiles[ti]
            nt = nblk * 128
            xt = xnT[ti % 2]
            XK = ("xnT", ti % 2)
            xnb = xnbs[ti % 2]
            for rnd in range(2):
                for b in range(nblk):
                    for j in range(4):
                        dc = rnd * 4 + j
                        S.op("pe", lambda e, b=b, dc=dc, j=j: e.transpose(out=pT[j // 2][:, (j % 2) * 512 + b * 128:(j % 2) * 512 + (b + 1) * 128], in_=xnb[:, b, dc * 128:(dc + 1) * 128], identity=ident[:]),
                             r=[("xnb", ti % 2, b), "ident"], w=[("pT", j // 2)])
                for j in range(4):
                    dc = rnd * 4 + j
                    if j % 2 == 0:
                        S.op("act", lambda e, dc=dc, j=j: e.activation(out=xt[:, dc, 0:nt], in_=pT[j // 2][:, (j % 2) * 512:(j % 2) * 512 + nt], func=AF.Copy, scale=gn[:, l, 1, dc:dc + 1]),
                             r=[("pT", j // 2), "gn"], w=[XK])
                    else:
                        S.op("dve", lambda e, dc=dc, j=j: e.tensor_scalar(out=xt[:, dc, 0:nt], in0=pT[j // 2][:, (j % 2) * 512:(j % 2) * 512 + nt], scalar1=gn[:, l, 1, dc:dc + 1], scalar2=None, op0=ALU.mult),
                             r=[("pT", j // 2), "gn"], w=[XK])

        def stageB(ti):
            b0, nblk = cfg.tiles[ti]
            nt = nblk * 128
            c0 = b0 * 128
            xt = xnT[ti % 2]
            XK = ("xnT", ti % 2)
            S.dma("sp", cs[:, 0:nt], cosT[:, c0:c0 + nt], w=["cs"])
            S.dma("act", sn[:, 0:nt], sinT[:, c0:c0 + nt], w=["sn"])
            bank = [0]
            evi = [0]

            def nextbank():
                i = bank[0] % 5
                bank[0] += 1
                return pA[i], ("pA", i)

            def evac(out, in_, r, w):
                evi[0] += 1
                if evi[0] % 2 == 0:
                    S.op("act", lambda e: e.copy(out=out, in_=in_), r=r, w=w)
                else:
                    S.op("dve", lambda e: e.tensor_copy(out=out, in_=in_), r=r, w=w)

            def win_group(name):
                off, wdt = GROUPS[name]
                p, pk = nextbank()
                for dc in range(8):
                    S.op("pe", lambda e, dc=dc: e.matmul(p[0:wdt, 0:nt], lhsT=Win[:, dc, off:off + wdt], rhs=xt[:, dc, 0:nt], start=(dc == 0), stop=(dc == 7)), r=[XK, "W"], w=[pk])
                return p, pk

            def rms_part1(chunks, slot0):
                for j, (p, pk, sbuf, sk) in enumerate(chunks):
                    S.op("act", lambda e, p=p, sbuf=sbuf: e.copy(out=sbuf, in_=p[:, 0:nt]), r=[pk], w=[sk])
                    S.op("act", lambda e, p=p, j=j: e.activation(out=sqb[slot0 + j][:, 0:nt], in_=p[:, 0:nt], func=AF.Square), r=[pk], w=[("sqb", slot0 + j)])

            def rms_part2(chunks, slot0, n_feat, gcol0, outs, rsd, rkey):
                p2, pk2 = nextbank()
                for j in range(len(chunks)):
                    S.op("pe", lambda e, j=j: e.matmul(p2[:, 0:nt], lhsT=ones_sb[:, 0:128], rhs=sqb[slot0 + j][:, 0:nt], start=(j == 0), stop=(j == len(chunks) - 1)), r=[("sqb", slot0 + j), "ones"], w=[pk2])
                S.op("dve", lambda e: e.tensor_scalar(out=rsd[:, 0:nt], in0=p2[:, 0:nt], scalar1=1.0 / n_feat, scalar2=RMS_EPS, op0=ALU.mult, op1=ALU.add), r=[pk2], w=[rkey])
                S.op("act", lambda e: e.activation(out=rsd[:, 0:nt], in_=rsd[:, 0:nt], func=AF.Sqrt), r=[rkey], w=[rkey])
                S.op("dve", lambda e: e.reciprocal(out=rsd[:, 0:nt], in_=rsd[:, 0:nt]), r=[rkey], w=[rkey])
                for j, (p, pk, sbuf, sk) in enumerate(chunks):
                    o, ok = outs[j]
                    S.op("dve", lambda e, sbuf=sbuf, o=o, j=j: e.scalar_tensor_tensor(out=o, in0=sbuf, scalar=qk_sb[:, l, gcol0 + j:gcol0 + j + 1], in1=rsd[:, 0:nt], op0=ALU.mult, op1=ALU.mult),
                         r=[sk, rkey, "qk"], w=[ok])

            def raw_groups(lo, hi):
                roff = GROUPS["r0"][0]
                for gi in range(lo, hi):
                    name = RW_GROUPS[gi]
                    off, wdt = GROUPS[name]
                    p, pk = win_group(name)
                    evac(rwb[gi % 4][0:wdt, 0:nt], p[0:wdt, 0:nt], [pk], [("rwb", gi % 4)])
                    S.dma(["sp", "act", "pool"][gi % 3], RAW[off - roff:off - roff + wdt, c0:c0 + nt], rwb[gi % 4][0:wdt, 0:nt], r=[("rwb", gi % 4)], w=[("RAW", ti, gi)])

            ch = []
            for j in range(2):
                p, pk = win_group(f"cq{j}")
                ch.append((p, pk, cq_sb[:, j, 0:nt], ("cq", j)))
            p, pk = win_group("ckv")
            chkv = [(p, pk, cq_sb[:, 2, 0:nt], ("cq", 2))]
            rms_part1(ch, 0)
            rms_part1(chkv, 2)
            raw_groups(0, 8)
            rms_part2(ch, 0, 256, 0, [(cqn[:, 0, 0:nt], ("cqn", 0)), (cqn[:, 1, 0:nt], ("cqn", 1))], rstd[0], "rstd0")
            rms_part2(chkv, 2, 128, 2, [(ckvn[:, 0:nt], "ckvn")], rstd[1], "rstd1")

            def rope(x1, x2, o1, o2, k1, k2, ok1, ok2):
                S.op("pool", lambda e: e.tensor_tensor(out=tt[0][:, 0:nt], in0=x1[:, 0:nt], in1=cs[:, 0:nt], op=ALU.mult), r=[k1, "cs"], w=[("tt", 0)])
                S.op("pool", lambda e: e.tensor_tensor(out=tt[1][:, 0:nt], in0=x2[:, 0:nt], in1=sn[:, 0:nt], op=ALU.mult), r=[k2, "sn"], w=[("tt", 1)])
                S.op("dve", lambda e: e.tensor_tensor(out=o1[:, 0:nt], in0=tt[0][:, 0:nt], in1=tt[1][:, 0:nt], op=ALU.subtract), r=[("tt", 0), ("tt", 1)], w=[ok1])
                S.op("pool", lambda e: e.tensor_tensor(out=tt[2][:, 0:nt], in0=x2[:, 0:nt], in1=cs[:, 0:nt], op=ALU.mult), r=[k2, "cs"], w=[("tt", 2)])
                S.op("pool", lambda e: e.tensor_tensor(out=tt[3][:, 0:nt], in0=x1[:, 0:nt], in1=sn[:, 0:nt], op=ALU.mult), r=[k1, "sn"], w=[("tt", 3)])
                S.op("dve", lambda e: e.tensor_tensor(out=o2[:, 0:nt], in0=tt[2][:, 0:nt], in1=tt[3][:, 0:nt], op=ALU.add), r=[("tt", 2), ("tt", 3)], w=[ok2])

            SUB = os.environ.get("QK_SUB", "namdep")

            def qk_side(which, nope_sb, dst, nd, slot):
                KSKIP = os.environ.get("K_SKIP", "")
                for c in range(0 if (which == "k" and "nope" in KSKIP) else 4):
                    p, pk = nextbank()
                    if which == "q":
                        for kc in range(2):
                            S.op("pe", lambda e, kc=kc, c=c: e.matmul(p[:, 0:nt], lhsT=Wuq[:, kc, c * 128:(c + 1) * 128], rhs=cqn[:, kc, 0:nt], start=(kc == 0), stop=(kc == 1)), r=[("cqn", kc), "W"], w=[pk])
                    else:
                        if os.environ.get("KSPLIT"):
                            for hh in range(2):
                                S.op("pe", lambda e, c=c, hh=hh: e.matmul(p[:, 0:nt], lhsT=Wukv[64 * hh:64 * hh + 64, c * 128:(c + 1) * 128], rhs=ckvn[64 * hh:64 * hh + 64, 0:nt], start=(hh == 0), stop=(hh == 1)), r=["ckvn", "W"], w=[pk])
                        else:
                            S.op("pe", lambda e, c=c: e.matmul(p[:, 0:nt], lhsT=Wukv[:, c * 128:(c + 1) * 128], rhs=ckvn[:, 0:nt], start=True, stop=True), r=["ckvn", "W"], w=[pk])
                    S.op("act", lambda e, c=c: e.copy(out=nope_sb[:, c, 0:nt], in_=p[:, 0:nt]), r=[pk], w=[(which + "n", c)])
                    S.op("act", lambda e, c=c: e.activation(out=sq6[:, c, 0:nt], in_=p[:, 0:nt], func=AF.Square), r=[pk], w=[("sq6", c)])
                if os.environ.get("KBAR"):
                    S.barrier()
                for j in range(0 if (which == "k" and "rope" in KSKIP) else 2):
                    if which == "q":
                        p, pk = nextbank()
                        for kc in range(2):
                            S.op("pe", lambda e, kc=kc, j=j: e.matmul(p[:, 0:nt], lhsT=Wuq[:, kc, 512 + j * 128:512 + (j + 1) * 128], rhs=cqn[:, kc, 0:nt], start=(kc == 0), stop=(kc == 1)), r=[("cqn", kc), "W"], w=[pk])
                    else:
                        p, pk = win_group(f"kr{j + 1}")
                    S.op("dve", lambda e, j=j: e.tensor_copy(out=xr[j][:, 0:nt], in_=p[:, 0:nt]), r=[pk], w=[("xr", j)])
                    S.op("act", lambda e, j=j: e.activation(out=sq6[:, 4 + j, 0:nt], in_=p[:, 0:nt], func=AF.Square), r=[pk], w=[("sq6", 4 + j)])
                if "n" in SUB:
                    for c in range(6):
                        S.op("pe", lambda e, c=c: e.matmul(pN[0:32, 0:nt], lhsT=(seln[:, c, :] if c < 4 else selr[:, :]), rhs=sq6[:, c, 0:nt], start=(c == 0), stop=(c == 5)), r=[("sq6", c), "seln", "selr"], w=["pN"])
                o1, o2 = rr[2 * slot], rr[2 * slot + 1]
                if "p" in SUB:
                    rope(xr[0], xr[1], o1, o2, ("xr", 0), ("xr", 1), ("rr", 2 * slot), ("rr", 2 * slot + 1))
                if "a" not in SUB:
                    pass
                elif which == "q":
                    S.op("act", lambda e: e.activation(out=nrm[:, slot, 0:nt], in_=pN[0:8, 0:nt], func=AF.Sqrt), r=["pN"], w=[("nrm", slot)])
                else:
                    S.op("act", lambda e: e.copy(out=nrm[:, slot, 0:nt], in_=pN[0:8, 0:nt]), r=["pN"], w=[("nrm", slot)])
                if "m" in SUB:
                    S.dma("pool", nd[:, c0:c0 + nt], nrm[:, slot, 0:nt], r=[("nrm", slot)], w=[(which + "nd", ti)])
                for two in range(2 if "d" in SUB else 0):
                    S.dma("sp" if two == 0 else "act", dst[0:64, :, c0:c0 + nt].rearrange("r (c two) n -> r c two n", two=2)[:, :, two, :], nope_sb[64 * two:64 * two + 64, :, 0:nt], r=[(which + "n", c) for c in range(4)], w=[(which + "T", ti, two)])
                if "e" in SUB:
                    S.dma("pool", dst[64:80, :, c0:c0 + nt].rearrange("i h n -> (i h) n"), o1[:, 0:nt], r=[("rr", 2 * slot)], w=[(which + "T", ti, 2)])
                    S.dma("pool", dst[80:96, :, c0:c0 + nt].rearrange("i h n -> (i h) n"), o2[:, 0:nt], r=[("rr", 2 * slot + 1)], w=[(which + "T", ti, 3)])

            PARTS = os.environ.get("PROJ_PARTS", "qkovr")
            if "q" in PARTS:
                qk_side("q", qn_sb, QTd, QNd, 0)
            if "k" in PARTS:
                qk_side("k", kn_sb, KTd, KNd, 1)
            if "o" in PARTS:
                S.dma("sp", KTd[96, :, c0:c0 + nt], onesrow_in[:, 0:nt], w=[("kT", ti, 4)])
            vt = VAt[ti % 2]
            VK = ("VAt", ti % 2)
            for b in range(nblk if "v" in PARTS else 0):
                p, pk = nextbank()
                S.op("pe", lambda e, b=b: e.matmul(p[:, 0:512], lhsT=ckvn[:, b * 128:(b + 1) * 128], rhs=Wukv[:, 512:1024], start=True, stop=True), r=["ckvn", "W"], w=[pk])
                evac(vt[:, b, :, 0:64], p[:, 0:512].rearrange("p (h d) -> p h d", d=64), [pk], [VK])
            if "v" in PARTS:
                S.dma("sp", VA[c0:c0 + nt, :, :].rearrange("(b p) h c -> p b h c", p=128), vt[:, 0:nblk, :, :], r=[VK], w=[("VA", ti)])
            raw_groups(8, 16)

        nT = len(cfg.tiles)
        stageA(0)
        stageA2(0)
        if nT > 1:
            stageA(1)
        for ti in range(nT):
            if ti + 1 < nT:
                stageA2(ti + 1)
            stageB(ti)
            if ti + 2 < nT:
                stageA(ti + 2)
        S.barrier()
        st.close()


    LPM = max(cfg.LP)
    NBM = max(cfg.NB)

    def shift_phase(l):
        S.mark(f"shift_L{l}")
        st = ExitStack()
        mu = sb(st, "mu", [128, 16, 2], F32)
        c0t = sb(st, "c0t", [128, 16], F32)
        X = [sb(st, f"X{i}", [128, 514], F32) for i in range(4)]
        T = [sb(st, f"T{i}", [128, 512], F32) for i in range(4)]
        Tb = [sb(st, f"Tb{i}", [128, 512], BF16) for i in range(2)]
        S.dma("sp", mu[:], mu_in[:, l, :, :], w=["mu"])
        S.op("dve", lambda e: e.tensor_tensor(out=c0t[:], in0=mu[:, :, 0], in1=mu[:, :, 1], op=ALU.add), r=["mu"], w=["c0"])
        S.op("dve", lambda e: e.tensor_scalar(out=c0t[:], in0=c0t[:], scalar1=-1.0, scalar2=1.0, op0=ALU.mult, op1=ALU.add), r=["c0"], w=["c0"])
        roff0 = GROUPS["r0"][0]
        it = 0
        for s_ in range(2):
            base, L, Lp = cfg.BASE[s_], cfg.LS[s_], cfg.LP[s_]
            for t0 in range(0, Lp, 512):
                nt = min(512, Lp - t0)
                valid = max(0, min(nt, L - t0))
                for gi, name in enumerate(RW_GROUPS):
                    off, wdt = GROUPS[name]
                    ro = off - roff0
                    x = X[it % 4]
                    XK = ("X", it % 4)
                    t = T[it % 4]
                    TK = ("T", it % 4)
                    eng = "dve"
                    lo = max(t0 - 1, 0)
                    hi = min(t0 + nt + 1, L)
                    jlo, jhi = lo - (t0 - 1), hi - (t0 - 1)
                    if jlo > 0:
                        S.op("pool", lambda e: e.memset(x[0:wdt, 0:jlo], 0.0), w=[XK])
                    if jhi < nt + 2:
                        S.op("pool", lambda e: e.memset(x[0:wdt, max(jhi, 0):nt + 2], 0.0), w=[XK])
                    if jhi > jlo:
                        S.dma("sp" if it % 2 == 0 else "act", x[0:wdt, jlo:jhi], RAW[ro:ro + wdt, base + lo:base + hi], r=[("RAW",)], w=[XK])
                    S.op(eng, lambda e: e.tensor_scalar(out=t[0:wdt, 0:nt], in0=x[0:wdt, 1:nt + 1], scalar1=c0t[0:wdt, gi:gi + 1], scalar2=None, op0=ALU.mult), r=[XK, "c0"], w=[TK])
                    S.op(eng, lambda e: e.scalar_tensor_tensor(out=t[0:wdt, 0:nt], in0=x[0:wdt, 0:nt], scalar=mu[0:wdt, gi, 0:1], in1=t[0:wdt, 0:nt], op0=ALU.mult, op1=ALU.add), r=[XK, TK, "mu"], w=[TK])
                    is_da = (name == "da")
                    if is_da:
                        tb = Tb[it % 2]
                        TBK = ("Tb", it % 2)
                        S.op(eng, lambda e: e.scalar_tensor_tensor(out=tb[0:wdt, 0:nt], in0=x[0:wdt, 2:nt + 2], scalar=mu[0:wdt, gi, 1:2], in1=t[0:wdt, 0:nt], op0=ALU.mult, op1=ALU.add), r=[XK, TK, "mu"], w=[TBK])
                        if valid < nt:
                            S.op(eng, lambda e: e.memset(tb[0:wdt, valid:nt], 0.0), w=[TBK])
                        S.dma("pool", LOR[128:256, base + t0:base + t0 + nt], tb[0:wdt, 0:nt], r=[TBK], w=[("LOR", it)])
                    else:
                        S.op(eng, lambda e: e.scalar_tensor_tensor(out=t[0:wdt, 0:nt], in0=x[0:wdt, 2:nt + 2], scalar=mu[0:wdt, gi, 1:2], in1=t[0:wdt, 0:nt], op0=ALU.mult, op1=ALU.add), r=[XK, TK, "mu"], w=[TK])
                        if valid < nt:
                            S.op(eng, lambda e: e.memset(t[0:wdt, valid:nt], 0.0), w=[TK])
                        if gi < 12:
                            S.dma("pool" if it % 2 == 0 else "act", RWS[ro:ro + wdt, base + t0:base + t0 + nt], t[0:wdt, 0:nt], r=[TK], w=[("RWS", it)])
                        else:
                            tb = Tb[it % 2]
                            TBK = ("Tb", it % 2)
                            fn_ = AF.Tanh if name == "dw" else AF.Sigmoid
                            S.op("act", lambda e: e.activation(out=tb[0:wdt, 0:nt], in_=t[0:wdt, 0:nt], func=fn_), r=[TK], w=[TBK])
                            lo_r = {"dw": 0, "dg0": 256, "dg1": 384}[name]
                            S.dma("pool", LOR[lo_r:lo_r + wdt, base + t0:base + t0 + nt], tb[0:wdt, 0:nt], r=[TBK], w=[("LOR", it)])
                    it += 1
        S.barrier()
        st.close()

    def attn_phase(l):
        S.mark(f"attn_L{l}")
        st = ExitStack()
        Ksb = [sb(st, f"Ksb{i}", [97, LPM], BF16) for i in range(2)]
        Qsb = [sb(st, f"Qsb{i}", [97, LPM], BF16) for i in range(2)]
        Vsb = [sb(st, f"Vsb{i}", [128, NBM, 65], BF16) for i in range(2)]
        knq = sb(st, "knq", [8, LPM], F32)
        augb = sb(st, "augb", [8, LPM], BF16)
        kmax = sb(st, "kmax", [8, 2], F32)
        Pt = [sb(st, f"Pt{i}", [128, 512], BF16) for i in range(5)]
        Osb = [sb(st, f"Osb{i}", [65, 512], F32) for i in range(2)]
        rec = [sb(st, f"rec{i}", [65, 512], F32) for i in range(2)]
        Ob = [sb(st, f"Ob{i}", [64, 512], BF16) for i in range(2)]
        pS = [ps(st, f"pS{i}", [128, 512]) for i in range(5)]
        pO = [ps(st, f"pO{i}", [128, 512]) for i in range(2)]
        pB = [ps(st, f"pB{i}", [128, 512]) for i in range(1)] * 2
        hs = 0
        qt = 0
        si = 0
        pending = []
        for s_ in range(2):
            base, L, Lp, nb = cfg.BASE[s_], cfg.LS[s_], cfg.LP[s_], cfg.NB[s_]
            S.dma("sp", knq[:, 0:L], KNd[:, base:base + L], r=[("KNd",)], w=["knq"])
            S.op("dve", lambda e: e.tensor_reduce(out=kmax[:, 0:1], in_=knq[:, 0:L], axis=AX.X, op=ALU.max), r=["knq"], w=["kmax"])
            S.op("act", lambda e: e.activation(out=kmax[:, 0:1], in_=kmax[:, 0:1], func=AF.Sqrt), r=["kmax"], w=["kmax"])
            S.dma("sp", knq[:, 0:L], QNd[:, base:base + L], r=[("QNd",)], w=["knq"])
            S.op("dve", lambda e: e.tensor_scalar(out=augb[:, 0:L], in0=knq[:, 0:L], scalar1=kmax[:, 0:1], scalar2=-1.0, op0=ALU.mult, op1=ALU.mult), r=["knq", "kmax"], w=["augb"])
            S.dma("sp", QTd[96, :, base:base + L], augb[:, 0:L], r=["augb"], w=[("QTd", s_)])
            for h in range(NH):
                bf = hs % 2
                hs += 1
                K_, Q_, V_ = Ksb[bf], Qsb[bf], Vsb[bf]
                KK, QK, VK = ("K", bf), ("Q", bf), ("V", bf)
                S.dma("sp", K_[:, 0:L], KTd[:, h, base:base + L], r=[("KTd",)], w=[KK])
                S.dma("act", Q_[:, 0:L], QTd[:, h, base:base + L], r=[("QTd", s_)], w=[QK])
                S.dma("pool", V_[:, 0:nb, :], VA[base:base + Lp, h, :].rearrange("(b p) c -> p b c", p=128), r=[("VA",)], w=[VK])
                nkb = -(-L // 128)
                for q0 in range(0, L, 512):
                    qw = min(512, L - q0)
                    j = qt % 2
                    qt += 1
                    OK_ = ("pO", j)

                    def emitS(kb):
                        kw = min(128, L - kb * 128)
                        i = (si + kb) % 5
                        S.op("pe", lambda e: e.matmul(pS[i][0:kw, 0:qw], lhsT=K_[:, kb * 128:kb * 128 + kw], rhs=Q_[:, q0:q0 + qw], start=True, stop=True), r=[KK, QK], w=[("pS", i)])
                        S.op("act", lambda e: e.activation(out=Pt[i][0:kw, 0:qw], in_=pS[i][0:kw, 0:qw], func=AF.Exp, scale=SCALE), r=[("pS", i)], w=[("Pt", i)])

                    def emitPV(kb):
                        kw = min(128, L - kb * 128)
                        i = (si + kb) % 5
                        S.op("pe", lambda e: e.matmul(pO[j][0:65, 0:qw], lhsT=V_[0:kw, kb, :], rhs=Pt[i][0:kw, 0:qw], start=(kb == 0), stop=(kb == nkb - 1)), r=[VK, ("Pt", i)], w=[OK_])

                    SK = 3
                    for kb in range(nkb + SK):
                        if kb < nkb:
                            emitS(kb)
                        if kb == 1 and pending:
                            pending.pop()()
                        if kb >= SK:
                            emitPV(kb - SK)
                    si = (si + nkb) % 5

                    def fin(j=j, qw=qw, q0=q0, h=h, base=base):
                        S.op("pe", lambda e: e.matmul(pB[j][0:64, 0:qw], lhsT=onesf[64:65, 0:64], rhs=rec[j][64:65, 0:qw], start=True, stop=True), r=[("rec", j), "onesf"], w=[("pB", 0)])
                        S.op("dve", lambda e: e.tensor_tensor(out=Ob[j][:, 0:qw], in0=Osb[j][0:64, 0:qw], in1=pB[j][0:64, 0:qw], op=ALU.mult), r=[("Osb", j), ("pB", 0)], w=[("Ob", j)])
                        S.dma("pool", MIXT[64 * h:64 * h + 64, base + q0:base + q0 + qw], Ob[j][:, 0:qw], r=[("Ob", j)], w=[("MIXT", h, base + q0)])

                    S.op("dve", lambda e: e.tensor_copy(out=Osb[j][:, 0:qw], in_=pO[j][0:65, 0:qw]), r=[OK_], w=[("Osb", j)])
                    S.op("dve", lambda e: e.reciprocal(out=rec[j][64:65, 0:qw], in_=Osb[j][64:65, 0:qw]), r=[("Osb", j)], w=[("rec", j)])
                    if pending:
                        pending.pop()()
                    pending.append(fin)
        while pending:
            pending.pop()()
        S.barrier()
        st.close()


    def rwkv_phase(l):
        S.mark(f"rwkv_L{l}")
        st = ExitStack()
        c_ = CDEC
        MK = sb(st, "MK", [128, 4, 128], F32)
        MP1 = sb(st, "MP1", [128, 2, 2, 128], F32)
        MP3 = sb(st, "MP3", [128, 2, 128], F32)
        vm = sb(st, "vm", [128, 2], F32)
        prm = sb(st, "prm", [128, 9, 64], F32)
        omk = sb(st, "omk", [128, 64], F32)
        LWf = sb(st, "LWf", [64, 2, 2, 64], F32)
        LW = sb(st, "LW", [64, 2, 2, 64], BF16)
        G2f = sb(st, "G2f", [128, 2, 64], F32)
        G2 = sb(st, "G2", [128, 2, 64], BF16)
        S.dma("sp", MK[:], masks_in, w=["MK"])
        S.dma("sp", vm[:], vmask_in, w=["vm"])
        for d, (ms, mi, m3) in enumerate(((2, 0, 3), (3, 1, 2))):
            S.op("dve", lambda e: e.tensor_copy(out=MP1[:, d, 0, :], in_=MK[:, ms, :]), r=["MK"], w=["MP"])
            S.op("dve", lambda e: e.tensor_copy(out=MP1[:, d, 1, :], in_=MK[:, mi, :]), r=["MK"], w=["MP"])
            S.op("dve", lambda e: e.tensor_copy(out=MP3[:, d, :], in_=MK[:, m3, :]), r=["MK"], w=["MP"])
        GT = sb(st, "GT", [64, NBM, 2, 128], BF16)
        PHIT = sb(st, "PHIT", [64, NBM, 2, 64], BF16)
        PSI = sb(st, "PSI", [64, NBM, 2, 64], F32)
        Yacc = sb(st, "Yacc", [128, NBM, 64], F32)
        Sb = [sb(st, f"Sb{i}", [64, 2, 64], BF16) for i in range(2)]
        ey = sb(st, "ey", [128, 4, 64], F32)
        es1 = sb(st, "es1", [128, 8], F32)
        eo = sb(st, "eo", [128, 4, 64], F32)
        eob = sb(st, "eob", [64, 512], BF16)
        egl = sb(st, "egl", [128, 4, 2, 64], F32)
        banks = [ps(st, f"pX{i}", [128, 512]) for i in range(8)]
        bki = [0]

        def nb_():
            i = bki[0] % 8
            bki[0] += 1
            return banks[i], ("pX", i)

        alt = [0]

        def ew(fn, r, w):
            alt[0] += 1
            S.op("dve" if alt[0] % 2 else "pool", fn, r=r, w=w)

        def bc(ap, shape):
            return ap.broadcast_to(shape)

        RW_STOP = os.environ.get("RW_STOP", "Z")

        GLOBAL_KEYS = ("GT", "PHIT", "PSI", "Yacc", "prm", "LW", "G2", "omk", "MK", "MP", "vm", "ident", "identf", "onesf", "RWS", "LOR", "EPI", "pX", "LWf", "G2f")

        def make_thread(tid):
            def mk(k):
                name = k[0] if isinstance(k, tuple) else k
                return k if name in GLOBAL_KEYS else (k, "t", tid)

            class _TS:
                @staticmethod
                def op(eng, fn, r=(), w=()):
                    return S.op(eng, fn, r=[mk(k) for k in r], w=[mk(k) for k in w])

                @staticmethod
                def dma(q, out, in_, r=(), w=()):
                    return S.dma(q, out, in_, r=[mk(k) for k in r], w=[mk(k) for k in w])
            TS = _TS
            alt = [tid]

            def ew(fn, r, w):
                alt[0] += 1
                TS.op("dve" if alt[0] % 3 == 0 else "pool", fn, r=r, w=w)
            RKs = sb(st, "RKs", [128, 256], F32)
            Vs = sb(st, "Vs", [64, 256], F32)
            TW = sb(st, "TW", [64, 2, 256], BF16)
            DAs = sb(st, "DAs", [64, 2, 256], BF16)
            SG0 = sb(st, "SG0", [128, 256], BF16)
            SG1 = sb(st, "SG1", [32, 256], BF16)
            RKtm = sb(st, "RKtm", [128, 2, 128], F32)
            Vtm = sb(st, "Vtm", [128, 2, 64], BF16)
            Vt32 = sb(st, "Vt32", [128, 2, 64], F32)
            SGM = sb(st, "SGM", [128, 2, 2, 64], F32)
            Aa = sb(st, "Aa", [128, 2, 2, 64], F32)
            EG = sb(st, "EG", [128, 2, 2, 64], F32)
            kkr = sb(st, "kkr", [128, 2, 64], F32)
            kk = sb(st, "kk", [128, 2, 64], F32)
            sq = sb(st, "sq", [128, 2, 64], F32)
            n2 = sb(st, "n2", [128, 8], F32)
            kd = sb(st, "kd", [128, 2, 2, 64], F32)
            be = sb(st, "be", [128, 2, 2, 64], F32)
            t1 = sb(st, "t1", [128, 2, 2, 64], F32)
            EX = [sb(st, f"EX{i}", [128, 2, 2, 64], F32) for i in range(5)]
            sc = [sb(st, f"sc{i}", [128, 2, 2, 64], BF16) for i in range(5)]
            TA = sb(st, "TA", [128, 4, 128], BF16)
            FT = sb(st, "FT", [64, 4, 4, 128], BF16)
            QM = sb(st, "QM", [128, 4, 2, 128], BF16)
            KM = sb(st, "KM", [128, 4, 2, 128], BF16)
            PP = [sb(st, f"PP{i}", [128, 4, 128], BF16) for i in range(2)]
            QQ = [sb(st, f"QQ{i}", [128, 4, 128], BF16) for i in range(2)]
            TT = [sb(st, f"TT{i}", [128, 4, 128], BF16) for i in range(2)]
            WU = sb(st, "WU", [128, 4, 128], BF16)
            tmpd = sb(st, "tmpd", [64, 4, 64], F32)

            def group(s_, h, g, base, L, Lp, nb):
                ng = min(2, nb - 2 * g)
                nq = 2 * ng
                ncol = 128 * ng
                cc0 = base + 256 * g
                TS.dma("sp", RKs[0:64, 0:ncol], RWS[64 * h:64 * h + 64, cc0:cc0 + ncol], r=[("RWS",)], w=["RKs"])
                TS.dma("act", RKs[64:128, 0:ncol], RWS[512 + 64 * h:512 + 64 * h + 64, cc0:cc0 + ncol], r=[("RWS",)], w=["RKs"])
                TS.dma("sp", Vs[:, 0:ncol], RWS[1024 + 64 * h:1024 + 64 * h + 64, cc0:cc0 + ncol], r=[("RWS",)], w=["Vs"])
                TS.dma("act", TW[:, :, 0:ncol], LOR[0:128, cc0:cc0 + ncol].rearrange("(d k) n -> k d n", d=2), r=[("LOR",)], w=["TW"])
                TS.dma("sp", DAs[:, :, 0:ncol], LOR[128:256, cc0:cc0 + ncol].rearrange("(d k) n -> k d n", d=2), r=[("LOR",)], w=["DAs"])
                TS.dma("act", SG0[:, 0:ncol], LOR[256:384, cc0:cc0 + ncol], r=[("LOR",)], w=["SG0"])
                TS.dma("sp", SG1[:, 0:ncol], LOR[384:416, cc0:cc0 + ncol], r=[("LOR",)], w=["SG1"])
                yield
                pb, pk = nb_()
                for j in range(ng):
                    TS.op("pe", lambda e, j=j: e.transpose(out=pb[:, j * 128:(j + 1) * 128], in_=RKs[:, j * 128:(j + 1) * 128], identity=identf[:]), r=["RKs", "identf"], w=[pk])
                TS.op("act", lambda e: e.copy(out=RKtm[:, 0:ng, :], in_=pb[:, 0:ncol].rearrange("p (j c) -> p j c", c=128)), r=[pk], w=["RKtm"])
                pb, pk = nb_()
                for j in range(ng):
                    TS.op("pe", lambda e, j=j: e.transpose(out=pb[:, j * 64:(j + 1) * 64], in_=Vs[:, j * 128:(j + 1) * 128], identity=identf[0:64, 0:64]), r=["Vs", "identf"], w=[pk])
                TS.op("act", lambda e: e.copy(out=Vtm[:, 0:ng, :], in_=pb[:, 0:64 * ng].rearrange("p (j c) -> p j c", c=64)), r=[pk], w=["Vtm"])
                TS.op("dve", lambda e: e.tensor_copy(out=Vt32[:, 0:ng, :], in_=pb[:, 0:64 * ng].rearrange("p (j c) -> p j c", c=64)), r=[pk], w=["Vt32"])
                if RW_STOP <= "A":
                    return
                yield
                pw, pwk = nb_()
                pa, pak = nb_()
                pg, pgk = nb_()
                for j in range(ng):
                    for d in range(1 if os.environ.get("RW_B2") == "d0only" else 2):
                        TS.op("pe", lambda e, j=j, d=d: e.matmul(pw[:, (j * 2 + d) * 64:(j * 2 + d + 1) * 64], lhsT=TW[:, d, j * 128:(j + 1) * 128], rhs=LW[:, d, 0, :], start=True, stop=True), r=["TW", "LW"], w=[pwk])
                for j in range(ng):
                    for d in range(1 if os.environ.get("RW_B2") == "d0only" else 2):
                        TS.op("pe", lambda e, j=j, d=d: e.matmul(pa[:, (j * 2 + d) * 64:(j * 2 + d + 1) * 64], lhsT=DAs[:, d, j * 128:(j + 1) * 128], rhs=LW[:, d, 1, :], start=True, stop=True), r=["DAs", "LW"], w=[pak])
                for j in range(0 if os.environ.get("RW_B2") == "nogate" else ng):
                    TS.op("pe", lambda e, j=j: e.matmul(pg[:, j * 64:(j + 1) * 64], lhsT=SG0[:, j * 128:(j + 1) * 128], rhs=G2[:, 0, :], start=True, stop=False), r=["SG0", "G2"], w=[pgk])
                    TS.op("pe", lambda e, j=j: e.matmul(pg[:, j * 64:(j + 1) * 64], lhsT=SG1[:, j * 128:(j + 1) * 128], rhs=G2[0:32, 1, :], start=False, stop=True), r=["SG1", "G2"], w=[pgk])
                v4 = [128, ng, 2, 64]
                if os.environ.get("RW_B") == "1":
                    return
                TS.op("dve", lambda e: e.tensor_tensor(out=SGM[:, 0:ng], in0=pw[:, 0:128 * ng].rearrange("p (j d c) -> p j d c", d=2, c=64), in1=bc(prm[:, 0:2, :].unsqueeze(1), v4), op=ALU.add), r=[pwk, "prm"], w=["SGM"])
                TS.op("act", lambda e: e.activation(out=SGM[:, 0:ng], in_=SGM[:, 0:ng], func=AF.Sigmoid), r=["SGM"], w=["SGM"])
                if 2 * g + ng == nb and L < Lp:
                    jl = ng - 1
                    TS.op("dve", lambda e: e.tensor_scalar(out=SGM[:, jl], in0=SGM[:, jl], scalar1=vm[:, s_:s_ + 1], scalar2=None, op0=ALU.mult), r=["SGM", "vm"], w=["SGM"])
                TS.op("dve", lambda e: e.tensor_tensor(out=Aa[:, 0:ng], in0=pa[:, 0:128 * ng].rearrange("p (j d c) -> p j d c", d=2, c=64), in1=bc(prm[:, 2:4, :].unsqueeze(1), v4), op=ALU.add), r=[pak, "prm"], w=["Aa"])
                TS.op("act", lambda e: e.activation(out=Aa[:, 0:ng], in_=Aa[:, 0:ng], func=AF.Sigmoid), r=["Aa"], w=["Aa"])
                TS.op("act", lambda e: e.copy(out=EG[:, 0:ng, 0, :], in_=pg[:, 0:64 * ng].rearrange("p (j c) -> p j c", c=64)), r=[pgk], w=["EG"])
                if RW_STOP <= "B":
                    return
                yield
                r_ = RKtm[:, 0:ng, 0:64]
                k_ = RKtm[:, 0:ng, 64:128]
                v3 = [128, ng, 64]
                ew(lambda e: e.tensor_tensor(out=kkr[:, 0:ng], in0=k_, in1=bc(prm[:, 4:5, :], v3), op=ALU.mult), ["RKtm", "prm"], ["kkr"])
                ew(lambda e: e.tensor_tensor(out=sq[:, 0:ng], in0=kkr[:, 0:ng], in1=kkr[:, 0:ng], op=ALU.mult), ["kkr"], ["sq"])
                TS.op("dve", lambda e: e.tensor_reduce(out=n2[:, 0:ng], in_=sq[:, 0:ng], axis=AX.X, op=ALU.add), r=["sq"], w=["n2"])
                TS.op("act", lambda e: e.activation(out=n2[:, 0:ng], in_=n2[:, 0:ng], func=AF.Sqrt), r=["n2"], w=["n2"])
                TS.op("dve", lambda e: e.tensor_scalar(out=n2[:, 0:ng], in0=n2[:, 0:ng], scalar1=1e-12, scalar2=None, op0=ALU.max), r=["n2"], w=["n2"])
                TS.op("dve", lambda e: e.reciprocal(out=n2[:, 0:ng], in_=n2[:, 0:ng]), r=["n2"], w=["n2"])
                ew(lambda e: e.tensor_tensor(out=kk[:, 0:ng], in0=kkr[:, 0:ng], in1=bc(n2[:, 0:ng].unsqueeze(2), v3), op=ALU.mult), ["kkr", "n2"], ["kk"])
                ew(lambda e: e.tensor_tensor(out=t1[:, 0:ng], in0=Aa[:, 0:ng], in1=bc(prm[:, 5:6, :].unsqueeze(1), v4), op=ALU.mult), ["Aa", "prm"], ["t1"])
                ew(lambda e: e.tensor_tensor(out=t1[:, 0:ng], in0=t1[:, 0:ng], in1=bc(omk[:, :].unsqueeze(1).unsqueeze(1), v4), op=ALU.add), ["t1", "omk"], ["t1"])
                ew(lambda e: e.tensor_tensor(out=kd[:, 0:ng], in0=t1[:, 0:ng], in1=bc(k_.unsqueeze(2), v4), op=ALU.mult), ["t1", "RKtm"], ["kd"])
                ew(lambda e: e.tensor_tensor(out=be[:, 0:ng], in0=Aa[:, 0:ng], in1=bc(kk[:, 0:ng].unsqueeze(2), v4), op=ALU.mult), ["Aa", "kk"], ["be"])
                ew(lambda e: e.tensor_tensor(out=sq[:, 0:ng], in0=kd[:, 0:ng, 0, :], in1=kd[:, 0:ng, 1, :], op=ALU.add), ["kd"], ["sq"])
                ew(lambda e: e.tensor_tensor(out=sq[:, 0:ng], in0=sq[:, 0:ng], in1=r_, op=ALU.mult), ["sq", "RKtm"], ["sq"])
                ew(lambda e: e.tensor_tensor(out=sq[:, 0:ng], in0=sq[:, 0:ng], in1=bc(prm[:, 8:9, :], v3), op=ALU.mult), ["sq", "prm"], ["sq"])
                TS.op("dve", lambda e: e.tensor_reduce(out=n2[:, 4:4 + ng], in_=sq[:, 0:ng], axis=AX.X, op=ALU.add), r=["sq"], w=["n2b"])
                ew(lambda e: e.tensor_tensor(out=sq[:, 0:ng], in0=Vt32[:, 0:ng], in1=bc(n2[:, 4:4 + ng].unsqueeze(2), v3), op=ALU.mult), ["Vt32", "n2b"], ["sq"])
                ew(lambda e: e.tensor_tensor(out=EG[:, 0:ng, 1, :], in0=sq[:, 0:ng], in1=EG[:, 0:ng, 0, :], op=ALU.mult), ["sq", "EG"], ["EG"])
                TS.dma("pool", EPI[cc0:cc0 + ncol, :, :].rearrange("(j p) a c -> p j a c", p=128), EG[:, 0:ng], r=["EG"], w=[("EPI", g)])
                if RW_STOP <= "C":
                    return
                yield
                pcs = []
                for kind, (mf, mb_) in enumerate(((0, 1), (2, 3), (3, 2), (None, None))):
                    pc, pck = nb_()
                    for d in range(2):
                        m = (mf, mb_)[d]
                        lhs = onesf[:, :] if m is None else MK[:, m, :]
                        TS.op("pe", lambda e, d=d, lhs=lhs: e.matmul(pc[:, d * 64 * ng:(d + 1) * 64 * ng], lhsT=lhs, rhs=SGM[:, 0:ng, d, :], start=True, stop=True), r=["SGM", "MK", "onesf"], w=[pck])
                    pcs.append((pc, pck))
                for i, (kind, sgn) in enumerate(((0, -1.0), (0, 1.0), (1, -1.0), (2, -1.0), (3, -1.0))):
                    pc, pck = pcs[kind]
                    TS.op("act", lambda e, i=i, pc=pc, sgn=sgn: e.activation(out=EX[i][:, :, 0:ng, :], in_=pc[:, 0:128 * ng].rearrange("p (d j c) -> p d j c", d=2, c=64), func=AF.Exp, scale=sgn * c_), r=[pck], w=[("EX", i)])
                Ep, Em, Ex_, Eh, PCb = [EX[i][:, :, 0:ng, :].rearrange("p d j c -> p j d c") for i in range(5)]
                ew(lambda e: e.tensor_tensor(out=sc[0][:, 0:ng], in0=Ep, in1=bc(r_.unsqueeze(2), v4), op=ALU.mult), [("EX", 0), "RKtm"], [("sc", 0)])
                ew(lambda e: e.tensor_tensor(out=sc[1][:, 0:ng], in0=kd[:, 0:ng], in1=Em, op=ALU.mult), [("EX", 1), "kd"], [("sc", 1)])
                ew(lambda e: e.tensor_tensor(out=sc[2][:, 0:ng], in0=be[:, 0:ng], in1=Em, op=ALU.mult), [("EX", 1), "be"], [("sc", 2)])
                ew(lambda e: e.tensor_scalar(out=kkr[:, 0:ng], in0=kk[:, 0:ng], scalar1=-1.0, scalar2=None, op0=ALU.mult), ["kk"], ["kkr"])
                ew(lambda e: e.tensor_tensor(out=TA[:, 0:nq, 0:64].rearrange("p (j d) c -> p j d c", d=2), in0=Ex_, in1=bc(kkr[:, 0:ng].unsqueeze(2), v4), op=ALU.mult), [("EX", 2), "kkr"], ["TAa"])
                ew(lambda e: e.tensor_tensor(out=sc[3][:, 0:ng], in0=be[:, 0:ng], in1=Eh, op=ALU.mult), [("EX", 3), "be"], [("sc", 3)])
                ew(lambda e: e.tensor_tensor(out=sc[4][:, 0:ng], in0=kd[:, 0:ng], in1=Eh, op=ALU.mult), [("EX", 3), "kd"], [("sc", 4)])
                if RW_STOP <= "D":
                    return
                yield
                srcs = [(lambda q: TA[:, q, 0:64], "TAa"), (lambda q: sc[0][:, q // 2, q % 2, :], ("sc", 0)), (lambda q: sc[2][:, q // 2, q % 2, :], ("sc", 2)), (lambda q: sc[1][:, q // 2, q % 2, :], ("sc", 1))]
                for a, (fsrc, skey) in enumerate(srcs):
                    pb, pk = nb_()
                    pbv = pb[:].bitcast(BF16)
                    for q in range(nq):
                        TS.op("pe", lambda e, q=q: e.transpose(out=pbv[0:64, q * 128:(q + 1) * 128], in_=fsrc(q), identity=ident[:]), r=[skey, "ident"], w=[pk])
                    if a % 2 == 0:
                        TS.op("act", lambda e: e.copy(out=FT[:, 0:nq, a, :], in_=pbv[0:64, 0:128 * nq].rearrange("p (q c) -> p q c", c=128)), r=[pk], w=[("FT", a)])
                    else:
                        TS.op("dve", lambda e: e.tensor_copy(out=FT[:, 0:nq, a, :], in_=pbv[0:64, 0:128 * nq].rearrange("p (q c) -> p q c", c=128)), r=[pk], w=[("FT", a)])
                FK = [("FT", a) for a in range(4)]
                if RW_STOP <= "E":
                    return
                yield
                for j in range(ng):
                    p1, p1k = nb_()
                    p2, p2k = nb_()
                    for d in range(2):
                        q = 2 * j + d
                        TS.op("pe", lambda e, q=q, d=d: e.matmul(p1[:, d * 256:(d + 1) * 256], lhsT=FT[:, q, 2, :], rhs=FT[:, q, 0:2, :], start=True, stop=True), r=FK, w=[p1k])
                        TS.op("pe", lambda e, q=q, d=d: e.matmul(p2[:, d * 256:(d + 1) * 256], lhsT=FT[:, q, 3, :], rhs=FT[:, q, 0:2, :], start=True, stop=True), r=FK, w=[p2k])
                    TS.op("dve", lambda e, j=j: e.tensor_tensor(out=QM[:, 2 * j:2 * j + 2], in0=p1[:, :].rearrange("p (d a c) -> p d a c", d=2, a=2), in1=MP1[:], op=ALU.mult), r=[p1k, "MP"], w=["QM"])
                    TS.op("dve", lambda e, j=j: e.tensor_tensor(out=KM[:, 2 * j:2 * j + 2], in0=p2[:, :].rearrange("p (d a c) -> p d a c", d=2, a=2), in1=MP1[:], op=ALU.mult), r=[p2k, "MP"], w=["KM"])
                for jj in range(0, ng, 2):
                    p3, p3k = nb_()
                    njj = min(2, ng - jj)
                    for j in range(jj, jj + njj):
                        for d in range(2):
                            q = 2 * j + d
                            TS.op("pe", lambda e, q=q, j=j, d=d: e.matmul(p3[:, ((j - jj) * 2 + d) * 128:((j - jj) * 2 + d + 1) * 128], lhsT=FT[:, q, 0, :], rhs=FT[:, q, 2, :], start=True, stop=True), r=FK, w=[p3k])
                    TS.op("dve", lambda e: e.tensor_tensor(out=PP[0][:, 2 * jj:2 * jj + 2 * njj].rearrange("p (j d) c -> p j d c", d=2), in0=p3[:, 0:256 * njj].rearrange("p (j d c) -> p j d c", d=2, c=128), in1=bc(MP3[:].unsqueeze(1), [128, njj, 2, 128]), op=ALU.mult), r=[p3k, "MP"], w=[("PP", 0)])
                if RW_STOP <= "F":
                    return
                yield
                TS.op("pool", lambda e: e.tensor_copy(out=QQ[0][:, 0:nq], in_=QM[:, 0:nq, 0, :]), r=["QM"], w=[("QQ", 0)])
                TS.op("dve", lambda e: e.tensor_tensor(out=TT[0][:, 0:nq], in0=QM[:, 0:nq, 0, :], in1=bc(ident[:, :].unsqueeze(1), [128, nq, 128]), op=ALU.add), r=["QM", "ident"], w=[("TT", 0)])
                cur = 0
                for lev in range(6):
                    nx = 1 - cur
                    for q0 in range(0, nq, 4):
                        nqq = min(4, nq - q0)
                        pp_, ppk = nb_()
                        pq_, pqk = nb_()
                        for q in range(q0, q0 + nqq):
                            TS.op("pe", lambda e, q=q: e.matmul(pp_[:, (q - q0) * 128:(q - q0 + 1) * 128], lhsT=QQ[cur][:, q, :], rhs=PP[cur][:, q, :], start=True, stop=True), r=[("QQ", cur), ("PP", cur)], w=[ppk])
                        for q in range(q0, q0 + nqq):
                            TS.op("pe", lambda e, q=q: e.matmul(pq_[:, (q - q0) * 128:(q - q0 + 1) * 128], lhsT=PP[cur][:, q, :], rhs=QQ[cur][:, q, :], start=True, stop=True), r=[("QQ", cur), ("PP", cur)], w=[pqk])
                        TS.op("act", lambda e: e.copy(out=PP[nx][:, q0:q0 + nqq], in_=pp_[:, 0:128 * nqq].rearrange("p (q c) -> p q c", c=128)), r=[ppk], w=[("PP", nx)])
                        TS.op("dve", lambda e: e.tensor_copy(out=QQ[nx][:, q0:q0 + nqq], in_=pq_[:, 0:128 * nqq].rearrange("p (q c) -> p q c", c=128)), r=[pqk], w=[("QQ", nx)])
                    yield
                    for q0 in range(0, nq, 4):
                        nqq = min(4, nq - q0)
                        pt_, ptk = nb_()
                        for q in range(q0, q0 + nqq):
                            TS.op("pe", lambda e, q=q: e.matmul(pt_[:, (q - q0) * 128:(q - q0 + 1) * 128], lhsT=PP[nx][:, q, :], rhs=TT[cur][:, q, :], start=True, stop=True), r=[("PP", nx), ("TT", cur)], w=[ptk])
                        TS.op("dve", lambda e: e.tensor_tensor(out=TT[nx][:, q0:q0 + nqq], in0=pt_[:, 0:128 * nqq].rearrange("p (q c) -> p q c", c=128), in1=TT[cur][:, q0:q0 + nqq], op=ALU.add), r=[ptk, ("TT", cur)], w=[("TT", nx)])
                    cur = nx
                    yield
                TTf = TT[cur]
                TTK = ("TT", cur)
                if RW_STOP <= "G":
                    return
                yield
                yield
                pb, pk = nb_()
                for q in range(nq):
                    TS.op("pe", lambda e, q=q: e.matmul(pb[:, q * 64:(q + 1) * 64], lhsT=KM[:, q, 0, :], rhs=Vtm[:, q // 2, :], start=True, stop=True), r=["KM", "Vtm"], w=[pk])
                TS.op("act", lambda e: e.copy(out=TA[:, 0:nq, 64:128], in_=pb[:, 0:64 * nq].rearrange("p (q c) -> p q c", c=64)), r=[pk], w=["TAx"])
                for q0 in range(0, nq, 4):
                    nqq = min(4, nq - q0)
                    pb, pk = nb_()
                    for q in range(q0, q0 + nqq):
                        TS.op("pe", lambda e, q=q: e.matmul(pb[:, (q - q0) * 128:(q - q0 + 1) * 128], lhsT=TTf[:, q, :], rhs=TA[:, q, :], start=True, stop=True), r=[TTK, "TAa", "TAx"], w=[pk])
                    TS.op("act", lambda e: e.copy(out=WU[:, q0:q0 + nqq], in_=pb[:, 0:128 * nqq].rearrange("p (q c) -> p q c", c=128)), r=[pk], w=["WU"])
                for q0 in range(0, nq, 4):
                    nqq = min(4, nq - q0)
                    pb, pk = nb_()
                    for q in range(q0, q0 + nqq):
                        TS.op("pe", lambda e, q=q: e.matmul(pb[0:64, (q - q0) * 128:(q - q0 + 1) * 128], lhsT=WU[:, q, 0:64], rhs=QM[:, q, 1, :], start=True, stop=True), r=["WU", "QM"], w=[pk])
                    TS.op("dve", lambda e: e.tensor_tensor(out=GT[:, 2 * g + q0 // 2:2 * g + (q0 + nqq) // 2].rearrange("p j d c -> p (j d) c"), in0=pb[0:64, 0:128 * nqq].rearrange("p (q c) -> p q c", c=128), in1=FT[:, q0:q0 + nqq, 1, :], op=ALU.add), r=[pk, ("FT", 1)], w=["GT"])
                yield
                pb, pk = nb_()
                for j in range(ng):
                    for d in range(2):
                        q = 2 * j + d
                        TS.op("pe", lambda e, q=q, j=j, d=d: e.matmul(pb[:, j * 64:(j + 1) * 64], lhsT=QM[:, q, 1, :], rhs=WU[:, q, 64:128], start=(d == 0), stop=False), r=["QM", "WU"], w=[pk])
                        TS.op("pe", lambda e, q=q, j=j, d=d: e.matmul(pb[:, j * 64:(j + 1) * 64], lhsT=KM[:, q, 1, :], rhs=Vtm[:, j, :], start=False, stop=(d == 1)), r=["KM", "Vtm"], w=[pk])
                TS.op("act", lambda e: e.copy(out=Yacc[:, 2 * g:2 * g + ng, :], in_=pb[:, 0:64 * ng].rearrange("p (j c) -> p j c", c=64)), r=[pk], w=["Yacc"])
                yield
                pb, pk = nb_()
                for q in range(nq):
                    TS.op("pe", lambda e, q=q: e.matmul(pb[0:64, q * 64:(q + 1) * 64], lhsT=WU[:, q, 0:64], rhs=sc[3][:, q // 2, q % 2, :], start=True, stop=True), r=["WU", ("sc", 3)], w=[pk])
                TS.op("pool", lambda e: e.tensor_tensor(out=tmpd[:, 0:nq].rearrange("p (j d) c -> p j d c", d=2), in0=PCb[0:64], in1=bc(identf[0:64, 0:64].unsqueeze(1).unsqueeze(1), [64, ng, 2, 64]), op=ALU.mult), r=[("EX", 4), "identf"], w=["tmpd"])
                TS.op("dve", lambda e: e.tensor_tensor(out=PHIT[:, 2 * g:2 * g + ng].rearrange("p j d c -> p (j d) c"), in0=pb[0:64, 0:64 * nq].rearrange("p (q c) -> p q c", c=64), in1=tmpd[:, 0:nq], op=ALU.add), r=[pk, "tmpd"], w=["PHIT"])
                yield
                pb, pk = nb_()
                for q in range(nq):
                    TS.op("pe", lambda e, q=q: e.matmul(pb[0:64, q * 64:(q + 1) * 64], lhsT=sc[3][:, q // 2, q % 2, :], rhs=WU[:, q, 64:128], start=True, stop=False), r=["WU", ("sc", 3)], w=[pk])
                    TS.op("pe", lambda e, q=q: e.matmul(pb[0:64, q * 64:(q + 1) * 64], lhsT=sc[4][:, q // 2, q % 2, :], rhs=Vtm[:, q // 2, :], start=False, stop=True), r=["Vtm", ("sc", 4)], w=[pk])
                TS.op("act", lambda e: e.copy(out=PSI[:, 2 * g:2 * g + ng].rearrange("p j d c -> p (j d) c"), in_=pb[0:64, 0:64 * nq].rearrange("p (q c) -> p q c", c=64)), r=[pk], w=["PSI"])

            return group

        threads = [make_thread(0), make_thread(1)]
        for s_ in range(2):
            base, L, Lp, nb = cfg.BASE[s_], cfg.LS[s_], cfg.LP[s_], cfg.NB[s_]
            ngrp = -(-nb // 4)
            for h in range(NH):
                S.dma("sp", prm[:], rwp[:, l, h, :, :], w=["prm"])
                S.dma("act", LWf[:], lw_in[l, h].rearrange("(d k) a c -> k d a c", d=2), w=["LWf"])
                S.dma("sp", G2f[:, 0, :], g2_in[l, h, 0:128, :], w=["G2f"])
                S.dma("act", G2f[0:32, 1, :], g2_in[l, h, 128:160, :], w=["G2f"])
                S.op("dve", lambda e: e.tensor_copy(out=LW[:], in_=LWf[:]), r=["LWf"], w=["LW"])
                S.op("dve", lambda e: e.tensor_copy(out=G2[:, 0, :], in_=G2f[:, 0, :]), r=["G2f"], w=["G2"])
                S.op("dve", lambda e: e.tensor_copy(out=G2[0:32, 1, :], in_=G2f[0:32, 1, :]), r=["G2f"], w=["G2"])
                S.op("dve", lambda e: e.tensor_scalar(out=omk[:], in0=prm[:, 5, :], scalar1=-1.0, scalar2=1.0, op0=ALU.mult, op1=ALU.add), r=["prm"], w=["omk"])
                gens = [threads[g % 2](s_, h, g, base, L, Lp, nb) for g in range(-(-nb // 2))]
                for gi in range(0, len(gens), 2):
                    pair = gens[gi:gi + 2]
                    live = list(pair)
                    while live:
                        for gn in list(live):
                            try:
                                next(gn)
                            except StopIteration:
                                live.remove(gn)
                if RW_STOP <= "H":
                    continue
                S.op("pool", lambda e: e.memset(Sb[0][:], 0.0), w=[("Sb", 0)])
                cur = 0
                for i in range(nb):
                    nx = 1 - cur
                    for d, c in ((0, i), (1, nb - 1 - i)):
                        py, pyk = nb_()
                        S.op("pe", lambda e, d=d, c=c: e.matmul(py[:, 0:64], lhsT=GT[:, c, d, :], rhs=Sb[cur][:, d, :], start=True, stop=True), r=["GT", ("Sb", cur)], w=[pyk])
                        S.op("pe", lambda e, d=d, c=c: e.matmul(py[0:64, 64:128], lhsT=PHIT[:, c, d, :], rhs=Sb[cur][:, d, :], start=True, stop=True), r=["PHIT", ("Sb", cur)], w=[pyk])
                        S.op("dve", lambda e, d=d, c=c: e.tensor_tensor(out=Sb[nx][:, d, :], in0=py[0:64, 64:128], in1=PSI[:, c, d, :], op=ALU.add), r=[pyk, "PSI"], w=[("Sb", nx)])
                        S.op("dve", lambda e, d=d, c=c: e.tensor_tensor(out=Yacc[:, c, :], in0=py[:, 0:64], in1=Yacc[:, c, :], op=ALU.add), r=[pyk, "Yacc"], w=["Yacc"])
                    cur = nx
                if RW_STOP <= "I":
                    continue
                for g in range(ngrp):
                    ng = min(4, nb - 4 * g)
                    ncol = 128 * ng
                    cc0 = base + 512 * g
                    v3 = [128, ng, 64]
                    S.dma("sp", egl[:, 0:ng], EPI[cc0:cc0 + ncol, :, :].rearrange("(j p) a c -> p j a c", p=128), r=[("EPI", 2 * g), ("EPI", 2 * g + 1)], w=["egl"])
                    y_ = Yacc[:, 4 * g:4 * g + ng, :]
                    S.op("dve", lambda e: e.tensor_reduce(out=es1[:, 0:ng], in_=y_, axis=AX.X, op=ALU.add), r=["Yacc"], w=["es1"])
                    S.op("dve", lambda e: e.tensor_scalar(out=es1[:, 0:ng], in0=es1[:, 0:ng], scalar1=-1.0 / 64, scalar2=None, op0=ALU.mult), r=["es1"], w=["es1"])
                    ew(lambda e: e.tensor_tensor(out=ey[:, 0:ng], in0=y_, in1=bc(es1[:, 0:ng].unsqueeze(2), v3), op=ALU.add), ["Yacc", "es1"], ["ey"])
                    ew(lambda e: e.tensor_tensor(out=eo[:, 0:ng], in0=ey[:, 0:ng], in1=ey[:, 0:ng], op=ALU.mult), ["ey"], ["eo"])
                    S.op("dve", lambda e: e.tensor_reduce(out=es1[:, 4:4 + ng], in_=eo[:, 0:ng], axis=AX.X, op=ALU.add), r=["eo"], w=["es2"])
                    S.op("dve", lambda e: e.tensor_scalar(out=es1[:, 4:4 + ng], in0=es1[:, 4:4 + ng], scalar1=1.0 / 64, scalar2=LNX_EPS, op0=ALU.mult, op1=ALU.add), r=["es2"], w=["es2"])
                    S.op("act", lambda e: e.activation(out=es1[:, 4:4 + ng], in_=es1[:, 4:4 + ng], func=AF.Sqrt), r=["es2"], w=["es2"])
                    S.op("dve", lambda e: e.reciprocal(out=es1[:, 4:4 + ng], in_=es1[:, 4:4 + ng]), r=["es2"], w=["es2"])
                    ew(lambda e: e.tensor_tensor(out=ey[:, 0:ng], in0=ey[:, 0:ng], in1=bc(es1[:, 4:4 + ng].unsqueeze(2), v3), op=ALU.mult), ["ey", "es2"], ["ey"])
                    ew(lambda e: e.tensor_tensor(out=ey[:, 0:ng], in0=ey[:, 0:ng], in1=bc(prm[:, 6:7, :], v3), op=ALU.mult), ["ey", "prm"], ["ey"])
                    ew(lambda e: e.tensor_tensor(out=ey[:, 0:ng], in0=ey[:, 0:ng], in1=bc(prm[:, 7:8, :], v3), op=ALU.add), ["ey", "prm"], ["ey"])
                    ew(lambda e: e.tensor_tensor(out=ey[:, 0:ng], in0=ey[:, 0:ng], in1=egl[:, 0:ng, 0, :], op=ALU.mult), ["ey", "egl"], ["ey"])
                    ew(lambda e: e.tensor_tensor(out=eo[:, 0:ng], in0=ey[:, 0:ng], in1=egl[:, 0:ng, 1, :], op=ALU.add), ["ey", "egl"], ["eo"])
                    pb, pk = nb_()
                    for j in range(ng):
                        S.op("pe", lambda e, j=j: e.transpose(out=pb[0:64, j * 128:(j + 1) * 128], in_=eo[:, j, :], identity=identf[:]), r=["eo", "identf"], w=[pk])
                    S.op("act", lambda e: e.copy(out=eob[:, 0:ncol], in_=pb[0:64, 0:ncol]), r=[pk], w=["eob"])
                    S.dma("pool", MIXT[512 + 64 * h:512 + 64 * h + 64, cc0:cc0 + ncol], eob[:, 0:ncol], r=["eob"], w=[("MIXT", h, g)])
        S.barrier()
        st.close()

    for l in range(NL):
        ffn_pass(l, 0, 0, True)
        ffn_pass(l, 0, 1, False)
        if cfg.stages != "ffn":
            proj_pass(l)
            if cfg.stages != "proj":
                shift_phase(l)
                if cfg.stages != "shift":
                    if cfg.stages != "rwkv":
                        attn_phase(l)
                    if cfg.stages != "attn":
                        rwkv_phase(l)
        ffn_pass(l, 1, 0, True)
        ffn_pass(l, 1, 1, False, final=(l == NL - 1))

    S.mark("end")
    S.finish()
    cst.close()
    es.close()
    return nc, S


def host_maps(cfg, inp, xs_list):
    NL = cfg.NL
    f32 = np.float32
    common = {}
    common["meta"] = np.ascontiguousarray(inp["meta_tokens"], f32)
    for k, nm in ((1, "ffn1"), (2, "ffn2")):
        common[f"ffn{k}_wg"] = np.ascontiguousarray(inp[f"{nm}_w_gate"][:NL], f32)
        common[f"ffn{k}_wu"] = np.ascontiguousarray(inp[f"{nm}_w_up"][:NL], f32)
        common[f"ffn{k}_wd"] = np.ascontiguousarray(inp[f"{nm}_w_down"][:NL], f32)
    g = np.stack([np.asarray(inp["ffn1_norm"][:NL], f32), np.asarray(inp["mix_norm"][:NL], f32), np.asarray(inp["ffn2_norm"][:NL], f32)], axis=1)
    common["gains"] = np.ascontiguousarray(g.reshape(NL, 3, 8, 128).transpose(3, 0, 1, 2))
    common["fnorm"] = np.ascontiguousarray(np.broadcast_to(np.asarray(inp["final_norm"], f32)[None, :], (128, D)))
    common["ident_bf"] = np.eye(128, dtype=f32).astype(ml_dtypes.bfloat16)
    common["zeros"] = np.zeros((128, D), f32)
    bf = ml_dtypes.bfloat16
    ih = [(i, h) for i in range(16) for h in range(NH)]
    perm = list(range(0, 384)) + [384 + i for (i, h) in ih] + [400 + i for (i, h) in ih] + list(range(416, 2368))
    assert len(perm) == NCOLP
    common["w_in"] = np.ascontiguousarray(np.asarray(inp["w_in"][:NL], f32)[:, :, perm])
    pq = [96 * h + j for h in range(NH) for j in range(64)] + [96 * h + 64 + i for (i, h) in ih] + [96 * h + 80 + i for (i, h) in ih]
    common["w_uq"] = np.ascontiguousarray(np.asarray(inp["w_uq"][:NL], f32)[:, :, pq])
    pkv = [128 * h + j for h in range(NH) for j in range(64)] + [128 * h + 64 + j for h in range(NH) for j in range(64)]
    common["w_ukv"] = np.ascontiguousarray(np.asarray(inp["w_ukv"][:NL], f32)[:, :, pkv])
    common["w_out"] = np.ascontiguousarray(inp["w_out"][:NL], f32)
    qn = np.asarray(inp["q_norm"][:NL], f32)
    kvn = np.asarray(inp["kv_norm"][:NL], f32)
    common["qkg"] = np.ascontiguousarray(np.stack([qn[:, 0:128], qn[:, 128:256], kvn], axis=-1).transpose(1, 0, 2))
    seln = np.zeros((128, 4, 32), f32)
    for row in range(128):
        for c in range(4):
            seln[row, c, 2 * c + row // 64] = 1
    selr = np.zeros((128, 32), f32)
    for row in range(128):
        selr[row, row % 8] = 1
    common["seln"] = seln.astype(bf)
    common["selr"] = selr.astype(bf)
    common["ones_bf"] = np.ones((128, 512), f32).astype(bf)
    common["ones_f"] = np.ones((128, 128), f32)
    common["ones_row"] = np.ones((NH, 512), f32).astype(bf)
    common["ident_f"] = np.eye(128, dtype=f32)
    idx = np.arange(128)
    rowi, coli = idx[:, None], idx[None, :]
    common["masks"] = np.ascontiguousarray(np.stack([rowi <= coli, rowi >= coli, rowi < coli, rowi > coli], axis=1).astype(f32))
    inv = (1.0 / (np.float32(10000.0) ** (np.arange(0, 32, 2, dtype=f32) / np.float32(32)))).astype(f32)
    pos = np.concatenate([np.arange(lp, dtype=f32) for lp in cfg.LP])
    ang = (pos[:, None] * inv[None, :]).astype(f32)
    common["cosT"] = np.ascontiguousarray(np.repeat(np.cos(ang).T.astype(f32), NH, axis=0))
    common["sinT"] = np.ascontiguousarray(np.repeat(np.sin(ang).T.astype(f32), NH, axis=0))
    smu = np.asarray(inp["shift_mu"][:NL], f32)
    mu = np.zeros((128, NL, 16, 2), f32)
    roff0 = GROUPS["r0"][0]
    for gi, name in enumerate(RW_GROUPS):
        off, wdt = GROUPS[name]
        mu[0:wdt, :, gi, :] = smu[:, :, off - roff0:off - roff0 + wdt].transpose(2, 0, 1)
    common["mu"] = mu
    common.update(rwkv_host(cfg, inp))
    maps = []
    for c in range(len(xs_list)):
        m = dict(common)
        m["x0"] = np.ascontiguousarray(xs_list[c][0], f32)
        m["x1"] = np.ascontiguousarray(xs_list[c][1], f32)
        maps.append(m)
    return maps


def rwkv_host(cfg, inp):
    NL = cfg.NL
    f32 = np.float32
    out = {}
    rwp = np.zeros((128, NL, NH, 9, 64), f32)
    lw = np.zeros((NL, NH, 128, 2, 64), f32)
    g2 = np.zeros((NL, NH, 160, 64), f32)
    for l in range(NL):
        for h in range(NH):
            hs = slice(64 * h, 64 * h + 64)
            rows = [inp["decay_w0"][l, 0, hs], inp["decay_w0"][l, 1, hs], inp["iclr_a0"][l, 0, hs], inp["iclr_a0"][l, 1, hs],
                    inp["key_k_k"][l, hs], inp["key_k_a"][l, hs], inp["lnx_w"][l, hs], inp["lnx_b"][l, hs], inp["bonus_r_k"][l, h, :]]
            rwp[:, l, h, :, :] = np.stack([np.asarray(r_, f32) for r_ in rows], 0)[None]
            for d in range(2):
                lw[l, h, 64 * d:64 * d + 64, 0, :] = inp["decay_w2"][l, d, :, hs]
                lw[l, h, 64 * d:64 * d + 64, 1, :] = inp["iclr_a2"][l, d, :, hs]
            g2[l, h] = inp["gate_g2"][l, :, hs]
    out["rwp"] = rwp
    out["lw"] = lw
    out["g2"] = g2
    vm = np.zeros((128, 2), f32)
    for s_ in range(2):
        nvalid = cfg.LS[s_] - 128 * (cfg.NB[s_] - 1)
        vm[:nvalid, s_] = 1
    out["vmask"] = vm
    return out


_CACHE = {}


def kernel(**inp):
    cfg = Cfg()
    if "nc" not in _CACHE:
        _CACHE["nc"] = build(cfg)[0]
    nc = _CACHE["nc"]
    xp = np.asarray(inp["x_prompt"])
    xsm = np.asarray(inp["x_sample"])
    xs_list = [(xsm[c], xp[c % 2]) for c in range(8)]
    maps = host_maps(cfg, inp, xs_list)
    res = run_bass_kernel_spmd(nc, maps, core_ids=list(range(8)))
    y_s = np.stack([np.asarray(res.results[c]["y0"], np.float32) for c in range(8)], axis=0)
    y_p = np.stack([np.asarray(res.results[c]["y1"], np.float32) for c in range(2)], axis=0)
    return (y_p, y_s)
```
